# Optimizing a Trainium2 kernel written in Bass

```python
import math
import jax
import jax.numpy as jnp
from jax import lax
import numpy as np

D_MODEL = 1024
BATCH = 8
SEQ = 4096
DEPTH = 4

N_MIXERS = 2
N_A_LAYERS = (DEPTH + N_MIXERS - 1) // N_MIXERS
N_B_LAYERS = DEPTH // N_MIXERS
RMS_EPS = 1e-6
D_FF = 2816

DIL_PAIRS = ((128, 1), (512, 4), (2048, 16))
N_GROUPS = len(DIL_PAIRS)
HEADS_PER_GROUP = 4
HEAD_DIM_A = 128
ATTN_WIDTH = N_GROUPS * HEADS_PER_GROUP * HEAD_DIM_A
ROPE_THETA = 10000.0

GDN_K_HEADS = 8
GDN_V_HEADS = 16
GDN_HEAD_DIM = 128
GDN_KD = GDN_K_HEADS * GDN_HEAD_DIM
GDN_VD = GDN_V_HEADS * GDN_HEAD_DIM
GDN_CONV_DIM = 2 * GDN_KD + GDN_VD
GDN_PROJ = GDN_CONV_DIM + GDN_VD + 2 * GDN_V_HEADS
GDN_CONV = 4
GDN_CHUNK = 64

kernel_name = "hybrid_dilated_attn_gated_deltanet_macaron"


def rms_norm(x, w):
    xf = x.astype(jnp.float32)
    y = xf * lax.rsqrt(jnp.mean(xf * xf, axis=-1, keepdims=True) + RMS_EPS)
    return (y * w.astype(jnp.float32)).astype(x.dtype)


def swiglu(h, w_in, w_out):
    gu = h @ w_in
    gate, up = gu[..., :D_FF], gu[..., D_FF:]
    return (jax.nn.silu(gate) * up) @ w_out


def rope_tables(seq, dim):
    inv_freq = 1.0 / (ROPE_THETA ** (jnp.arange(0, dim, 2, dtype=jnp.float32) / dim))
    ang = jnp.arange(seq, dtype=jnp.float32)[:, None] * inv_freq[None, :]
    return jnp.cos(ang), jnp.sin(ang)


def apply_rope(x, cos, sin):
    xf = x.astype(jnp.float32)
    half = xf.shape[-1] // 2
    x1, x2 = xf[..., :half], xf[..., half:]
    c, s = cos[None, :, None, :], sin[None, :, None, :]
    return jnp.concatenate([x1 * c - x2 * s, x2 * c + x1 * s], axis=-1).astype(x.dtype)


def dilated_band_attention(q, k, v, dilation, span):
    B, S, H, E = q.shape
    Ls = S // dilation
    nb = -(-Ls // span)
    Lp = nb * span

    def fold(t):
        t = t.reshape(B, Ls, dilation, H, E).transpose(0, 2, 3, 1, 4)
        t = jnp.pad(t, ((0, 0), (0, 0), (0, 0), (0, Lp - Ls), (0, 0)))
        return t.reshape(B, dilation, H, nb, span, E)

    def with_prev(t):
        prev = jnp.pad(t[:, :, :, :-1], ((0, 0), (0, 0), (0, 0), (1, 0), (0, 0), (0, 0)))
        return jnp.concatenate([prev, t], axis=4)

    qb = fold(q)
    kw = with_prev(fold(k))
    vw = with_prev(fold(v))
    s = jnp.einsum('bdhnqe,bdhnke->bdhnqk', qb, kw).astype(jnp.float32)
    blk = jnp.arange(nb)[:, None, None]
    qi = jnp.arange(span)[None, :, None] + span
    kj = jnp.arange(2 * span)[None, None, :]
    dist = qi - kj
    mask = (dist >= 0) & (dist <= span) & (blk * span - span + kj >= 0)
    s = jnp.where(mask, s, -jnp.inf)
    m = jnp.max(s, axis=-1, keepdims=True)
    p = jnp.exp(s - m)
    den = jnp.sum(p, axis=-1, keepdims=True)
    o = jnp.einsum('bdhnqk,bdhnke->bdhnqe', (p / den).astype(v.dtype), vw)
    lse = (m + jnp.log(den))[..., 0]
    o = o.reshape(B, dilation, H, Lp, E)[:, :, :, :Ls].transpose(0, 3, 1, 2, 4).reshape(B, S, H, E)
    lse = lse.reshape(B, dilation, H, Lp)[:, :, :, :Ls].transpose(0, 3, 1, 2).reshape(B, S, H)
    return o, lse


def dilated_attention(h, w_in, w_out, cos, sin):
    B, S, _ = h.shape
    qkv = (h @ w_in).reshape(B, S, 3, N_GROUPS * HEADS_PER_GROUP, HEAD_DIM_A)
    q = apply_rope(qkv[:, :, 0], cos, sin) * (HEAD_DIM_A ** -0.5)
    k = apply_rope(qkv[:, :, 1], cos, sin)
    v = qkv[:, :, 2]
    gshape = (B, S, N_GROUPS, HEADS_PER_GROUP, HEAD_DIM_A)
    q, k, v = q.reshape(gshape), k.reshape(gshape), v.reshape(gshape)
    outs, lses = [], []
    for g, (window, dil) in enumerate(DIL_PAIRS):
        o, l = dilated_band_attention(q[:, :, g], k[:, :, g], v[:, :, g], dil, window // dil)
        outs.append(o)
        lses.append(l)
    alpha = jax.nn.softmax(jnp.stack(lses, axis=2), axis=2)
    o = jnp.stack(outs, axis=2) * alpha[..., None].astype(v.dtype)
    return o.reshape(B, S, ATTN_WIDTH) @ w_out


def causal_depthwise_conv(x, w):
    K = w.shape[0]
    S = x.shape[1]
    xp = jnp.pad(x, ((0, 0), (K - 1, 0), (0, 0)))
    y = xp[:, 0:S] * w[0]
    for j in range(1, K):
        y = y + xp[:, j:j + S] * w[j]
    return y


def l2norm(x):
    xf = x.astype(jnp.float32)
    return xf * lax.rsqrt(jnp.sum(xf * xf, axis=-1, keepdims=True) + 1e-6)


def chunk_gated_delta_rule(q, k, v, g, beta):
    B, S, H, DK = q.shape
    DV = v.shape[-1]
    C = GDN_CHUNK
    nc = S // C

    def chunks4(t):
        return t.reshape(B, nc, C, H, t.shape[-1]).transpose(1, 0, 3, 2, 4)

    def chunks3(t):
        return t.reshape(B, nc, C, H).transpose(1, 0, 3, 2)

    qc, kc, vc = chunks4(q), chunks4(k), chunks4(v)
    gc, bc = chunks3(g), chunks3(beta)
    gcum = jnp.cumsum(gc, axis=-1)
    idx = jnp.arange(C)
    incl = idx[:, None] >= idx[None, :]
    strict = idx[:, None] > idx[None, :]
    diff = gcum[..., :, None] - gcum[..., None, :]
    decay = jnp.where(incl, jnp.exp(jnp.where(incl, diff, 0.0)), 0.0)
    kk = jnp.einsum('nbhie,nbhje->nbhij', kc, kc)
    L = jnp.where(strict, bc[..., :, None] * kk * decay, 0.0)
    rhs = jnp.concatenate([vc * bc[..., None], kc * (bc * jnp.exp(gcum))[..., None]], axis=-1)
    sol = lax.linalg.triangular_solve(jnp.eye(C, dtype=jnp.float32) + L, rhs,
                                      left_side=True, lower=True, unit_diagonal=True)
    u, w = sol[..., :DV], sol[..., DV:]
    qk = jnp.einsum('nbhie,nbhje->nbhij', qc, kc) * decay
    q_dec = qc * jnp.exp(gcum)[..., None]
    k_dec = kc * jnp.exp(gcum[..., -1:] - gcum)[..., None]
    c_dec = jnp.exp(gcum[..., -1])

    def step(state, inp):
        u_c, w_c, qk_c, qd_c, kd_c, cd_c = inp
        v_new = u_c - jnp.einsum('bhce,bhef->bhcf', w_c, state)
        o_c = jnp.einsum('bhce,bhef->bhcf', qd_c, state) + jnp.einsum('bhij,bhjf->bhif', qk_c, v_new)
        state = state * cd_c[..., None, None] + jnp.einsum('bhce,bhcf->bhef', kd_c, v_new)
        return state, o_c

    s0 = jnp.zeros((B, H, DK, DV), jnp.float32)
    _, o = lax.scan(step, s0, (u, w, qk, q_dec, k_dec, c_dec))
    return o.transpose(1, 0, 3, 2, 4).reshape(B, S, H, DV)


def gated_deltanet(h, w_in, conv_w, a_log, dt_bias, norm_w, w_out):
    B, S, _ = h.shape
    E = GDN_HEAD_DIM
    proj = h @ w_in
    qkv = jax.nn.silu(causal_depthwise_conv(proj[..., :GDN_CONV_DIM], conv_w))
    z = proj[..., GDN_CONV_DIM:GDN_CONV_DIM + GDN_VD]
    b = proj[..., GDN_CONV_DIM + GDN_VD:GDN_CONV_DIM + GDN_VD + GDN_V_HEADS]
    a = proj[..., GDN_CONV_DIM + GDN_VD + GDN_V_HEADS:]
    q = qkv[..., :GDN_KD].reshape(B, S, GDN_K_HEADS, E)
    k = qkv[..., GDN_KD:2 * GDN_KD].reshape(B, S, GDN_K_HEADS, E)
    v = qkv[..., 2 * GDN_KD:].reshape(B, S, GDN_V_HEADS, E).astype(jnp.float32)
    rep = GDN_V_HEADS // GDN_K_HEADS
    q = jnp.repeat(l2norm(q), rep, axis=2) * (E ** -0.5)
    k = jnp.repeat(l2norm(k), rep, axis=2)
    beta = jax.nn.sigmoid(b.astype(jnp.float32))
    g = -jnp.exp(a_log.astype(jnp.float32)) * jax.nn.softplus(a.astype(jnp.float32) + dt_bias.astype(jnp.float32))
    o = chunk_gated_delta_rule(q, k, v, g, beta)
    o = rms_norm(o, norm_w) * jax.nn.silu(z.reshape(B, S, GDN_V_HEADS, E).astype(jnp.float32))
    return o.astype(h.dtype).reshape(B, S, GDN_VD) @ w_out


def setup_inputs(seed: int = 0) -> dict:
    key = jax.random.key(seed)
    ks = jax.random.split(key, 16)
    f32 = jnp.float32
    x = jax.random.normal(ks[0], (BATCH, SEQ, D_MODEL), f32)
    norm_w = 1.0 + 0.02 * jax.random.normal(ks[1], (DEPTH, 3, D_MODEL), f32)
    ffn_w_in = jax.random.normal(ks[2], (DEPTH, 2, D_MODEL, 2 * D_FF), f32) * D_MODEL ** -0.5
    ffn_w_out = jax.random.normal(ks[3], (DEPTH, 2, D_FF, D_MODEL), f32) * D_FF ** -0.5
    attn_w_in = jax.random.normal(ks[4], (N_A_LAYERS, D_MODEL, 3 * ATTN_WIDTH), f32) * D_MODEL ** -0.5
    attn_w_out = jax.random.normal(ks[5], (N_A_LAYERS, ATTN_WIDTH, D_MODEL), f32) * ATTN_WIDTH ** -0.5
    gdn_w_in = jax.random.normal(ks[6], (N_B_LAYERS, D_MODEL, GDN_PROJ), f32) * D_MODEL ** -0.5
    gdn_conv_w = jax.random.normal(ks[7], (N_B_LAYERS, GDN_CONV, GDN_CONV_DIM), f32) * GDN_CONV ** -0.5
    gdn_a_log = jnp.log(jax.random.uniform(ks[8], (N_B_LAYERS, GDN_V_HEADS), f32, 1.0, 16.0))
    dt = jnp.exp(jax.random.uniform(ks[9], (N_B_LAYERS, GDN_V_HEADS), f32, math.log(1e-3), math.log(1e-1)))
    gdn_dt_bias = dt + jnp.log(-jnp.expm1(-dt))
    gdn_norm_w = 1.0 + 0.02 * jax.random.normal(ks[10], (N_B_LAYERS, GDN_HEAD_DIM), f32)
    gdn_w_out = jax.random.normal(ks[11], (N_B_LAYERS, GDN_VD, D_MODEL), f32) * GDN_VD ** -0.5
    final_norm_w = 1.0 + 0.02 * jax.random.normal(ks[12], (D_MODEL,), f32)
    return {"x": x, "norm_w": norm_w, "ffn_w_in": ffn_w_in, "ffn_w_out": ffn_w_out,
            "attn_w_in": attn_w_in, "attn_w_out": attn_w_out,
            "gdn_w_in": gdn_w_in, "gdn_conv_w": gdn_conv_w, "gdn_a_log": gdn_a_log,
            "gdn_dt_bias": gdn_dt_bias, "gdn_norm_w": gdn_norm_w, "gdn_w_out": gdn_w_out,
            "final_norm_w": final_norm_w}


def reference(x, norm_w, ffn_w_in, ffn_w_out, attn_w_in, attn_w_out, gdn_w_in, gdn_conv_w,
              gdn_a_log, gdn_dt_bias, gdn_norm_w, gdn_w_out, final_norm_w):
    S = x.shape[1]
    cos, sin = rope_tables(S, HEAD_DIM_A)
    ia, ib = 0, 0
    for i in range(DEPTH):
        x = x + 0.5 * swiglu(rms_norm(x, norm_w[i, 0]), ffn_w_in[i, 0], ffn_w_out[i, 0])
        h = rms_norm(x, norm_w[i, 1])
        if i % N_MIXERS == 0:
            x = x + dilated_attention(h, attn_w_in[ia], attn_w_out[ia], cos, sin)
            ia += 1
        else:
            x = x + gated_deltanet(h, gdn_w_in[ib], gdn_conv_w[ib], gdn_a_log[ib], gdn_dt_bias[ib],
                                   gdn_norm_w[ib], gdn_w_out[ib])
            ib += 1
        x = x + 0.5 * swiglu(rms_norm(x, norm_w[i, 2]), ffn_w_in[i, 1], ffn_w_out[i, 1])
    return rms_norm(x, final_norm_w)
```

```python
from contextlib import ExitStack
import numpy as np
import concourse.bass as bass
import concourse.mybir as mybir
from concourse.bass_utils import run_bass_kernel_spmd

F32 = mybir.dt.float32
BF16 = mybir.dt.bfloat16
ALU = mybir.AluOpType
AF = mybir.ActivationFunctionType

ENGS = ("pe", "act", "dve", "pool", "sp")
NAMES = []


class _Op:
    __slots__ = ("eng", "fn", "deps", "idx", "is_dma", "dsem", "dval", "need_inc", "cnt")

    def __init__(self, eng, fn, is_dma):
        self.eng = eng
        self.fn = fn
        self.deps = []
        self.is_dma = is_dma
        self.dsem = None
        self.dval = 0
        self.need_inc = False
        self.cnt = 0


class Ctx:
    def __init__(self, nc):
        self.nc = nc
        self.q = {e: [] for e in ENGS}
        self.wr = {}
        self.rd = {}
        self.dsems = {}
        self.pending_dma = []
        self.sem_pool = {}
        self.psum_keys = set()
        self.stack = ExitStack()
        self.scopes = []
        self.esem = {e: nc.alloc_semaphore("sem_" + e) for e in ENGS}
        self.n_psum = 0

    def push(self):
        st = ExitStack()
        self.scopes.append(st)

    def pop(self):
        self.scopes.pop().close()

    def _scope(self):
        return self.scopes[-1] if self.scopes else self.stack

    def sbuf(self, name, shape, dtype):
        self.n_psum += 1
        NAMES.append("%s_%d" % (name, self.n_psum))
        return self._scope().enter_context(self.nc.sbuf_tensor("%s_%d" % (name, self.n_psum), list(shape), dtype))

    def psum(self, name, shape, dtype=F32):
        self.n_psum += 1
        return self._scope().enter_context(self.nc.psum_tensor("%s_%d" % (name, self.n_psum), list(shape), dtype))

    def _track(self, op, reads, writes):
        deps = op.deps
        for k in reads:
            for o in self.wr.get(k, {}).values():
                deps.append(o)
            if k in self.psum_keys:
                for ek2, o in self.rd.get(k, {}).items():
                    if o.eng != op.eng:
                        deps.append(o)
        for k in writes:
            for o in self.wr.get(k, {}).values():
                deps.append(o)
            for o in self.rd.get(k, {}).values():
                deps.append(o)
        ek = id(op) if op.is_dma else op.eng
        for k in reads:
            self.rd.setdefault(k, {})[ek] = op
        for k in writes:
            self.wr[k] = {ek: op}
            self.rd[k] = {}

    def op(self, eng, fn, reads=(), writes=()):
        o = _Op(eng, fn, False)
        self._track(o, reads, writes)
        o.idx = len(self.q[eng])
        self.q[eng].append(o)
        return o

    def dma(self, eng, out, in_, reads=(), writes=(), skey=None, **kw):
        key = ("dma", skey)
        o = _Op(eng, lambda e: e.dma_start(out=out, in_=in_, **kw), True)
        self._track(o, reads, writes)
        key = (eng == "pool", key)
        if key not in self.dsems:
            pool = self.sem_pool.setdefault(eng == "pool", [])
            i = sum(1 for k in self.dsems if k[0] == key[0])
            if i >= len(pool):
                pool.append([self.nc.alloc_semaphore("d%s%d" % ("s" if key[0] else "h", i)), 0])
            self.dsems[key] = pool[i]
        ds = self.dsems[key]
        ds[1] += 16
        o.dsem = ds[0]
        o.dval = ds[1]
        o.idx = len(self.q[eng])
        self.q[eng].append(o)
        self.pending_dma.append(o)
        return o

    def barrier(self):
        lasts = [self.q[e][-1] for e in ENGS if self.q[e] and not self.q[e][-1].is_dma]
        lasts += [o for e in ENGS for o in self.q[e][-1:] if False]
        dm = list(self.pending_dma)
        self.pending_dma = []
        for e in ENGS:
            o = _Op(e, None, False)
            for e2 in ENGS:
                for p in reversed(self.q[e2]):
                    if not p.is_dma and p.fn is not None:
                        o.deps.append(p)
                        break
            o.deps.extend(dm)
            o.idx = len(self.q[e])
            self.q[e].append(o)
        self.wr = {}
        self.rd = {}
        self.dsems = {}

    def finish(self):
        nc = self.nc
        self.barrier()
        for e in ENGS:
            for o in self.q[e]:
                for d in o.deps:
                    if not d.is_dma and (d.eng != o.eng or e != "pe"):
                        d.need_inc = True
        for e in ENGS:
            c = 0
            for o in self.q[e]:
                if o.need_inc:
                    c += 1
                o.cnt = c
        engobj = {"pe": "tensor", "act": "scalar", "dve": "vector", "pool": "gpsimd", "sp": "sync"}
        with nc.Block() as block:
            for e in ENGS:
                def body(eng, e=e):
                    known = {}
                    for o in self.q[e]:
                        waits = {}
                        for d in o.deps:
                            if d.is_dma:
                                s, v = d.dsem, d.dval
                            else:
                                if d.eng == e and e == "pe":
                                    continue
                                s, v = self.esem[d.eng], d.cnt
                            if v > waits.get(s, (None, 0))[1]:
                                waits[s] = (s, v)
                        for s, v in waits.values():
                            if known.get(s, 0) >= v:
                                continue
                            known[s] = v
                            eng.wait_ge(s, v)
                        if o.fn is None:
                            continue
                        ins = o.fn(eng)
                        if o.is_dma:
                            ins.then_inc(o.dsem, 16)
                        elif o.need_inc:
                            ins.then_inc(self.esem[e], 1)
                getattr(block, engobj[e])(body)
        self.stack.close()


D = 1024
T = 4096
KC = D // 128
DFF = 2816
JC = DFF // 128
TT = 512
NTT = T // TT
EPS = 1e-6


def consts(cx):
    nc = cx.nc
    c = {}
    c["ones_bf"] = cx.sbuf("ones_bf", [128, 128], BF16)
    cx.op("pool", lambda e: e.memset(c["ones_bf"][:], 1.0), writes=["ones_bf"])
    return c


def rmsnorm_tile(cx, c, xt, nw, hT_out, tag, rd_keys, wr_key, ps_key, ps, scr):
    sq, rstd = scr["sq"], scr["rstd"]
    cx.op("act", lambda e: e.activation(sq[:], xt[:], AF.Square), reads=rd_keys, writes=[tag + "sq"])

    def mm(e):
        ins = None
        for kc in range(KC):
            ins = e.matmul(ps[:], c["ones_bf"][:], sq[:, kc, :], start=(kc == 0), stop=(kc == KC - 1))
        return ins
    cx.op("pe", mm, reads=[tag + "sq", "ones_bf"], writes=[ps_key])
    cx.op("act", lambda e: e.activation(rstd[:], ps[:], AF.Sqrt, bias=EPS, scale=1.0 / D),
          reads=[ps_key], writes=[tag + "rstd"])
    cx.op("dve", lambda e: e.reciprocal(rstd[:], rstd[:]), reads=[tag + "rstd"], writes=[tag + "rstd"])
    for kc in range(KC):
        cx.op("dve", lambda e, kc=kc: e.scalar_tensor_tensor(
            out=hT_out(kc), in0=xt[:, kc, :], scalar=nw[:, kc:kc + 1], in1=rstd[:],
            op0=ALU.mult, op1=ALU.mult), reads=rd_keys + [tag + "rstd", tag + "nw"], writes=[wr_key])


def ffn_phase(cx, c, tag, x_in, x_out, nw_d, w_in, w_out, xkey_in, xkey_out):
    nc = cx.nc
    cx.push()
    HT = 2048
    NH = T // HT
    NT = HT // TT
    hT = cx.sbuf(tag + "hT", [128, KC, HT], BF16)
    act = cx.sbuf(tag + "act", [128, JC, HT], BF16)
    xt = cx.sbuf(tag + "xt", [128, KC, TT], F32)
    sq = cx.sbuf(tag + "sq", [128, KC, TT], BF16)
    rstd = cx.sbuf(tag + "rstd", [128, TT], F32)
    nw = cx.sbuf(tag + "nw", [128, KC], F32)
    WB = 256
    wg = [cx.sbuf(tag + "wg%d" % i, [128, KC, WB], BF16) for i in range(2)]
    wu = [cx.sbuf(tag + "wu%d" % i, [128, KC, WB], BF16) for i in range(2)]
    wo = [cx.sbuf(tag + "wo%d" % i, [128, JC, WB], BF16) for i in range(2)]
    sg = [cx.sbuf(tag + "sg%d" % i, [128, TT], F32) for i in range(2)]
    xr = [cx.sbuf(tag + "xr%d" % i, [128, TT], F32) for i in range(2)]
    xn = [cx.sbuf(tag + "xn%d" % i, [128, TT], F32) for i in range(2)]
    ps = [cx.psum(tag + "ps%d" % i, [128, TT], F32) for i in range(7)]
    pk = [tag + "ps%d" % i for i in range(7)]
    cx.psum_keys.update(pk)

    cx.dma("sp", nw[:], nw_d, writes=[tag + "nw"], skey="nw")
    w_in_r = w_in.rearrange("(kc p) n -> p kc n", p=128)
    w_out_r = w_out.rearrange("(kc p) n -> p kc n", p=128)
    x_in_r = x_in.rearrange("(kc p) t -> p kc t", p=128)
    cnt = {"g": 0, "o": 0, "s": 0, "x": 0}
    for h in range(NH):
        t0 = h * HT
        for tt in range(NT):
            gt = (t0 + tt * TT) // TT
            cx.dma("sp", xt[:], x_in_r[:, :, t0 + tt * TT: t0 + (tt + 1) * TT],
                   reads=[(xkey_in, gt, kc) for kc in range(KC)], writes=[tag + "xt"], skey="xt")
            rmsnorm_tile(cx, c, xt, nw, lambda kc, tt=tt: hT[:, kc, tt * TT:(tt + 1) * TT], tag,
                         [tag + "xt"], (tag + "hT", tt), pk[6], ps[6], {"sq": sq, "rstd": rstd})
        for jb in range(JC * 128 // WB):
            s = cnt["g"] % 2
            cnt["g"] += 1
            cx.dma("pool", wg[s][:], w_in_r[:, :, jb * WB:(jb + 1) * WB], writes=[(tag + "wg", s)],
                   skey=("wg", s))
            cx.dma("pool", wu[s][:], w_in_r[:, :, DFF + jb * WB: DFF + (jb + 1) * WB], writes=[(tag + "wu", s)],
                   skey=("wu", s))
            for jj in range(WB // 128):
                j = jb * (WB // 128) + jj
                for tt in range(NT):
                    b = cnt["s"] % 2
                    cnt["s"] += 1
                    pg, pu = ps[b], ps[2 + b]

                    def mm(e, s=s, jj=jj, tt=tt, pg=pg, pu=pu):
                        ins = None
                        for kc in range(KC):
                            ins = e.matmul(pg[:], wg[s][:, kc, jj * 128:(jj + 1) * 128],
                                           hT[:, kc, tt * TT:(tt + 1) * TT], start=(kc == 0), stop=(kc == KC - 1))
                        for kc in range(KC):
                            ins = e.matmul(pu[:], wu[s][:, kc, jj * 128:(jj + 1) * 128],
                                           hT[:, kc, tt * TT:(tt + 1) * TT], start=(kc == 0), stop=(kc == KC - 1))
                        return ins
                    cx.op("pe", mm, reads=[(tag + "wg", s), (tag + "wu", s), (tag + "hT", tt)],
                          writes=[pk[b], pk[2 + b]])
                    cx.op("act", lambda e, b=b, pg=pg: e.activation(sg[b][:], pg[:], AF.Silu),
                          reads=[pk[b]], writes=[(tag + "sg", b)])
                    cx.op("dve", lambda e, b=b, pu=pu, j=j, tt=tt: e.tensor_tensor(
                        out=act[:, j, tt * TT:(tt + 1) * TT], in0=sg[b][:], in1=pu[:], op=ALU.mult),
                        reads=[(tag + "sg", b), pk[2 + b]], writes=[(tag + "act", j, tt)])
        for mb in range(D // WB):
            s = cnt["o"] % 2
            cnt["o"] += 1
            cx.dma("pool", wo[s][:], w_out_r[:, :, mb * WB:(mb + 1) * WB], writes=[(tag + "wo", s)],
                   skey=("wo", s))
            for mm_ in range(WB // 128):
                mc = mb * (WB // 128) + mm_
                for tt in range(NT):
                    gt = (t0 + tt * TT) // TT
                    b = cnt["x"] % 2
                    cnt["x"] += 1
                    py = ps[4 + b]
                    cx.dma("sp", xr[b][:], x_in[mc * 128:(mc + 1) * 128, t0 + tt * TT: t0 + (tt + 1) * TT],
                           reads=[(xkey_in, gt, mc)], writes=[(tag + "xr", b)], skey=("xr", b))

                    def mm2(e, s=s, mm_=mm_, tt=tt, py=py):
                        ins = None
                        for kc in range(JC):
                            ins = e.matmul(py[:], wo[s][:, kc, mm_ * 128:(mm_ + 1) * 128],
                                           act[:, kc, tt * TT:(tt + 1) * TT], start=(kc == 0), stop=(kc == JC - 1))
                        return ins
                    cx.op("pe", mm2, reads=[(tag + "wo", s)] + [(tag + "act", j, tt) for j in range(JC)],
                          writes=[pk[4 + b]])
                    cx.op("dve", lambda e, b=b, py=py: e.scalar_tensor_tensor(
                        out=xn[b][:], in0=py[:], scalar=0.5, in1=xr[b][:], op0=ALU.mult, op1=ALU.add),
                        reads=[pk[4 + b], (tag + "xr", b)], writes=[(tag + "xn", b)])
                    cx.dma("sp", x_out[mc * 128:(mc + 1) * 128, t0 + tt * TT: t0 + (tt + 1) * TT], xn[b][:],
                           reads=[(tag + "xn", b)], writes=[(xkey_out, gt, mc)], skey=("xn", b))
    cx.barrier()
    cx.pop()


def build_ffn_prog():
    nc = bass.Bass("TRN2", target_bir_lowering=False)
    x = nc.dram_tensor("x", [D, T], F32, kind="ExternalInput").ap()
    nw = nc.dram_tensor("nw", [128, KC], F32, kind="ExternalInput").ap()
    w_in = nc.dram_tensor("w_in", [D, 2 * DFF], F32, kind="ExternalInput").ap()
    w_out = nc.dram_tensor("w_out", [DFF, D], F32, kind="ExternalInput").ap()
    y = nc.dram_tensor("y", [D, T], F32, kind="ExternalOutput").ap()
    cx = Ctx(nc)
    c = consts(cx)
    ffn_phase(cx, c, "f_", x, y, nw, w_in, w_out, "xin", "xout")
    cx.finish()
    return nc


NH_A = 12
DIL = (1, 4, 16)
NEG = -30000.0
SCALE_A = 128 ** -0.5
C_IDENT = 0
C_PSWAP = 128
C_MASK = 256
C_TRI = 512
C_TRIS = 640
C_MINC = 768
C_MSTR = 896
C_LAST = 1024
CST_W = 1152


def host_consts():
    m = np.zeros((128, CST_W), np.float32)
    k = np.arange(128)[:, None]
    q = np.arange(128)[None, :]
    m[:, C_IDENT:C_IDENT + 128] = (k == q)
    m[:, C_PSWAP:C_PSWAP + 128] = (k == (q + 64) % 128)
    m[:, C_MASK:C_MASK + 128] = np.where(k <= q, 0.0, NEG)
    m[:, C_MASK + 128:C_MASK + 256] = np.where(k >= q, 0.0, NEG)
    m[:, C_TRI:C_TRI + 128] = (k <= q)
    m[:, C_TRIS:C_TRIS + 128] = (k > q)
    m[:, C_MINC:C_MINC + 128] = (k <= q)
    m[:, C_MSTR:C_MSTR + 128] = (k < q)
    m[:, C_LAST:C_LAST + 128] = (k == 127)
    return m


def host_rope():
    inv = (1.0 / (10000.0 ** (np.arange(0, 128, 2, dtype=np.float32) / np.float32(128)))).astype(np.float32)
    ang = (np.arange(T, dtype=np.float32)[None, :] * inv[:, None]).astype(np.float32)
    cs, sn = np.cos(ang).astype(np.float32), np.sin(ang).astype(np.float32)
    return np.concatenate([cs, cs], 0), np.concatenate([-sn, sn], 0)


def norm_stage(cx, c, tag, x_in, nw_d, hT, xkey_in, ps, pskey):
    cx.push()
    xt = [cx.sbuf(tag + "nxt%d" % i, [128, KC, TT], F32) for i in range(2)]
    sq = cx.sbuf(tag + "nsq", [128, KC, TT], BF16)
    rstd = cx.sbuf(tag + "nrstd", [128, TT], F32)
    nw = cx.sbuf(tag + "nnw", [128, KC], F32)
    cx.dma("sp", nw[:], nw_d, writes=[tag + "nw"], skey="nw")
    x_in_r = x_in.rearrange("(kc p) t -> p kc t", p=128)
    for tt in range(NTT):
        b = tt % 2
        cx.dma("sp", xt[b][:], x_in_r[:, :, tt * TT:(tt + 1) * TT],
               reads=[(xkey_in, tt, kc) for kc in range(KC)], writes=[(tag + "xt", b)], skey=("xt", b))
        rmsnorm_tile(cx, c, xt[b], nw, lambda kc, tt=tt: hT[:, kc, tt * TT:(tt + 1) * TT], tag,
                     [(tag + "xt", b)], (tag + "hT", tt), pskey, ps, {"sq": sq, "rstd": rstd})
    cx.barrier()
    cx.pop()


def attn_phase(cx, c, tag, x_in, os_d, nw_d, w_in, ctab_d, stab_d, cst_d, xkey_in, okey):
    nc = cx.nc
    cx.push()
    hT = cx.sbuf(tag + "hT", [128, KC, T], BF16)
    ctab = cx.sbuf(tag + "ctab", [128, T], F32)
    stab = cx.sbuf(tag + "stab", [128, T], F32)
    ident = cx.sbuf(tag + "ident", [128, 128], BF16)
    pswap = cx.sbuf(tag + "pswap", [128, 128], BF16)
    maskb = cx.sbuf(tag + "maskb", [128, 256], BF16)
    ps = [cx.psum(tag + "ps%d" % i, [128, TT], F32) for i in range(8)]
    pk = [tag + "ps%d" % i for i in range(8)]
    cx.psum_keys.update(pk)
    cx.dma("sp", ctab[:], ctab_d, writes=[tag + "ctab"], skey="ctab")
    cx.dma("sp", stab[:], stab_d, writes=[tag + "stab"], skey="stab")
    cx.dma("pool", ident[:], cst_d[:, C_IDENT:C_IDENT + 128], writes=[tag + "ident"], skey="ident")
    cx.dma("pool", pswap[:], cst_d[:, C_PSWAP:C_PSWAP + 128], writes=[tag + "pswap"], skey="pswap")
    cx.dma("pool", maskb[:], cst_d[:, C_MASK:C_MASK + 256], writes=[tag + "maskb"], skey="maskb")
    norm_stage(cx, c, tag, x_in, nw_d, hT, xkey_in, ps[6], pk[6])
    hkeys = []

    Qd = cx.sbuf(tag + "Qd", [128, T], BF16)
    Kd = cx.sbuf(tag + "Kd", [128, T], BF16)
    Vd = cx.sbuf(tag + "Vd", [128, 32, 128], BF16)
    numT = cx.sbuf(tag + "numT", [128, 3, T], BF16)
    dent = cx.sbuf(tag + "dent", [128, T], F32)
    wq = [cx.sbuf(tag + "wq%d" % i, [128, KC, 128], BF16) for i in range(2)]
    wk = [cx.sbuf(tag + "wk%d" % i, [128, KC, 128], BF16) for i in range(2)]
    wv = [cx.sbuf(tag + "wv%d" % i, [128, KC, 128], BF16) for i in range(2)]
    qb = [cx.sbuf(tag + "qb%d" % i, [128, TT], BF16) for i in range(2)]
    t1 = [cx.sbuf(tag + "t1%d" % i, [128, TT], F32) for i in range(2)]
    t2 = [cx.sbuf(tag + "t2%d" % i, [128, TT], F32) for i in range(2)]
    pt = [cx.sbuf(tag + "pt%d" % i, [128, 256], BF16) for i in range(4)]
    osc = [cx.sbuf(tag + "osc%d" % i, [128, TT], BF16) for i in range(2)]
    w_in_r = w_in.rearrange("(kc p) n -> p kc n", p=128)
    AW = NH_A * 128
    cnt = {"w": 0, "r": 0, "o": 0}

    def load_w(hd):
        s = cnt["w"] % 2
        cnt["w"] += 1
        for nm, wt, off in (("wq", wq, 0), ("wk", wk, AW), ("wv", wv, 2 * AW)):
            cx.dma("pool", wt[s][:], w_in_r[:, :, off + hd * 128: off + (hd + 1) * 128],
                   writes=[(tag + nm, s)], skey=(nm, s))
        return s

    for hg in range(4):
        for g in range(3):
            d = DIL[g]
            Ls = T // d
            nb = Ls // 128
            hd = g * 4 + hg
            s = load_w(hd)
            for which, wt, dst, dkey in (("q", wq, Qd, tag + "Qd"), ("k", wk, Kd, tag + "Kd")):
                dst_v = dst[:, :].rearrange("p (r j) -> p j r", r=d)
                for tt in range(NTT):
                    b = cnt["r"] % 2
                    cnt["r"] += 1
                    pp, sw = ps[b], ps[2 + b]

                    def mm(e, wt=wt, s=s, tt=tt, pp=pp):
                        ins = None
                        for kc in range(KC):
                            ins = e.matmul(pp[:], wt[s][:, kc, :], hT[:, kc, tt * TT:(tt + 1) * TT],
                                           start=(kc == 0), stop=(kc == KC - 1))
                        return ins
                    cx.op("pe", mm, reads=[(tag + "w" + which, s)], writes=[pk[b]])
                    cx.op("act", lambda e, b=b, pp=pp: e.activation(qb[b][:], pp[:], AF.Copy),
                          reads=[pk[b]], writes=[(tag + "qb", b)])
                    cx.op("pe", lambda e, b=b, sw=sw: e.matmul(sw[:], pswap[:], qb[b][:], start=True, stop=True),
                          reads=[(tag + "qb", b), tag + "pswap"], writes=[pk[2 + b]])
                    cx.op("pool", lambda e, b=b, tt=tt: e.tensor_tensor(
                        out=t1[b][:], in0=qb[b][:], in1=ctab[:, tt * TT:(tt + 1) * TT], op=ALU.mult),
                        reads=[(tag + "qb", b), tag + "ctab"], writes=[(tag + "t1", b)])
                    cx.op("dve", lambda e, b=b, tt=tt, sw=sw: e.tensor_tensor(
                        out=t2[b][:], in0=sw[:], in1=stab[:, tt * TT:(tt + 1) * TT], op=ALU.mult),
                        reads=[pk[2 + b], tag + "stab"], writes=[(tag + "t2", b)])
                    j0 = tt * TT // d
                    cx.op("dve", lambda e, b=b, dst_v=dst_v, j0=j0, d=d: e.tensor_tensor(
                        out=dst_v[:, j0:j0 + TT // d, :],
                        in0=t1[b][:, :].rearrange("p (j r) -> p j r", r=d),
                        in1=t2[b][:, :].rearrange("p (j r) -> p j r", r=d), op=ALU.add),
                        reads=[(tag + "t1", b), (tag + "t2", b)], writes=[(dkey, tt)])
            for b4 in range(8):
                b = cnt["r"] % 2
                cnt["r"] += 1
                pp = ps[b]

                def mmv(e, s=s, b4=b4, pp=pp, d=d, Ls=Ls):
                    ins = None
                    for i in range(4):
                        B = b4 * 4 + i
                        r, n = divmod(B, Ls // 128)
                        t0 = n * 128 * d + r
                        for kc in range(KC):
                            ins = e.matmul(pp[:, i * 128:(i + 1) * 128],
                                           hT[:, kc, t0: t0 + 127 * d + 1: d], wv[s][:, kc, :],
                                           start=(kc == 0), stop=(kc == KC - 1))
                    return ins
                cx.op("pe", mmv, reads=[(tag + "wv", s)], writes=[pk[b]])
                cx.op("act", lambda e, b4=b4, pp=pp: e.activation(
                    Vd[:, b4 * 4:(b4 + 1) * 4, :], pp[:, :].rearrange("p (i e) -> p i e", i=4), AF.Copy),
                    reads=[pk[b]], writes=[(tag + "Vd", b4)])
            qk_keys = [(tag + "Qd", tt) for tt in range(NTT)] + [(tag + "Kd", tt) for tt in range(NTT)]

            def scores(B, nb=nb):
                sb = 4 + (B % 2)
                nq = 256 if (B + 1) % nb != 0 else 128
                st = ps[sb]

                def mm(e, B=B, nq=nq, st=st):
                    e.matmul(st[:, 0:nq], Kd[:, B * 128:(B + 1) * 128], Qd[:, B * 128:B * 128 + nq],
                             start=True, stop=False)
                    return e.matmul(st[:, 0:nq], ident[:], maskb[:, 0:nq], start=False, stop=True)
                cx.op("pe", mm, reads=qk_keys + [tag + "ident", tag + "maskb"], writes=[pk[sb]])
                cx.op("act", lambda e, B=B, nq=nq, st=st: e.activation(
                    pt[B % 4][:, 0:nq], st[:, 0:nq], AF.Exp, scale=SCALE_A),
                    reads=[pk[sb]], writes=[(tag + "pt", B % 4)])

            def pv(B, nb=nb, g=g, d=d, Ls=Ls):
                first = (B % nb == 0)
                col = (B % 4) * 128

                def mm(e, B=B, first=first, col=col):
                    ins = None
                    for dst, lhs_fn in ((ps[6], lambda blk: Vd[:, blk, :]), (ps[7], lambda blk: c["ones_bf"][:])):
                        if not first:
                            e.matmul(dst[:, col:col + 128], lhs_fn(B - 1), pt[(B - 1) % 4][:, 128:256],
                                     start=True, stop=False)
                        ins = e.matmul(dst[:, col:col + 128], lhs_fn(B), pt[B % 4][:, 0:128],
                                       start=first, stop=True)
                    return ins
                rk = [(tag + "pt", B % 4), (tag + "Vd", B // 4), "ones_bf"]
                if not first:
                    rk += [(tag + "pt", (B - 1) % 4), (tag + "Vd", (B - 1) // 4)]
                cx.op("pe", mm, reads=rk, writes=[pk[6], pk[7]])
                if B % 4 == 3:
                    u0 = (B - 3) * 128
                    if d == 1:
                        def nat(ap2d):
                            return ap2d[:, u0:u0 + 512]
                        def src(p):
                            return p[:, :]
                    else:
                        r0, j0 = divmod(u0, Ls)
                        nr = max(1, 512 // Ls)
                        nj = 512 // nr
                        def nat(ap2d, r0=r0, j0=j0, nr=nr, nj=nj, d=d):
                            return ap2d.rearrange("p (j r) -> p r j", r=d)[:, r0:r0 + nr, j0:j0 + nj]
                        def src(p, nr=nr):
                            return p[:, :].rearrange("p (r j) -> p r j", r=nr)
                    cx.op("act", lambda e, nat=nat, src=src, g=g: e.activation(
                        nat(numT[:, g, :]), src(ps[6]), AF.Copy),
                        reads=[pk[6]], writes=[(tag + "numT", g)])
                    if g == 0:
                        cx.op("dve", lambda e, nat=nat, src=src: e.tensor_copy(nat(dent[:, :]), src(ps[7])),
                              reads=[pk[7]], writes=[tag + "dent"])
                    else:
                        cx.op("dve", lambda e, nat=nat, src=src: e.tensor_tensor(
                            out=nat(dent[:, :]), in0=nat(dent[:, :]), in1=src(ps[7]), op=ALU.add),
                            reads=[pk[7], tag + "dent"], writes=[tag + "dent"])

            scores(0)
            for B in range(32):
                if B + 1 < 32:
                    scores(B + 1)
                pv(B)
        cx.op("dve", lambda e: e.reciprocal(dent[:, :], dent[:, :]), reads=[tag + "dent"], writes=[tag + "dent"])
        for g in range(3):
            hd = g * 4 + hg
            for tt in range(NTT):
                b = cnt["o"] % 2
                cnt["o"] += 1
                cx.op("dve", lambda e, b=b, g=g, tt=tt: e.tensor_tensor(
                    out=osc[b][:], in0=numT[:, g, tt * TT:(tt + 1) * TT], in1=dent[:, tt * TT:(tt + 1) * TT],
                    op=ALU.mult), reads=[tag + "dent", (tag + "numT", g)], writes=[(tag + "osc", b)])
                cx.dma("sp", os_d[hd * 128:(hd + 1) * 128, tt * TT:(tt + 1) * TT], osc[b][:],
                       reads=[(tag + "osc", b)], writes=[(okey, hd, tt)], skey=("osc", b))
    cx.barrier()
    cx.pop()


def outproj_phase(cx, c, tag, x_in, x_out, os_d, w_out, nheads, xkey_in, xkey_out, okey):
    cx.push()
    HT = 2048
    NT = HT // TT
    WB = 256
    osb = cx.sbuf(tag + "osb", [128, nheads, HT], BF16)
    wo = [cx.sbuf(tag + "wo%d" % i, [128, nheads, WB], BF16) for i in range(2)]
    xr = [cx.sbuf(tag + "xr%d" % i, [128, TT], F32) for i in range(2)]
    xn = [cx.sbuf(tag + "xn%d" % i, [128, TT], F32) for i in range(2)]
    ps = [cx.psum(tag + "ps%d" % i, [128, TT], F32) for i in range(2)]
    pk = [tag + "ps%d" % i for i in range(2)]
    cx.psum_keys.update(pk)
    w_out_r = w_out.rearrange("(kc p) n -> p kc n", p=128)
    os_r = os_d.rearrange("(h p) t -> p h t", p=128)
    cnt = {"o": 0, "x": 0}
    for h in range(T // HT):
        t0 = h * HT
        for hd in range(nheads):
            cx.dma("sp", osb[:, hd, :], os_r[:, hd, t0:t0 + HT],
                   reads=[(okey, hd, (t0 // TT) + i) for i in range(NT)], writes=[(tag + "osb", hd)], skey=("osb", hd % 4))
        for mb in range(D // WB):
            s = cnt["o"] % 2
            cnt["o"] += 1
            cx.dma("pool", wo[s][:], w_out_r[:, :, mb * WB:(mb + 1) * WB], writes=[(tag + "wo", s)], skey=("wo", s))
            for mm_ in range(WB // 128):
                mc = mb * (WB // 128) + mm_
                for tt in range(NT):
                    gt = (t0 + tt * TT) // TT
                    b = cnt["x"] % 2
                    cnt["x"] += 1
                    py = ps[b]
                    cx.dma("sp", xr[b][:], x_in[mc * 128:(mc + 1) * 128, t0 + tt * TT: t0 + (tt + 1) * TT],
                           reads=[(xkey_in, gt, mc)], writes=[(tag + "xr", b)], skey=("xr", b))

                    def mm2(e, s=s, mm_=mm_, tt=tt, py=py):
                        ins = None
                        for kc in range(nheads):
                            ins = e.matmul(py[:], wo[s][:, kc, mm_ * 128:(mm_ + 1) * 128],
                                           osb[:, kc, tt * TT:(tt + 1) * TT], start=(kc == 0), stop=(kc == nheads - 1))
                        return ins
                    cx.op("pe", mm2, reads=[(tag + "wo", s)] + [(tag + "osb", hd) for hd in range(nheads)],
                          writes=[pk[b]])
                    cx.op("dve", lambda e, b=b, py=py: e.tensor_tensor(
                        out=xn[b][:], in0=py[:], in1=xr[b][:], op=ALU.add),
                        reads=[pk[b], (tag + "xr", b)], writes=[(tag + "xn", b)])
                    cx.dma("sp", x_out[mc * 128:(mc + 1) * 128, t0 + tt * TT: t0 + (tt + 1) * TT], xn[b][:],
                           reads=[(tag + "xn", b)], writes=[(xkey_out, gt, mc)], skey=("xn", b))
    cx.barrier()
    cx.pop()


def build_attn_prog():
    nc = bass.Bass("TRN2", target_bir_lowering=False)
    x = nc.dram_tensor("x", [D, T], F32, kind="ExternalInput").ap()
    nw = nc.dram_tensor("nw", [128, KC], F32, kind="ExternalInput").ap()
    w_in = nc.dram_tensor("w_in", [D, 3 * NH_A * 128], F32, kind="ExternalInput").ap()
    w_out = nc.dram_tensor("w_out", [NH_A * 128, D], F32, kind="ExternalInput").ap()
    ctab = nc.dram_tensor("ctab", [128, T], F32, kind="ExternalInput").ap()
    stab = nc.dram_tensor("stab", [128, T], F32, kind="ExternalInput").ap()
    cst = nc.dram_tensor("cst", [128, CST_W], F32, kind="ExternalInput").ap()
    os_d = nc.dram_tensor("os_scratch", [NH_A * 128, T], BF16).ap()
    y = nc.dram_tensor("y", [D, T], F32, kind="ExternalOutput").ap()
    cx = Ctx(nc)
    c = consts(cx)
    attn_phase(cx, c, "a_", x, os_d, nw, w_in, ctab, stab, cst, "xin", "os")
    outproj_phase(cx, c, "ao_", x, y, os_d, w_out, NH_A, "xin", "xout", "os")
    cx.finish()
    return nc


NVH = 16
NKH = 8
GP = 6176
NTILE = T // 128
DBG = {}


class _Stop(Exception):
    pass


def _chk(n):
    if DBG.get("stop", 99) <= n:
        raise _Stop()


def gdn_core_phase(*a):
    cx = a[0]
    depth = len(cx.scopes)
    try:
        _gdn_core_phase(*a)
    except _Stop:
        cx.barrier()
        while len(cx.scopes) > depth:
            cx.pop()


def _gdn_core_phase(cx, c, tag, x_in, oraw_d, nw_d, w_in, convw_d, alog_d, dtb_d, cst_d, xkey_in, okey):
    cx.push()
    hT = cx.sbuf(tag + "hT", [128, KC, T], BF16)
    ident = cx.sbuf(tag + "ident", [128, 128], BF16)
    id4 = cx.sbuf(tag + "id4", [128, 4, 128], F32)
    minc4 = cx.sbuf(tag + "minc4", [128, 4, 128], F32)
    mlow4 = cx.sbuf(tag + "mlow4", [128, 4, 128], F32)
    cw = cx.sbuf(tag + "cw", [128, 32, 4], F32)
    beta = cx.sbuf(tag + "beta", [128, NTILE, NVH], F32)
    nbeta = cx.sbuf(tag + "nbeta", [128, NTILE, NVH], F32)
    gc = cx.sbuf(tag + "gc", [128, NTILE, NVH], F32)
    kbs = cx.sbuf(tag + "kbs", [128, NTILE, NVH], F32)
    egr = cx.sbuf(tag + "egr", [128, NTILE, NVH], F32)
    cd = cx.sbuf(tag + "cd", [128, NTILE, NVH], F32)
    gchi = cx.sbuf(tag + "gchi", [128, NTILE, NVH], BF16)
    gclo = cx.sbuf(tag + "gclo", [128, NTILE, NVH], BF16)
    ps = [cx.psum(tag + "ps%d" % i, [128, TT], F32) for i in range(8)]
    pk = [tag + "ps%d" % i for i in range(8)]
    cx.psum_keys.update(pk)
    K = lambda n: tag + n
    cx.push()
    alog = cx.sbuf(tag + "alog", [128, 512], F32)
    dtb = cx.sbuf(tag + "dtb", [128, 512], F32)
    gt = cx.sbuf(tag + "gt", [128, NTILE, NVH], F32)
    wab = cx.sbuf(tag + "wab", [128, KC, 32], BF16)
    ghi = cx.sbuf(tag + "ghi", [128, NTILE, NVH], BF16)
    glo = cx.sbuf(tag + "glo", [128, NTILE, NVH], BF16)
    trib = cx.sbuf(tag + "trib", [128, 128], BF16)
    trisb = cx.sbuf(tag + "trisb", [128, 128], BF16)
    cx.dma("pool", ident[:], cst_d[:, C_IDENT:C_IDENT + 128], writes=[K("ident")], skey="ident")
    for i in range(4):
        cx.dma("sp", id4[:, i, :], cst_d[:, C_IDENT:C_IDENT + 128], writes=[K("id4")], skey=("id4", i))
        cx.dma("sp", minc4[:, i, :], cst_d[:, C_MINC:C_MINC + 128], writes=[K("minc4")], skey=("minc4", i))
        cx.dma("sp", mlow4[:, i, :], cst_d[:, C_TRIS:C_TRIS + 128], writes=[K("mlow4")], skey=("mlow4", i))
    cx.dma("sp", cw[:], convw_d, writes=[K("cw")], skey="cw")
    cx.dma("sp", alog[:], alog_d, writes=[K("alog")], skey="alog")
    cx.dma("sp", dtb[:], dtb_d, writes=[K("dtb")], skey="dtb")
    w_in_r = w_in.rearrange("(kc p) n -> p kc n", p=128)
    cx.dma("pool", wab[:], w_in_r[:, :, 6144:6176], writes=[K("wab")], skey="wab")
    _chk(0)
    norm_stage(cx, c, tag, x_in, nw_d, hT, xkey_in, ps[6], pk[6])
    _chk(1)

    f2 = lambda t: t[:, :, :].rearrange("p c h -> p (c h)")
    for which in range(2):
        def mm(e, which=which):
            ins = None
            for ct in range(NTILE):
                for kc in range(KC):
                    ins = e.matmul(ps[which][:, ct * 16:(ct + 1) * 16], hT[:, kc, ct * 128:(ct + 1) * 128],
                                   wab[:, kc, which * 16:(which + 1) * 16], start=(kc == 0), stop=(kc == KC - 1))
            return ins
        cx.op("pe", mm, reads=[], writes=[pk[which]])
    _chk(1.05)
    cx.op("act", lambda e: e.activation(f2(beta), ps[0][:], AF.Sigmoid), reads=[pk[0]], writes=[K("beta")])
    _chk(1.1)
    cx.op("dve", lambda e: e.tensor_tensor(out=f2(gt), in0=ps[1][:], in1=dtb[:], op=ALU.add),
          reads=[pk[1], K("dtb")], writes=[K("gt")])
    _chk(1.2)
    cx.op("act", lambda e: e.activation(f2(gt), f2(gt), AF.Exp), reads=[K("gt")], writes=[K("gt")])
    _chk(1.4)
    cx.op("act", lambda e: e.activation(f2(gt), f2(gt), AF.Ln, bias=1.0), reads=[K("gt")], writes=[K("gt")])
    _chk(1.6)
    cx.op("act", lambda e: e.activation(alog[:], alog[:], AF.Exp), reads=[K("alog")], writes=[K("alog")])
    _chk(1.8)
    cx.op("dve", lambda e: e.scalar_tensor_tensor(out=f2(gt), in0=f2(gt), scalar=-1.0, in1=alog[:],
                                                   op0=ALU.mult, op1=ALU.mult),
          reads=[K("gt"), K("alog")], writes=[K("gt")])
    cx.op("dve", lambda e: e.tensor_scalar(f2(nbeta), f2(beta), -1.0, None, ALU.mult),
          reads=[K("beta")], writes=[K("nbeta")])
    _chk(2)

    cx.dma("pool", trib[:], cst_d[:, C_TRI:C_TRI + 128], writes=[K("trib")], skey="trib")
    cx.dma("pool", trisb[:], cst_d[:, C_TRIS:C_TRIS + 128], writes=[K("trisb")], skey="trisb")
    cx.op("dve", lambda e: e.tensor_copy(f2(ghi), f2(gt)), reads=[K("gt")], writes=[K("ghi")])
    cx.op("dve", lambda e: e.tensor_tensor(out=f2(glo), in0=f2(gt), in1=f2(ghi), op=ALU.subtract),
          reads=[K("gt"), K("ghi")], writes=[K("glo")])
    _chk(2.2)

    def mmg(e):
        ins = None
        for ct in range(NTILE):
            for dst, m in ((ps[2], trib), (ps[3], trisb), (ps[4], c["ones_bf"])):
                e.matmul(dst[:, ct * 16:(ct + 1) * 16], m[:], ghi[:, ct, :], start=True, stop=False)
                ins = e.matmul(dst[:, ct * 16:(ct + 1) * 16], m[:], glo[:, ct, :], start=False, stop=True)
        return ins
    cx.op("pe", mmg, reads=[K("ghi"), K("glo"), K("trib"), K("trisb"), "ones_bf"], writes=[pk[2], pk[3], pk[4]])
    _chk(2.4)
    cx.op("dve", lambda e: e.tensor_copy(f2(gchi), ps[2][:]), reads=[pk[2]], writes=[K("gchi")])
    cx.op("dve", lambda e: e.tensor_tensor(out=f2(gclo), in0=ps[2][:], in1=f2(gchi), op=ALU.subtract),
          reads=[pk[2], K("gchi")], writes=[K("gclo")])
    _chk(2.6)
    cx.op("dve", lambda e: e.tensor_copy(f2(gc), ps[2][:]), reads=[pk[2]], writes=[K("gc")])
    cx.op("act", lambda e: e.activation(f2(kbs), ps[2][:], AF.Exp), reads=[pk[2]], writes=[K("kbs")])
    _chk(2.7)
    cx.op("dve", lambda e: e.tensor_tensor(out=f2(kbs), in0=f2(kbs), in1=f2(beta), op=ALU.mult),
          reads=[K("kbs"), K("beta")], writes=[K("kbs")])
    _chk(2.8)
    cx.op("act", lambda e: e.activation(f2(egr), ps[3][:], AF.Exp), reads=[pk[3]], writes=[K("egr")])
    cx.op("act", lambda e: e.activation(f2(cd), ps[4][:], AF.Exp), reads=[pk[4]], writes=[K("cd")])
    _chk(2.9)
    cx.barrier()
    cx.pop()
    _chk(3)

    qT = cx.sbuf(tag + "qT", [128, 2, T], BF16)
    kT = cx.sbuf(tag + "kT", [128, 2, T], BF16)
    ktok = cx.sbuf(tag + "ktok", [128, NTILE, 2, 128], BF16)
    vtok = cx.sbuf(tag + "vtok", [128, NTILE, 4, 128], BF16)
    PW = 1024
    for hgp in range(DBG.get('hgps', 4)):
        cx.push()
        wch = [cx.sbuf(tag + "wch%d" % i, [128, KC, 128], BF16) for i in range(2)]
        xc = [cx.sbuf(tag + "xc%d" % i, [128, 3 + PW], F32) for i in range(2)]
        acc = cx.sbuf(tag + "acc", [128, PW], F32)
        sil = cx.sbuf(tag + "sil", [128, PW], F32)
        silb = cx.sbuf(tag + "silb", [128, PW], BF16)
        sqb = cx.sbuf(tag + "sqb", [128, PW], BF16)
        rn = cx.sbuf(tag + "rn", [128, PW], F32)
        pst = cx.psum(tag + "pst", [128, PW], BF16) if False else None
        chunks = [("q", 0), ("q", 1), ("k", 0), ("k", 1), ("v", 0), ("v", 1), ("v", 2), ("v", 3)]
        wcnt = 0
        for kind, li in chunks:
            if kind == "q":
                gch = 2 * hgp + li
            elif kind == "k":
                gch = 8 + 2 * hgp + li
            else:
                gch = 16 + 4 * hgp + li
            ws = wcnt % 2
            wcnt += 1
            cx.dma("pool", wch[ws][:], w_in_r[:, :, gch * 128:(gch + 1) * 128], writes=[(K("wch"), ws)],
                   skey=("wch", ws))
            cx.op("pool", lambda e: e.memset(xc[0][:, 0:3], 0.0), writes=[(K("xc"), 0)])
            for pc in range(T // PW):
                pb = pc % 2
                for half in range(PW // TT):
                    tt = pc * (PW // TT) + half
                    pp = ps[half]

                    def mm(e, ws=ws, tt=tt, pp=pp):
                        ins = None
                        for kc in range(KC):
                            ins = e.matmul(pp[:], wch[ws][:, kc, :], hT[:, kc, tt * TT:(tt + 1) * TT],
                                           start=(kc == 0), stop=(kc == KC - 1))
                        return ins
                    cx.op("pe", mm, reads=[(K("wch"), ws)], writes=[pk[half]])
                    cx.op("act", lambda e, pb=pb, half=half, pp=pp: e.activation(
                        xc[pb][:, 3 + half * TT: 3 + (half + 1) * TT], pp[:], AF.Copy),
                        reads=[pk[half]], writes=[(K("xc"), pb)])
                for j in range(4):
                    if j == 0:
                        cx.op("dve", lambda e, pb=pb, gch=gch: e.tensor_scalar(
                            acc[:], xc[pb][:, 0:PW], cw[:, gch, 0:1], None, ALU.mult),
                            reads=[(K("xc"), pb), K("cw")], writes=[K("acc")])
                    else:
                        cx.op("dve", lambda e, pb=pb, gch=gch, j=j: e.scalar_tensor_tensor(
                            out=acc[:], in0=xc[pb][:, j:j + PW], scalar=cw[:, gch, j:j + 1], in1=acc[:],
                            op0=ALU.mult, op1=ALU.add), reads=[(K("xc"), pb), K("cw"), K("acc")], writes=[K("acc")])
                cx.op("dve", lambda e, pb=pb: e.tensor_copy(xc[1 - pb][:, 0:3], xc[pb][:, PW:PW + 3]),
                      reads=[(K("xc"), pb)], writes=[(K("xc"), 1 - pb)])
                tsl = slice(pc * PW, (pc + 1) * PW)
                _chk(4)
                if kind == "v":
                    cx.op("act", lambda e: e.activation(silb[:], acc[:], AF.Silu), reads=[K("acc")], writes=[K("silb")])
                    src_t = silb
                else:
                    cx.op("act", lambda e: e.activation(sil[:], acc[:], AF.Silu), reads=[K("acc")], writes=[K("sil")])
                    cx.op("act", lambda e: e.activation(sqb[:], sil[:], AF.Square), reads=[K("sil")], writes=[K("sqb")])

                    def mms(e):
                        ins = None
                        for half in range(PW // TT):
                            ins = e.matmul(ps[2 + half][:], c["ones_bf"][:], sqb[:, half * TT:(half + 1) * TT],
                                           start=True, stop=True)
                        return ins
                    cx.op("pe", mms, reads=[K("sqb"), "ones_bf"], writes=[pk[2], pk[3]])
                    for half in range(PW // TT):
                        cx.op("act", lambda e, half=half: e.activation(
                            rn[:, half * TT:(half + 1) * TT], ps[2 + half][:], AF.Sqrt, bias=1e-6, scale=1.0),
                            reads=[pk[2 + half]], writes=[K("rn")])
                    cx.op("dve", lambda e: e.reciprocal(rn[:], rn[:]), reads=[K("rn")], writes=[K("rn")])
                    dstT = qT if kind == "q" else kT
                    sc = (128 ** -0.5) if kind == "q" else 1.0
                    cx.op("dve", lambda e, dstT=dstT, li=li, tsl=tsl, sc=sc: e.scalar_tensor_tensor(
                        out=dstT[:, li, tsl], in0=sil[:], scalar=sc, in1=rn[:], op0=ALU.mult, op1=ALU.mult),
                        reads=[K("sil"), K("rn")], writes=[(K(kind + "T"), li)])
                    src_t = None
                if kind in ("k", "v"):
                    for q4 in range(PW // 512):
                        pbank = 4 + q4 % 2
                        ptv = ps[pbank][:, :].bitcast(BF16)

                        def mmt(e, q4=q4, ptv=ptv, kind=kind, li=li, pc=pc, src_t=src_t):
                            ins = None
                            for i in range(4):
                                col = q4 * 512 + i * 128
                                if kind == "v":
                                    src = src_t[:, col:col + 128]
                                else:
                                    src = kT[:, li, pc * PW + col: pc * PW + col + 128]
                                ins = e.transpose(ptv[:, i * 128:(i + 1) * 128], src, ident[:])
                            return ins
                        rk = [K("silb")] if kind == "v" else [(K("kT"), li)]
                        cx.op("pe", mmt, reads=rk + [K("ident")], writes=[pk[pbank]])
                        t0 = pc * (PW // 128) + q4 * 4
                        dst = vtok if kind == "v" else ktok
                        cx.op("act", lambda e, dst=dst, t0=t0, li=li, ptv=ptv: e.activation(
                            dst[:, t0:t0 + 4, li, :], ptv[:, 0:512].rearrange("p (i x) -> p i x", i=4), AF.Copy),
                            reads=[pk[pbank]], writes=[(K(kind + "tok"), li)])
        cx.barrier()
        cx.pop()
        _chk(5)

        cx.push()
        Abuf = [cx.sbuf(tag + "A%d" % i, [128, 4, 128], BF16) for i in range(2)]
        Atbuf = [cx.sbuf(tag + "At%d" % i, [128, 4, 128], BF16) for i in range(2)]
        Tt = cx.sbuf(tag + "Tt", [128, 4, 128], BF16)
        A_O = cx.sbuf(tag + "A_O", [128, 4, 128], BF16)
        AtO = cx.sbuf(tag + "AtO", [128, 4, 128], BF16)
        dmT = cx.sbuf(tag + "dmT", [128, 4, 128], F32)
        dm2 = cx.sbuf(tag + "dm2", [128, 4, 128], F32)
        eB = cx.sbuf(tag + "eB", [128, 4, 128], F32)
        qkT = cx.sbuf(tag + "qkT", [128, 4, 128], BF16)
        qdT = cx.sbuf(tag + "qdT", [128, 4, 128], BF16)
        vb = cx.sbuf(tag + "vb", [128, 4, 128], BF16)
        kb = cx.sbuf(tag + "kb", [128, 4, 128], BF16)
        kd = cx.sbuf(tag + "kd", [128, 4, 128], BF16)
        u = cx.sbuf(tag + "u", [128, 4, 128], F32)
        wT = cx.sbuf(tag + "wT", [128, 4, 128], BF16)
        vnew = cx.sbuf(tag + "vnew", [128, 4, 128], BF16)
        S32 = cx.sbuf(tag + "S32", [128, 4, 128], F32)
        S16 = cx.sbuf(tag + "S16", [128, 4, 128], BF16)
        ost = [cx.sbuf(tag + "ost%d" % i, [128, 4, 256], F32) for i in range(2)]
        fl = lambda t: t[:, :, :].rearrange("p h x -> p (h x)")
        cx.op("dve", lambda e: e.memset(fl(S32), 0.0), writes=[K("S32")])
        cx.op("dve", lambda e: e.memset(fl(S16), 0.0), writes=[K("S16")])
        for ct in range(DBG.get('tiles', NTILE)):
            tsl = slice(ct * 128, (ct + 1) * 128)
            hs = [4 * hgp + hl for hl in range(4)]

            def mm1(e, ct=ct, tsl=tsl, hs=hs):
                for hl in range(4):
                    e.matmul(ps[0][:, hl * 128:(hl + 1) * 128], gchi[:, ct, hs[hl]:hs[hl] + 1].to_broadcast([128, 128]),
                             ident[:], start=True, stop=False)
                    e.matmul(ps[0][:, hl * 128:(hl + 1) * 128], gclo[:, ct, hs[hl]:hs[hl] + 1].to_broadcast([128, 128]),
                             ident[:], start=False, stop=True)
                ins = None
                for khl in range(2):
                    e.matmul(ps[1][:, khl * 128:(khl + 1) * 128], kT[:, khl, tsl], kT[:, khl, tsl], start=True, stop=True)
                    ins = e.matmul(ps[1][:, (2 + khl) * 128:(3 + khl) * 128], kT[:, khl, tsl], qT[:, khl, tsl],
                                   start=True, stop=True)
                return ins
            cx.op("pe", mm1, reads=[K("ident")], writes=[pk[0], pk[1]])
            _chk(6)
            for hl in range(4):
                h = hs[hl]
                cx.op("dve", lambda e, hl=hl, h=h, ct=ct: e.tensor_scalar(
                    dmT[:, hl, :], ps[0][:, hl * 128:(hl + 1) * 128], gc[:, ct, h:h + 1], 0.0, ALU.subtract, ALU.min),
                    reads=[pk[0]], writes=[K("dmT")])
                cx.op("dve", lambda e, hl=hl, h=h, ct=ct: e.tensor_scalar(
                    dm2[:, hl, :], ps[0][:, hl * 128:(hl + 1) * 128], gc[:, ct, h:h + 1], 0.0, ALU.subtract, ALU.max),
                    reads=[pk[0]], writes=[K("dm2")])
            cx.op("act", lambda e: e.activation(fl(dmT), fl(dmT), AF.Exp), reads=[K("dmT")], writes=[K("dmT")])
            cx.op("act", lambda e: e.activation(fl(dm2), fl(dm2), AF.Exp, scale=-1.0), reads=[K("dm2")], writes=[K("dm2")])
            cx.op("act", lambda e: e.activation(fl(eB), ps[0][:], AF.Exp), reads=[pk[0]], writes=[K("eB")])
            cx.op("dve", lambda e: e.tensor_tensor(out=fl(dmT), in0=fl(dmT), in1=fl(minc4), op=ALU.mult),
                  reads=[K("dmT")], writes=[K("dmT")])
            cx.op("dve", lambda e: e.tensor_tensor(out=fl(dm2), in0=fl(dm2), in1=fl(mlow4), op=ALU.mult),
                  reads=[K("dm2")], writes=[K("dm2")])
            A, At = A_O, AtO
            for hl in range(4):
                h = hs[hl]
                khl = hl // 2
                cx.op("dve", lambda e, hl=hl, h=h, khl=khl, ct=ct, A=A: e.scalar_tensor_tensor(
                    out=A[:, hl, :], in0=ps[1][:, khl * 128:(khl + 1) * 128], scalar=nbeta[:, ct, h:h + 1],
                    in1=dm2[:, hl, :], op0=ALU.mult, op1=ALU.mult), reads=[pk[1], K("dm2")], writes=[(K("A"), 2)])
                cx.op("dve", lambda e, hl=hl, khl=khl: e.tensor_tensor(
                    out=qkT[:, hl, :], in0=ps[1][:, (2 + khl) * 128:(3 + khl) * 128], in1=dmT[:, hl, :], op=ALU.mult),
                    reads=[pk[1], K("dmT")], writes=[K("qkT")])
                cx.op("dve", lambda e, hl=hl, khl=khl, tsl=tsl: e.tensor_tensor(
                    out=qdT[:, hl, :], in0=qT[:, khl, tsl], in1=eB[:, hl, :], op=ALU.mult),
                    reads=[K("eB")], writes=[K("qdT")])
                cx.op("pool", lambda e, hl=hl, h=h, ct=ct: e.tensor_scalar(
                    vb[:, hl, :], vtok[:, ct, hl, :], beta[:, ct, h:h + 1], None, ALU.mult), writes=[K("vb")])
                cx.op("pool", lambda e, hl=hl, h=h, khl=khl, ct=ct: e.tensor_scalar(
                    kb[:, hl, :], ktok[:, ct, khl, :], kbs[:, ct, h:h + 1], None, ALU.mult), writes=[K("kb")])
                cx.op("pool", lambda e, hl=hl, h=h, khl=khl, ct=ct: e.tensor_scalar(
                    kd[:, hl, :], ktok[:, ct, khl, :], egr[:, ct, h:h + 1], None, ALU.mult), writes=[K("kd")])
            _chk(7)
            ptv = ps[2][:, :].bitcast(BF16)

            def mmT(e, A=A, ptv=ptv):
                ins = None
                for hl in range(4):
                    ins = e.transpose(ptv[:, hl * 128:(hl + 1) * 128], A[:, hl, :], ident[:])
                return ins
            cx.op("pe", mmT, reads=[(K("A"), 2), K("ident")], writes=[pk[2]])
            cx.op("act", lambda e, At=At, ptv=ptv: e.activation(fl(At), ptv[:, 0:512], AF.Copy),
                  reads=[pk[2]], writes=[(K("At"), 2)])
            cx.op("dve", lambda e, At=At: e.tensor_tensor(out=fl(Tt), in0=fl(At), in1=fl(id4), op=ALU.add),
                  reads=[(K("At"), 2), K("id4")], writes=[K("Tt")])
            _chk(8)
            bufs = [(Abuf[0], Atbuf[0]), (Abuf[1], Atbuf[1]), (A_O, AtO)]
            cur = 2
            NSTEP = 6
            for p in range(1, NSTEP + 1):
                nxt = 0 if cur == 2 else 1 - cur
                A, At = bufs[cur]
                An, Atn = bufs[nxt]
                last = (p == NSTEP)

                def mmsq(e, A=A, At=At, last=last):
                    ins = None
                    for hl in range(4):
                        ins = e.matmul(ps[3][:, hl * 128:(hl + 1) * 128], At[:, hl, :], A[:, hl, :], start=True, stop=True)
                    if not last:
                        for hl in range(4):
                            ins = e.matmul(ps[4][:, hl * 128:(hl + 1) * 128], A[:, hl, :], At[:, hl, :],
                                           start=True, stop=True)
                    return ins
                cx.op("pe", mmsq, reads=[(K("A"), cur), (K("At"), cur)], writes=[pk[3]] + ([] if last else [pk[4]]))
                cx.op("act", lambda e, An=An: e.activation(fl(An), ps[3][:], AF.Copy),
                      reads=[pk[3]], writes=[(K("A"), nxt)])
                if not last:
                    cx.op("act", lambda e, Atn=Atn: e.activation(fl(Atn), ps[4][:], AF.Copy),
                          reads=[pk[4]], writes=[(K("At"), nxt)])

                def mmtu(e, An=An):
                    ins = None
                    for hl in range(4):
                        ins = e.matmul(ps[5][:, hl * 128:(hl + 1) * 128], An[:, hl, :], Tt[:, hl, :], start=True, stop=True)
                    return ins
                cx.op("pe", mmtu, reads=[(K("A"), nxt), K("Tt")], writes=[pk[5]])
                cx.op("dve", lambda e: e.tensor_tensor(out=fl(Tt), in0=fl(Tt), in1=ps[5][:], op=ALU.add),
                      reads=[pk[5], K("Tt")], writes=[K("Tt")])
                cur = nxt
            T0, Rb = Atbuf[0], Abuf[0]
            ptv3 = ps[3][:, :].bitcast(BF16)

            def mmT0(e, ptv3=ptv3):
                ins = None
                for hl in range(4):
                    ins = e.transpose(ptv3[:, hl * 128:(hl + 1) * 128], Tt[:, hl, :], ident[:])
                return ins
            cx.op("pe", mmT0, reads=[K("Tt"), K("ident")], writes=[pk[3]])
            cx.op("act", lambda e, ptv3=ptv3: e.activation(fl(T0), ptv3[:, 0:512], AF.Copy),
                  reads=[pk[3]], writes=[(K("At"), 0)])

            def mmR(e):
                ins = None
                for hl in range(4):
                    ins = e.matmul(ps[4][:, hl * 128:(hl + 1) * 128], AtO[:, hl, :], T0[:, hl, :], start=True, stop=True)
                return ins
            cx.op("pe", mmR, reads=[(K("At"), 2), (K("At"), 0)], writes=[pk[4]])
            cx.op("dve", lambda e: e.scalar_tensor_tensor(out=fl(dm2), in0=fl(T0), scalar=-1.0, in1=ps[4][:],
                                                           op0=ALU.mult, op1=ALU.add),
                  reads=[(K("At"), 0), pk[4]], writes=[K("dm2")])
            cx.op("dve", lambda e: e.tensor_tensor(out=fl(Rb), in0=fl(dm2), in1=fl(id4), op=ALU.add),
                  reads=[K("dm2"), K("id4")], writes=[(K("A"), 0)])

            def mmU(e):
                ins = None
                for hl in range(4):
                    ins = e.matmul(ps[5][:, hl * 128:(hl + 1) * 128], Rb[:, hl, :], Tt[:, hl, :], start=True, stop=True)
                return ins
            cx.op("pe", mmU, reads=[(K("A"), 0), K("Tt")], writes=[pk[5]])
            cx.op("dve", lambda e: e.tensor_tensor(out=fl(Tt), in0=fl(Tt), in1=ps[5][:], op=ALU.add),
                  reads=[pk[5], K("Tt")], writes=[K("Tt")])

            def mmuw(e):
                ins = None
                for hl in range(4):
                    e.matmul(ps[6][:, hl * 128:(hl + 1) * 128], Tt[:, hl, :], vb[:, hl, :], start=True, stop=True)
                    ins = e.matmul(ps[7][:, hl * 128:(hl + 1) * 128], kb[:, hl, :], Tt[:, hl, :], start=True, stop=True)
                return ins
            cx.op("pe", mmuw, reads=[K("Tt"), K("vb"), K("kb")], writes=[pk[6], pk[7]])
            cx.op("act", lambda e: e.activation(fl(u), ps[6][:], AF.Copy), reads=[pk[6]], writes=[K("u")])
            cx.op("act", lambda e: e.activation(fl(wT), ps[7][:], AF.Copy), reads=[pk[7]], writes=[K("wT")])
            _chk(9)

            def mmvn(e):
                ins = None
                for hl in range(4):
                    ins = e.matmul(ps[0][:, hl * 128:(hl + 1) * 128], wT[:, hl, :], S16[:, hl, :], start=True, stop=True)
                return ins
            cx.op("pe", mmvn, reads=[K("wT"), K("S16")], writes=[pk[0]])
            cx.op("dve", lambda e: e.tensor_tensor(out=fl(vnew), in0=fl(u), in1=ps[0][:], op=ALU.subtract),
                  reads=[K("u"), pk[0]], writes=[K("vnew")])

            def mmo(e):
                ins = None
                for hl in range(4):
                    e.matmul(ps[1][:, hl * 128:(hl + 1) * 128], S16[:, hl, :], qdT[:, hl, :], start=True, stop=False)
                    e.matmul(ps[1][:, hl * 128:(hl + 1) * 128], vnew[:, hl, :], qkT[:, hl, :], start=False, stop=True)
                for hl in range(4):
                    ins = e.matmul(ps[2][:, hl * 128:(hl + 1) * 128], kd[:, hl, :], vnew[:, hl, :], start=True, stop=True)
                return ins
            cx.op("pe", mmo, reads=[K("S16"), K("qdT"), K("vnew"), K("qkT"), K("kd")], writes=[pk[1], pk[2]])
            ob = (ct // 2) % 2
            cx.op("act", lambda e, ob=ob, ct=ct: e.activation(
                ost[ob][:, :, (ct % 2) * 128:(ct % 2 + 1) * 128], ps[1][:, :].rearrange("p (h x) -> p h x", h=4), AF.Copy),
                reads=[pk[1]], writes=[(K("ost"), ob)])
            for hl in range(4):
                h = hs[hl]
                cx.op("dve", lambda e, hl=hl, h=h, ct=ct: e.scalar_tensor_tensor(
                    out=S32[:, hl, :], in0=S32[:, hl, :], scalar=cd[:, ct, h:h + 1], in1=ps[2][:, hl * 128:(hl + 1) * 128],
                    op0=ALU.mult, op1=ALU.add), reads=[pk[2], K("S32")], writes=[K("S32")])
            cx.op("act", lambda e: e.activation(fl(S16), fl(S32), AF.Copy), reads=[K("S32")], writes=[K("S16")])
            if ct % 2 == 1:
                for hl in range(4):
                    h = hs[hl]
                    cx.dma("sp", oraw_d[h * 128:(h + 1) * 128, (ct - 1) * 128:(ct + 1) * 128], ost[ob][:, hl, :],
                           reads=[(K("ost"), ob)], writes=[(okey, h, ct // 4)], skey=("ost", ob, hl))
        cx.barrier()
        cx.pop()
    cx.barrier()
    cx.pop()


def gdn_out_phase(cx, c, tag, x_in, x_out, oraw_d, nw_d, gnw_d, w_in, w_out, xkey_in, xkey_out, okey):
    cx.push()
    HT = 2048
    NT = HT // TT
    WB = 256
    nheads = NVH
    K = lambda n: tag + n
    hT = cx.sbuf(tag + "hT", [128, KC, HT], BF16)
    osb = cx.sbuf(tag + "osb", [128, nheads, HT], BF16)
    gnw = cx.sbuf(tag + "gnw", [128, 1], F32)
    nw = cx.sbuf(tag + "nw", [128, KC], F32)
    xt = [cx.sbuf(tag + "xt%d" % i, [128, KC, TT], F32) for i in range(1)]
    sq = cx.sbuf(tag + "sq", [128, KC, TT], BF16)
    rstd = cx.sbuf(tag + "rstd", [128, TT], F32)
    wz = [cx.sbuf(tag + "wz%d" % i, [128, KC, 128], BF16) for i in range(2)]
    orw = [cx.sbuf(tag + "orw%d" % i, [128, TT], F32) for i in range(2)]
    osq = [cx.sbuf(tag + "osq%d" % i, [128, TT], BF16) for i in range(2)]
    orn = [cx.sbuf(tag + "orn%d" % i, [128, TT], F32) for i in range(2)]
    sz = [cx.sbuf(tag + "sz%d" % i, [128, TT], F32) for i in range(2)]
    wo = [cx.sbuf(tag + "wo%d" % i, [128, nheads, WB], BF16) for i in range(2)]
    xr = [cx.sbuf(tag + "xr%d" % i, [128, TT], F32) for i in range(2)]
    xn = [cx.sbuf(tag + "xn%d" % i, [128, TT], F32) for i in range(2)]
    ps = [cx.psum(tag + "ps%d" % i, [128, TT], F32) for i in range(7)]
    pk = [tag + "ps%d" % i for i in range(7)]
    cx.psum_keys.update(pk)
    w_in_r = w_in.rearrange("(kc p) n -> p kc n", p=128)
    w_out_r = w_out.rearrange("(kc p) n -> p kc n", p=128)
    x_in_r = x_in.rearrange("(kc p) t -> p kc t", p=128)
    cx.dma("sp", nw[:], nw_d, writes=[K("nw")], skey="nw")
    cx.dma("sp", gnw[:], gnw_d, writes=[K("gnw")], skey="gnw")
    cnt = {"o": 0, "x": 0, "z": 0, "w": 0}
    for hf in range(T // HT):
        t0 = hf * HT
        for tt in range(NT):
            gt_ = (t0 + tt * TT) // TT
            cx.dma("sp", xt[0][:], x_in_r[:, :, t0 + tt * TT: t0 + (tt + 1) * TT],
                   reads=[(xkey_in, gt_, kc) for kc in range(KC)], writes=[K("xt")], skey="xt")
            rmsnorm_tile(cx, c, xt[0], nw, lambda kc, tt=tt: hT[:, kc, tt * TT:(tt + 1) * TT], tag,
                         [K("xt")], (K("hT"), tt), pk[6], ps[6], {"sq": sq, "rstd": rstd})
        for h in range(nheads):
            ws = cnt["w"] % 2
            cnt["w"] += 1
            cx.dma("pool", wz[ws][:], w_in_r[:, :, 4096 + h * 128: 4096 + (h + 1) * 128], writes=[(K("wz"), ws)],
                   skey=("wz", ws))
            for tt in range(NT):
                gt_ = (t0 + tt * TT) // TT
                b = cnt["z"] % 2
                cnt["z"] += 1
                cx.dma("sp", orw[b][:], oraw_d[h * 128:(h + 1) * 128, t0 + tt * TT: t0 + (tt + 1) * TT],
                       reads=[(okey, h, gt_)], writes=[(K("orw"), b)], skey=("orw", b))

                def mmz(e, ws=ws, tt=tt, b=b):
                    ins = None
                    for kc in range(KC):
                        ins = e.matmul(ps[b][:], wz[ws][:, kc, :], hT[:, kc, tt * TT:(tt + 1) * TT],
                                       start=(kc == 0), stop=(kc == KC - 1))
                    return ins
                cx.op("pe", mmz, reads=[(K("wz"), ws), (K("hT"), tt)], writes=[pk[b]])
                cx.op("act", lambda e, b=b: e.activation(osq[b][:], orw[b][:], AF.Square),
                      reads=[(K("orw"), b)], writes=[(K("osq"), b)])
                cx.op("pe", lambda e, b=b: e.matmul(ps[2 + b][:], c["ones_bf"][:], osq[b][:], start=True, stop=True),
                      reads=[(K("osq"), b), "ones_bf"], writes=[pk[2 + b]])
                cx.op("act", lambda e, b=b: e.activation(orn[b][:], ps[2 + b][:], AF.Sqrt, bias=EPS, scale=1.0 / 128),
                      reads=[pk[2 + b]], writes=[(K("orn"), b)])
                cx.op("dve", lambda e, b=b: e.reciprocal(orn[b][:], orn[b][:]), reads=[(K("orn"), b)],
                      writes=[(K("orn"), b)])
                cx.op("act", lambda e, b=b: e.activation(sz[b][:], ps[b][:], AF.Silu), reads=[pk[b]],
                      writes=[(K("sz"), b)])
                cx.op("dve", lambda e, b=b: e.scalar_tensor_tensor(
                    out=orn[b][:], in0=orw[b][:], scalar=gnw[:, 0:1], in1=orn[b][:], op0=ALU.mult, op1=ALU.mult),
                    reads=[(K("orw"), b), (K("orn"), b), K("gnw")], writes=[(K("orn"), b)])
                cx.op("dve", lambda e, b=b, h=h, tt=tt: e.tensor_tensor(
                    out=osb[:, h, tt * TT:(tt + 1) * TT], in0=orn[b][:], in1=sz[b][:], op=ALU.mult),
                    reads=[(K("orn"), b), (K("sz"), b)], writes=[(K("osb"), h)])
        for mb in range(D // WB):
            s = cnt["o"] % 2
            cnt["o"] += 1
            cx.dma("pool", wo[s][:], w_out_r[:, :, mb * WB:(mb + 1) * WB], writes=[(K("wo"), s)], skey=("wo", s))
            for mm_ in range(WB // 128):
                mc = mb * (WB // 128) + mm_
                for tt in range(NT):
                    gt_ = (t0 + tt * TT) // TT
                    b = cnt["x"] % 2
                    cnt["x"] += 1
                    py = ps[4 + b]
                    cx.dma("sp", xr[b][:], x_in[mc * 128:(mc + 1) * 128, t0 + tt * TT: t0 + (tt + 1) * TT],
                           reads=[(xkey_in, gt_, mc)], writes=[(K("xr"), b)], skey=("xr", b))

                    def mm2(e, s=s, mm_=mm_, tt=tt, py=py):
                        ins = None
                        for kc in range(nheads):
                            ins = e.matmul(py[:], wo[s][:, kc, mm_ * 128:(mm_ + 1) * 128],
                                           osb[:, kc, tt * TT:(tt + 1) * TT], start=(kc == 0), stop=(kc == nheads - 1))
                        return ins
                    cx.op("pe", mm2, reads=[(K("wo"), s)] + [(K("osb"), hd) for hd in range(nheads)],
                          writes=[pk[4 + b]])
                    cx.op("dve", lambda e, b=b, py=py: e.tensor_tensor(
                        out=xn[b][:], in0=py[:], in1=xr[b][:], op=ALU.add),
                        reads=[pk[4 + b], (K("xr"), b)], writes=[(K("xn"), b)])
                    cx.dma("sp", x_out[mc * 128:(mc + 1) * 128, t0 + tt * TT: t0 + (tt + 1) * TT], xn[b][:],
                           reads=[(K("xn"), b)], writes=[(xkey_out, gt_, mc)], skey=("xn", b))
    cx.barrier()
    cx.pop()


def build_gdn_prog():
    nc = bass.Bass("TRN2", target_bir_lowering=False)
    x = nc.dram_tensor("x", [D, T], F32, kind="ExternalInput").ap()
    nw = nc.dram_tensor("nw", [128, KC], F32, kind="ExternalInput").ap()
    w_in = nc.dram_tensor("w_in", [D, GP], F32, kind="ExternalInput").ap()
    w_out = nc.dram_tensor("w_out", [NVH * 128, D], F32, kind="ExternalInput").ap()
    convw = nc.dram_tensor("convw", [128, 32, 4], F32, kind="ExternalInput").ap()
    alog = nc.dram_tensor("alog", [128, 512], F32, kind="ExternalInput").ap()
    dtb = nc.dram_tensor("dtb", [128, 512], F32, kind="ExternalInput").ap()
    gnw = nc.dram_tensor("gnw", [128, 1], F32, kind="ExternalInput").ap()
    cst = nc.dram_tensor("cst", [128, CST_W], F32, kind="ExternalInput").ap()
    oraw = nc.dram_tensor("oraw_scratch", [NVH * 128, T], F32).ap()
    y = nc.dram_tensor("y", [D, T], F32, kind="ExternalOutput").ap()
    cx = Ctx(nc)
    c = consts(cx)
    gdn_core_phase(cx, c, "g_", x, oraw, nw, w_in, convw, alog, dtb, cst, "xin", "or")
    if not DBG.get("noout"):
        gdn_out_phase(cx, c, "go_", x, y, oraw, nw, gnw, w_in, w_out, "xin", "xout", "or")
    cx.finish()
    return nc


def final_norm_phase(cx, c, tag, x_in, y_out, nw_d, xkey_in):
    cx.push()
    xt = [cx.sbuf(tag + "xt%d" % i, [128, KC, TT], F32) for i in range(2)]
    yo = [cx.sbuf(tag + "yo%d" % i, [128, KC, TT], F32) for i in range(2)]
    sq = cx.sbuf(tag + "sq", [128, KC, TT], BF16)
    rstd = cx.sbuf(tag + "rstd", [128, TT], F32)
    nw = cx.sbuf(tag + "nw", [128, KC], F32)
    ps = cx.psum(tag + "ps", [128, TT], F32)
    cx.psum_keys.add(tag + "ps")
    cx.dma("sp", nw[:], nw_d, writes=[tag + "nw"], skey="nw")
    x_in_r = x_in.rearrange("(kc p) t -> p kc t", p=128)
    y_out_r = y_out.rearrange("(kc p) t -> p kc t", p=128)
    for tt in range(NTT):
        b = tt % 2
        cx.dma("sp", xt[b][:], x_in_r[:, :, tt * TT:(tt + 1) * TT],
               reads=[(xkey_in, tt, kc) for kc in range(KC)], writes=[(tag + "xt", b)], skey=("xt", b))
        rmsnorm_tile(cx, c, xt[b], nw, lambda kc, b=b: yo[b][:, kc, :], tag,
                     [(tag + "xt", b)], (tag + "yo", b), tag + "ps", ps, {"sq": sq, "rstd": rstd})
        cx.dma("sp", y_out_r[:, :, tt * TT:(tt + 1) * TT], yo[b][:], reads=[(tag + "yo", b)],
               writes=[("yout", tt)], skey=("yo", b))
    cx.barrier()
    cx.pop()


def build_fin_prog():
    nc = bass.Bass("TRN2", target_bir_lowering=False)
    x = nc.dram_tensor("x", [D, T], F32, kind="ExternalInput").ap()
    nw = nc.dram_tensor("nw", [128, KC], F32, kind="ExternalInput").ap()
    y = nc.dram_tensor("y", [D, T], F32, kind="ExternalOutput").ap()
    cx = Ctx(nc)
    c = consts(cx)
    final_norm_phase(cx, c, "n_", x, y, nw, "xin")
    cx.finish()
    return nc


def build_full_prog():
    nc = bass.Bass("TRN2", target_bir_lowering=False)
    dt = lambda name, shape, kind="ExternalInput", dtype=F32: nc.dram_tensor(name, list(shape), dtype, kind=kind).ap()
    x = dt("x", [D, T])
    nws = dt("nws", [13, 128, KC])
    ffn_w_in = dt("ffn_w_in", [4, 2, D, 2 * DFF])
    ffn_w_out = dt("ffn_w_out", [4, 2, DFF, D])
    attn_w_in = dt("attn_w_in", [2, D, 3 * NH_A * 128])
    attn_w_out = dt("attn_w_out", [2, NH_A * 128, D])
    gdn_w_in = dt("gdn_w_in", [2, D, GP])
    gdn_w_out = dt("gdn_w_out", [2, NVH * 128, D])
    convw = dt("convw", [2, 128, 32, 4])
    alog = dt("alog", [2, 128, 512])
    dtb = dt("dtb", [2, 128, 512])
    gnw = dt("gnw", [2, 128, 1])
    ctab = dt("ctab", [128, T])
    stab = dt("stab", [128, T])
    cst = dt("cst", [128, CST_W])
    y = dt("y", [D, T], kind="ExternalOutput")
    xbuf = [nc.dram_tensor("xres%d" % i, [D, T], F32).ap() for i in range(2)]
    os_d = nc.dram_tensor("os_scratch", [NH_A * 128, T], BF16).ap()
    oraw = nc.dram_tensor("oraw_scratch", [NVH * 128, T], F32).ap()
    cx = Ctx(nc)
    c = consts(cx)
    cur, nb_, ph = x, 0, 0
    ia = ib = 0
    for i in range(4):
        ffn_phase(cx, c, "f%da_" % i, cur, xbuf[nb_], nws[ph], ffn_w_in[i, 0], ffn_w_out[i, 0], "xi", "xo")
        cur, nb_, ph = xbuf[nb_], 1 - nb_, ph + 1
        if i % 2 == 0:
            attn_phase(cx, c, "a%d_" % i, cur, os_d, nws[ph], attn_w_in[ia], ctab, stab, cst, "xi", "os")
            outproj_phase(cx, c, "ao%d_" % i, cur, xbuf[nb_], os_d, attn_w_out[ia], NH_A, "xi", "xo", "os")
            ia += 1
        else:
            gdn_core_phase(cx, c, "g%d_" % i, cur, oraw, nws[ph], gdn_w_in[ib], convw[ib], alog[ib], dtb[ib], cst, "xi", "or")
            gdn_out_phase(cx, c, "go%d_" % i, cur, xbuf[nb_], oraw, nws[ph], gnw[ib], gdn_w_in[ib], gdn_w_out[ib],
                          "xi", "xo", "or")
            ib += 1
        cur, nb_, ph = xbuf[nb_], 1 - nb_, ph + 1
        ffn_phase(cx, c, "f%db_" % i, cur, xbuf[nb_], nws[ph], ffn_w_in[i, 1], ffn_w_out[i, 1], "xi", "xo")
        cur, nb_, ph = xbuf[nb_], 1 - nb_, ph + 1
    final_norm_phase(cx, c, "n_", cur, y, nws[ph], "xi")
    cx.finish()
    return nc


_PROGS = {}


def _prog(name):
    if name not in _PROGS:
        _PROGS[name] = {"ffn": build_ffn_prog, "attn": build_attn_prog, "gdn": build_gdn_prog,
                        "fin": build_fin_prog}[name]()
    return _PROGS[name]


def _lay_nw(v):
    return np.ascontiguousarray(np.asarray(v, np.float32).reshape(KC, 128).T)


def _launch(name, xs, shared):
    nc = _prog(name)
    n = len(xs)
    in_maps = [dict(shared, x=xs[i]) for i in range(n)]
    res = run_bass_kernel_spmd(nc, in_maps, core_ids=list(range(n)))
    return [np.asarray(res.results[i]["y"]) for i in range(n)]


def kernel(x, norm_w, ffn_w_in, ffn_w_out, attn_w_in, attn_w_out, gdn_w_in, gdn_conv_w, gdn_a_log,
           gdn_dt_bias, gdn_norm_w, gdn_w_out, final_norm_w):
    f = lambda a: np.ascontiguousarray(np.asarray(a, np.float32))
    x = f(x)
    B = x.shape[0]
    ctab, stab = host_rope()
    nws = np.stack([_lay_nw(norm_w[i, j]) for i in range(4) for j in range(3)] + [_lay_nw(final_norm_w)], 0)
    shared = {
        "nws": np.ascontiguousarray(nws), "ffn_w_in": f(ffn_w_in), "ffn_w_out": f(ffn_w_out),
        "attn_w_in": f(attn_w_in), "attn_w_out": f(attn_w_out), "gdn_w_in": f(gdn_w_in), "gdn_w_out": f(gdn_w_out),
        "convw": np.ascontiguousarray(f(gdn_conv_w).reshape(2, 4, 32, 128).transpose(0, 3, 2, 1)),
        "alog": np.ascontiguousarray(np.tile(f(gdn_a_log)[:, None, :], (1, 128, 32))),
        "dtb": np.ascontiguousarray(np.tile(f(gdn_dt_bias)[:, None, :], (1, 128, 32))),
        "gnw": np.ascontiguousarray(f(gdn_norm_w).reshape(2, 128, 1)),
        "ctab": ctab, "stab": stab, "cst": host_consts(),
    }
    if "full" not in _PROGS:
        _PROGS["full"] = build_full_prog()
    nc = _PROGS["full"]
    in_maps = [dict(shared, x=np.ascontiguousarray(x[b].T)) for b in range(B)]
    res = run_bass_kernel_spmd(nc, in_maps, core_ids=list(range(B)))
    return np.stack([np.asarray(res.results[b]["y"]).T for b in range(B)], 0).astype(np.float32)
```

```python
from contextlib import ExitStack
import numpy as np
import concourse.bass as bass
import concourse.mybir as mybir
from concourse.bass_utils import run_bass_kernel_spmd

F32 = mybir.dt.float32
BF16 = mybir.dt.bfloat16
ALU = mybir.AluOpType
AF = mybir.ActivationFunctionType

ENGS = ("pe", "act", "dve", "pool", "sp")
NAMES = []


class _Op:
    __slots__ = ("eng", "fn", "deps", "idx", "is_dma", "dsem", "dval", "need_inc", "cnt")

    def __init__(self, eng, fn, is_dma):
        self.eng = eng
        self.fn = fn
        self.deps = []
        self.is_dma = is_dma
        self.dsem = None
        self.dval = 0
        self.need_inc = False
        self.cnt = 0


class Ctx:
    def __init__(self, nc):
        self.nc = nc
        self.q = {e: [] for e in ENGS}
        self.wr = {}
        self.rd = {}
        self.dsems = {}
        self.pending_dma = []
        self.sem_pool = {}
        self.psum_keys = set()
        self.stack = ExitStack()
        self.scopes = []
        self.esem = {e: nc.alloc_semaphore("sem_" + e) for e in ENGS}
        self.n_psum = 0

    def push(self):
        st = ExitStack()
        self.scopes.append(st)

    def pop(self):
        self.scopes.pop().close()

    def _scope(self):
        return self.scopes[-1] if self.scopes else self.stack

    def sbuf(self, name, shape, dtype):
        self.n_psum += 1
        NAMES.append("%s_%d" % (name, self.n_psum))
        return self._scope().enter_context(self.nc.sbuf_tensor("%s_%d" % (name, self.n_psum), list(shape), dtype))

    def psum(self, name, shape, dtype=F32):
        self.n_psum += 1
        return self._scope().enter_context(self.nc.psum_tensor("%s_%d" % (name, self.n_psum), list(shape), dtype))

    def _track(self, op, reads, writes):
        deps = op.deps
        for k in reads:
            for o in self.wr.get(k, {}).values():
                deps.append(o)
            if k in self.psum_keys:
                for ek2, o in self.rd.get(k, {}).items():
                    if o.eng != op.eng:
                        deps.append(o)
        for k in writes:
            for o in self.wr.get(k, {}).values():
                deps.append(o)
            for o in self.rd.get(k, {}).values():
                deps.append(o)
        ek = id(op) if op.is_dma else op.eng
        for k in reads:
            self.rd.setdefault(k, {})[ek] = op
        for k in writes:
            self.wr[k] = {ek: op}
            self.rd[k] = {}

    def op(self, eng, fn, reads=(), writes=()):
        o = _Op(eng, fn, False)
        self._track(o, reads, writes)
        o.idx = len(self.q[eng])
        self.q[eng].append(o)
        return o

    def dma(self, eng, out, in_, reads=(), writes=(), skey=None, **kw):
        key = ("dma", skey)
        o = _Op(eng, lambda e: e.dma_start(out=out, in_=in_, **kw), True)
        self._track(o, reads, writes)
        key = (eng == "pool", key)
        if key not in self.dsems:
            pool = self.sem_pool.setdefault(eng == "pool", [])
            i = sum(1 for k in self.dsems if k[0] == key[0])
            if i >= len(pool):
                pool.append([self.nc.alloc_semaphore("d%s%d" % ("s" if key[0] else "h", i)), 0])
            self.dsems[key] = pool[i]
        ds = self.dsems[key]
        ds[1] += 16
        o.dsem = ds[0]
        o.dval = ds[1]
        o.idx = len(self.q[eng])
        self.q[eng].append(o)
        self.pending_dma.append(o)
        return o

    def barrier(self):
        lasts = [self.q[e][-1] for e in ENGS if self.q[e] and not self.q[e][-1].is_dma]
        lasts += [o for e in ENGS for o in self.q[e][-1:] if False]
        dm = list(self.pending_dma)
        self.pending_dma = []
        for e in ENGS:
            o = _Op(e, None, False)
            for e2 in ENGS:
                for p in reversed(self.q[e2]):
                    if not p.is_dma and p.fn is not None:
                        o.deps.append(p)
                        break
            o.deps.extend(dm)
            o.idx = len(self.q[e])
            self.q[e].append(o)
        self.wr = {}
        self.rd = {}
        self.dsems = {}

    def finish(self):
        nc = self.nc
        self.barrier()
        for e in ENGS:
            for o in self.q[e]:
                for d in o.deps:
                    if not d.is_dma and (d.eng != o.eng or e != "pe"):
                        d.need_inc = True
        for e in ENGS:
            c = 0
            for o in self.q[e]:
                if o.need_inc:
                    c += 1
                o.cnt = c
        engobj = {"pe": "tensor", "act": "scalar", "dve": "vector", "pool": "gpsimd", "sp": "sync"}
        with nc.Block() as block:
            for e in ENGS:
                def body(eng, e=e):
                    known = {}
                    for o in self.q[e]:
                        waits = {}
                        for d in o.deps:
                            if d.is_dma:
                                s, v = d.dsem, d.dval
                            else:
                                if d.eng == e and e == "pe":
                                    continue
                                s, v = self.esem[d.eng], d.cnt
                            if v > waits.get(s, (None, 0))[1]:
                                waits[s] = (s, v)
                        for s, v in waits.values():
                            if known.get(s, 0) >= v:
                                continue
                            known[s] = v
                            eng.wait_ge(s, v)
                        if o.fn is None:
                            continue
                        ins = o.fn(eng)
                        if o.is_dma:
                            ins.then_inc(o.dsem, 16)
                        elif o.need_inc:
                            ins.then_inc(self.esem[e], 1)
                getattr(block, engobj[e])(body)
        self.stack.close()


D = 1024
T = 4096
KC = D // 128
DFF = 2816
JC = DFF // 128
TT = 512
NTT = T // TT
EPS = 1e-6


def consts(cx):
    nc = cx.nc
    c = {}
    c["ones_bf"] = cx.sbuf("ones_bf", [128, 128], BF16)
    cx.op("pool", lambda e: e.memset(c["ones_bf"][:], 1.0), writes=["ones_bf"])
    return c


def rmsnorm_tile(cx, c, xt, nw, hT_out, tag, rd_keys, wr_key, ps_key, ps, scr):
    sq, rstd = scr["sq"], scr["rstd"]
    cx.op("act", lambda e: e.activation(sq[:], xt[:], AF.Square), reads=rd_keys, writes=[tag + "sq"])

    def mm(e):
        ins = None
        for kc in range(KC):
            ins = e.matmul(ps[:], c["ones_bf"][:], sq[:, kc, :], start=(kc == 0), stop=(kc == KC - 1))
        return ins
    cx.op("pe", mm, reads=[tag + "sq", "ones_bf"], writes=[ps_key])
    cx.op("act", lambda e: e.activation(rstd[:], ps[:], AF.Sqrt, bias=EPS, scale=1.0 / D),
          reads=[ps_key], writes=[tag + "rstd"])
    cx.op("dve", lambda e: e.reciprocal(rstd[:], rstd[:]), reads=[tag + "rstd"], writes=[tag + "rstd"])
    for kc in range(KC):
        cx.op("dve", lambda e, kc=kc: e.scalar_tensor_tensor(
            out=hT_out(kc), in0=xt[:, kc, :], scalar=nw[:, kc:kc + 1], in1=rstd[:],
            op0=ALU.mult, op1=ALU.mult), reads=rd_keys + [tag + "rstd", tag + "nw"], writes=[wr_key])


def ffn_phase(cx, c, tag, x_in, x_out, nw_d, w_in, w_out, xkey_in, xkey_out):
    nc = cx.nc
    cx.push()
    HT = 2048
    NH = T // HT
    NT = HT // TT
    hT = cx.sbuf(tag + "hT", [128, KC, HT], BF16)
    act = cx.sbuf(tag + "act", [128, JC, HT], BF16)
    xt = cx.sbuf(tag + "xt", [128, KC, TT], F32)
    sq = cx.sbuf(tag + "sq", [128, KC, TT], BF16)
    rstd = cx.sbuf(tag + "rstd", [128, TT], F32)
    nw = cx.sbuf(tag + "nw", [128, KC], F32)
    WB = 256
    wg = [cx.sbuf(tag + "wg%d" % i, [128, KC, WB], BF16) for i in range(2)]
    wu = [cx.sbuf(tag + "wu%d" % i, [128, KC, WB], BF16) for i in range(2)]
    wo = [cx.sbuf(tag + "wo%d" % i, [128, JC, WB], BF16) for i in range(2)]
    sg = [cx.sbuf(tag + "sg%d" % i, [128, TT], F32) for i in range(2)]
    xr = [cx.sbuf(tag + "xr%d" % i, [128, TT], F32) for i in range(2)]
    xn = [cx.sbuf(tag + "xn%d" % i, [128, TT], F32) for i in range(2)]
    ps = [cx.psum(tag + "ps%d" % i, [128, TT], F32) for i in range(7)]
    pk = [tag + "ps%d" % i for i in range(7)]
    cx.psum_keys.update(pk)

    cx.dma("sp", nw[:], nw_d, writes=[tag + "nw"], skey="nw")
    w_in_r = w_in.rearrange("(kc p) n -> p kc n", p=128)
    w_out_r = w_out.rearrange("(kc p) n -> p kc n", p=128)
    x_in_r = x_in.rearrange("(kc p) t -> p kc t", p=128)
    cnt = {"g": 0, "o": 0, "s": 0, "x": 0}
    for h in range(NH):
        t0 = h * HT
        for tt in range(NT):
            gt = (t0 + tt * TT) // TT
            cx.dma("sp", xt[:], x_in_r[:, :, t0 + tt * TT: t0 + (tt + 1) * TT],
                   reads=[(xkey_in, gt, kc) for kc in range(KC)], writes=[tag + "xt"], skey="xt")
            rmsnorm_tile(cx, c, xt, nw, lambda kc, tt=tt: hT[:, kc, tt * TT:(tt + 1) * TT], tag,
                         [tag + "xt"], (tag + "hT", tt), pk[6], ps[6], {"sq": sq, "rstd": rstd})
        for jb in range(JC * 128 // WB):
            s = cnt["g"] % 2
            cnt["g"] += 1
            cx.dma("pool", wg[s][:], w_in_r[:, :, jb * WB:(jb + 1) * WB], writes=[(tag + "wg", s)],
                   skey=("wg", s))
            cx.dma("pool", wu[s][:], w_in_r[:, :, DFF + jb * WB: DFF + (jb + 1) * WB], writes=[(tag + "wu", s)],
                   skey=("wu", s))
            for jj in range(WB // 128):
                j = jb * (WB // 128) + jj
                for tt in range(NT):
                    b = cnt["s"] % 2
                    cnt["s"] += 1
                    pg, pu = ps[b], ps[2 + b]

                    def mm(e, s=s, jj=jj, tt=tt, pg=pg, pu=pu):
                        ins = None
                        for kc in range(KC):
                            ins = e.matmul(pg[:], wg[s][:, kc, jj * 128:(jj + 1) * 128],
                                           hT[:, kc, tt * TT:(tt + 1) * TT], start=(kc == 0), stop=(kc == KC - 1))
                        for kc in range(KC):
                            ins = e.matmul(pu[:], wu[s][:, kc, jj * 128:(jj + 1) * 128],
                                           hT[:, kc, tt * TT:(tt + 1) * TT], start=(kc == 0), stop=(kc == KC - 1))
                        return ins
                    cx.op("pe", mm, reads=[(tag + "wg", s), (tag + "wu", s), (tag + "hT", tt)],
                          writes=[pk[b], pk[2 + b]])
                    cx.op("act", lambda e, b=b, pg=pg: e.activation(sg[b][:], pg[:], AF.Silu),
                          reads=[pk[b]], writes=[(tag + "sg", b)])
                    cx.op("dve", lambda e, b=b, pu=pu, j=j, tt=tt: e.tensor_tensor(
                        out=act[:, j, tt * TT:(tt + 1) * TT], in0=sg[b][:], in1=pu[:], op=ALU.mult),
                        reads=[(tag + "sg", b), pk[2 + b]], writes=[(tag + "act", j, tt)])
        for mb in range(D // WB):
            s = cnt["o"] % 2
            cnt["o"] += 1
            cx.dma("pool", wo[s][:], w_out_r[:, :, mb * WB:(mb + 1) * WB], writes=[(tag + "wo", s)],
                   skey=("wo", s))
            for mm_ in range(WB // 128):
                mc = mb * (WB // 128) + mm_
                for tt in range(NT):
                    gt = (t0 + tt * TT) // TT
                    b = cnt["x"] % 2
                    cnt["x"] += 1
                    py = ps[4 + b]
                    cx.dma("sp", xr[b][:], x_in[mc * 128:(mc + 1) * 128, t0 + tt * TT: t0 + (tt + 1) * TT],
                           reads=[(xkey_in, gt, mc)], writes=[(tag + "xr", b)], skey=("xr", b))

                    def mm2(e, s=s, mm_=mm_, tt=tt, py=py):
                        ins = None
                        for kc in range(JC):
                            ins = e.matmul(py[:], wo[s][:, kc, mm_ * 128:(mm_ + 1) * 128],
                                           act[:, kc, tt * TT:(tt + 1) * TT], start=(kc == 0), stop=(kc == JC - 1))
                        return ins
                    cx.op("pe", mm2, reads=[(tag + "wo", s)] + [(tag + "act", j, tt) for j in range(JC)],
                          writes=[pk[4 + b]])
                    cx.op("dve", lambda e, b=b, py=py: e.scalar_tensor_tensor(
                        out=xn[b][:], in0=py[:], scalar=0.5, in1=xr[b][:], op0=ALU.mult, op1=ALU.add),
                        reads=[pk[4 + b], (tag + "xr", b)], writes=[(tag + "xn", b)])
                    cx.dma("sp", x_out[mc * 128:(mc + 1) * 128, t0 + tt * TT: t0 + (tt + 1) * TT], xn[b][:],
                           reads=[(tag + "xn", b)], writes=[(xkey_out, gt, mc)], skey=("xn", b))
    cx.barrier()
    cx.pop()


def build_ffn_prog():
    nc = bass.Bass("TRN2", target_bir_lowering=False)
    x = nc.dram_tensor("x", [D, T], F32, kind="ExternalInput").ap()
    nw = nc.dram_tensor("nw", [128, KC], F32, kind="ExternalInput").ap()
    w_in = nc.dram_tensor("w_in", [D, 2 * DFF], F32, kind="ExternalInput").ap()
    w_out = nc.dram_tensor("w_out", [DFF, D], F32, kind="ExternalInput").ap()
    y = nc.dram_tensor("y", [D, T], F32, kind="ExternalOutput").ap()
    cx = Ctx(nc)
    c = consts(cx)
    ffn_phase(cx, c, "f_", x, y, nw, w_in, w_out, "xin", "xout")
    cx.finish()
    return nc


NH_A = 12
DIL = (1, 4, 16)
NEG = -30000.0
SCALE_A = 128 ** -0.5
C_IDENT = 0
C_PSWAP = 128
C_MASK = 256
C_TRI = 512
C_TRIS = 640
C_MINC = 768
C_MSTR = 896
C_LAST = 1024
CST_W = 1152


def host_consts():
    m = np.zeros((128, CST_W), np.float32)
    k = np.arange(128)[:, None]
    q = np.arange(128)[None, :]
    m[:, C_IDENT:C_IDENT + 128] = (k == q)
    m[:, C_PSWAP:C_PSWAP + 128] = (k == (q + 64) % 128)
    m[:, C_MASK:C_MASK + 128] = np.where(k <= q, 0.0, NEG)
    m[:, C_MASK + 128:C_MASK + 256] = np.where(k >= q, 0.0, NEG)
    m[:, C_TRI:C_TRI + 128] = (k <= q)
    m[:, C_TRIS:C_TRIS + 128] = (k > q)
    m[:, C_MINC:C_MINC + 128] = (k <= q)
    m[:, C_MSTR:C_MSTR + 128] = (k < q)
    m[:, C_LAST:C_LAST + 128] = (k == 127)
    return m


def host_rope():
    inv = (1.0 / (10000.0 ** (np.arange(0, 128, 2, dtype=np.float32) / np.float32(128)))).astype(np.float32)
    ang = (np.arange(T, dtype=np.float32)[None, :] * inv[:, None]).astype(np.float32)
    cs, sn = np.cos(ang).astype(np.float32), np.sin(ang).astype(np.float32)
    return np.concatenate([cs, cs], 0), np.concatenate([-sn, sn], 0)


def norm_stage(cx, c, tag, x_in, nw_d, hT, xkey_in, ps, pskey):
    cx.push()
    xt = [cx.sbuf(tag + "nxt%d" % i, [128, KC, TT], F32) for i in range(2)]
    sq = cx.sbuf(tag + "nsq", [128, KC, TT], BF16)
    rstd = cx.sbuf(tag + "nrstd", [128, TT], F32)
    nw = cx.sbuf(tag + "nnw", [128, KC], F32)
    cx.dma("sp", nw[:], nw_d, writes=[tag + "nw"], skey="nw")
    x_in_r = x_in.rearrange("(kc p) t -> p kc t", p=128)
    for tt in range(NTT):
        b = tt % 2
        cx.dma("sp", xt[b][:], x_in_r[:, :, tt * TT:(tt + 1) * TT],
               reads=[(xkey_in, tt, kc) for kc in range(KC)], writes=[(tag + "xt", b)], skey=("xt", b))
        rmsnorm_tile(cx, c, xt[b], nw, lambda kc, tt=tt: hT[:, kc, tt * TT:(tt + 1) * TT], tag,
                     [(tag + "xt", b)], (tag + "hT", tt), pskey, ps, {"sq": sq, "rstd": rstd})
    cx.barrier()
    cx.pop()


def attn_phase(cx, c, tag, x_in, os_d, nw_d, w_in, ctab_d, stab_d, cst_d, xkey_in, okey):
    nc = cx.nc
    cx.push()
    hT = cx.sbuf(tag + "hT", [128, KC, T], BF16)
    ctab = cx.sbuf(tag + "ctab", [128, T], F32)
    stab = cx.sbuf(tag + "stab", [128, T], F32)
    ident = cx.sbuf(tag + "ident", [128, 128], BF16)
    pswap = cx.sbuf(tag + "pswap", [128, 128], BF16)
    maskb = cx.sbuf(tag + "maskb", [128, 256], BF16)
    ps = [cx.psum(tag + "ps%d" % i, [128, TT], F32) for i in range(8)]
    pk = [tag + "ps%d" % i for i in range(8)]
    cx.psum_keys.update(pk)
    cx.dma("sp", ctab[:], ctab_d, writes=[tag + "ctab"], skey="ctab")
    cx.dma("sp", stab[:], stab_d, writes=[tag + "stab"], skey="stab")
    cx.dma("pool", ident[:], cst_d[:, C_IDENT:C_IDENT + 128], writes=[tag + "ident"], skey="ident")
    cx.dma("pool", pswap[:], cst_d[:, C_PSWAP:C_PSWAP + 128], writes=[tag + "pswap"], skey="pswap")
    cx.dma("pool", maskb[:], cst_d[:, C_MASK:C_MASK + 256], writes=[tag + "maskb"], skey="maskb")
    norm_stage(cx, c, tag, x_in, nw_d, hT, xkey_in, ps[6], pk[6])
    hkeys = []

    Qd = cx.sbuf(tag + "Qd", [128, T], BF16)
    Kd = cx.sbuf(tag + "Kd", [128, T], BF16)
    Vd = cx.sbuf(tag + "Vd", [128, 32, 128], BF16)
    numT = cx.sbuf(tag + "numT", [128, 3, T], BF16)
    dent = cx.sbuf(tag + "dent", [128, T], F32)
    wq = [cx.sbuf(tag + "wq%d" % i, [128, KC, 128], BF16) for i in range(2)]
    wk = [cx.sbuf(tag + "wk%d" % i, [128, KC, 128], BF16) for i in range(2)]
    wv = [cx.sbuf(tag + "wv%d" % i, [128, KC, 128], BF16) for i in range(2)]
    qb = [cx.sbuf(tag + "qb%d" % i, [128, TT], BF16) for i in range(2)]
    t1 = [cx.sbuf(tag + "t1%d" % i, [128, TT], F32) for i in range(2)]
    t2 = [cx.sbuf(tag + "t2%d" % i, [128, TT], F32) for i in range(2)]
    pt = [cx.sbuf(tag + "pt%d" % i, [128, 256], BF16) for i in range(4)]
    osc = [cx.sbuf(tag + "osc%d" % i, [128, TT], BF16) for i in range(2)]
    w_in_r = w_in.rearrange("(kc p) n -> p kc n", p=128)
    AW = NH_A * 128
    cnt = {"w": 0, "r": 0, "o": 0}

    def load_w(hd):
        s = cnt["w"] % 2
        cnt["w"] += 1
        for nm, wt, off in (("wq", wq, 0), ("wk", wk, AW), ("wv", wv, 2 * AW)):
            cx.dma("pool", wt[s][:], w_in_r[:, :, off + hd * 128: off + (hd + 1) * 128],
                   writes=[(tag + nm, s)], skey=(nm, s))
        return s

    for hg in range(4):
        for g in range(3):
            d = DIL[g]
            Ls = T // d
            nb = Ls // 128
            hd = g * 4 + hg
            s = load_w(hd)
            for which, wt, dst, dkey in (("q", wq, Qd, tag + "Qd"), ("k", wk, Kd, tag + "Kd")):
                dst_v = dst[:, :].rearrange("p (r j) -> p j r", r=d)
                for tt in range(NTT):
                    b = cnt["r"] % 2
                    cnt["r"] += 1
                    pp, sw = ps[b], ps[2 + b]

                    def mm(e, wt=wt, s=s, tt=tt, pp=pp):
                        ins = None
                        for kc in range(KC):
                            ins = e.matmul(pp[:], wt[s][:, kc, :], hT[:, kc, tt * TT:(tt + 1) * TT],
                                           start=(kc == 0), stop=(kc == KC - 1))
                        return ins
                    cx.op("pe", mm, reads=[(tag + "w" + which, s)], writes=[pk[b]])
                    cx.op("act", lambda e, b=b, pp=pp: e.activation(qb[b][:], pp[:], AF.Copy),
                          reads=[pk[b]], writes=[(tag + "qb", b)])
                    cx.op("pe", lambda e, b=b, sw=sw: e.matmul(sw[:], pswap[:], qb[b][:], start=True, stop=True),
                          reads=[(tag + "qb", b), tag + "pswap"], writes=[pk[2 + b]])
                    cx.op("pool", lambda e, b=b, tt=tt: e.tensor_tensor(
                        out=t1[b][:], in0=qb[b][:], in1=ctab[:, tt * TT:(tt + 1) * TT], op=ALU.mult),
                        reads=[(tag + "qb", b), tag + "ctab"], writes=[(tag + "t1", b)])
                    cx.op("dve", lambda e, b=b, tt=tt, sw=sw: e.tensor_tensor(
                        out=t2[b][:], in0=sw[:], in1=stab[:, tt * TT:(tt + 1) * TT], op=ALU.mult),
                        reads=[pk[2 + b], tag + "stab"], writes=[(tag + "t2", b)])
                    j0 = tt * TT // d
                    cx.op("dve", lambda e, b=b, dst_v=dst_v, j0=j0, d=d: e.tensor_tensor(
                        out=dst_v[:, j0:j0 + TT // d, :],
                        in0=t1[b][:, :].rearrange("p (j r) -> p j r", r=d),
                        in1=t2[b][:, :].rearrange("p (j r) -> p j r", r=d), op=ALU.add),
                        reads=[(tag + "t1", b), (tag + "t2", b)], writes=[(dkey, tt)])
            for b4 in range(8):
                b = cnt["r"] % 2
                cnt["r"] += 1
                pp = ps[b]

                def mmv(e, s=s, b4=b4, pp=pp, d=d, Ls=Ls):
                    ins = None
                    for i in range(4):
                        B = b4 * 4 + i
                        r, n = divmod(B, Ls // 128)
                        t0 = n * 128 * d + r
                        for kc in range(KC):
                            ins = e.matmul(pp[:, i * 128:(i + 1) * 128],
                                           hT[:, kc, t0: t0 + 127 * d + 1: d], wv[s][:, kc, :],
                                           start=(kc == 0), stop=(kc == KC - 1))
                    return ins
                cx.op("pe", mmv, reads=[(tag + "wv", s)], writes=[pk[b]])
                cx.op("act", lambda e, b4=b4, pp=pp: e.activation(
                    Vd[:, b4 * 4:(b4 + 1) * 4, :], pp[:, :].rearrange("p (i e) -> p i e", i=4), AF.Copy),
                    reads=[pk[b]], writes=[(tag + "Vd", b4)])
            qk_keys = [(tag + "Qd", tt) for tt in range(NTT)] + [(tag + "Kd", tt) for tt in range(NTT)]

            def scores(B, nb=nb):
                sb = 4 + (B % 2)
                nq = 256 if (B + 1) % nb != 0 else 128
                st = ps[sb]

                def mm(e, B=B, nq=nq, st=st):
                    e.matmul(st[:, 0:nq], Kd[:, B * 128:(B + 1) * 128], Qd[:, B * 128:B * 128 + nq],
                             start=True, stop=False)
                    return e.matmul(st[:, 0:nq], ident[:], maskb[:, 0:nq], start=False, stop=True)
                cx.op("pe", mm, reads=qk_keys + [tag + "ident", tag + "maskb"], writes=[pk[sb]])
                cx.op("act", lambda e, B=B, nq=nq, st=st: e.activation(
                    pt[B % 4][:, 0:nq], st[:, 0:nq], AF.Exp, scale=SCALE_A),
                    reads=[pk[sb]], writes=[(tag + "pt", B % 4)])

            def pv(B, nb=nb, g=g, d=d, Ls=Ls):
                first = (B % nb == 0)
                col = (B % 4) * 128

                def mm(e, B=B, first=first, col=col):
                    ins = None
                    for dst, lhs_fn in ((ps[6], lambda blk: Vd[:, blk, :]), (ps[7], lambda blk: c["ones_bf"][:])):
                        if not first:
                            e.matmul(dst[:, col:col + 128], lhs_fn(B - 1), pt[(B - 1) % 4][:, 128:256],
                                     start=True, stop=False)
                        ins = e.matmul(dst[:, col:col + 128], lhs_fn(B), pt[B % 4][:, 0:128],
                                       start=first, stop=True)
                    return ins
                rk = [(tag + "pt", B % 4), (tag + "Vd", B // 4), "ones_bf"]
                if not first:
                    rk += [(tag + "pt", (B - 1) % 4), (tag + "Vd", (B - 1) // 4)]
                cx.op("pe", mm, reads=rk, writes=[pk[6], pk[7]])
                if B % 4 == 3:
                    u0 = (B - 3) * 128
                    if d == 1:
                        def nat(ap2d):
                            return ap2d[:, u0:u0 + 512]
                        def src(p):
                            return p[:, :]
                    else:
                        r0, j0 = divmod(u0, Ls)
                        nr = max(1, 512 // Ls)
                        nj = 512 // nr
                        def nat(ap2d, r0=r0, j0=j0, nr=nr, nj=nj, d=d):
                            return ap2d.rearrange("p (j r) -> p r j", r=d)[:, r0:r0 + nr, j0:j0 + nj]
                        def src(p, nr=nr):
                            return p[:, :].rearrange("p (r j) -> p r j", r=nr)
                    cx.op("act", lambda e, nat=nat, src=src, g=g: e.activation(
                        nat(numT[:, g, :]), src(ps[6]), AF.Copy),
                        reads=[pk[6]], writes=[(tag + "numT", g)])
                    if g == 0:
                        cx.op("dve", lambda e, nat=nat, src=src: e.tensor_copy(nat(dent[:, :]), src(ps[7])),
                              reads=[pk[7]], writes=[tag + "dent"])
                    else:
                        cx.op("dve", lambda e, nat=nat, src=src: e.tensor_tensor(
                            out=nat(dent[:, :]), in0=nat(dent[:, :]), in1=src(ps[7]), op=ALU.add),
                            reads=[pk[7], tag + "dent"], writes=[tag + "dent"])

            scores(0)
            for B in range(32):
                if B + 1 < 32:
                    scores(B + 1)
                pv(B)
        cx.op("dve", lambda e: e.reciprocal(dent[:, :], dent[:, :]), reads=[tag + "dent"], writes=[tag + "dent"])
        for g in range(3):
            hd = g * 4 + hg
            for tt in range(NTT):
                b = cnt["o"] % 2
                cnt["o"] += 1
                cx.op("dve", lambda e, b=b, g=g, tt=tt: e.tensor_tensor(
                    out=osc[b][:], in0=numT[:, g, tt * TT:(tt + 1) * TT], in1=dent[:, tt * TT:(tt + 1) * TT],
                    op=ALU.mult), reads=[tag + "dent", (tag + "numT", g)], writes=[(tag + "osc", b)])
                cx.dma("sp", os_d[hd * 128:(hd + 1) * 128, tt * TT:(tt + 1) * TT], osc[b][:],
                       reads=[(tag + "osc", b)], writes=[(okey, hd, tt)], skey=("osc", b))
    cx.barrier()
    cx.pop()


def outproj_phase(cx, c, tag, x_in, x_out, os_d, w_out, nheads, xkey_in, xkey_out, okey):
    cx.push()
    HT = 2048
    NT = HT // TT
    WB = 256
    osb = cx.sbuf(tag + "osb", [128, nheads, HT], BF16)
    wo = [cx.sbuf(tag + "wo%d" % i, [128, nheads, WB], BF16) for i in range(2)]
    xr = [cx.sbuf(tag + "xr%d" % i, [128, TT], F32) for i in range(2)]
    xn = [cx.sbuf(tag + "xn%d" % i, [128, TT], F32) for i in range(2)]
    ps = [cx.psum(tag + "ps%d" % i, [128, TT], F32) for i in range(2)]
    pk = [tag + "ps%d" % i for i in range(2)]
    cx.psum_keys.update(pk)
    w_out_r = w_out.rearrange("(kc p) n -> p kc n", p=128)
    os_r = os_d.rearrange("(h p) t -> p h t", p=128)
    cnt = {"o": 0, "x": 0}
    for h in range(T // HT):
        t0 = h * HT
        for hd in range(nheads):
            cx.dma("sp", osb[:, hd, :], os_r[:, hd, t0:t0 + HT],
                   reads=[(okey, hd, (t0 // TT) + i) for i in range(NT)], writes=[(tag + "osb", hd)], skey=("osb", hd % 4))
        for mb in range(D // WB):
            s = cnt["o"] % 2
            cnt["o"] += 1
            cx.dma("pool", wo[s][:], w_out_r[:, :, mb * WB:(mb + 1) * WB], writes=[(tag + "wo", s)], skey=("wo", s))
            for mm_ in range(WB // 128):
                mc = mb * (WB // 128) + mm_
                for tt in range(NT):
                    gt = (t0 + tt * TT) // TT
                    b = cnt["x"] % 2
                    cnt["x"] += 1
                    py = ps[b]
                    cx.dma("sp", xr[b][:], x_in[mc * 128:(mc + 1) * 128, t0 + tt * TT: t0 + (tt + 1) * TT],
                           reads=[(xkey_in, gt, mc)], writes=[(tag + "xr", b)], skey=("xr", b))

                    def mm2(e, s=s, mm_=mm_, tt=tt, py=py):
                        ins = None
                        for kc in range(nheads):
                            ins = e.matmul(py[:], wo[s][:, kc, mm_ * 128:(mm_ + 1) * 128],
                                           osb[:, kc, tt * TT:(tt + 1) * TT], start=(kc == 0), stop=(kc == nheads - 1))
                        return ins
                    cx.op("pe", mm2, reads=[(tag + "wo", s)] + [(tag + "osb", hd) for hd in range(nheads)],
                          writes=[pk[b]])
                    cx.op("dve", lambda e, b=b, py=py: e.tensor_tensor(
                        out=xn[b][:], in0=py[:], in1=xr[b][:], op=ALU.add),
                        reads=[pk[b], (tag + "xr", b)], writes=[(tag + "xn", b)])
                    cx.dma("sp", x_out[mc * 128:(mc + 1) * 128, t0 + tt * TT: t0 + (tt + 1) * TT], xn[b][:],
                           reads=[(tag + "xn", b)], writes=[(xkey_out, gt, mc)], skey=("xn", b))
    cx.barrier()
    cx.pop()


def build_attn_prog():
    nc = bass.Bass("TRN2", target_bir_lowering=False)
    x = nc.dram_tensor("x", [D, T], F32, kind="ExternalInput").ap()
    nw = nc.dram_tensor("nw", [128, KC], F32, kind="ExternalInput").ap()
    w_in = nc.dram_tensor("w_in", [D, 3 * NH_A * 128], F32, kind="ExternalInput").ap()
    w_out = nc.dram_tensor("w_out", [NH_A * 128, D], F32, kind="ExternalInput").ap()
    ctab = nc.dram_tensor("ctab", [128, T], F32, kind="ExternalInput").ap()
    stab = nc.dram_tensor("stab", [128, T], F32, kind="ExternalInput").ap()
    cst = nc.dram_tensor("cst", [128, CST_W], F32, kind="ExternalInput").ap()
    os_d = nc.dram_tensor("os_scratch", [NH_A * 128, T], BF16).ap()
    y = nc.dram_tensor("y", [D, T], F32, kind="ExternalOutput").ap()
    cx = Ctx(nc)
    c = consts(cx)
    attn_phase(cx, c, "a_", x, os_d, nw, w_in, ctab, stab, cst, "xin", "os")
    outproj_phase(cx, c, "ao_", x, y, os_d, w_out, NH_A, "xin", "xout", "os")
    cx.finish()
    return nc


NVH = 16
NKH = 8
GP = 6176
NTILE = T // 128
DBG = {}


class _Stop(Exception):
    pass


def _chk(n):
    if DBG.get("stop", 99) <= n:
        raise _Stop()


def gdn_core_phase(*a):
    cx = a[0]
    depth = len(cx.scopes)
    try:
        _gdn_core_phase(*a)
    except _Stop:
        cx.barrier()
        while len(cx.scopes) > depth:
            cx.pop()


def _gdn_core_phase(cx, c, tag, x_in, oraw_d, nw_d, w_in, convw_d, alog_d, dtb_d, cst_d, xkey_in, okey):
    cx.push()
    hT = cx.sbuf(tag + "hT", [128, KC, T], BF16)
    ident = cx.sbuf(tag + "ident", [128, 128], BF16)
    id4 = cx.sbuf(tag + "id4", [128, 4, 128], BF16)
    minc4 = cx.sbuf(tag + "minc4", [128, 4, 128], BF16)
    mlow4 = cx.sbuf(tag + "mlow4", [128, 4, 128], BF16)
    cw = cx.sbuf(tag + "cw", [128, 32, 4], F32)
    beta = cx.sbuf(tag + "beta", [128, NTILE, NVH], F32)
    nbeta = cx.sbuf(tag + "nbeta", [128, NTILE, NVH], F32)
    gc = cx.sbuf(tag + "gc", [128, NTILE, NVH], F32)
    kbs = cx.sbuf(tag + "kbs", [128, NTILE, NVH], F32)
    egr = cx.sbuf(tag + "egr", [128, NTILE, NVH], F32)
    cd = cx.sbuf(tag + "cd", [128, NTILE, NVH], F32)
    gchi = cx.sbuf(tag + "gchi", [128, NTILE, NVH], BF16)
    gclo = cx.sbuf(tag + "gclo", [128, NTILE, NVH], BF16)
    ps = [cx.psum(tag + "ps%d" % i, [128, TT], F32) for i in range(8)]
    pk = [tag + "ps%d" % i for i in range(8)]
    cx.psum_keys.update(pk)
    K = lambda n: tag + n
    cx.push()
    alog = cx.sbuf(tag + "alog", [128, 512], F32)
    dtb = cx.sbuf(tag + "dtb", [128, 512], F32)
    gt = cx.sbuf(tag + "gt", [128, NTILE, NVH], F32)
    wab = cx.sbuf(tag + "wab", [128, KC, 32], BF16)
    ghi = cx.sbuf(tag + "ghi", [128, NTILE, NVH], BF16)
    glo = cx.sbuf(tag + "glo", [128, NTILE, NVH], BF16)
    trib = cx.sbuf(tag + "trib", [128, 128], BF16)
    trisb = cx.sbuf(tag + "trisb", [128, 128], BF16)
    cx.dma("pool", ident[:], cst_d[:, C_IDENT:C_IDENT + 128], writes=[K("ident")], skey="ident")
    for i in range(4):
        cx.dma("pool", id4[:, i, :], cst_d[:, C_IDENT:C_IDENT + 128], writes=[K("id4")], skey=("id4", i))
        cx.dma("pool", minc4[:, i, :], cst_d[:, C_MINC:C_MINC + 128], writes=[K("minc4")], skey=("minc4", i))
        cx.dma("pool", mlow4[:, i, :], cst_d[:, C_TRIS:C_TRIS + 128], writes=[K("mlow4")], skey=("mlow4", i))
    cx.dma("sp", cw[:], convw_d, writes=[K("cw")], skey="cw")
    cx.dma("sp", alog[:], alog_d, writes=[K("alog")], skey="alog")
    cx.dma("sp", dtb[:], dtb_d, writes=[K("dtb")], skey="dtb")
    w_in_r = w_in.rearrange("(kc p) n -> p kc n", p=128)
    cx.dma("pool", wab[:], w_in_r[:, :, 6144:6176], writes=[K("wab")], skey="wab")
    _chk(0)
    norm_stage(cx, c, tag, x_in, nw_d, hT, xkey_in, ps[6], pk[6])
    _chk(1)

    f2 = lambda t: t[:, :, :].rearrange("p c h -> p (c h)")
    for which in range(2):
        def mm(e, which=which):
            ins = None
            for ct in range(NTILE):
                for kc in range(KC):
                    ins = e.matmul(ps[which][:, ct * 16:(ct + 1) * 16], hT[:, kc, ct * 128:(ct + 1) * 128],
                                   wab[:, kc, which * 16:(which + 1) * 16], start=(kc == 0), stop=(kc == KC - 1))
            return ins
        cx.op("pe", mm, reads=[], writes=[pk[which]])
    _chk(1.05)
    cx.op("act", lambda e: e.activation(f2(beta), ps[0][:], AF.Sigmoid), reads=[pk[0]], writes=[K("beta")])
    _chk(1.1)
    cx.op("dve", lambda e: e.tensor_tensor(out=f2(gt), in0=ps[1][:], in1=dtb[:], op=ALU.add),
          reads=[pk[1], K("dtb")], writes=[K("gt")])
    _chk(1.2)
    cx.op("act", lambda e: e.activation(f2(gt), f2(gt), AF.Exp), reads=[K("gt")], writes=[K("gt")])
    _chk(1.4)
    cx.op("act", lambda e: e.activation(f2(gt), f2(gt), AF.Ln, bias=1.0), reads=[K("gt")], writes=[K("gt")])
    _chk(1.6)
    cx.op("act", lambda e: e.activation(alog[:], alog[:], AF.Exp), reads=[K("alog")], writes=[K("alog")])
    _chk(1.8)
    cx.op("dve", lambda e: e.scalar_tensor_tensor(out=f2(gt), in0=f2(gt), scalar=-1.0, in1=alog[:],
                                                   op0=ALU.mult, op1=ALU.mult),
          reads=[K("gt"), K("alog")], writes=[K("gt")])
    cx.op("dve", lambda e: e.tensor_scalar(f2(nbeta), f2(beta), -1.0, None, ALU.mult),
          reads=[K("beta")], writes=[K("nbeta")])
    _chk(2)

    cx.dma("pool", trib[:], cst_d[:, C_TRI:C_TRI + 128], writes=[K("trib")], skey="trib")
    cx.dma("pool", trisb[:], cst_d[:, C_TRIS:C_TRIS + 128], writes=[K("trisb")], skey="trisb")
    cx.op("dve", lambda e: e.tensor_copy(f2(ghi), f2(gt)), reads=[K("gt")], writes=[K("ghi")])
    cx.op("dve", lambda e: e.tensor_tensor(out=f2(glo), in0=f2(gt), in1=f2(ghi), op=ALU.subtract),
          reads=[K("gt"), K("ghi")], writes=[K("glo")])
    _chk(2.2)

    def mmg(e):
        ins = None
        for ct in range(NTILE):
            for dst, m in ((ps[2], trib), (ps[3], trisb), (ps[4], c["ones_bf"])):
                e.matmul(dst[:, ct * 16:(ct + 1) * 16], m[:], ghi[:, ct, :], start=True, stop=False)
                ins = e.matmul(dst[:, ct * 16:(ct + 1) * 16], m[:], glo[:, ct, :], start=False, stop=True)
        return ins
    cx.op("pe", mmg, reads=[K("ghi"), K("glo"), K("trib"), K("trisb"), "ones_bf"], writes=[pk[2], pk[3], pk[4]])
    _chk(2.4)
    cx.op("dve", lambda e: e.tensor_copy(f2(gchi), ps[2][:]), reads=[pk[2]], writes=[K("gchi")])
    cx.op("dve", lambda e: e.tensor_tensor(out=f2(gclo), in0=ps[2][:], in1=f2(gchi), op=ALU.subtract),
          reads=[pk[2], K("gchi")], writes=[K("gclo")])
    _chk(2.6)
    cx.op("dve", lambda e: e.tensor_copy(f2(gc), ps[2][:]), reads=[pk[2]], writes=[K("gc")])
    cx.op("act", lambda e: e.activation(f2(kbs), ps[2][:], AF.Exp), reads=[pk[2]], writes=[K("kbs")])
    _chk(2.7)
    cx.op("dve", lambda e: e.tensor_tensor(out=f2(kbs), in0=f2(kbs), in1=f2(beta), op=ALU.mult),
          reads=[K("kbs"), K("beta")], writes=[K("kbs")])
    _chk(2.8)
    cx.op("act", lambda e: e.activation(f2(egr), ps[3][:], AF.Exp), reads=[pk[3]], writes=[K("egr")])
    cx.op("act", lambda e: e.activation(f2(cd), ps[4][:], AF.Exp), reads=[pk[4]], writes=[K("cd")])
    _chk(2.9)
    cx.barrier()
    cx.pop()
    _chk(3)

    qT = cx.sbuf(tag + "qT", [128, 2, T], BF16)
    kT = cx.sbuf(tag + "kT", [128, 2, T], BF16)
    ktok = cx.sbuf(tag + "ktok", [128, NTILE, 2, 128], BF16)
    vtok = cx.sbuf(tag + "vtok", [128, NTILE, 4, 128], BF16)
    PW = 1024
    for hgp in range(DBG.get('hgps', 4)):
        cx.push()
        wch = [cx.sbuf(tag + "wch%d" % i, [128, KC, 128], BF16) for i in range(2)]
        xc = [cx.sbuf(tag + "xc%d" % i, [128, 3 + PW], F32) for i in range(2)]
        acc = cx.sbuf(tag + "acc", [128, PW], F32)
        sil = cx.sbuf(tag + "sil", [128, PW], F32)
        silb = cx.sbuf(tag + "silb", [128, PW], BF16)
        sqb = cx.sbuf(tag + "sqb", [128, PW], BF16)
        rn = cx.sbuf(tag + "rn", [128, PW], F32)
        pst = cx.psum(tag + "pst", [128, PW], BF16) if False else None
        chunks = [("q", 0), ("q", 1), ("k", 0), ("k", 1), ("v", 0), ("v", 1), ("v", 2), ("v", 3)]
        wcnt = 0
        for kind, li in chunks:
            if kind == "q":
                gch = 2 * hgp + li
            elif kind == "k":
                gch = 8 + 2 * hgp + li
            else:
                gch = 16 + 4 * hgp + li
            ws = wcnt % 2
            wcnt += 1
            cx.dma("pool", wch[ws][:], w_in_r[:, :, gch * 128:(gch + 1) * 128], writes=[(K("wch"), ws)],
                   skey=("wch", ws))
            cx.op("pool", lambda e: e.memset(xc[0][:, 0:3], 0.0), writes=[(K("xc"), 0)])
            for pc in range(T // PW):
                pb = pc % 2
                for half in range(PW // TT):
                    tt = pc * (PW // TT) + half
                    pp = ps[half]

                    def mm(e, ws=ws, tt=tt, pp=pp):
                        ins = None
                        for kc in range(KC):
                            ins = e.matmul(pp[:], wch[ws][:, kc, :], hT[:, kc, tt * TT:(tt + 1) * TT],
                                           start=(kc == 0), stop=(kc == KC - 1))
                        return ins
                    cx.op("pe", mm, reads=[(K("wch"), ws)], writes=[pk[half]])
                    cx.op("act", lambda e, pb=pb, half=half, pp=pp: e.activation(
                        xc[pb][:, 3 + half * TT: 3 + (half + 1) * TT], pp[:], AF.Copy),
                        reads=[pk[half]], writes=[(K("xc"), pb)])
                for j in range(4):
                    if j == 0:
                        cx.op("dve", lambda e, pb=pb, gch=gch: e.tensor_scalar(
                            acc[:], xc[pb][:, 0:PW], cw[:, gch, 0:1], None, ALU.mult),
                            reads=[(K("xc"), pb), K("cw")], writes=[K("acc")])
                    else:
                        cx.op("dve", lambda e, pb=pb, gch=gch, j=j: e.scalar_tensor_tensor(
                            out=acc[:], in0=xc[pb][:, j:j + PW], scalar=cw[:, gch, j:j + 1], in1=acc[:],
                            op0=ALU.mult, op1=ALU.add), reads=[(K("xc"), pb), K("cw"), K("acc")], writes=[K("acc")])
                cx.op("dve", lambda e, pb=pb: e.tensor_copy(xc[1 - pb][:, 0:3], xc[pb][:, PW:PW + 3]),
                      reads=[(K("xc"), pb)], writes=[(K("xc"), 1 - pb)])
                tsl = slice(pc * PW, (pc + 1) * PW)
                _chk(4)
                if kind == "v":
                    cx.op("act", lambda e: e.activation(silb[:], acc[:], AF.Silu), reads=[K("acc")], writes=[K("silb")])
                    src_t = silb
                else:
                    cx.op("act", lambda e: e.activation(sil[:], acc[:], AF.Silu), reads=[K("acc")], writes=[K("sil")])
                    cx.op("act", lambda e: e.activation(sqb[:], sil[:], AF.Square), reads=[K("sil")], writes=[K("sqb")])

                    def mms(e):
                        ins = None
                        for half in range(PW // TT):
                            ins = e.matmul(ps[2 + half][:], c["ones_bf"][:], sqb[:, half * TT:(half + 1) * TT],
                                           start=True, stop=True)
                        return ins
                    cx.op("pe", mms, reads=[K("sqb"), "ones_bf"], writes=[pk[2], pk[3]])
                    for half in range(PW // TT):
                        cx.op("act", lambda e, half=half: e.activation(
                            rn[:, half * TT:(half + 1) * TT], ps[2 + half][:], AF.Sqrt, bias=1e-6, scale=1.0),
                            reads=[pk[2 + half]], writes=[K("rn")])
                    cx.op("dve", lambda e: e.reciprocal(rn[:], rn[:]), reads=[K("rn")], writes=[K("rn")])
                    dstT = qT if kind == "q" else kT
                    sc = (128 ** -0.5) if kind == "q" else 1.0
                    cx.op("dve", lambda e, dstT=dstT, li=li, tsl=tsl, sc=sc: e.scalar_tensor_tensor(
                        out=dstT[:, li, tsl], in0=sil[:], scalar=sc, in1=rn[:], op0=ALU.mult, op1=ALU.mult),
                        reads=[K("sil"), K("rn")], writes=[(K(kind + "T"), li)])
                    src_t = None
                if kind in ("k", "v"):
                    for q4 in range(PW // 512):
                        pbank = 4 + q4 % 2
                        ptv = ps[pbank][:, :].bitcast(BF16)

                        def mmt(e, q4=q4, ptv=ptv, kind=kind, li=li, pc=pc, src_t=src_t):
                            ins = None
                            for i in range(4):
                                col = q4 * 512 + i * 128
                                if kind == "v":
                                    src = src_t[:, col:col + 128]
                                else:
                                    src = kT[:, li, pc * PW + col: pc * PW + col + 128]
                                ins = e.transpose(ptv[:, i * 128:(i + 1) * 128], src, ident[:])
                            return ins
                        rk = [K("silb")] if kind == "v" else [(K("kT"), li)]
                        cx.op("pe", mmt, reads=rk + [K("ident")], writes=[pk[pbank]])
                        t0 = pc * (PW // 128) + q4 * 4
                        dst = vtok if kind == "v" else ktok
                        cx.op("act", lambda e, dst=dst, t0=t0, li=li, ptv=ptv: e.activation(
                            dst[:, t0:t0 + 4, li, :], ptv[:, 0:512].rearrange("p (i x) -> p i x", i=4), AF.Copy),
                            reads=[pk[pbank]], writes=[(K(kind + "tok"), li)])
        cx.barrier()
        cx.pop()
        _chk(5)

        cx.push()
        mk = lambda nm, dt_, n=2: [cx.sbuf(tag + nm + "%d" % i, [128, 4, 128], dt_) for i in range(n)]
        Abuf0, Abuf1, Atbuf0, Atbuf1 = mk("Aa", BF16), mk("Ab", BF16), mk("Ata", BF16), mk("Atb", BF16)
        A_O, AtO, Tt = mk("A_O", BF16), mk("AtO", BF16), mk("Tt", BF16)
        qkT, qdT, kd, vb, kb = mk("qkT", BF16), mk("qdT", BF16), mk("kd", BF16), mk("vb", BF16), mk("kb", BF16)
        dmT = cx.sbuf(tag + "dmT", [128, 4, 128], F32)
        dm2 = cx.sbuf(tag + "dm2", [128, 4, 128], F32)
        eB = cx.sbuf(tag + "eB", [128, 4, 128], F32)
        rr = cx.sbuf(tag + "rr", [128, 4, 128], F32)
        u = cx.sbuf(tag + "u", [128, 4, 128], F32)
        wT = cx.sbuf(tag + "wT", [128, 4, 128], BF16)
        vnew = cx.sbuf(tag + "vnew", [128, 4, 128], BF16)
        S32 = cx.sbuf(tag + "S32", [128, 4, 128], F32)
        S16 = cx.sbuf(tag + "S16", [128, 4, 128], BF16)
        ost = [cx.sbuf(tag + "ost%d" % i, [128, 4, 128], F32) for i in range(2)]
        fl = lambda t: t[:, :, :].rearrange("p h x -> p (h x)")
        cx.op("dve", lambda e: e.memset(fl(S32), 0.0), writes=[K("S32")])
        cx.op("dve", lambda e: e.memset(fl(S16), 0.0), writes=[K("S16")])
        hs = [4 * hgp + hl for hl in range(4)]
        B0, B1 = 0, 1

        def g2_tile(ct, hs, hgp=hgp):
            par = ct % 2
            bA, bAt, bTu = 2 + 3 * par, 3 + 3 * par, 4 + 3 * par
            tsl = slice(ct * 128, (ct + 1) * 128)
            P = lambda n: (K(n), par)
            AO, AtOp, Ttp = A_O[par], AtO[par], Tt[par]
            bufs = [(Abuf0[par], Atbuf0[par]), (Abuf1[par], Atbuf1[par]), (AO, AtOp)]

            def mm1(e):
                for hl in range(4):
                    e.matmul(ps[B0][:, hl * 128:(hl + 1) * 128], gchi[:, ct, hs[hl]:hs[hl] + 1].to_broadcast([128, 128]),
                             ident[:], start=True, stop=False)
                    e.matmul(ps[B0][:, hl * 128:(hl + 1) * 128], gclo[:, ct, hs[hl]:hs[hl] + 1].to_broadcast([128, 128]),
                             ident[:], start=False, stop=True)
                ins = None
                for khl in range(2):
                    e.matmul(ps[B1][:, khl * 128:(khl + 1) * 128], kT[:, khl, tsl], kT[:, khl, tsl], start=True, stop=True)
                    ins = e.matmul(ps[B1][:, (2 + khl) * 128:(3 + khl) * 128], kT[:, khl, tsl], qT[:, khl, tsl],
                                   start=True, stop=True)
                return ins
            cx.op("pe", mm1, reads=[K("ident")], writes=[pk[B0], pk[B1]])
            yield
            for hl in range(4):
                h = hs[hl]
                cx.op("dve", lambda e, hl=hl, h=h: e.tensor_scalar(
                    dmT[:, hl, :], ps[B0][:, hl * 128:(hl + 1) * 128], gc[:, ct, h:h + 1], 0.0, ALU.subtract, ALU.min),
                    reads=[pk[B0]], writes=[K("dmT")])
                cx.op("dve", lambda e, hl=hl, h=h: e.tensor_scalar(
                    dm2[:, hl, :], ps[B0][:, hl * 128:(hl + 1) * 128], gc[:, ct, h:h + 1], 0.0, ALU.subtract, ALU.max),
                    reads=[pk[B0]], writes=[K("dm2")])
                yield
            cx.op("act", lambda e: e.activation(fl(eB), ps[B0][:], AF.Exp), reads=[pk[B0]], writes=[K("eB")])
            cx.op("act", lambda e: e.activation(fl(dmT), fl(dmT), AF.Exp), reads=[K("dmT")], writes=[K("dmT")])
            cx.op("act", lambda e: e.activation(fl(dm2), fl(dm2), AF.Exp, scale=-1.0), reads=[K("dm2")], writes=[K("dm2")])
            yield
            cx.op("dve", lambda e: e.tensor_tensor(out=fl(dmT), in0=fl(dmT), in1=fl(minc4), op=ALU.mult),
                  reads=[K("dmT")], writes=[K("dmT")])
            cx.op("dve", lambda e: e.tensor_tensor(out=fl(dm2), in0=fl(dm2), in1=fl(mlow4), op=ALU.mult),
                  reads=[K("dm2")], writes=[K("dm2")])
            yield
            for hl in range(4):
                h = hs[hl]
                khl = hl // 2
                cx.op("dve", lambda e, hl=hl, h=h, khl=khl: e.scalar_tensor_tensor(
                    out=AO[:, hl, :], in0=ps[B1][:, khl * 128:(khl + 1) * 128], scalar=nbeta[:, ct, h:h + 1],
                    in1=dm2[:, hl, :], op0=ALU.mult, op1=ALU.mult), reads=[pk[B1], K("dm2")], writes=[(P("A"), 2)])
                cx.op("dve", lambda e, hl=hl, khl=khl: e.tensor_tensor(
                    out=qkT[par][:, hl, :], in0=ps[B1][:, (2 + khl) * 128:(3 + khl) * 128], in1=dmT[:, hl, :], op=ALU.mult),
                    reads=[pk[B1], K("dmT")], writes=[P("qkT")])
                cx.op("dve", lambda e, hl=hl, khl=khl: e.tensor_tensor(
                    out=qdT[par][:, hl, :], in0=qT[:, khl, tsl], in1=eB[:, hl, :], op=ALU.mult),
                    reads=[K("eB")], writes=[P("qdT")])
                cx.op("pool", lambda e, hl=hl, h=h: e.tensor_scalar(
                    vb[par][:, hl, :], vtok[:, ct, hl, :], beta[:, ct, h:h + 1], None, ALU.mult), writes=[P("vb")])
                cx.op("pool", lambda e, hl=hl, h=h, khl=khl: e.tensor_scalar(
                    kb[par][:, hl, :], ktok[:, ct, khl, :], kbs[:, ct, h:h + 1], None, ALU.mult), writes=[P("kb")])
                cx.op("pool", lambda e, hl=hl, h=h, khl=khl: e.tensor_scalar(
                    kd[par][:, hl, :], ktok[:, ct, khl, :], egr[:, ct, h:h + 1], None, ALU.mult), writes=[P("kd")])
                yield
            ptv = ps[bA][:, :].bitcast(BF16)

            def mmT(e):
                ins = None
                for hl in range(4):
                    ins = e.transpose(ptv[:, hl * 128:(hl + 1) * 128], AO[:, hl, :], ident[:])
                return ins
            cx.op("pe", mmT, reads=[(P("A"), 2), K("ident")], writes=[pk[bA]])
            yield
            cx.op("act", lambda e: e.activation(fl(AtOp), ptv[:, 0:512], AF.Copy),
                  reads=[pk[bA]], writes=[(P("At"), 2)])
            yield
            cx.op("dve", lambda e: e.tensor_tensor(out=fl(Ttp), in0=fl(AtOp), in1=fl(id4), op=ALU.add),
                  reads=[(P("At"), 2), K("id4")], writes=[P("Tt")])
            yield
            cur = 2
            NSTEP = 6
            for p in range(1, NSTEP + 1):
                nxt = 0 if cur == 2 else 1 - cur
                A, At = bufs[cur]
                An, Atn = bufs[nxt]
                last = (p == NSTEP)

                def mmsq(e, A=A, At=At, last=last):
                    ins = None
                    for hl in range(4):
                        ins = e.matmul(ps[bA][:, hl * 128:(hl + 1) * 128], At[:, hl, :], A[:, hl, :], start=True, stop=True)
                    if not last:
                        for hl in range(4):
                            ins = e.matmul(ps[bAt][:, hl * 128:(hl + 1) * 128], A[:, hl, :], At[:, hl, :],
                                           start=True, stop=True)
                    return ins
                cx.op("pe", mmsq, reads=[(P("A"), cur), (P("At"), cur)], writes=[pk[bA]] + ([] if last else [pk[bAt]]))
                yield
                cx.op("act", lambda e, An=An: e.activation(fl(An), ps[bA][:], AF.Copy),
                      reads=[pk[bA]], writes=[(P("A"), nxt)])
                if not last:
                    cx.op("act", lambda e, Atn=Atn: e.activation(fl(Atn), ps[bAt][:], AF.Copy),
                          reads=[pk[bAt]], writes=[(P("At"), nxt)])
                yield

                def mmtu(e, An=An):
                    ins = None
                    for hl in range(4):
                        ins = e.matmul(ps[bTu][:, hl * 128:(hl + 1) * 128], An[:, hl, :], Ttp[:, hl, :], start=True, stop=True)
                    return ins
                cx.op("pe", mmtu, reads=[(P("A"), nxt), P("Tt")], writes=[pk[bTu]])
                yield
                cx.op("dve", lambda e: e.tensor_tensor(out=fl(Ttp), in0=fl(Ttp), in1=ps[bTu][:], op=ALU.add),
                      reads=[pk[bTu], P("Tt")], writes=[P("Tt")])
                cur = nxt
                if p == 3:
                    yield "MID"
                else:
                    yield
            T0, Rb = Atbuf0[par], Abuf0[par]
            ptv3 = ps[bAt][:, :].bitcast(BF16)

            def mmT0(e):
                ins = None
                for hl in range(4):
                    ins = e.transpose(ptv3[:, hl * 128:(hl + 1) * 128], Ttp[:, hl, :], ident[:])
                return ins
            cx.op("pe", mmT0, reads=[P("Tt"), K("ident")], writes=[pk[bAt]])
            yield
            cx.op("act", lambda e: e.activation(fl(T0), ptv3[:, 0:512], AF.Copy),
                  reads=[pk[bAt]], writes=[(P("At"), 0)])
            yield

            def mmR(e):
                ins = None
                for hl in range(4):
                    ins = e.matmul(ps[bA][:, hl * 128:(hl + 1) * 128], AtOp[:, hl, :], T0[:, hl, :], start=True, stop=True)
                return ins
            cx.op("pe", mmR, reads=[(P("At"), 2), (P("At"), 0)], writes=[pk[bA]])
            yield
            cx.op("dve", lambda e: e.scalar_tensor_tensor(out=fl(rr), in0=fl(T0), scalar=-1.0, in1=ps[bA][:],
                                                           op0=ALU.mult, op1=ALU.add),
                  reads=[(P("At"), 0), pk[bA]], writes=[K("rr")])
            cx.op("dve", lambda e: e.tensor_tensor(out=fl(Rb), in0=fl(rr), in1=fl(id4), op=ALU.add),
                  reads=[K("rr"), K("id4")], writes=[(P("A"), 0)])
            yield

            def mmU(e):
                ins = None
                for hl in range(4):
                    ins = e.matmul(ps[bTu][:, hl * 128:(hl + 1) * 128], Rb[:, hl, :], Ttp[:, hl, :], start=True, stop=True)
                return ins
            cx.op("pe", mmU, reads=[(P("A"), 0), P("Tt")], writes=[pk[bTu]])
            yield
            cx.op("dve", lambda e: e.tensor_tensor(out=fl(Ttp), in0=fl(Ttp), in1=ps[bTu][:], op=ALU.add),
                  reads=[pk[bTu], P("Tt")], writes=[P("Tt")])
            yield

            def mmuw(e):
                ins = None
                for hl in range(4):
                    e.matmul(ps[bA][:, hl * 128:(hl + 1) * 128], Ttp[:, hl, :], vb[par][:, hl, :], start=True, stop=True)
                    ins = e.matmul(ps[bAt][:, hl * 128:(hl + 1) * 128], kb[par][:, hl, :], Ttp[:, hl, :], start=True, stop=True)
                return ins
            cx.op("pe", mmuw, reads=[P("Tt"), P("vb"), P("kb")], writes=[pk[bA], pk[bAt]])
            yield
            cx.op("act", lambda e: e.activation(fl(u), ps[bA][:], AF.Copy), reads=[pk[bA]], writes=[K("u")])
            cx.op("act", lambda e: e.activation(fl(wT), ps[bAt][:], AF.Copy), reads=[pk[bAt]], writes=[K("wT")])
            yield

            def mmvn(e):
                ins = None
                for hl in range(4):
                    ins = e.matmul(ps[bTu][:, hl * 128:(hl + 1) * 128], wT[:, hl, :], S16[:, hl, :], start=True, stop=True)
                return ins
            cx.op("pe", mmvn, reads=[K("wT"), K("S16")], writes=[pk[bTu]])
            yield
            cx.op("dve", lambda e: e.tensor_tensor(out=fl(vnew), in0=fl(u), in1=ps[bTu][:], op=ALU.subtract),
                  reads=[K("u"), pk[bTu]], writes=[K("vnew")])
            yield

            def mmo(e):
                ins = None
                for hl in range(4):
                    e.matmul(ps[bA][:, hl * 128:(hl + 1) * 128], S16[:, hl, :], qdT[par][:, hl, :], start=True, stop=False)
                    e.matmul(ps[bA][:, hl * 128:(hl + 1) * 128], vnew[:, hl, :], qkT[par][:, hl, :], start=False, stop=True)
                for hl in range(4):
                    ins = e.matmul(ps[bAt][:, hl * 128:(hl + 1) * 128], kd[par][:, hl, :], vnew[:, hl, :], start=True, stop=True)
                return ins
            cx.op("pe", mmo, reads=[K("S16"), P("qdT"), K("vnew"), P("qkT"), P("kd")], writes=[pk[bA], pk[bAt]])
            yield
            ob = ct % 2
            for hl in range(4):
                h = hs[hl]
                cx.op("dve", lambda e, hl=hl, h=h: e.scalar_tensor_tensor(
                    out=S32[:, hl, :], in0=S32[:, hl, :], scalar=cd[:, ct, h:h + 1], in1=ps[bAt][:, hl * 128:(hl + 1) * 128],
                    op0=ALU.mult, op1=ALU.add), reads=[pk[bAt], K("S32")], writes=[K("S32")])
            cx.op("act", lambda e: e.activation(fl(S16), fl(S32), AF.Copy), reads=[K("S32")], writes=[K("S16")])
            cx.op("act", lambda e: e.activation(fl(ost[ob]), ps[bA][:], AF.Copy),
                  reads=[pk[bA]], writes=[(K("ost"), ob)])
            for hl in range(4):
                h = hs[hl]
                cx.dma("sp", oraw_d[h * 128:(h + 1) * 128, ct * 128:(ct + 1) * 128], ost[ob][:, hl, :],
                       reads=[(K("ost"), ob)], writes=[(okey, h, ct)], skey=("ost", ob, hl))
            yield

        ntile = DBG.get('tiles', NTILE)
        active, nxt_t = [], [0]

        def start():
            if nxt_t[0] < ntile and len(active) < 2:
                active.append(g2_tile(nxt_t[0], list(hs)))
                nxt_t[0] += 1
        start()
        while active:
            for g in list(active):
                try:
                    v = next(g)
                except StopIteration:
                    active.remove(g)
                    start()
                    continue
                if v == "MID":
                    start()
        cx.barrier()
        cx.pop()
    cx.barrier()
    cx.pop()


def gdn_out_phase(cx, c, tag, x_in, x_out, oraw_d, nw_d, gnw_d, w_in, w_out, xkey_in, xkey_out, okey):
    cx.push()
    HT = 2048
    NT = HT // TT
    WB = 256
    nheads = NVH
    K = lambda n: tag + n
    hT = cx.sbuf(tag + "hT", [128, KC, HT], BF16)
    osb = cx.sbuf(tag + "osb", [128, nheads, HT], BF16)
    gnw = cx.sbuf(tag + "gnw", [128, 1], F32)
    nw = cx.sbuf(tag + "nw", [128, KC], F32)
    xt = [cx.sbuf(tag + "xt%d" % i, [128, KC, TT], F32) for i in range(1)]
    sq = cx.sbuf(tag + "sq", [128, KC, TT], BF16)
    rstd = cx.sbuf(tag + "rstd", [128, TT], F32)
    wz = [cx.sbuf(tag + "wz%d" % i, [128, KC, 128], BF16) for i in range(2)]
    orw = [cx.sbuf(tag + "orw%d" % i, [128, TT], F32) for i in range(2)]
    osq = [cx.sbuf(tag + "osq%d" % i, [128, TT], BF16) for i in range(2)]
    orn = [cx.sbuf(tag + "orn%d" % i, [128, TT], F32) for i in range(2)]
    sz = [cx.sbuf(tag + "sz%d" % i, [128, TT], F32) for i in range(2)]
    wo = [cx.sbuf(tag + "wo%d" % i, [128, nheads, WB], BF16) for i in range(2)]
    xr = [cx.sbuf(tag + "xr%d" % i, [128, TT], F32) for i in range(2)]
    xn = [cx.sbuf(tag + "xn%d" % i, [128, TT], F32) for i in range(2)]
    ps = [cx.psum(tag + "ps%d" % i, [128, TT], F32) for i in range(7)]
    pk = [tag + "ps%d" % i for i in range(7)]
    cx.psum_keys.update(pk)
    w_in_r = w_in.rearrange("(kc p) n -> p kc n", p=128)
    w_out_r = w_out.rearrange("(kc p) n -> p kc n", p=128)
    x_in_r = x_in.rearrange("(kc p) t -> p kc t", p=128)
    cx.dma("sp", nw[:], nw_d, writes=[K("nw")], skey="nw")
    cx.dma("sp", gnw[:], gnw_d, writes=[K("gnw")], skey="gnw")
    cnt = {"o": 0, "x": 0, "z": 0, "w": 0}
    for hf in range(T // HT):
        t0 = hf * HT
        for tt in range(NT):
            gt_ = (t0 + tt * TT) // TT
            cx.dma("sp", xt[0][:], x_in_r[:, :, t0 + tt * TT: t0 + (tt + 1) * TT],
                   reads=[(xkey_in, gt_, kc) for kc in range(KC)], writes=[K("xt")], skey="xt")
            rmsnorm_tile(cx, c, xt[0], nw, lambda kc, tt=tt: hT[:, kc, tt * TT:(tt + 1) * TT], tag,
                         [K("xt")], (K("hT"), tt), pk[6], ps[6], {"sq": sq, "rstd": rstd})
        for h in range(nheads):
            ws = cnt["w"] % 2
            cnt["w"] += 1
            cx.dma("pool", wz[ws][:], w_in_r[:, :, 4096 + h * 128: 4096 + (h + 1) * 128], writes=[(K("wz"), ws)],
                   skey=("wz", ws))
            for tt in range(NT):
                gt_ = (t0 + tt * TT) // TT
                b = cnt["z"] % 2
                cnt["z"] += 1
                cx.dma("sp", orw[b][:], oraw_d[h * 128:(h + 1) * 128, t0 + tt * TT: t0 + (tt + 1) * TT],
                       reads=[(okey, h, gt_)], writes=[(K("orw"), b)], skey=("orw", b))

                def mmz(e, ws=ws, tt=tt, b=b):
                    ins = None
                    for kc in range(KC):
                        ins = e.matmul(ps[b][:], wz[ws][:, kc, :], hT[:, kc, tt * TT:(tt + 1) * TT],
                                       start=(kc == 0), stop=(kc == KC - 1))
                    return ins
                cx.op("pe", mmz, reads=[(K("wz"), ws), (K("hT"), tt)], writes=[pk[b]])
                cx.op("act", lambda e, b=b: e.activation(osq[b][:], orw[b][:], AF.Square),
                      reads=[(K("orw"), b)], writes=[(K("osq"), b)])
                cx.op("pe", lambda e, b=b: e.matmul(ps[2 + b][:], c["ones_bf"][:], osq[b][:], start=True, stop=True),
                      reads=[(K("osq"), b), "ones_bf"], writes=[pk[2 + b]])
                cx.op("act", lambda e, b=b: e.activation(orn[b][:], ps[2 + b][:], AF.Sqrt, bias=EPS, scale=1.0 / 128),
                      reads=[pk[2 + b]], writes=[(K("orn"), b)])
                cx.op("dve", lambda e, b=b: e.reciprocal(orn[b][:], orn[b][:]), reads=[(K("orn"), b)],
                      writes=[(K("orn"), b)])
                cx.op("act", lambda e, b=b: e.activation(sz[b][:], ps[b][:], AF.Silu), reads=[pk[b]],
                      writes=[(K("sz"), b)])
                cx.op("dve", lambda e, b=b: e.scalar_tensor_tensor(
                    out=orn[b][:], in0=orw[b][:], scalar=gnw[:, 0:1], in1=orn[b][:], op0=ALU.mult, op1=ALU.mult),
                    reads=[(K("orw"), b), (K("orn"), b), K("gnw")], writes=[(K("orn"), b)])
                cx.op("dve", lambda e, b=b, h=h, tt=tt: e.tensor_tensor(
                    out=osb[:, h, tt * TT:(tt + 1) * TT], in0=orn[b][:], in1=sz[b][:], op=ALU.mult),
                    reads=[(K("orn"), b), (K("sz"), b)], writes=[(K("osb"), h)])
        for mb in range(D // WB):
            s = cnt["o"] % 2
            cnt["o"] += 1
            cx.dma("pool", wo[s][:], w_out_r[:, :, mb * WB:(mb + 1) * WB], writes=[(K("wo"), s)], skey=("wo", s))
            for mm_ in range(WB // 128):
                mc = mb * (WB // 128) + mm_
                for tt in range(NT):
                    gt_ = (t0 + tt * TT) // TT
                    b = cnt["x"] % 2
                    cnt["x"] += 1
                    py = ps[4 + b]
                    cx.dma("sp", xr[b][:], x_in[mc * 128:(mc + 1) * 128, t0 + tt * TT: t0 + (tt + 1) * TT],
                           reads=[(xkey_in, gt_, mc)], writes=[(K("xr"), b)], skey=("xr", b))

                    def mm2(e, s=s, mm_=mm_, tt=tt, py=py):
                        ins = None
                        for kc in range(nheads):
                            ins = e.matmul(py[:], wo[s][:, kc, mm_ * 128:(mm_ + 1) * 128],
                                           osb[:, kc, tt * TT:(tt + 1) * TT], start=(kc == 0), stop=(kc == nheads - 1))
                        return ins
                    cx.op("pe", mm2, reads=[(K("wo"), s)] + [(K("osb"), hd) for hd in range(nheads)],
                          writes=[pk[4 + b]])
                    cx.op("dve", lambda e, b=b, py=py: e.tensor_tensor(
                        out=xn[b][:], in0=py[:], in1=xr[b][:], op=ALU.add),
                        reads=[pk[4 + b], (K("xr"), b)], writes=[(K("xn"), b)])
                    cx.dma("sp", x_out[mc * 128:(mc + 1) * 128, t0 + tt * TT: t0 + (tt + 1) * TT], xn[b][:],
                           reads=[(K("xn"), b)], writes=[(xkey_out, gt_, mc)], skey=("xn", b))
    cx.barrier()
    cx.pop()


def build_gdn_prog():
    nc = bass.Bass("TRN2", target_bir_lowering=False)
    x = nc.dram_tensor("x", [D, T], F32, kind="ExternalInput").ap()
    nw = nc.dram_tensor("nw", [128, KC], F32, kind="ExternalInput").ap()
    w_in = nc.dram_tensor("w_in", [D, GP], F32, kind="ExternalInput").ap()
    w_out = nc.dram_tensor("w_out", [NVH * 128, D], F32, kind="ExternalInput").ap()
    convw = nc.dram_tensor("convw", [128, 32, 4], F32, kind="ExternalInput").ap()
    alog = nc.dram_tensor("alog", [128, 512], F32, kind="ExternalInput").ap()
    dtb = nc.dram_tensor("dtb", [128, 512], F32, kind="ExternalInput").ap()
    gnw = nc.dram_tensor("gnw", [128, 1], F32, kind="ExternalInput").ap()
    cst = nc.dram_tensor("cst", [128, CST_W], F32, kind="ExternalInput").ap()
    oraw = nc.dram_tensor("oraw_scratch", [NVH * 128, T], F32, kind=("ExternalOutput" if DBG.get("noout") else "Internal")).ap()
    y = nc.dram_tensor("y", [D, T], F32, kind="ExternalOutput").ap()
    cx = Ctx(nc)
    c = consts(cx)
    gdn_core_phase(cx, c, "g_", x, oraw, nw, w_in, convw, alog, dtb, cst, "xin", "or")
    if not DBG.get("noout"):
        gdn_out_phase(cx, c, "go_", x, y, oraw, nw, gnw, w_in, w_out, "xin", "xout", "or")
    cx.finish()
    return nc


def final_norm_phase(cx, c, tag, x_in, y_out, nw_d, xkey_in):
    cx.push()
    xt = [cx.sbuf(tag + "xt%d" % i, [128, KC, TT], F32) for i in range(2)]
    yo = [cx.sbuf(tag + "yo%d" % i, [128, KC, TT], F32) for i in range(2)]
    sq = cx.sbuf(tag + "sq", [128, KC, TT], BF16)
    rstd = cx.sbuf(tag + "rstd", [128, TT], F32)
    nw = cx.sbuf(tag + "nw", [128, KC], F32)
    ps = cx.psum(tag + "ps", [128, TT], F32)
    cx.psum_keys.add(tag + "ps")
    cx.dma("sp", nw[:], nw_d, writes=[tag + "nw"], skey="nw")
    x_in_r = x_in.rearrange("(kc p) t -> p kc t", p=128)
    y_out_r = y_out.rearrange("(kc p) t -> p kc t", p=128)
    for tt in range(NTT):
        b = tt % 2
        cx.dma("sp", xt[b][:], x_in_r[:, :, tt * TT:(tt + 1) * TT],
               reads=[(xkey_in, tt, kc) for kc in range(KC)], writes=[(tag + "xt", b)], skey=("xt", b))
        rmsnorm_tile(cx, c, xt[b], nw, lambda kc, b=b: yo[b][:, kc, :], tag,
                     [(tag + "xt", b)], (tag + "yo", b), tag + "ps", ps, {"sq": sq, "rstd": rstd})
        cx.dma("sp", y_out_r[:, :, tt * TT:(tt + 1) * TT], yo[b][:], reads=[(tag + "yo", b)],
               writes=[("yout", tt)], skey=("yo", b))
    cx.barrier()
    cx.pop()


def build_fin_prog():
    nc = bass.Bass("TRN2", target_bir_lowering=False)
    x = nc.dram_tensor("x", [D, T], F32, kind="ExternalInput").ap()
    nw = nc.dram_tensor("nw", [128, KC], F32, kind="ExternalInput").ap()
    y = nc.dram_tensor("y", [D, T], F32, kind="ExternalOutput").ap()
    cx = Ctx(nc)
    c = consts(cx)
    final_norm_phase(cx, c, "n_", x, y, nw, "xin")
    cx.finish()
    return nc


def build_full_prog():
    nc = bass.Bass("TRN2", target_bir_lowering=False)
    dt = lambda name, shape, kind="ExternalInput", dtype=F32: nc.dram_tensor(name, list(shape), dtype, kind=kind).ap()
    x = dt("x", [D, T])
    nws = dt("nws", [13, 128, KC])
    ffn_w_in = dt("ffn_w_in", [4, 2, D, 2 * DFF])
    ffn_w_out = dt("ffn_w_out", [4, 2, DFF, D])
    attn_w_in = dt("attn_w_in", [2, D, 3 * NH_A * 128])
    attn_w_out = dt("attn_w_out", [2, NH_A * 128, D])
    gdn_w_in = dt("gdn_w_in", [2, D, GP])
    gdn_w_out = dt("gdn_w_out", [2, NVH * 128, D])
    convw = dt("convw", [2, 128, 32, 4])
    alog = dt("alog", [2, 128, 512])
    dtb = dt("dtb", [2, 128, 512])
    gnw = dt("gnw", [2, 128, 1])
    ctab = dt("ctab", [128, T])
    stab = dt("stab", [128, T])
    cst = dt("cst", [128, CST_W])
    y = dt("y", [D, T], kind="ExternalOutput")
    xbuf = [nc.dram_tensor("xres%d" % i, [D, T], F32).ap() for i in range(2)]
    os_d = nc.dram_tensor("os_scratch", [NH_A * 128, T], BF16).ap()
    oraw = nc.dram_tensor("oraw_scratch", [NVH * 128, T], F32).ap()
    cx = Ctx(nc)
    c = consts(cx)
    cur, nb_, ph = x, 0, 0
    ia = ib = 0
    for i in range(4):
        ffn_phase(cx, c, "f%da_" % i, cur, xbuf[nb_], nws[ph], ffn_w_in[i, 0], ffn_w_out[i, 0], "xi", "xo")
        cur, nb_, ph = xbuf[nb_], 1 - nb_, ph + 1
        if i % 2 == 0:
            attn_phase(cx, c, "a%d_" % i, cur, os_d, nws[ph], attn_w_in[ia], ctab, stab, cst, "xi", "os")
            outproj_phase(cx, c, "ao%d_" % i, cur, xbuf[nb_], os_d, attn_w_out[ia], NH_A, "xi", "xo", "os")
            ia += 1
        else:
            gdn_core_phase(cx, c, "g%d_" % i, cur, oraw, nws[ph], gdn_w_in[ib], convw[ib], alog[ib], dtb[ib], cst, "xi", "or")
            gdn_out_phase(cx, c, "go%d_" % i, cur, xbuf[nb_], oraw, nws[ph], gnw[ib], gdn_w_in[ib], gdn_w_out[ib],
                          "xi", "xo", "or")
            ib += 1
        cur, nb_, ph = xbuf[nb_], 1 - nb_, ph + 1
        ffn_phase(cx, c, "f%db_" % i, cur, xbuf[nb_], nws[ph], ffn_w_in[i, 1], ffn_w_out[i, 1], "xi", "xo")
        cur, nb_, ph = xbuf[nb_], 1 - nb_, ph + 1
    final_norm_phase(cx, c, "n_", cur, y, nws[ph], "xi")
    cx.finish()
    return nc


_PROGS = {}


def _prog(name):
    if name not in _PROGS:
        _PROGS[name] = {"ffn": build_ffn_prog, "attn": build_attn_prog, "gdn": build_gdn_prog,
                        "fin": build_fin_prog}[name]()
    return _PROGS[name]


def _lay_nw(v):
    return np.ascontiguousarray(np.asarray(v, np.float32).reshape(KC, 128).T)


def _launch(name, xs, shared):
    nc = _prog(name)
    n = len(xs)
    in_maps = [dict(shared, x=xs[i]) for i in range(n)]
    res = run_bass_kernel_spmd(nc, in_maps, core_ids=list(range(n)))
    return [np.asarray(res.results[i]["y"]) for i in range(n)]


def kernel(x, norm_w, ffn_w_in, ffn_w_out, attn_w_in, attn_w_out, gdn_w_in, gdn_conv_w, gdn_a_log,
           gdn_dt_bias, gdn_norm_w, gdn_w_out, final_norm_w):
    f = lambda a: np.ascontiguousarray(np.asarray(a, np.float32))
    x = f(x)
    B = x.shape[0]
    ctab, stab = host_rope()
    nws = np.stack([_lay_nw(norm_w[i, j]) for i in range(4) for j in range(3)] + [_lay_nw(final_norm_w)], 0)
    shared = {
        "nws": np.ascontiguousarray(nws), "ffn_w_in": f(ffn_w_in), "ffn_w_out": f(ffn_w_out),
        "attn_w_in": f(attn_w_in), "attn_w_out": f(attn_w_out), "gdn_w_in": f(gdn_w_in), "gdn_w_out": f(gdn_w_out),
        "convw": np.ascontiguousarray(f(gdn_conv_w).reshape(2, 4, 32, 128).transpose(0, 3, 2, 1)),
        "alog": np.ascontiguousarray(np.tile(f(gdn_a_log)[:, None, :], (1, 128, 32))),
        "dtb": np.ascontiguousarray(np.tile(f(gdn_dt_bias)[:, None, :], (1, 128, 32))),
        "gnw": np.ascontiguousarray(f(gdn_norm_w).reshape(2, 128, 1)),
        "ctab": ctab, "stab": stab, "cst": host_consts(),
    }
    if "full" not in _PROGS:
        _PROGS["full"] = build_full_prog()
    nc = _PROGS["full"]
    in_maps = [dict(shared, x=np.ascontiguousarray(x[b].T)) for b in range(B)]
    res = run_bass_kernel_spmd(nc, in_maps, core_ids=list(range(B)))
    return np.stack([np.asarray(res.results[b]["y"]).T for b in range(B)], 0).astype(np.float32)
```

```python
from contextlib import ExitStack
import numpy as np
import concourse.bass as bass
import concourse.mybir as mybir
from concourse.bass_utils import run_bass_kernel_spmd

F32 = mybir.dt.float32
BF16 = mybir.dt.bfloat16
ALU = mybir.AluOpType
AF = mybir.ActivationFunctionType

ENGS = ("pe", "act", "dve", "pool", "sp")
NAMES = []


class _Op:
    __slots__ = ("eng", "fn", "deps", "idx", "is_dma", "dsem", "dval", "need_inc", "cnt")

    def __init__(self, eng, fn, is_dma):
        self.eng = eng
        self.fn = fn
        self.deps = []
        self.is_dma = is_dma
        self.dsem = None
        self.dval = 0
        self.need_inc = False
        self.cnt = 0


class Ctx:
    def __init__(self, nc):
        self.nc = nc
        self.q = {e: [] for e in ENGS}
        self.wr = {}
        self.rd = {}
        self.dsems = {}
        self.pending_dma = []
        self.sem_pool = {}
        self.psum_keys = set()
        self.stack = ExitStack()
        self.scopes = []
        self.esem = {e: nc.alloc_semaphore("sem_" + e) for e in ENGS}
        self.n_psum = 0

    def push(self):
        st = ExitStack()
        self.scopes.append(st)

    def pop(self):
        self.scopes.pop().close()

    def _scope(self):
        return self.scopes[-1] if self.scopes else self.stack

    def sbuf(self, name, shape, dtype):
        self.n_psum += 1
        NAMES.append("%s_%d" % (name, self.n_psum))
        return self._scope().enter_context(self.nc.sbuf_tensor("%s_%d" % (name, self.n_psum), list(shape), dtype))

    def psum(self, name, shape, dtype=F32):
        self.n_psum += 1
        return self._scope().enter_context(self.nc.psum_tensor("%s_%d" % (name, self.n_psum), list(shape), dtype))

    def _track(self, op, reads, writes):
        deps = op.deps
        for k in reads:
            for o in self.wr.get(k, {}).values():
                deps.append(o)
            if k in self.psum_keys:
                for ek2, o in self.rd.get(k, {}).items():
                    if o.eng != op.eng:
                        deps.append(o)
        for k in writes:
            for o in self.wr.get(k, {}).values():
                deps.append(o)
            for o in self.rd.get(k, {}).values():
                deps.append(o)
        ek = id(op) if op.is_dma else op.eng
        for k in reads:
            self.rd.setdefault(k, {})[ek] = op
        for k in writes:
            self.wr[k] = {ek: op}
            self.rd[k] = {}

    def op(self, eng, fn, reads=(), writes=()):
        o = _Op(eng, fn, False)
        self._track(o, reads, writes)
        o.idx = len(self.q[eng])
        self.q[eng].append(o)
        return o

    def dma(self, eng, out, in_, reads=(), writes=(), skey=None, **kw):
        key = ("dma", skey)
        o = _Op(eng, lambda e: e.dma_start(out=out, in_=in_, **kw), True)
        self._track(o, reads, writes)
        key = (eng == "pool", key)
        if key not in self.dsems:
            pool = self.sem_pool.setdefault(eng == "pool", [])
            i = sum(1 for k in self.dsems if k[0] == key[0])
            if i >= len(pool):
                pool.append([self.nc.alloc_semaphore("d%s%d" % ("s" if key[0] else "h", i)), 0])
            self.dsems[key] = pool[i]
        ds = self.dsems[key]
        ds[1] += 16
        o.dsem = ds[0]
        o.dval = ds[1]
        o.idx = len(self.q[eng])
        self.q[eng].append(o)
        self.pending_dma.append(o)
        return o

    def barrier(self):
        lasts = [self.q[e][-1] for e in ENGS if self.q[e] and not self.q[e][-1].is_dma]
        lasts += [o for e in ENGS for o in self.q[e][-1:] if False]
        dm = list(self.pending_dma)
        self.pending_dma = []
        for e in ENGS:
            o = _Op(e, None, False)
            for e2 in ENGS:
                for p in reversed(self.q[e2]):
                    if not p.is_dma and p.fn is not None:
                        o.deps.append(p)
                        break
            o.deps.extend(dm)
            o.idx = len(self.q[e])
            self.q[e].append(o)
        self.wr = {}
        self.rd = {}
        self.dsems = {}

    def finish(self):
        nc = self.nc
        self.barrier()
        for e in ENGS:
            for o in self.q[e]:
                for d in o.deps:
                    if not d.is_dma and (d.eng != o.eng or e != "pe"):
                        d.need_inc = True
        for e in ENGS:
            c = 0
            for o in self.q[e]:
                if o.need_inc:
                    c += 1
                o.cnt = c
        engobj = {"pe": "tensor", "act": "scalar", "dve": "vector", "pool": "gpsimd", "sp": "sync"}
        with nc.Block() as block:
            for e in ENGS:
                def body(eng, e=e):
                    known = {}
                    for o in self.q[e]:
                        waits = {}
                        for d in o.deps:
                            if d.is_dma:
                                s, v = d.dsem, d.dval
                            else:
                                if d.eng == e and e == "pe":
                                    continue
                                s, v = self.esem[d.eng], d.cnt
                            if v > waits.get(s, (None, 0))[1]:
                                waits[s] = (s, v)
                        for s, v in waits.values():
                            if known.get(s, 0) >= v:
                                continue
                            known[s] = v
                            eng.wait_ge(s, v)
                        if o.fn is None:
                            continue
                        ins = o.fn(eng)
                        if o.is_dma:
                            ins.then_inc(o.dsem, 16)
                        elif o.need_inc:
                            ins.then_inc(self.esem[e], 1)
                getattr(block, engobj[e])(body)
        self.stack.close()


D = 1024
T = 4096
KC = D // 128
DFF = 2816
JC = DFF // 128
TT = 512
NTT = T // TT
EPS = 1e-6


def consts(cx):
    nc = cx.nc
    c = {}
    c["ones_bf"] = cx.sbuf("ones_bf", [128, 128], BF16)
    cx.op("pool", lambda e: e.memset(c["ones_bf"][:], 1.0), writes=["ones_bf"])
    return c


def rmsnorm_tile(cx, c, xt, nw, hT_out, tag, rd_keys, wr_key, ps_key, ps, scr):
    sq, rstd = scr["sq"], scr["rstd"]
    cx.op("act", lambda e: e.activation(sq[:], xt[:], AF.Square), reads=rd_keys, writes=[tag + "sq"])

    def mm(e):
        ins = None
        for kc in range(KC):
            ins = e.matmul(ps[:], c["ones_bf"][:], sq[:, kc, :], start=(kc == 0), stop=(kc == KC - 1))
        return ins
    cx.op("pe", mm, reads=[tag + "sq", "ones_bf"], writes=[ps_key])
    cx.op("act", lambda e: e.activation(rstd[:], ps[:], AF.Sqrt, bias=EPS, scale=1.0 / D),
          reads=[ps_key], writes=[tag + "rstd"])
    cx.op("dve", lambda e: e.reciprocal(rstd[:], rstd[:]), reads=[tag + "rstd"], writes=[tag + "rstd"])
    for kc in range(KC):
        cx.op("dve", lambda e, kc=kc: e.scalar_tensor_tensor(
            out=hT_out(kc), in0=xt[:, kc, :], scalar=nw[:, kc:kc + 1], in1=rstd[:],
            op0=ALU.mult, op1=ALU.mult), reads=rd_keys + [tag + "rstd", tag + "nw"], writes=[wr_key])


def ffn_phase(cx, c, tag, x_in, x_out, nw_d, w_in, w_out, xkey_in, xkey_out):
    nc = cx.nc
    cx.push()
    HT = 2048
    NH = T // HT
    NT = HT // TT
    hT = cx.sbuf(tag + "hT", [128, KC, HT], BF16)
    act = cx.sbuf(tag + "act", [128, JC, HT], BF16)
    xt = cx.sbuf(tag + "xt", [128, KC, TT], F32)
    sq = cx.sbuf(tag + "sq", [128, KC, TT], BF16)
    rstd = cx.sbuf(tag + "rstd", [128, TT], F32)
    nw = cx.sbuf(tag + "nw", [128, KC], F32)
    WB = 256
    wg = [cx.sbuf(tag + "wg%d" % i, [128, KC, WB], BF16) for i in range(2)]
    wu = [cx.sbuf(tag + "wu%d" % i, [128, KC, WB], BF16) for i in range(2)]
    wo = [cx.sbuf(tag + "wo%d" % i, [128, JC, WB], BF16) for i in range(2)]
    sg = [cx.sbuf(tag + "sg%d" % i, [128, TT], F32) for i in range(2)]
    xr = [cx.sbuf(tag + "xr%d" % i, [128, TT], F32) for i in range(2)]
    xn = [cx.sbuf(tag + "xn%d" % i, [128, TT], F32) for i in range(2)]
    ps = [cx.psum(tag + "ps%d" % i, [128, TT], F32) for i in range(7)]
    pk = [tag + "ps%d" % i for i in range(7)]
    cx.psum_keys.update(pk)

    cx.dma("sp", nw[:], nw_d, writes=[tag + "nw"], skey="nw")
    w_in_r = w_in.rearrange("(kc p) n -> p kc n", p=128)
    w_out_r = w_out.rearrange("(kc p) n -> p kc n", p=128)
    x_in_r = x_in.rearrange("(kc p) t -> p kc t", p=128)
    cnt = {"g": 0, "o": 0, "s": 0, "x": 0}
    for h in range(NH):
        t0 = h * HT
        for tt in range(NT):
            gt = (t0 + tt * TT) // TT
            cx.dma("sp", xt[:], x_in_r[:, :, t0 + tt * TT: t0 + (tt + 1) * TT],
                   reads=[(xkey_in, gt, kc) for kc in range(KC)], writes=[tag + "xt"], skey="xt")
            rmsnorm_tile(cx, c, xt, nw, lambda kc, tt=tt: hT[:, kc, tt * TT:(tt + 1) * TT], tag,
                         [tag + "xt"], (tag + "hT", tt), pk[6], ps[6], {"sq": sq, "rstd": rstd})
        for jb in range(JC * 128 // WB):
            s = cnt["g"] % 2
            cnt["g"] += 1
            cx.dma("pool", wg[s][:], w_in_r[:, :, jb * WB:(jb + 1) * WB], writes=[(tag + "wg", s)],
                   skey=("wg", s))
            cx.dma("pool", wu[s][:], w_in_r[:, :, DFF + jb * WB: DFF + (jb + 1) * WB], writes=[(tag + "wu", s)],
                   skey=("wu", s))
            for jj in range(WB // 128):
                j = jb * (WB // 128) + jj
                for tt in range(NT):
                    b = cnt["s"] % 2
                    cnt["s"] += 1
                    pg, pu = ps[b], ps[2 + b]

                    def mm(e, s=s, jj=jj, tt=tt, pg=pg, pu=pu):
                        ins = None
                        for kc in range(KC):
                            ins = e.matmul(pg[:], wg[s][:, kc, jj * 128:(jj + 1) * 128],
                                           hT[:, kc, tt * TT:(tt + 1) * TT], start=(kc == 0), stop=(kc == KC - 1))
                        for kc in range(KC):
                            ins = e.matmul(pu[:], wu[s][:, kc, jj * 128:(jj + 1) * 128],
                                           hT[:, kc, tt * TT:(tt + 1) * TT], start=(kc == 0), stop=(kc == KC - 1))
                        return ins
                    cx.op("pe", mm, reads=[(tag + "wg", s), (tag + "wu", s), (tag + "hT", tt)],
                          writes=[pk[b], pk[2 + b]])
                    cx.op("act", lambda e, b=b, pg=pg: e.activation(sg[b][:], pg[:], AF.Silu),
                          reads=[pk[b]], writes=[(tag + "sg", b)])
                    cx.op("dve", lambda e, b=b, pu=pu, j=j, tt=tt: e.tensor_tensor(
                        out=act[:, j, tt * TT:(tt + 1) * TT], in0=sg[b][:], in1=pu[:], op=ALU.mult),
                        reads=[(tag + "sg", b), pk[2 + b]], writes=[(tag + "act", j, tt)])
        for mb in range(D // WB):
            s = cnt["o"] % 2
            cnt["o"] += 1
            cx.dma("pool", wo[s][:], w_out_r[:, :, mb * WB:(mb + 1) * WB], writes=[(tag + "wo", s)],
                   skey=("wo", s))
            for mm_ in range(WB // 128):
                mc = mb * (WB // 128) + mm_
                for tt in range(NT):
                    gt = (t0 + tt * TT) // TT
                    b = cnt["x"] % 2
                    cnt["x"] += 1
                    py = ps[4 + b]
                    cx.dma("sp", xr[b][:], x_in[mc * 128:(mc + 1) * 128, t0 + tt * TT: t0 + (tt + 1) * TT],
                           reads=[(xkey_in, gt, mc)], writes=[(tag + "xr", b)], skey=("xr", b))

                    def mm2(e, s=s, mm_=mm_, tt=tt, py=py):
                        ins = None
                        for kc in range(JC):
                            ins = e.matmul(py[:], wo[s][:, kc, mm_ * 128:(mm_ + 1) * 128],
                                           act[:, kc, tt * TT:(tt + 1) * TT], start=(kc == 0), stop=(kc == JC - 1))
                        return ins
                    cx.op("pe", mm2, reads=[(tag + "wo", s)] + [(tag + "act", j, tt) for j in range(JC)],
                          writes=[pk[4 + b]])
                    cx.op("dve", lambda e, b=b, py=py: e.scalar_tensor_tensor(
                        out=xn[b][:], in0=py[:], scalar=0.5, in1=xr[b][:], op0=ALU.mult, op1=ALU.add),
                        reads=[pk[4 + b], (tag + "xr", b)], writes=[(tag + "xn", b)])
                    cx.dma("sp", x_out[mc * 128:(mc + 1) * 128, t0 + tt * TT: t0 + (tt + 1) * TT], xn[b][:],
                           reads=[(tag + "xn", b)], writes=[(xkey_out, gt, mc)], skey=("xn", b))
    cx.barrier()
    cx.pop()


def build_ffn_prog():
    nc = bass.Bass("TRN2", target_bir_lowering=False)
    x = nc.dram_tensor("x", [D, T], F32, kind="ExternalInput").ap()
    nw = nc.dram_tensor("nw", [128, KC], F32, kind="ExternalInput").ap()
    w_in = nc.dram_tensor("w_in", [D, 2 * DFF], F32, kind="ExternalInput").ap()
    w_out = nc.dram_tensor("w_out", [DFF, D], F32, kind="ExternalInput").ap()
    y = nc.dram_tensor("y", [D, T], F32, kind="ExternalOutput").ap()
    cx = Ctx(nc)
    c = consts(cx)
    ffn_phase(cx, c, "f_", x, y, nw, w_in, w_out, "xin", "xout")
    cx.finish()
    return nc


NH_A = 12
DIL = (1, 4, 16)
NEG = -30000.0
SCALE_A = 128 ** -0.5
C_IDENT = 0
C_PSWAP = 128
C_MASK = 256
C_TRI = 512
C_TRIS = 640
C_MINC = 768
C_MSTR = 896
C_LAST = 1024
CST_W = 1152


def host_consts():
    m = np.zeros((128, CST_W), np.float32)
    k = np.arange(128)[:, None]
    q = np.arange(128)[None, :]
    m[:, C_IDENT:C_IDENT + 128] = (k == q)
    m[:, C_PSWAP:C_PSWAP + 128] = (k == (q + 64) % 128)
    m[:, C_MASK:C_MASK + 128] = np.where(k <= q, 0.0, NEG)
    m[:, C_MASK + 128:C_MASK + 256] = np.where(k >= q, 0.0, NEG)
    m[:, C_TRI:C_TRI + 128] = (k <= q)
    m[:, C_TRIS:C_TRIS + 128] = (k > q)
    m[:, C_MINC:C_MINC + 128] = (k <= q)
    m[:, C_MSTR:C_MSTR + 128] = (k < q)
    m[:, C_LAST:C_LAST + 128] = (k == 127)
    return m


def host_rope():
    inv = (1.0 / (10000.0 ** (np.arange(0, 128, 2, dtype=np.float32) / np.float32(128)))).astype(np.float32)
    ang = (np.arange(T, dtype=np.float32)[None, :] * inv[:, None]).astype(np.float32)
    cs, sn = np.cos(ang).astype(np.float32), np.sin(ang).astype(np.float32)
    return np.concatenate([cs, cs], 0), np.concatenate([-sn, sn], 0)


def norm_stage(cx, c, tag, x_in, nw_d, hT, xkey_in, ps, pskey):
    cx.push()
    xt = [cx.sbuf(tag + "nxt%d" % i, [128, KC, TT], F32) for i in range(2)]
    sq = cx.sbuf(tag + "nsq", [128, KC, TT], BF16)
    rstd = cx.sbuf(tag + "nrstd", [128, TT], F32)
    nw = cx.sbuf(tag + "nnw", [128, KC], F32)
    cx.dma("sp", nw[:], nw_d, writes=[tag + "nw"], skey="nw")
    x_in_r = x_in.rearrange("(kc p) t -> p kc t", p=128)
    for tt in range(NTT):
        b = tt % 2
        cx.dma("sp", xt[b][:], x_in_r[:, :, tt * TT:(tt + 1) * TT],
               reads=[(xkey_in, tt, kc) for kc in range(KC)], writes=[(tag + "xt", b)], skey=("xt", b))
        rmsnorm_tile(cx, c, xt[b], nw, lambda kc, tt=tt: hT[:, kc, tt * TT:(tt + 1) * TT], tag,
                     [(tag + "xt", b)], (tag + "hT", tt), pskey, ps, {"sq": sq, "rstd": rstd})
    cx.barrier()
    cx.pop()


def attn_phase(cx, c, tag, x_in, os_d, nw_d, w_in, ctab_d, stab_d, cst_d, xkey_in, okey):
    nc = cx.nc
    cx.push()
    hT = cx.sbuf(tag + "hT", [128, KC, T], BF16)
    ctab = cx.sbuf(tag + "ctab", [128, T], F32)
    stab = cx.sbuf(tag + "stab", [128, T], F32)
    ident = cx.sbuf(tag + "ident", [128, 128], BF16)
    pswap = cx.sbuf(tag + "pswap", [128, 128], BF16)
    maskb = cx.sbuf(tag + "maskb", [128, 256], BF16)
    ps = [cx.psum(tag + "ps%d" % i, [128, TT], F32) for i in range(8)]
    pk = [tag + "ps%d" % i for i in range(8)]
    cx.psum_keys.update(pk)
    cx.dma("sp", ctab[:], ctab_d, writes=[tag + "ctab"], skey="ctab")
    cx.dma("sp", stab[:], stab_d, writes=[tag + "stab"], skey="stab")
    cx.dma("pool", ident[:], cst_d[:, C_IDENT:C_IDENT + 128], writes=[tag + "ident"], skey="ident")
    cx.dma("pool", pswap[:], cst_d[:, C_PSWAP:C_PSWAP + 128], writes=[tag + "pswap"], skey="pswap")
    cx.dma("pool", maskb[:], cst_d[:, C_MASK:C_MASK + 256], writes=[tag + "maskb"], skey="maskb")
    norm_stage(cx, c, tag, x_in, nw_d, hT, xkey_in, ps[6], pk[6])
    hkeys = []

    Qd = cx.sbuf(tag + "Qd", [128, T], BF16)
    Kd = cx.sbuf(tag + "Kd", [128, T], BF16)
    Vd = cx.sbuf(tag + "Vd", [128, 32, 128], BF16)
    numT = cx.sbuf(tag + "numT", [128, 3, T], BF16)
    dent = cx.sbuf(tag + "dent", [128, T], F32)
    wq = [cx.sbuf(tag + "wq%d" % i, [128, KC, 128], BF16) for i in range(2)]
    wk = [cx.sbuf(tag + "wk%d" % i, [128, KC, 128], BF16) for i in range(2)]
    wv = [cx.sbuf(tag + "wv%d" % i, [128, KC, 128], BF16) for i in range(2)]
    qb = [cx.sbuf(tag + "qb%d" % i, [128, TT], BF16) for i in range(2)]
    t1 = [cx.sbuf(tag + "t1%d" % i, [128, TT], F32) for i in range(2)]
    t2 = [cx.sbuf(tag + "t2%d" % i, [128, TT], F32) for i in range(2)]
    pt = [cx.sbuf(tag + "pt%d" % i, [128, 256], BF16) for i in range(4)]
    osc = [cx.sbuf(tag + "osc%d" % i, [128, TT], BF16) for i in range(2)]
    w_in_r = w_in.rearrange("(kc p) n -> p kc n", p=128)
    AW = NH_A * 128
    cnt = {"w": 0, "r": 0, "o": 0}

    def load_w(hd):
        s = cnt["w"] % 2
        cnt["w"] += 1
        for nm, wt, off in (("wq", wq, 0), ("wk", wk, AW), ("wv", wv, 2 * AW)):
            cx.dma("pool", wt[s][:], w_in_r[:, :, off + hd * 128: off + (hd + 1) * 128],
                   writes=[(tag + nm, s)], skey=(nm, s))
        return s

    for hg in range(4):
        for g in range(3):
            d = DIL[g]
            Ls = T // d
            nb = Ls // 128
            hd = g * 4 + hg
            s = load_w(hd)
            for which, wt, dst, dkey in (("q", wq, Qd, tag + "Qd"), ("k", wk, Kd, tag + "Kd")):
                dst_v = dst[:, :].rearrange("p (r j) -> p j r", r=d)
                for tt in range(NTT):
                    b = cnt["r"] % 2
                    cnt["r"] += 1
                    pp, sw = ps[b], ps[2 + b]

                    def mm(e, wt=wt, s=s, tt=tt, pp=pp):
                        ins = None
                        for kc in range(KC):
                            ins = e.matmul(pp[:], wt[s][:, kc, :], hT[:, kc, tt * TT:(tt + 1) * TT],
                                           start=(kc == 0), stop=(kc == KC - 1))
                        return ins
                    cx.op("pe", mm, reads=[(tag + "w" + which, s)], writes=[pk[b]])
                    cx.op("act", lambda e, b=b, pp=pp: e.activation(qb[b][:], pp[:], AF.Copy),
                          reads=[pk[b]], writes=[(tag + "qb", b)])
                    cx.op("pe", lambda e, b=b, sw=sw: e.matmul(sw[:], pswap[:], qb[b][:], start=True, stop=True),
                          reads=[(tag + "qb", b), tag + "pswap"], writes=[pk[2 + b]])
                    cx.op("pool", lambda e, b=b, tt=tt: e.tensor_tensor(
                        out=t1[b][:], in0=qb[b][:], in1=ctab[:, tt * TT:(tt + 1) * TT], op=ALU.mult),
                        reads=[(tag + "qb", b), tag + "ctab"], writes=[(tag + "t1", b)])
                    cx.op("dve", lambda e, b=b, tt=tt, sw=sw: e.tensor_tensor(
                        out=t2[b][:], in0=sw[:], in1=stab[:, tt * TT:(tt + 1) * TT], op=ALU.mult),
                        reads=[pk[2 + b], tag + "stab"], writes=[(tag + "t2", b)])
                    j0 = tt * TT // d
                    cx.op("dve", lambda e, b=b, dst_v=dst_v, j0=j0, d=d: e.tensor_tensor(
                        out=dst_v[:, j0:j0 + TT // d, :],
                        in0=t1[b][:, :].rearrange("p (j r) -> p j r", r=d),
                        in1=t2[b][:, :].rearrange("p (j r) -> p j r", r=d), op=ALU.add),
                        reads=[(tag + "t1", b), (tag + "t2", b)], writes=[(dkey, tt)])
            for b4 in range(8):
                b = cnt["r"] % 2
                cnt["r"] += 1
                pp = ps[b]

                def mmv(e, s=s, b4=b4, pp=pp, d=d, Ls=Ls):
                    ins = None
                    for i in range(4):
                        B = b4 * 4 + i
                        r, n = divmod(B, Ls // 128)
                        t0 = n * 128 * d + r
                        for kc in range(KC):
                            ins = e.matmul(pp[:, i * 128:(i + 1) * 128],
                                           hT[:, kc, t0: t0 + 127 * d + 1: d], wv[s][:, kc, :],
                                           start=(kc == 0), stop=(kc == KC - 1))
                    return ins
                cx.op("pe", mmv, reads=[(tag + "wv", s)], writes=[pk[b]])
                cx.op("act", lambda e, b4=b4, pp=pp: e.activation(
                    Vd[:, b4 * 4:(b4 + 1) * 4, :], pp[:, :].rearrange("p (i e) -> p i e", i=4), AF.Copy),
                    reads=[pk[b]], writes=[(tag + "Vd", b4)])
            qk_keys = [(tag + "Qd", tt) for tt in range(NTT)] + [(tag + "Kd", tt) for tt in range(NTT)]

            def scores(B, nb=nb):
                sb = 4 + (B % 2)
                nq = 256 if (B + 1) % nb != 0 else 128
                st = ps[sb]

                def mm(e, B=B, nq=nq, st=st):
                    e.matmul(st[:, 0:nq], Kd[:, B * 128:(B + 1) * 128], Qd[:, B * 128:B * 128 + nq],
                             start=True, stop=False)
                    return e.matmul(st[:, 0:nq], ident[:], maskb[:, 0:nq], start=False, stop=True)
                cx.op("pe", mm, reads=qk_keys + [tag + "ident", tag + "maskb"], writes=[pk[sb]])
                cx.op("act", lambda e, B=B, nq=nq, st=st: e.activation(
                    pt[B % 4][:, 0:nq], st[:, 0:nq], AF.Exp, scale=SCALE_A),
                    reads=[pk[sb]], writes=[(tag + "pt", B % 4)])

            def pv(B, nb=nb, g=g, d=d, Ls=Ls):
                first = (B % nb == 0)
                col = (B % 4) * 128

                def mm(e, B=B, first=first, col=col):
                    ins = None
                    for dst, lhs_fn in ((ps[6], lambda blk: Vd[:, blk, :]), (ps[7], lambda blk: c["ones_bf"][:])):
                        if not first:
                            e.matmul(dst[:, col:col + 128], lhs_fn(B - 1), pt[(B - 1) % 4][:, 128:256],
                                     start=True, stop=False)
                        ins = e.matmul(dst[:, col:col + 128], lhs_fn(B), pt[B % 4][:, 0:128],
                                       start=first, stop=True)
                    return ins
                rk = [(tag + "pt", B % 4), (tag + "Vd", B // 4), "ones_bf"]
                if not first:
                    rk += [(tag + "pt", (B - 1) % 4), (tag + "Vd", (B - 1) // 4)]
                cx.op("pe", mm, reads=rk, writes=[pk[6], pk[7]])
                if B % 4 == 3:
                    u0 = (B - 3) * 128
                    if d == 1:
                        def nat(ap2d):
                            return ap2d[:, u0:u0 + 512]
                        def src(p):
                            return p[:, :]
                    else:
                        r0, j0 = divmod(u0, Ls)
                        nr = max(1, 512 // Ls)
                        nj = 512 // nr
                        def nat(ap2d, r0=r0, j0=j0, nr=nr, nj=nj, d=d):
                            return ap2d.rearrange("p (j r) -> p r j", r=d)[:, r0:r0 + nr, j0:j0 + nj]
                        def src(p, nr=nr):
                            return p[:, :].rearrange("p (r j) -> p r j", r=nr)
                    cx.op("act", lambda e, nat=nat, src=src, g=g: e.activation(
                        nat(numT[:, g, :]), src(ps[6]), AF.Copy),
                        reads=[pk[6]], writes=[(tag + "numT", g)])
                    if g == 0:
                        cx.op("dve", lambda e, nat=nat, src=src: e.tensor_copy(nat(dent[:, :]), src(ps[7])),
                              reads=[pk[7]], writes=[tag + "dent"])
                    else:
                        cx.op("dve", lambda e, nat=nat, src=src: e.tensor_tensor(
                            out=nat(dent[:, :]), in0=nat(dent[:, :]), in1=src(ps[7]), op=ALU.add),
                            reads=[pk[7], tag + "dent"], writes=[tag + "dent"])

            scores(0)
            for B in range(32):
                if B + 1 < 32:
                    scores(B + 1)
                pv(B)
        cx.op("dve", lambda e: e.reciprocal(dent[:, :], dent[:, :]), reads=[tag + "dent"], writes=[tag + "dent"])
        for g in range(3):
            hd = g * 4 + hg
            for tt in range(NTT):
                b = cnt["o"] % 2
                cnt["o"] += 1
                cx.op("dve", lambda e, b=b, g=g, tt=tt: e.tensor_tensor(
                    out=osc[b][:], in0=numT[:, g, tt * TT:(tt + 1) * TT], in1=dent[:, tt * TT:(tt + 1) * TT],
                    op=ALU.mult), reads=[tag + "dent", (tag + "numT", g)], writes=[(tag + "osc", b)])
                cx.dma("sp", os_d[hd * 128:(hd + 1) * 128, tt * TT:(tt + 1) * TT], osc[b][:],
                       reads=[(tag + "osc", b)], writes=[(okey, hd, tt)], skey=("osc", b))
    cx.barrier()
    cx.pop()


def outproj_phase(cx, c, tag, x_in, x_out, os_d, w_out, nheads, xkey_in, xkey_out, okey):
    cx.push()
    HT = 2048
    NT = HT // TT
    WB = 256
    osb = cx.sbuf(tag + "osb", [128, nheads, HT], BF16)
    wo = [cx.sbuf(tag + "wo%d" % i, [128, nheads, WB], BF16) for i in range(2)]
    xr = [cx.sbuf(tag + "xr%d" % i, [128, TT], F32) for i in range(2)]
    xn = [cx.sbuf(tag + "xn%d" % i, [128, TT], F32) for i in range(2)]
    ps = [cx.psum(tag + "ps%d" % i, [128, TT], F32) for i in range(2)]
    pk = [tag + "ps%d" % i for i in range(2)]
    cx.psum_keys.update(pk)
    w_out_r = w_out.rearrange("(kc p) n -> p kc n", p=128)
    os_r = os_d.rearrange("(h p) t -> p h t", p=128)
    cnt = {"o": 0, "x": 0}
    for h in range(T // HT):
        t0 = h * HT
        for hd in range(nheads):
            cx.dma("sp", osb[:, hd, :], os_r[:, hd, t0:t0 + HT],
                   reads=[(okey, hd, (t0 // TT) + i) for i in range(NT)], writes=[(tag + "osb", hd)], skey=("osb", hd % 4))
        for mb in range(D // WB):
            s = cnt["o"] % 2
            cnt["o"] += 1
            cx.dma("pool", wo[s][:], w_out_r[:, :, mb * WB:(mb + 1) * WB], writes=[(tag + "wo", s)], skey=("wo", s))
            for mm_ in range(WB // 128):
                mc = mb * (WB // 128) + mm_
                for tt in range(NT):
                    gt = (t0 + tt * TT) // TT
                    b = cnt["x"] % 2
                    cnt["x"] += 1
                    py = ps[b]
                    cx.dma("sp", xr[b][:], x_in[mc * 128:(mc + 1) * 128, t0 + tt * TT: t0 + (tt + 1) * TT],
                           reads=[(xkey_in, gt, mc)], writes=[(tag + "xr", b)], skey=("xr", b))

                    def mm2(e, s=s, mm_=mm_, tt=tt, py=py):
                        ins = None
                        for kc in range(nheads):
                            ins = e.matmul(py[:], wo[s][:, kc, mm_ * 128:(mm_ + 1) * 128],
                                           osb[:, kc, tt * TT:(tt + 1) * TT], start=(kc == 0), stop=(kc == nheads - 1))
                        return ins
                    cx.op("pe", mm2, reads=[(tag + "wo", s)] + [(tag + "osb", hd) for hd in range(nheads)],
                          writes=[pk[b]])
                    cx.op("dve", lambda e, b=b, py=py: e.tensor_tensor(
                        out=xn[b][:], in0=py[:], in1=xr[b][:], op=ALU.add),
                        reads=[pk[b], (tag + "xr", b)], writes=[(tag + "xn", b)])
                    cx.dma("sp", x_out[mc * 128:(mc + 1) * 128, t0 + tt * TT: t0 + (tt + 1) * TT], xn[b][:],
                           reads=[(tag + "xn", b)], writes=[(xkey_out, gt, mc)], skey=("xn", b))
    cx.barrier()
    cx.pop()


def build_attn_prog():
    nc = bass.Bass("TRN2", target_bir_lowering=False)
    x = nc.dram_tensor("x", [D, T], F32, kind="ExternalInput").ap()
    nw = nc.dram_tensor("nw", [128, KC], F32, kind="ExternalInput").ap()
    w_in = nc.dram_tensor("w_in", [D, 3 * NH_A * 128], F32, kind="ExternalInput").ap()
    w_out = nc.dram_tensor("w_out", [NH_A * 128, D], F32, kind="ExternalInput").ap()
    ctab = nc.dram_tensor("ctab", [128, T], F32, kind="ExternalInput").ap()
    stab = nc.dram_tensor("stab", [128, T], F32, kind="ExternalInput").ap()
    cst = nc.dram_tensor("cst", [128, CST_W], F32, kind="ExternalInput").ap()
    os_d = nc.dram_tensor("os_scratch", [NH_A * 128, T], BF16).ap()
    y = nc.dram_tensor("y", [D, T], F32, kind="ExternalOutput").ap()
    cx = Ctx(nc)
    c = consts(cx)
    attn_phase(cx, c, "a_", x, os_d, nw, w_in, ctab, stab, cst, "xin", "os")
    outproj_phase(cx, c, "ao_", x, y, os_d, w_out, NH_A, "xin", "xout", "os")
    cx.finish()
    return nc


NVH = 16
NKH = 8
GP = 6176
NTILE = T // 128
DBG = {}


class _Stop(Exception):
    pass


def _chk(n):
    if DBG.get("stop", 99) <= n:
        raise _Stop()


def gdn_core_phase(*a):
    cx = a[0]
    depth = len(cx.scopes)
    try:
        _gdn_core_phase(*a)
    except _Stop:
        cx.barrier()
        while len(cx.scopes) > depth:
            cx.pop()


def _gdn_core_phase(cx, c, tag, x_in, oraw_d, nw_d, w_in, convw_d, alog_d, dtb_d, cst_d, xkey_in, okey):
    cx.push()
    hT = cx.sbuf(tag + "hT", [128, KC, T], BF16)
    ident = cx.sbuf(tag + "ident", [128, 128], BF16)
    id4 = cx.sbuf(tag + "id4", [128, 4, 128], BF16)
    minc4 = cx.sbuf(tag + "minc4", [128, 4, 128], BF16)
    mlow4 = cx.sbuf(tag + "mlow4", [128, 4, 128], BF16)
    cw = cx.sbuf(tag + "cw", [128, 32, 4], F32)
    beta = cx.sbuf(tag + "beta", [128, NTILE, NVH], F32)
    nbeta = cx.sbuf(tag + "nbeta", [128, NTILE, NVH], F32)
    gc = cx.sbuf(tag + "gc", [128, NTILE, NVH], F32)
    kbs = cx.sbuf(tag + "kbs", [128, NTILE, NVH], F32)
    egr = cx.sbuf(tag + "egr", [128, NTILE, NVH], F32)
    cd = cx.sbuf(tag + "cd", [128, NTILE, NVH], F32)
    gchi = cx.sbuf(tag + "gchi", [128, NTILE, NVH], BF16)
    gclo = cx.sbuf(tag + "gclo", [128, NTILE, NVH], BF16)
    ps = [cx.psum(tag + "ps%d" % i, [128, TT], F32) for i in range(8)]
    pk = [tag + "ps%d" % i for i in range(8)]
    cx.psum_keys.update(pk)
    K = lambda n: tag + n
    cx.push()
    alog = cx.sbuf(tag + "alog", [128, 512], F32)
    dtb = cx.sbuf(tag + "dtb", [128, 512], F32)
    gt = cx.sbuf(tag + "gt", [128, NTILE, NVH], F32)
    wab = cx.sbuf(tag + "wab", [128, KC, 32], BF16)
    ghi = cx.sbuf(tag + "ghi", [128, NTILE, NVH], BF16)
    glo = cx.sbuf(tag + "glo", [128, NTILE, NVH], BF16)
    trib = cx.sbuf(tag + "trib", [128, 128], BF16)
    trisb = cx.sbuf(tag + "trisb", [128, 128], BF16)
    cx.dma("pool", ident[:], cst_d[:, C_IDENT:C_IDENT + 128], writes=[K("ident")], skey="ident")
    for i in range(4):
        cx.dma("pool", id4[:, i, :], cst_d[:, C_IDENT:C_IDENT + 128], writes=[K("id4")], skey=("id4", i))
        cx.dma("pool", minc4[:, i, :], cst_d[:, C_MINC:C_MINC + 128], writes=[K("minc4")], skey=("minc4", i))
        cx.dma("pool", mlow4[:, i, :], cst_d[:, C_TRIS:C_TRIS + 128], writes=[K("mlow4")], skey=("mlow4", i))
    cx.dma("sp", cw[:], convw_d, writes=[K("cw")], skey="cw")
    cx.dma("sp", alog[:], alog_d, writes=[K("alog")], skey="alog")
    cx.dma("sp", dtb[:], dtb_d, writes=[K("dtb")], skey="dtb")
    w_in_r = w_in.rearrange("(kc p) n -> p kc n", p=128)
    cx.dma("pool", wab[:], w_in_r[:, :, 6144:6176], writes=[K("wab")], skey="wab")
    _chk(0)
    norm_stage(cx, c, tag, x_in, nw_d, hT, xkey_in, ps[6], pk[6])
    _chk(1)

    f2 = lambda t: t[:, :, :].rearrange("p c h -> p (c h)")
    for which in range(2):
        def mm(e, which=which):
            ins = None
            for ct in range(NTILE):
                for kc in range(KC):
                    ins = e.matmul(ps[which][:, ct * 16:(ct + 1) * 16], hT[:, kc, ct * 128:(ct + 1) * 128],
                                   wab[:, kc, which * 16:(which + 1) * 16], start=(kc == 0), stop=(kc == KC - 1))
            return ins
        cx.op("pe", mm, reads=[], writes=[pk[which]])
    _chk(1.05)
    cx.op("act", lambda e: e.activation(f2(beta), ps[0][:], AF.Sigmoid), reads=[pk[0]], writes=[K("beta")])
    _chk(1.1)
    cx.op("dve", lambda e: e.tensor_tensor(out=f2(gt), in0=ps[1][:], in1=dtb[:], op=ALU.add),
          reads=[pk[1], K("dtb")], writes=[K("gt")])
    _chk(1.2)
    cx.op("act", lambda e: e.activation(f2(gt), f2(gt), AF.Exp), reads=[K("gt")], writes=[K("gt")])
    _chk(1.4)
    cx.op("act", lambda e: e.activation(f2(gt), f2(gt), AF.Ln, bias=1.0), reads=[K("gt")], writes=[K("gt")])
    _chk(1.6)
    cx.op("act", lambda e: e.activation(alog[:], alog[:], AF.Exp), reads=[K("alog")], writes=[K("alog")])
    _chk(1.8)
    cx.op("dve", lambda e: e.scalar_tensor_tensor(out=f2(gt), in0=f2(gt), scalar=-1.0, in1=alog[:],
                                                   op0=ALU.mult, op1=ALU.mult),
          reads=[K("gt"), K("alog")], writes=[K("gt")])
    cx.op("dve", lambda e: e.tensor_scalar(f2(nbeta), f2(beta), -1.0, None, ALU.mult),
          reads=[K("beta")], writes=[K("nbeta")])
    _chk(2)

    cx.dma("pool", trib[:], cst_d[:, C_TRI:C_TRI + 128], writes=[K("trib")], skey="trib")
    cx.dma("pool", trisb[:], cst_d[:, C_TRIS:C_TRIS + 128], writes=[K("trisb")], skey="trisb")
    cx.op("dve", lambda e: e.tensor_copy(f2(ghi), f2(gt)), reads=[K("gt")], writes=[K("ghi")])
    cx.op("dve", lambda e: e.tensor_tensor(out=f2(glo), in0=f2(gt), in1=f2(ghi), op=ALU.subtract),
          reads=[K("gt"), K("ghi")], writes=[K("glo")])
    _chk(2.2)

    def mmg(e):
        ins = None
        for ct in range(NTILE):
            for dst, m in ((ps[2], trib), (ps[3], trisb), (ps[4], c["ones_bf"])):
                e.matmul(dst[:, ct * 16:(ct + 1) * 16], m[:], ghi[:, ct, :], start=True, stop=False)
                ins = e.matmul(dst[:, ct * 16:(ct + 1) * 16], m[:], glo[:, ct, :], start=False, stop=True)
        return ins
    cx.op("pe", mmg, reads=[K("ghi"), K("glo"), K("trib"), K("trisb"), "ones_bf"], writes=[pk[2], pk[3], pk[4]])
    _chk(2.4)
    cx.op("dve", lambda e: e.tensor_copy(f2(gchi), ps[2][:]), reads=[pk[2]], writes=[K("gchi")])
    cx.op("dve", lambda e: e.tensor_tensor(out=f2(gclo), in0=ps[2][:], in1=f2(gchi), op=ALU.subtract),
          reads=[pk[2], K("gchi")], writes=[K("gclo")])
    _chk(2.6)
    cx.op("dve", lambda e: e.tensor_copy(f2(gc), ps[2][:]), reads=[pk[2]], writes=[K("gc")])
    cx.op("act", lambda e: e.activation(f2(kbs), ps[2][:], AF.Exp), reads=[pk[2]], writes=[K("kbs")])
    _chk(2.7)
    cx.op("dve", lambda e: e.tensor_tensor(out=f2(kbs), in0=f2(kbs), in1=f2(beta), op=ALU.mult),
          reads=[K("kbs"), K("beta")], writes=[K("kbs")])
    _chk(2.8)
    cx.op("act", lambda e: e.activation(f2(egr), ps[3][:], AF.Exp), reads=[pk[3]], writes=[K("egr")])
    cx.op("act", lambda e: e.activation(f2(cd), ps[4][:], AF.Exp), reads=[pk[4]], writes=[K("cd")])
    _chk(2.9)
    cx.barrier()
    cx.pop()
    _chk(3)

    qT = cx.sbuf(tag + "qT", [128, 2, T], BF16)
    kT = cx.sbuf(tag + "kT", [128, 2, T], BF16)
    ktok = cx.sbuf(tag + "ktok", [128, NTILE, 2, 128], BF16)
    vtok = cx.sbuf(tag + "vtok", [128, NTILE, 4, 128], BF16)
    PW = 1024
    for hgp in range(DBG.get('hgps', 4)):
        cx.push()
        NB = 3
        wch = [cx.sbuf(tag + "wch%d" % i, [128, KC, 128], BF16) for i in range(2)]
        dg = [cx.sbuf(tag + "dg%d" % i, [128, 4, 128], BF16) for i in range(2)]
        xcb = [cx.sbuf(tag + "xcb%d" % i, [128, 4 + TT], BF16) for i in range(NB)]
        sil = [cx.sbuf(tag + "sil%d" % i, [128, TT], F32) for i in range(NB)]
        silb = [cx.sbuf(tag + "silb%d" % i, [128, TT], BF16) for i in range(NB)]
        sqb = [cx.sbuf(tag + "sqb%d" % i, [128, TT], BF16) for i in range(NB)]
        rn = [cx.sbuf(tag + "rn%d" % i, [128, TT], F32) for i in range(NB)]
        chunks = [("q", 0), ("q", 1), ("k", 0), ("k", 1), ("v", 0), ("v", 1), ("v", 2), ("v", 3)]

        def g1_piece(kind, li, gch, ws, tt, item):
            b = item % NB
            i2 = item % 2
            bP, bC, bS, bT = i2, 2 + i2, 4 + i2, 6 + i2
            XK = lambda i: (K("xcb"), i)
            if tt == 0:
                cx.op("act", lambda e: e.activation(xcb[b][:, 0:4], xcb[b][:, 0:4], AF.Copy, scale=0.0),
                      reads=[], writes=[(K("xh"), b)])

            def mm(e):
                ins = None
                for kc in range(KC):
                    ins = e.matmul(ps[bP][:], wch[ws][:, kc, :], hT[:, kc, tt * TT:(tt + 1) * TT],
                                   start=(kc == 0), stop=(kc == KC - 1))
                return ins
            cx.op("pe", mm, reads=[(K("wch"), ws)], writes=[pk[bP]])
            yield
            cx.op("act", lambda e: e.activation(xcb[b][:, 4:4 + TT], ps[bP][:], AF.Copy),
                  reads=[pk[bP]], writes=[XK(b)])
            if tt < NTT - 1:
                nb_ = (item + 1) % NB
                cx.op("act", lambda e: e.activation(xcb[nb_][:, 0:4], xcb[b][:, TT:TT + 4], AF.Copy),
                      reads=[XK(b)], writes=[(K("xh"), nb_)])
            yield "NEXT"

            def mmc(e):
                ins = None
                for j in range(4):
                    ins = e.matmul(ps[bC][:], dg[ws][:, j, :], xcb[b][:, 1 + j:1 + j + TT], start=(j == 0), stop=(j == 3))
                return ins
            cx.op("pe", mmc, reads=[XK(b), (K("xh"), b), (K("dg"), ws)], writes=[pk[bC]])
            yield
            tsl = slice(tt * TT, (tt + 1) * TT)
            if kind == "v":
                cx.op("act", lambda e: e.activation(silb[b][:], ps[bC][:], AF.Silu), reads=[pk[bC]], writes=[(K("silb"), b)])
                yield
            else:
                cx.op("act", lambda e: e.activation(sil[b][:], ps[bC][:], AF.Silu), reads=[pk[bC]], writes=[(K("sil"), b)])
                yield
                cx.op("dve", lambda e: e.tensor_tensor(out=sqb[b][:], in0=sil[b][:], in1=sil[b][:], op=ALU.mult),
                      reads=[(K("sil"), b)], writes=[(K("sqb"), b)])
                yield
                cx.op("pe", lambda e: e.matmul(ps[bS][:], c["ones_bf"][:], sqb[b][:], start=True, stop=True),
                      reads=[(K("sqb"), b), "ones_bf"], writes=[pk[bS]])
                yield
                cx.op("act", lambda e: e.activation(rn[b][:], ps[bS][:], AF.Sqrt, bias=1e-6, scale=1.0),
                      reads=[pk[bS]], writes=[(K("rn"), b)])
                yield
                cx.op("dve", lambda e: e.reciprocal(rn[b][:], rn[b][:]), reads=[(K("rn"), b)], writes=[(K("rn"), b)])
                dstT = qT if kind == "q" else kT
                sc = (128 ** -0.5) if kind == "q" else 1.0
                cx.op("dve", lambda e: e.scalar_tensor_tensor(
                    out=dstT[:, li, tsl], in0=sil[b][:], scalar=sc, in1=rn[b][:], op0=ALU.mult, op1=ALU.mult),
                    reads=[(K("sil"), b), (K("rn"), b)], writes=[(K(kind + "T"), li, tt)])
                yield
            if kind in ("k", "v"):
                ptv = ps[bT][:, :].bitcast(BF16)

                def mmt(e):
                    ins = None
                    for i in range(4):
                        if kind == "v":
                            src = silb[b][:, i * 128:(i + 1) * 128]
                        else:
                            src = kT[:, li, tt * TT + i * 128: tt * TT + (i + 1) * 128]
                        ins = e.transpose(ptv[:, i * 128:(i + 1) * 128], src, ident[:])
                    return ins
                rk = [(K("silb"), b)] if kind == "v" else [(K("kT"), li, tt)]
                cx.op("pe", mmt, reads=rk + [K("ident")], writes=[pk[bT]])
                yield
                dst = vtok if kind == "v" else ktok
                cx.op("act", lambda e: e.activation(
                    dst[:, tt * 4:tt * 4 + 4, li, :], ptv[:, 0:512].rearrange("p (i x) -> p i x", i=4), AF.Copy),
                    reads=[pk[bT]], writes=[(K(kind + "tok"), li, tt)])
                yield

        items = []
        wcnt = 0
        for kind, li in chunks:
            gch = {"q": 2 * hgp + li, "k": 8 + 2 * hgp + li, "v": 16 + 4 * hgp + li}[kind]
            ws = wcnt % 2
            wcnt += 1
            for tt in range(NTT):
                items.append((kind, li, gch, ws, tt))
        active, nxt_i = [], [0]

        def start1():
            while nxt_i[0] < len(items) and len(active) < NB:
                kind, li, gch, ws, tt = items[nxt_i[0]]
                if tt == 0:
                    cx.dma("pool", wch[ws][:], w_in_r[:, :, gch * 128:(gch + 1) * 128], writes=[(K("wch"), ws)],
                           skey=("wch", ws))
                    for j in range(4):
                        cx.op("dve", lambda e, ws=ws, gch=gch, j=j: e.tensor_scalar(
                            dg[ws][:, j, :], ident[:], cw[:, gch, j:j + 1], None, ALU.mult),
                            reads=[K("ident"), K("cw")], writes=[(K("dg"), ws)])
                active.append(g1_piece(kind, li, gch, ws, tt, nxt_i[0]))
                nxt_i[0] += 1
                return
        start1()
        while active:
            for g in list(active):
                try:
                    v = next(g)
                except StopIteration:
                    active.remove(g)
                    start1()
                    continue
                if v == "NEXT" and g is active[-1]:
                    start1()
        cx.barrier()
        cx.pop()
        _chk(5)

        cx.push()
        mk = lambda nm, dt_, n=2: [cx.sbuf(tag + nm + "%d" % i, [128, 4, 128], dt_) for i in range(n)]
        Abuf0, Abuf1, Atbuf0, Atbuf1 = mk("Aa", BF16), mk("Ab", BF16), mk("Ata", BF16), mk("Atb", BF16)
        A_O, AtO, Tt = mk("A_O", BF16), mk("AtO", BF16), mk("Tt", BF16)
        qkT, qdT, kd, vb, kb = mk("qkT", BF16), mk("qdT", BF16), mk("kd", BF16), mk("vb", BF16), mk("kb", BF16)
        dmT = cx.sbuf(tag + "dmT", [128, 4, 128], F32)
        dm2 = cx.sbuf(tag + "dm2", [128, 4, 128], F32)
        eB = cx.sbuf(tag + "eB", [128, 4, 128], F32)
        rr = cx.sbuf(tag + "rr", [128, 4, 128], F32)
        u = cx.sbuf(tag + "u", [128, 4, 128], F32)
        wT = cx.sbuf(tag + "wT", [128, 4, 128], BF16)
        vnew = cx.sbuf(tag + "vnew", [128, 4, 128], BF16)
        S32 = cx.sbuf(tag + "S32", [128, 4, 128], F32)
        S16 = cx.sbuf(tag + "S16", [128, 4, 128], BF16)
        ost = [cx.sbuf(tag + "ost%d" % i, [128, 4, 128], F32) for i in range(2)]
        fl = lambda t: t[:, :, :].rearrange("p h x -> p (h x)")
        cx.op("dve", lambda e: e.memset(fl(S32), 0.0), writes=[K("S32")])
        cx.op("dve", lambda e: e.memset(fl(S16), 0.0), writes=[K("S16")])
        hs = [4 * hgp + hl for hl in range(4)]
        B0, B1 = 0, 1

        def g2_tile(ct, hs, hgp=hgp):
            par = ct % 2
            bA, bAt, bTu = 2 + 3 * par, 3 + 3 * par, 4 + 3 * par
            tsl = slice(ct * 128, (ct + 1) * 128)
            P = lambda n: (K(n), par)
            AO, AtOp, Ttp = A_O[par], AtO[par], Tt[par]
            bufs = [(Abuf0[par], Atbuf0[par]), (Abuf1[par], Atbuf1[par]), (AO, AtOp)]

            def mm1(e):
                for hl in range(4):
                    e.matmul(ps[B0][:, hl * 128:(hl + 1) * 128], gchi[:, ct, hs[hl]:hs[hl] + 1].to_broadcast([128, 128]),
                             ident[:], start=True, stop=False)
                    e.matmul(ps[B0][:, hl * 128:(hl + 1) * 128], gclo[:, ct, hs[hl]:hs[hl] + 1].to_broadcast([128, 128]),
                             ident[:], start=False, stop=True)
                ins = None
                for khl in range(2):
                    e.matmul(ps[B1][:, khl * 128:(khl + 1) * 128], kT[:, khl, tsl], kT[:, khl, tsl], start=True, stop=True)
                    ins = e.matmul(ps[B1][:, (2 + khl) * 128:(3 + khl) * 128], kT[:, khl, tsl], qT[:, khl, tsl],
                                   start=True, stop=True)
                return ins
            cx.op("pe", mm1, reads=[K("ident")], writes=[pk[B0], pk[B1]])
            yield
            for hl in range(4):
                h = hs[hl]
                cx.op("dve", lambda e, hl=hl, h=h: e.tensor_scalar(
                    dmT[:, hl, :], ps[B0][:, hl * 128:(hl + 1) * 128], gc[:, ct, h:h + 1], 0.0, ALU.subtract, ALU.min),
                    reads=[pk[B0]], writes=[K("dmT")])
                cx.op("dve", lambda e, hl=hl, h=h: e.tensor_scalar(
                    dm2[:, hl, :], ps[B0][:, hl * 128:(hl + 1) * 128], gc[:, ct, h:h + 1], 0.0, ALU.subtract, ALU.max),
                    reads=[pk[B0]], writes=[K("dm2")])
                yield
            cx.op("act", lambda e: e.activation(fl(eB), ps[B0][:], AF.Exp), reads=[pk[B0]], writes=[K("eB")])
            cx.op("act", lambda e: e.activation(fl(dmT), fl(dmT), AF.Exp), reads=[K("dmT")], writes=[K("dmT")])
            cx.op("act", lambda e: e.activation(fl(dm2), fl(dm2), AF.Exp, scale=-1.0), reads=[K("dm2")], writes=[K("dm2")])
            yield
            cx.op("dve", lambda e: e.tensor_tensor(out=fl(dmT), in0=fl(dmT), in1=fl(minc4), op=ALU.mult),
                  reads=[K("dmT")], writes=[K("dmT")])
            cx.op("dve", lambda e: e.tensor_tensor(out=fl(dm2), in0=fl(dm2), in1=fl(mlow4), op=ALU.mult),
                  reads=[K("dm2")], writes=[K("dm2")])
            yield
            for hl in range(4):
                h = hs[hl]
                khl = hl // 2
                cx.op("dve", lambda e, hl=hl, h=h, khl=khl: e.scalar_tensor_tensor(
                    out=AO[:, hl, :], in0=ps[B1][:, khl * 128:(khl + 1) * 128], scalar=nbeta[:, ct, h:h + 1],
                    in1=dm2[:, hl, :], op0=ALU.mult, op1=ALU.mult), reads=[pk[B1], K("dm2")], writes=[(P("A"), 2)])
                cx.op("dve", lambda e, hl=hl, khl=khl: e.tensor_tensor(
                    out=qkT[par][:, hl, :], in0=ps[B1][:, (2 + khl) * 128:(3 + khl) * 128], in1=dmT[:, hl, :], op=ALU.mult),
                    reads=[pk[B1], K("dmT")], writes=[P("qkT")])
                cx.op("dve", lambda e, hl=hl, khl=khl: e.tensor_tensor(
                    out=qdT[par][:, hl, :], in0=qT[:, khl, tsl], in1=eB[:, hl, :], op=ALU.mult),
                    reads=[K("eB")], writes=[P("qdT")])
                cx.op("act", lambda e, hl=hl, h=h: e.activation(
                    vb[par][:, hl, :], vtok[:, ct, hl, :], AF.Copy, scale=beta[:, ct, h:h + 1]), writes=[P("vb")])
                cx.op("act", lambda e, hl=hl, h=h, khl=khl: e.activation(
                    kb[par][:, hl, :], ktok[:, ct, khl, :], AF.Copy, scale=kbs[:, ct, h:h + 1]), writes=[P("kb")])
                cx.op("act", lambda e, hl=hl, h=h, khl=khl: e.activation(
                    kd[par][:, hl, :], ktok[:, ct, khl, :], AF.Copy, scale=egr[:, ct, h:h + 1]), writes=[P("kd")])
                yield
            ptv = ps[bA][:, :].bitcast(BF16)

            def mmT(e):
                ins = None
                for hl in range(4):
                    ins = e.transpose(ptv[:, hl * 128:(hl + 1) * 128], AO[:, hl, :], ident[:])
                return ins
            cx.op("pe", mmT, reads=[(P("A"), 2), K("ident")], writes=[pk[bA]])
            yield
            cx.op("act", lambda e: e.activation(fl(AtOp), ptv[:, 0:512], AF.Copy),
                  reads=[pk[bA]], writes=[(P("At"), 2)])
            yield
            cx.op("dve", lambda e: e.tensor_tensor(out=fl(Ttp), in0=fl(AtOp), in1=fl(id4), op=ALU.add),
                  reads=[(P("At"), 2), K("id4")], writes=[P("Tt")])
            yield
            cur = 2
            NSTEP = 6
            for p in range(1, NSTEP + 1):
                nxt = 0 if cur == 2 else 1 - cur
                A, At = bufs[cur]
                An, Atn = bufs[nxt]
                last = (p == NSTEP)

                def mmsq(e, A=A, At=At, last=last):
                    ins = None
                    for hl in range(4):
                        ins = e.matmul(ps[bA][:, hl * 128:(hl + 1) * 128], At[:, hl, :], A[:, hl, :], start=True, stop=True)
                    if not last:
                        for hl in range(4):
                            ins = e.matmul(ps[bAt][:, hl * 128:(hl + 1) * 128], A[:, hl, :], At[:, hl, :],
                                           start=True, stop=True)
                    return ins
                cx.op("pe", mmsq, reads=[(P("A"), cur), (P("At"), cur)], writes=[pk[bA]] + ([] if last else [pk[bAt]]))
                yield
                cx.op("act", lambda e, An=An: e.activation(fl(An), ps[bA][:], AF.Copy),
                      reads=[pk[bA]], writes=[(P("A"), nxt)])
                if not last:
                    cx.op("act", lambda e, Atn=Atn: e.activation(fl(Atn), ps[bAt][:], AF.Copy),
                          reads=[pk[bAt]], writes=[(P("At"), nxt)])
                yield

                def mmtu(e, An=An):
                    ins = None
                    for hl in range(4):
                        ins = e.matmul(ps[bTu][:, hl * 128:(hl + 1) * 128], An[:, hl, :], Ttp[:, hl, :], start=True, stop=True)
                    return ins
                cx.op("pe", mmtu, reads=[(P("A"), nxt), P("Tt")], writes=[pk[bTu]])
                yield
                cx.op("dve", lambda e: e.tensor_tensor(out=fl(Ttp), in0=fl(Ttp), in1=ps[bTu][:], op=ALU.add),
                      reads=[pk[bTu], P("Tt")], writes=[P("Tt")])
                cur = nxt
                if p == 3:
                    yield "MID"
                else:
                    yield
            T0, Rb = Atbuf0[par], Abuf0[par]
            ptv3 = ps[bAt][:, :].bitcast(BF16)

            def mmT0(e):
                ins = None
                for hl in range(4):
                    ins = e.transpose(ptv3[:, hl * 128:(hl + 1) * 128], Ttp[:, hl, :], ident[:])
                return ins
            cx.op("pe", mmT0, reads=[P("Tt"), K("ident")], writes=[pk[bAt]])
            yield
            cx.op("act", lambda e: e.activation(fl(T0), ptv3[:, 0:512], AF.Copy),
                  reads=[pk[bAt]], writes=[(P("At"), 0)])
            yield

            def mmR(e):
                ins = None
                for hl in range(4):
                    ins = e.matmul(ps[bA][:, hl * 128:(hl + 1) * 128], AtOp[:, hl, :], T0[:, hl, :], start=True, stop=True)
                return ins
            cx.op("pe", mmR, reads=[(P("At"), 2), (P("At"), 0)], writes=[pk[bA]])
            yield
            cx.op("dve", lambda e: e.scalar_tensor_tensor(out=fl(rr), in0=fl(T0), scalar=-1.0, in1=ps[bA][:],
                                                           op0=ALU.mult, op1=ALU.add),
                  reads=[(P("At"), 0), pk[bA]], writes=[K("rr")])
            cx.op("dve", lambda e: e.tensor_tensor(out=fl(Rb), in0=fl(rr), in1=fl(id4), op=ALU.add),
                  reads=[K("rr"), K("id4")], writes=[(P("A"), 0)])
            yield

            def mmU(e):
                ins = None
                for hl in range(4):
                    ins = e.matmul(ps[bTu][:, hl * 128:(hl + 1) * 128], Rb[:, hl, :], Ttp[:, hl, :], start=True, stop=True)
                return ins
            cx.op("pe", mmU, reads=[(P("A"), 0), P("Tt")], writes=[pk[bTu]])
            yield
            cx.op("dve", lambda e: e.tensor_tensor(out=fl(Ttp), in0=fl(Ttp), in1=ps[bTu][:], op=ALU.add),
                  reads=[pk[bTu], P("Tt")], writes=[P("Tt")])
            yield

            def mmuw(e):
                ins = None
                for hl in range(4):
                    e.matmul(ps[bA][:, hl * 128:(hl + 1) * 128], Ttp[:, hl, :], vb[par][:, hl, :], start=True, stop=True)
                    ins = e.matmul(ps[bAt][:, hl * 128:(hl + 1) * 128], kb[par][:, hl, :], Ttp[:, hl, :], start=True, stop=True)
                return ins
            cx.op("pe", mmuw, reads=[P("Tt"), P("vb"), P("kb")], writes=[pk[bA], pk[bAt]])
            yield
            cx.op("act", lambda e: e.activation(fl(u), ps[bA][:], AF.Copy), reads=[pk[bA]], writes=[K("u")])
            cx.op("act", lambda e: e.activation(fl(wT), ps[bAt][:], AF.Copy), reads=[pk[bAt]], writes=[K("wT")])
            yield

            def mmvn(e):
                ins = None
                for hl in range(4):
                    ins = e.matmul(ps[bTu][:, hl * 128:(hl + 1) * 128], wT[:, hl, :], S16[:, hl, :], start=True, stop=True)
                return ins
            cx.op("pe", mmvn, reads=[K("wT"), K("S16")], writes=[pk[bTu]])
            yield
            cx.op("dve", lambda e: e.tensor_tensor(out=fl(vnew), in0=fl(u), in1=ps[bTu][:], op=ALU.subtract),
                  reads=[K("u"), pk[bTu]], writes=[K("vnew")])
            yield

            def mmo(e):
                ins = None
                for hl in range(4):
                    e.matmul(ps[bA][:, hl * 128:(hl + 1) * 128], S16[:, hl, :], qdT[par][:, hl, :], start=True, stop=False)
                    e.matmul(ps[bA][:, hl * 128:(hl + 1) * 128], vnew[:, hl, :], qkT[par][:, hl, :], start=False, stop=True)
                for hl in range(4):
                    ins = e.matmul(ps[bAt][:, hl * 128:(hl + 1) * 128], kd[par][:, hl, :], vnew[:, hl, :], start=True, stop=True)
                return ins
            cx.op("pe", mmo, reads=[K("S16"), P("qdT"), K("vnew"), P("qkT"), P("kd")], writes=[pk[bA], pk[bAt]])
            yield
            ob = ct % 2
            for hl in range(4):
                h = hs[hl]
                cx.op("dve", lambda e, hl=hl, h=h: e.scalar_tensor_tensor(
                    out=S32[:, hl, :], in0=S32[:, hl, :], scalar=cd[:, ct, h:h + 1], in1=ps[bAt][:, hl * 128:(hl + 1) * 128],
                    op0=ALU.mult, op1=ALU.add), reads=[pk[bAt], K("S32")], writes=[K("S32")])
            cx.op("act", lambda e: e.activation(fl(S16), fl(S32), AF.Copy), reads=[K("S32")], writes=[K("S16")])
            cx.op("act", lambda e: e.activation(fl(ost[ob]), ps[bA][:], AF.Copy),
                  reads=[pk[bA]], writes=[(K("ost"), ob)])
            for hl in range(4):
                h = hs[hl]
                cx.dma("sp", oraw_d[h * 128:(h + 1) * 128, ct * 128:(ct + 1) * 128], ost[ob][:, hl, :],
                       reads=[(K("ost"), ob)], writes=[(okey, h, ct)], skey=("ost", ob, hl))
            yield

        ntile = DBG.get('tiles', NTILE)
        active, nxt_t = [], [0]

        def start():
            if nxt_t[0] < ntile and len(active) < 2:
                active.append(g2_tile(nxt_t[0], list(hs)))
                nxt_t[0] += 1
        start()
        while active:
            for g in list(active):
                try:
                    v = next(g)
                except StopIteration:
                    active.remove(g)
                    start()
                    continue
                if v == "MID":
                    start()
        cx.barrier()
        cx.pop()
    cx.barrier()
    cx.pop()


def gdn_out_phase(cx, c, tag, x_in, x_out, oraw_d, nw_d, gnw_d, w_in, w_out, xkey_in, xkey_out, okey):
    cx.push()
    HT = 2048
    NT = HT // TT
    WB = 256
    nheads = NVH
    K = lambda n: tag + n
    hT = cx.sbuf(tag + "hT", [128, KC, HT], BF16)
    osb = cx.sbuf(tag + "osb", [128, nheads, HT], BF16)
    gnw = cx.sbuf(tag + "gnw", [128, 1], F32)
    nw = cx.sbuf(tag + "nw", [128, KC], F32)
    xt = [cx.sbuf(tag + "xt%d" % i, [128, KC, TT], F32) for i in range(1)]
    sq = cx.sbuf(tag + "sq", [128, KC, TT], BF16)
    rstd = cx.sbuf(tag + "rstd", [128, TT], F32)
    wz = [cx.sbuf(tag + "wz%d" % i, [128, KC, 128], BF16) for i in range(2)]
    orw = [cx.sbuf(tag + "orw%d" % i, [128, TT], F32) for i in range(2)]
    osq = [cx.sbuf(tag + "osq%d" % i, [128, TT], BF16) for i in range(2)]
    orn = [cx.sbuf(tag + "orn%d" % i, [128, TT], F32) for i in range(2)]
    sz = [cx.sbuf(tag + "sz%d" % i, [128, TT], F32) for i in range(2)]
    wo = [cx.sbuf(tag + "wo%d" % i, [128, nheads, WB], BF16) for i in range(2)]
    xr = [cx.sbuf(tag + "xr%d" % i, [128, TT], F32) for i in range(2)]
    xn = [cx.sbuf(tag + "xn%d" % i, [128, TT], F32) for i in range(2)]
    ps = [cx.psum(tag + "ps%d" % i, [128, TT], F32) for i in range(7)]
    pk = [tag + "ps%d" % i for i in range(7)]
    cx.psum_keys.update(pk)
    w_in_r = w_in.rearrange("(kc p) n -> p kc n", p=128)
    w_out_r = w_out.rearrange("(kc p) n -> p kc n", p=128)
    x_in_r = x_in.rearrange("(kc p) t -> p kc t", p=128)
    cx.dma("sp", nw[:], nw_d, writes=[K("nw")], skey="nw")
    cx.dma("sp", gnw[:], gnw_d, writes=[K("gnw")], skey="gnw")
    cnt = {"o": 0, "x": 0, "z": 0, "w": 0}
    for hf in range(T // HT):
        t0 = hf * HT
        for tt in range(NT):
            gt_ = (t0 + tt * TT) // TT
            cx.dma("sp", xt[0][:], x_in_r[:, :, t0 + tt * TT: t0 + (tt + 1) * TT],
                   reads=[(xkey_in, gt_, kc) for kc in range(KC)], writes=[K("xt")], skey="xt")
            rmsnorm_tile(cx, c, xt[0], nw, lambda kc, tt=tt: hT[:, kc, tt * TT:(tt + 1) * TT], tag,
                         [K("xt")], (K("hT"), tt), pk[6], ps[6], {"sq": sq, "rstd": rstd})
        for h in range(nheads):
            ws = cnt["w"] % 2
            cnt["w"] += 1
            cx.dma("pool", wz[ws][:], w_in_r[:, :, 4096 + h * 128: 4096 + (h + 1) * 128], writes=[(K("wz"), ws)],
                   skey=("wz", ws))
            for tt in range(NT):
                gt_ = (t0 + tt * TT) // TT
                b = cnt["z"] % 2
                cnt["z"] += 1
                cx.dma("sp", orw[b][:], oraw_d[h * 128:(h + 1) * 128, t0 + tt * TT: t0 + (tt + 1) * TT],
                       reads=[(okey, h, gt_)], writes=[(K("orw"), b)], skey=("orw", b))

                def mmz(e, ws=ws, tt=tt, b=b):
                    ins = None
                    for kc in range(KC):
                        ins = e.matmul(ps[b][:], wz[ws][:, kc, :], hT[:, kc, tt * TT:(tt + 1) * TT],
                                       start=(kc == 0), stop=(kc == KC - 1))
                    return ins
                cx.op("pe", mmz, reads=[(K("wz"), ws), (K("hT"), tt)], writes=[pk[b]])
                cx.op("act", lambda e, b=b: e.activation(osq[b][:], orw[b][:], AF.Square),
                      reads=[(K("orw"), b)], writes=[(K("osq"), b)])
                cx.op("pe", lambda e, b=b: e.matmul(ps[2 + b][:], c["ones_bf"][:], osq[b][:], start=True, stop=True),
                      reads=[(K("osq"), b), "ones_bf"], writes=[pk[2 + b]])
                cx.op("act", lambda e, b=b: e.activation(orn[b][:], ps[2 + b][:], AF.Sqrt, bias=EPS, scale=1.0 / 128),
                      reads=[pk[2 + b]], writes=[(K("orn"), b)])
                cx.op("dve", lambda e, b=b: e.reciprocal(orn[b][:], orn[b][:]), reads=[(K("orn"), b)],
                      writes=[(K("orn"), b)])
                cx.op("act", lambda e, b=b: e.activation(sz[b][:], ps[b][:], AF.Silu), reads=[pk[b]],
                      writes=[(K("sz"), b)])
                cx.op("dve", lambda e, b=b: e.scalar_tensor_tensor(
                    out=orn[b][:], in0=orw[b][:], scalar=gnw[:, 0:1], in1=orn[b][:], op0=ALU.mult, op1=ALU.mult),
                    reads=[(K("orw"), b), (K("orn"), b), K("gnw")], writes=[(K("orn"), b)])
                cx.op("dve", lambda e, b=b, h=h, tt=tt: e.tensor_tensor(
                    out=osb[:, h, tt * TT:(tt + 1) * TT], in0=orn[b][:], in1=sz[b][:], op=ALU.mult),
                    reads=[(K("orn"), b), (K("sz"), b)], writes=[(K("osb"), h)])
        for mb in range(D // WB):
            s = cnt["o"] % 2
            cnt["o"] += 1
            cx.dma("pool", wo[s][:], w_out_r[:, :, mb * WB:(mb + 1) * WB], writes=[(K("wo"), s)], skey=("wo", s))
            for mm_ in range(WB // 128):
                mc = mb * (WB // 128) + mm_
                for tt in range(NT):
                    gt_ = (t0 + tt * TT) // TT
                    b = cnt["x"] % 2
                    cnt["x"] += 1
                    py = ps[4 + b]
                    cx.dma("sp", xr[b][:], x_in[mc * 128:(mc + 1) * 128, t0 + tt * TT: t0 + (tt + 1) * TT],
                           reads=[(xkey_in, gt_, mc)], writes=[(K("xr"), b)], skey=("xr", b))

                    def mm2(e, s=s, mm_=mm_, tt=tt, py=py):
                        ins = None
                        for kc in range(nheads):
                            ins = e.matmul(py[:], wo[s][:, kc, mm_ * 128:(mm_ + 1) * 128],
                                           osb[:, kc, tt * TT:(tt + 1) * TT], start=(kc == 0), stop=(kc == nheads - 1))
                        return ins
                    cx.op("pe", mm2, reads=[(K("wo"), s)] + [(K("osb"), hd) for hd in range(nheads)],
                          writes=[pk[4 + b]])
                    cx.op("dve", lambda e, b=b, py=py: e.tensor_tensor(
                        out=xn[b][:], in0=py[:], in1=xr[b][:], op=ALU.add),
                        reads=[pk[4 + b], (K("xr"), b)], writes=[(K("xn"), b)])
                    cx.dma("sp", x_out[mc * 128:(mc + 1) * 128, t0 + tt * TT: t0 + (tt + 1) * TT], xn[b][:],
                           reads=[(K("xn"), b)], writes=[(xkey_out, gt_, mc)], skey=("xn", b))
    cx.barrier()
    cx.pop()


def build_gdn_prog():
    nc = bass.Bass("TRN2", target_bir_lowering=False)
    x = nc.dram_tensor("x", [D, T], F32, kind="ExternalInput").ap()
    nw = nc.dram_tensor("nw", [128, KC], F32, kind="ExternalInput").ap()
    w_in = nc.dram_tensor("w_in", [D, GP], F32, kind="ExternalInput").ap()
    w_out = nc.dram_tensor("w_out", [NVH * 128, D], F32, kind="ExternalInput").ap()
    convw = nc.dram_tensor("convw", [128, 32, 4], F32, kind="ExternalInput").ap()
    alog = nc.dram_tensor("alog", [128, 512], F32, kind="ExternalInput").ap()
    dtb = nc.dram_tensor("dtb", [128, 512], F32, kind="ExternalInput").ap()
    gnw = nc.dram_tensor("gnw", [128, 1], F32, kind="ExternalInput").ap()
    cst = nc.dram_tensor("cst", [128, CST_W], F32, kind="ExternalInput").ap()
    oraw = nc.dram_tensor("oraw_scratch", [NVH * 128, T], F32, kind=("ExternalOutput" if DBG.get("noout") else "Internal")).ap()
    y = nc.dram_tensor("y", [D, T], F32, kind="ExternalOutput").ap()
    cx = Ctx(nc)
    c = consts(cx)
    gdn_core_phase(cx, c, "g_", x, oraw, nw, w_in, convw, alog, dtb, cst, "xin", "or")
    if not DBG.get("noout"):
        gdn_out_phase(cx, c, "go_", x, y, oraw, nw, gnw, w_in, w_out, "xin", "xout", "or")
    cx.finish()
    return nc


def final_norm_phase(cx, c, tag, x_in, y_out, nw_d, xkey_in):
    cx.push()
    xt = [cx.sbuf(tag + "xt%d" % i, [128, KC, TT], F32) for i in range(2)]
    yo = [cx.sbuf(tag + "yo%d" % i, [128, KC, TT], F32) for i in range(2)]
    sq = cx.sbuf(tag + "sq", [128, KC, TT], BF16)
    rstd = cx.sbuf(tag + "rstd", [128, TT], F32)
    nw = cx.sbuf(tag + "nw", [128, KC], F32)
    ps = cx.psum(tag + "ps", [128, TT], F32)
    cx.psum_keys.add(tag + "ps")
    cx.dma("sp", nw[:], nw_d, writes=[tag + "nw"], skey="nw")
    x_in_r = x_in.rearrange("(kc p) t -> p kc t", p=128)
    y_out_r = y_out.rearrange("(kc p) t -> p kc t", p=128)
    for tt in range(NTT):
        b = tt % 2
        cx.dma("sp", xt[b][:], x_in_r[:, :, tt * TT:(tt + 1) * TT],
               reads=[(xkey_in, tt, kc) for kc in range(KC)], writes=[(tag + "xt", b)], skey=("xt", b))
        rmsnorm_tile(cx, c, xt[b], nw, lambda kc, b=b: yo[b][:, kc, :], tag,
                     [(tag + "xt", b)], (tag + "yo", b), tag + "ps", ps, {"sq": sq, "rstd": rstd})
        cx.dma("sp", y_out_r[:, :, tt * TT:(tt + 1) * TT], yo[b][:], reads=[(tag + "yo", b)],
               writes=[("yout", tt)], skey=("yo", b))
    cx.barrier()
    cx.pop()


def build_fin_prog():
    nc = bass.Bass("TRN2", target_bir_lowering=False)
    x = nc.dram_tensor("x", [D, T], F32, kind="ExternalInput").ap()
    nw = nc.dram_tensor("nw", [128, KC], F32, kind="ExternalInput").ap()
    y = nc.dram_tensor("y", [D, T], F32, kind="ExternalOutput").ap()
    cx = Ctx(nc)
    c = consts(cx)
    final_norm_phase(cx, c, "n_", x, y, nw, "xin")
    cx.finish()
    return nc


def build_full_prog():
    nc = bass.Bass("TRN2", target_bir_lowering=False)
    dt = lambda name, shape, kind="ExternalInput", dtype=F32: nc.dram_tensor(name, list(shape), dtype, kind=kind).ap()
    x = dt("x", [D, T])
    nws = dt("nws", [13, 128, KC])
    ffn_w_in = dt("ffn_w_in", [4, 2, D, 2 * DFF])
    ffn_w_out = dt("ffn_w_out", [4, 2, DFF, D])
    attn_w_in = dt("attn_w_in", [2, D, 3 * NH_A * 128])
    attn_w_out = dt("attn_w_out", [2, NH_A * 128, D])
    gdn_w_in = dt("gdn_w_in", [2, D, GP])
    gdn_w_out = dt("gdn_w_out", [2, NVH * 128, D])
    convw = dt("convw", [2, 128, 32, 4])
    alog = dt("alog", [2, 128, 512])
    dtb = dt("dtb", [2, 128, 512])
    gnw = dt("gnw", [2, 128, 1])
    ctab = dt("ctab", [128, T])
    stab = dt("stab", [128, T])
    cst = dt("cst", [128, CST_W])
    y = dt("y", [D, T], kind="ExternalOutput")
    xbuf = [nc.dram_tensor("xres%d" % i, [D, T], F32).ap() for i in range(2)]
    os_d = nc.dram_tensor("os_scratch", [NH_A * 128, T], BF16).ap()
    oraw = nc.dram_tensor("oraw_scratch", [NVH * 128, T], F32).ap()
    cx = Ctx(nc)
    c = consts(cx)
    cur, nb_, ph = x, 0, 0
    ia = ib = 0
    for i in range(4):
        ffn_phase(cx, c, "f%da_" % i, cur, xbuf[nb_], nws[ph], ffn_w_in[i, 0], ffn_w_out[i, 0], "xi", "xo")
        cur, nb_, ph = xbuf[nb_], 1 - nb_, ph + 1
        if i % 2 == 0:
            attn_phase(cx, c, "a%d_" % i, cur, os_d, nws[ph], attn_w_in[ia], ctab, stab, cst, "xi", "os")
            outproj_phase(cx, c, "ao%d_" % i, cur, xbuf[nb_], os_d, attn_w_out[ia], NH_A, "xi", "xo", "os")
            ia += 1
        else:
            gdn_core_phase(cx, c, "g%d_" % i, cur, oraw, nws[ph], gdn_w_in[ib], convw[ib], alog[ib], dtb[ib], cst, "xi", "or")
            gdn_out_phase(cx, c, "go%d_" % i, cur, xbuf[nb_], oraw, nws[ph], gnw[ib], gdn_w_in[ib], gdn_w_out[ib],
                          "xi", "xo", "or")
            ib += 1
        cur, nb_, ph = xbuf[nb_], 1 - nb_, ph + 1
        ffn_phase(cx, c, "f%db_" % i, cur, xbuf[nb_], nws[ph], ffn_w_in[i, 1], ffn_w_out[i, 1], "xi", "xo")
        cur, nb_, ph = xbuf[nb_], 1 - nb_, ph + 1
    final_norm_phase(cx, c, "n_", cur, y, nws[ph], "xi")
    cx.finish()
    return nc


_PROGS = {}


def _prog(name):
    if name not in _PROGS:
        _PROGS[name] = {"ffn": build_ffn_prog, "attn": build_attn_prog, "gdn": build_gdn_prog,
                        "fin": build_fin_prog}[name]()
    return _PROGS[name]


def _lay_nw(v):
    return np.ascontiguousarray(np.asarray(v, np.float32).reshape(KC, 128).T)


def _launch(name, xs, shared):
    nc = _prog(name)
    n = len(xs)
    in_maps = [dict(shared, x=xs[i]) for i in range(n)]
    res = run_bass_kernel_spmd(nc, in_maps, core_ids=list(range(n)))
    return [np.asarray(res.results[i]["y"]) for i in range(n)]


def kernel(x, norm_w, ffn_w_in, ffn_w_out, attn_w_in, attn_w_out, gdn_w_in, gdn_conv_w, gdn_a_log,
           gdn_dt_bias, gdn_norm_w, gdn_w_out, final_norm_w):
    f = lambda a: np.ascontiguousarray(np.asarray(a, np.float32))
    x = f(x)
    B = x.shape[0]
    ctab, stab = host_rope()
    nws = np.stack([_lay_nw(norm_w[i, j]) for i in range(4) for j in range(3)] + [_lay_nw(final_norm_w)], 0)
    shared = {
        "nws": np.ascontiguousarray(nws), "ffn_w_in": f(ffn_w_in), "ffn_w_out": f(ffn_w_out),
        "attn_w_in": f(attn_w_in), "attn_w_out": f(attn_w_out), "gdn_w_in": f(gdn_w_in), "gdn_w_out": f(gdn_w_out),
        "convw": np.ascontiguousarray(f(gdn_conv_w).reshape(2, 4, 32, 128).transpose(0, 3, 2, 1)),
        "alog": np.ascontiguousarray(np.tile(f(gdn_a_log)[:, None, :], (1, 128, 32))),
        "dtb": np.ascontiguousarray(np.tile(f(gdn_dt_bias)[:, None, :], (1, 128, 32))),
        "gnw": np.ascontiguousarray(f(gdn_norm_w).reshape(2, 128, 1)),
        "ctab": ctab, "stab": stab, "cst": host_consts(),
    }
    if "full" not in _PROGS:
        _PROGS["full"] = build_full_prog()
    nc = _PROGS["full"]
    in_maps = [dict(shared, x=np.ascontiguousarray(x[b].T)) for b in range(B)]
    res = run_bass_kernel_spmd(nc, in_maps, core_ids=list(range(B)))
    return np.stack([np.asarray(res.results[b]["y"]).T for b in range(B)], 0).astype(np.float32)
```

```python
from contextlib import ExitStack
import numpy as np
import concourse.bass as bass
import concourse.mybir as mybir
from concourse.bass_utils import run_bass_kernel_spmd

F32 = mybir.dt.float32
BF16 = mybir.dt.bfloat16
ALU = mybir.AluOpType
AF = mybir.ActivationFunctionType

ENGS = ("pe", "act", "dve", "pool", "sp")
NAMES = []


class _Op:
    __slots__ = ("eng", "fn", "deps", "idx", "is_dma", "dsem", "dval", "need_inc", "cnt")

    def __init__(self, eng, fn, is_dma):
        self.eng = eng
        self.fn = fn
        self.deps = []
        self.is_dma = is_dma
        self.dsem = None
        self.dval = 0
        self.need_inc = False
        self.cnt = 0


class Ctx:
    def __init__(self, nc):
        self.nc = nc
        self.q = {e: [] for e in ENGS}
        self.wr = {}
        self.rd = {}
        self.dsems = {}
        self.pending_dma = []
        self.sem_pool = {}
        self.psum_keys = set()
        self.stack = ExitStack()
        self.scopes = []
        self.esem = {e: nc.alloc_semaphore("sem_" + e) for e in ENGS}
        self.n_psum = 0

    def push(self):
        st = ExitStack()
        self.scopes.append(st)

    def pop(self):
        self.scopes.pop().close()

    def _scope(self):
        return self.scopes[-1] if self.scopes else self.stack

    def sbuf(self, name, shape, dtype):
        self.n_psum += 1
        NAMES.append("%s_%d" % (name, self.n_psum))
        return self._scope().enter_context(self.nc.sbuf_tensor("%s_%d" % (name, self.n_psum), list(shape), dtype))

    def psum(self, name, shape, dtype=F32):
        self.n_psum += 1
        return self._scope().enter_context(self.nc.psum_tensor("%s_%d" % (name, self.n_psum), list(shape), dtype))

    def _track(self, op, reads, writes):
        deps = op.deps
        for k in reads:
            for o in self.wr.get(k, {}).values():
                deps.append(o)
            if k in self.psum_keys:
                for ek2, o in self.rd.get(k, {}).items():
                    if o.eng != op.eng:
                        deps.append(o)
        for k in writes:
            for o in self.wr.get(k, {}).values():
                deps.append(o)
            for o in self.rd.get(k, {}).values():
                deps.append(o)
        ek = id(op) if op.is_dma else op.eng
        for k in reads:
            self.rd.setdefault(k, {})[ek] = op
        for k in writes:
            self.wr[k] = {ek: op}
            self.rd[k] = {}

    def op(self, eng, fn, reads=(), writes=()):
        o = _Op(eng, fn, False)
        self._track(o, reads, writes)
        o.idx = len(self.q[eng])
        self.q[eng].append(o)
        return o

    def dma(self, eng, out, in_, reads=(), writes=(), skey=None, **kw):
        key = ("dma", skey)
        o = _Op(eng, lambda e: e.dma_start(out=out, in_=in_, **kw), True)
        self._track(o, reads, writes)
        key = (eng == "pool", key)
        if key not in self.dsems:
            pool = self.sem_pool.setdefault(eng == "pool", [])
            i = sum(1 for k in self.dsems if k[0] == key[0])
            if i >= len(pool):
                pool.append([self.nc.alloc_semaphore("d%s%d" % ("s" if key[0] else "h", i)), 0])
            self.dsems[key] = pool[i]
        ds = self.dsems[key]
        ds[1] += 16
        o.dsem = ds[0]
        o.dval = ds[1]
        o.idx = len(self.q[eng])
        self.q[eng].append(o)
        self.pending_dma.append(o)
        return o

    def barrier(self):
        lasts = [self.q[e][-1] for e in ENGS if self.q[e] and not self.q[e][-1].is_dma]
        lasts += [o for e in ENGS for o in self.q[e][-1:] if False]
        dm = list(self.pending_dma)
        self.pending_dma = []
        for e in ENGS:
            o = _Op(e, None, False)
            for e2 in ENGS:
                for p in reversed(self.q[e2]):
                    if not p.is_dma and p.fn is not None:
                        o.deps.append(p)
                        break
            o.deps.extend(dm)
            o.idx = len(self.q[e])
            self.q[e].append(o)
        self.wr = {}
        self.rd = {}
        self.dsems = {}

    def finish(self):
        nc = self.nc
        self.barrier()
        for e in ENGS:
            for o in self.q[e]:
                for d in o.deps:
                    if not d.is_dma and (d.eng != o.eng or e != "pe"):
                        d.need_inc = True
        for e in ENGS:
            c = 0
            for o in self.q[e]:
                if o.need_inc:
                    c += 1
                o.cnt = c
        engobj = {"pe": "tensor", "act": "scalar", "dve": "vector", "pool": "gpsimd", "sp": "sync"}
        with nc.Block() as block:
            for e in ENGS:
                def body(eng, e=e):
                    known = {}
                    for o in self.q[e]:
                        waits = {}
                        for d in o.deps:
                            if d.is_dma:
                                s, v = d.dsem, d.dval
                            else:
                                if d.eng == e and e == "pe":
                                    continue
                                s, v = self.esem[d.eng], d.cnt
                            if v > waits.get(s, (None, 0))[1]:
                                waits[s] = (s, v)
                        for s, v in waits.values():
                            if known.get(s, 0) >= v:
                                continue
                            known[s] = v
                            eng.wait_ge(s, v)
                        if o.fn is None:
                            continue
                        ins = o.fn(eng)
                        if o.is_dma:
                            ins.then_inc(o.dsem, 16)
                        elif o.need_inc:
                            ins.then_inc(self.esem[e], 1)
                getattr(block, engobj[e])(body)
        self.stack.close()


D = 1024
T = 4096
KC = D // 128
DFF = 2816
JC = DFF // 128
TT = 512
NTT = T // TT
EPS = 1e-6


def consts(cx):
    nc = cx.nc
    c = {}
    c["ones_bf"] = cx.sbuf("ones_bf", [128, 128], BF16)
    cx.op("pool", lambda e: e.memset(c["ones_bf"][:], 1.0), writes=["ones_bf"])
    return c


def rmsnorm_tile(cx, c, xt, nw, hT_out, tag, rd_keys, wr_key, ps_key, ps, scr):
    sq, rstd = scr["sq"], scr["rstd"]
    cx.op("act", lambda e: e.activation(sq[:], xt[:], AF.Square), reads=rd_keys, writes=[tag + "sq"])

    def mm(e):
        ins = None
        for kc in range(KC):
            ins = e.matmul(ps[:], c["ones_bf"][:], sq[:, kc, :], start=(kc == 0), stop=(kc == KC - 1))
        return ins
    cx.op("pe", mm, reads=[tag + "sq", "ones_bf"], writes=[ps_key])
    cx.op("act", lambda e: e.activation(rstd[:], ps[:], AF.Sqrt, bias=EPS, scale=1.0 / D),
          reads=[ps_key], writes=[tag + "rstd"])
    cx.op("dve", lambda e: e.reciprocal(rstd[:], rstd[:]), reads=[tag + "rstd"], writes=[tag + "rstd"])
    for kc in range(KC):
        cx.op("dve", lambda e, kc=kc: e.scalar_tensor_tensor(
            out=hT_out(kc), in0=xt[:, kc, :], scalar=nw[:, kc:kc + 1], in1=rstd[:],
            op0=ALU.mult, op1=ALU.mult), reads=rd_keys + [tag + "rstd", tag + "nw"], writes=[wr_key])


def ffn_phase(cx, c, tag, x_in, x_out, nw_d, w_in, w_out, xkey_in, xkey_out):
    nc = cx.nc
    cx.push()
    HT = 2048
    NH = T // HT
    NT = HT // TT
    hT = cx.sbuf(tag + "hT", [128, KC, HT], BF16)
    act = cx.sbuf(tag + "act", [128, JC, HT], BF16)
    xt = cx.sbuf(tag + "xt", [128, KC, TT], F32)
    sq = cx.sbuf(tag + "sq", [128, KC, TT], BF16)
    rstd = cx.sbuf(tag + "rstd", [128, TT], F32)
    nw = cx.sbuf(tag + "nw", [128, KC], F32)
    WB = 256
    wg = [cx.sbuf(tag + "wg%d" % i, [128, KC, WB], BF16) for i in range(2)]
    wu = [cx.sbuf(tag + "wu%d" % i, [128, KC, WB], BF16) for i in range(2)]
    wo = [cx.sbuf(tag + "wo%d" % i, [128, JC, WB], BF16) for i in range(2)]
    sg = [cx.sbuf(tag + "sg%d" % i, [128, TT], F32) for i in range(2)]
    xr = [cx.sbuf(tag + "xr%d" % i, [128, TT], F32) for i in range(2)]
    xn = [cx.sbuf(tag + "xn%d" % i, [128, TT], F32) for i in range(2)]
    ps = [cx.psum(tag + "ps%d" % i, [128, TT], F32) for i in range(7)]
    pk = [tag + "ps%d" % i for i in range(7)]
    cx.psum_keys.update(pk)

    cx.dma("sp", nw[:], nw_d, writes=[tag + "nw"], skey="nw")
    w_in_r = w_in.rearrange("(kc p) n -> p kc n", p=128)
    w_out_r = w_out.rearrange("(kc p) n -> p kc n", p=128)
    x_in_r = x_in.rearrange("(kc p) t -> p kc t", p=128)
    cnt = {"g": 0, "o": 0, "s": 0, "x": 0}
    for h in range(NH):
        t0 = h * HT
        for tt in range(NT):
            gt = (t0 + tt * TT) // TT
            cx.dma("sp", xt[:], x_in_r[:, :, t0 + tt * TT: t0 + (tt + 1) * TT],
                   reads=[(xkey_in, gt, kc) for kc in range(KC)], writes=[tag + "xt"], skey="xt")
            rmsnorm_tile(cx, c, xt, nw, lambda kc, tt=tt: hT[:, kc, tt * TT:(tt + 1) * TT], tag,
                         [tag + "xt"], (tag + "hT", tt), pk[6], ps[6], {"sq": sq, "rstd": rstd})
        for jb in range(JC * 128 // WB):
            s = cnt["g"] % 2
            cnt["g"] += 1
            cx.dma("pool", wg[s][:], w_in_r[:, :, jb * WB:(jb + 1) * WB], writes=[(tag + "wg", s)],
                   skey=("wg", s))
            cx.dma("pool", wu[s][:], w_in_r[:, :, DFF + jb * WB: DFF + (jb + 1) * WB], writes=[(tag + "wu", s)],
                   skey=("wu", s))
            for jj in range(WB // 128):
                j = jb * (WB // 128) + jj
                for tt in range(NT):
                    b = cnt["s"] % 2
                    cnt["s"] += 1
                    pg, pu = ps[b], ps[2 + b]

                    def mm(e, s=s, jj=jj, tt=tt, pg=pg, pu=pu):
                        ins = None
                        for kc in range(KC):
                            ins = e.matmul(pg[:], wg[s][:, kc, jj * 128:(jj + 1) * 128],
                                           hT[:, kc, tt * TT:(tt + 1) * TT], start=(kc == 0), stop=(kc == KC - 1))
                        for kc in range(KC):
                            ins = e.matmul(pu[:], wu[s][:, kc, jj * 128:(jj + 1) * 128],
                                           hT[:, kc, tt * TT:(tt + 1) * TT], start=(kc == 0), stop=(kc == KC - 1))
                        return ins
                    cx.op("pe", mm, reads=[(tag + "wg", s), (tag + "wu", s), (tag + "hT", tt)],
                          writes=[pk[b], pk[2 + b]])
                    cx.op("act", lambda e, b=b, pg=pg: e.activation(sg[b][:], pg[:], AF.Silu),
                          reads=[pk[b]], writes=[(tag + "sg", b)])
                    cx.op("dve", lambda e, b=b, pu=pu, j=j, tt=tt: e.tensor_tensor(
                        out=act[:, j, tt * TT:(tt + 1) * TT], in0=sg[b][:], in1=pu[:], op=ALU.mult),
                        reads=[(tag + "sg", b), pk[2 + b]], writes=[(tag + "act", j, tt)])
        for mb in range(D // WB):
            s = cnt["o"] % 2
            cnt["o"] += 1
            cx.dma("pool", wo[s][:], w_out_r[:, :, mb * WB:(mb + 1) * WB], writes=[(tag + "wo", s)],
                   skey=("wo", s))
            for mm_ in range(WB // 128):
                mc = mb * (WB // 128) + mm_
                for tt in range(NT):
                    gt = (t0 + tt * TT) // TT
                    b = cnt["x"] % 2
                    cnt["x"] += 1
                    py = ps[4 + b]
                    cx.dma("sp", xr[b][:], x_in[mc * 128:(mc + 1) * 128, t0 + tt * TT: t0 + (tt + 1) * TT],
                           reads=[(xkey_in, gt, mc)], writes=[(tag + "xr", b)], skey=("xr", b))

                    def mm2(e, s=s, mm_=mm_, tt=tt, py=py):
                        ins = None
                        for kc in range(JC):
                            ins = e.matmul(py[:], wo[s][:, kc, mm_ * 128:(mm_ + 1) * 128],
                                           act[:, kc, tt * TT:(tt + 1) * TT], start=(kc == 0), stop=(kc == JC - 1))
                        return ins
                    cx.op("pe", mm2, reads=[(tag + "wo", s)] + [(tag + "act", j, tt) for j in range(JC)],
                          writes=[pk[4 + b]])
                    cx.op("dve", lambda e, b=b, py=py: e.scalar_tensor_tensor(
                        out=xn[b][:], in0=py[:], scalar=0.5, in1=xr[b][:], op0=ALU.mult, op1=ALU.add),
                        reads=[pk[4 + b], (tag + "xr", b)], writes=[(tag + "xn", b)])
                    cx.dma("sp", x_out[mc * 128:(mc + 1) * 128, t0 + tt * TT: t0 + (tt + 1) * TT], xn[b][:],
                           reads=[(tag + "xn", b)], writes=[(xkey_out, gt, mc)], skey=("xn", b))
    cx.barrier()
    cx.pop()


def build_ffn_prog():
    nc = bass.Bass("TRN2", target_bir_lowering=False)
    x = nc.dram_tensor("x", [D, T], F32, kind="ExternalInput").ap()
    nw = nc.dram_tensor("nw", [128, KC], F32, kind="ExternalInput").ap()
    w_in = nc.dram_tensor("w_in", [D, 2 * DFF], F32, kind="ExternalInput").ap()
    w_out = nc.dram_tensor("w_out", [DFF, D], F32, kind="ExternalInput").ap()
    y = nc.dram_tensor("y", [D, T], F32, kind="ExternalOutput").ap()
    cx = Ctx(nc)
    c = consts(cx)
    ffn_phase(cx, c, "f_", x, y, nw, w_in, w_out, "xin", "xout")
    cx.finish()
    return nc


NH_A = 12
DIL = (1, 4, 16)
NEG = -30000.0
SCALE_A = 128 ** -0.5
C_IDENT = 0
C_PSWAP = 128
C_MASK = 256
C_TRI = 512
C_TRIS = 640
C_MINC = 768
C_MSTR = 896
C_LAST = 1024
CST_W = 1152


def host_consts():
    m = np.zeros((128, CST_W), np.float32)
    k = np.arange(128)[:, None]
    q = np.arange(128)[None, :]
    m[:, C_IDENT:C_IDENT + 128] = (k == q)
    m[:, C_PSWAP:C_PSWAP + 128] = (k == (q + 64) % 128)
    m[:, C_MASK:C_MASK + 128] = np.where(k <= q, 0.0, NEG)
    m[:, C_MASK + 128:C_MASK + 256] = np.where(k >= q, 0.0, NEG)
    m[:, C_TRI:C_TRI + 128] = (k <= q)
    m[:, C_TRIS:C_TRIS + 128] = (k > q)
    m[:, C_MINC:C_MINC + 128] = (k <= q)
    m[:, C_MSTR:C_MSTR + 128] = (k < q)
    m[:, C_LAST:C_LAST + 128] = (k == 127)
    return m


def host_rope():
    inv = (1.0 / (10000.0 ** (np.arange(0, 128, 2, dtype=np.float32) / np.float32(128)))).astype(np.float32)
    ang = (np.arange(T, dtype=np.float32)[None, :] * inv[:, None]).astype(np.float32)
    cs, sn = np.cos(ang).astype(np.float32), np.sin(ang).astype(np.float32)
    return np.concatenate([cs, cs], 0), np.concatenate([-sn, sn], 0)


def norm_stage(cx, c, tag, x_in, nw_d, hT, xkey_in, ps, pskey):
    cx.push()
    xt = [cx.sbuf(tag + "nxt%d" % i, [128, KC, TT], F32) for i in range(2)]
    sq = cx.sbuf(tag + "nsq", [128, KC, TT], BF16)
    rstd = cx.sbuf(tag + "nrstd", [128, TT], F32)
    nw = cx.sbuf(tag + "nnw", [128, KC], F32)
    cx.dma("sp", nw[:], nw_d, writes=[tag + "nw"], skey="nw")
    x_in_r = x_in.rearrange("(kc p) t -> p kc t", p=128)
    for tt in range(NTT):
        b = tt % 2
        cx.dma("sp", xt[b][:], x_in_r[:, :, tt * TT:(tt + 1) * TT],
               reads=[(xkey_in, tt, kc) for kc in range(KC)], writes=[(tag + "xt", b)], skey=("xt", b))
        rmsnorm_tile(cx, c, xt[b], nw, lambda kc, tt=tt: hT[:, kc, tt * TT:(tt + 1) * TT], tag,
                     [(tag + "xt", b)], (tag + "hT", tt), pskey, ps, {"sq": sq, "rstd": rstd})
    cx.barrier()
    cx.pop()


def attn_phase(cx, c, tag, x_in, os_d, nw_d, w_in, ctab_d, stab_d, cst_d, xkey_in, okey):
    nc = cx.nc
    cx.push()
    hT = cx.sbuf(tag + "hT", [128, KC, T], BF16)
    ctab = cx.sbuf(tag + "ctab", [128, T], F32)
    stab = cx.sbuf(tag + "stab", [128, T], F32)
    ident = cx.sbuf(tag + "ident", [128, 128], BF16)
    pswap = cx.sbuf(tag + "pswap", [128, 128], BF16)
    maskb = cx.sbuf(tag + "maskb", [128, 256], BF16)
    ps = [cx.psum(tag + "ps%d" % i, [128, TT], F32) for i in range(8)]
    pk = [tag + "ps%d" % i for i in range(8)]
    cx.psum_keys.update(pk)
    cx.dma("sp", ctab[:], ctab_d, writes=[tag + "ctab"], skey="ctab")
    cx.dma("sp", stab[:], stab_d, writes=[tag + "stab"], skey="stab")
    cx.dma("pool", ident[:], cst_d[:, C_IDENT:C_IDENT + 128], writes=[tag + "ident"], skey="ident")
    cx.dma("pool", pswap[:], cst_d[:, C_PSWAP:C_PSWAP + 128], writes=[tag + "pswap"], skey="pswap")
    cx.dma("pool", maskb[:], cst_d[:, C_MASK:C_MASK + 256], writes=[tag + "maskb"], skey="maskb")
    norm_stage(cx, c, tag, x_in, nw_d, hT, xkey_in, ps[6], pk[6])
    hkeys = []

    Qd = cx.sbuf(tag + "Qd", [128, T], BF16)
    Kd = cx.sbuf(tag + "Kd", [128, T], BF16)
    Vd = cx.sbuf(tag + "Vd", [128, 32, 128], BF16)
    numT = cx.sbuf(tag + "numT", [128, 3, T], BF16)
    dent = cx.sbuf(tag + "dent", [128, T], F32)
    wq = [cx.sbuf(tag + "wq%d" % i, [128, KC, 128], BF16) for i in range(2)]
    wk = [cx.sbuf(tag + "wk%d" % i, [128, KC, 128], BF16) for i in range(2)]
    wv = [cx.sbuf(tag + "wv%d" % i, [128, KC, 128], BF16) for i in range(2)]
    qb = [cx.sbuf(tag + "qb%d" % i, [128, TT], BF16) for i in range(3)]
    t1 = [cx.sbuf(tag + "t1%d" % i, [128, TT], F32) for i in range(3)]
    t2 = [cx.sbuf(tag + "t2%d" % i, [128, TT], F32) for i in range(3)]
    pt = [cx.sbuf(tag + "pt%d" % i, [128, 256], BF16) for i in range(4)]
    osc = [cx.sbuf(tag + "osc%d" % i, [128, TT], BF16) for i in range(2)]
    w_in_r = w_in.rearrange("(kc p) n -> p kc n", p=128)
    AW = NH_A * 128
    cnt = {"w": 0, "r": 0, "o": 0}

    def load_w(hd):
        s = cnt["w"] % 2
        cnt["w"] += 1
        for nm, wt, off in (("wq", wq, 0), ("wk", wk, AW), ("wv", wv, 2 * AW)):
            cx.dma("pool", wt[s][:], w_in_r[:, :, off + hd * 128: off + (hd + 1) * 128],
                   writes=[(tag + nm, s)], skey=(nm, s))
        return s

    for hg in range(4):
        for g in range(3):
            d = DIL[g]
            Ls = T // d
            nb = Ls // 128
            hd = g * 4 + hg
            s = load_w(hd)
            def rope_piece(which, wt, dst, dkey, tt, item, s=s, d=d):
                b = item % 3
                bP, bS = (0, 1, 4)[b], (2, 3, 5)[b]
                pp, sw = ps[bP], ps[bS]
                dst_v = dst[:, :].rearrange("p (r j) -> p j r", r=d)

                def mm(e):
                    ins = None
                    for kc in range(KC):
                        ins = e.matmul(pp[:], wt[s][:, kc, :], hT[:, kc, tt * TT:(tt + 1) * TT],
                                       start=(kc == 0), stop=(kc == KC - 1))
                    return ins
                cx.op("pe", mm, reads=[(tag + "w" + which, s)], writes=[pk[bP]])
                yield
                cx.op("act", lambda e: e.activation(qb[b][:], pp[:], AF.Copy),
                      reads=[pk[bP]], writes=[(tag + "qb", b)])
                yield "NEXT"
                cx.op("pe", lambda e: e.matmul(sw[:], pswap[:], qb[b][:], start=True, stop=True),
                      reads=[(tag + "qb", b), tag + "pswap"], writes=[pk[bS]])
                cx.op("dve", lambda e: e.tensor_tensor(
                    out=t1[b][:], in0=qb[b][:], in1=ctab[:, tt * TT:(tt + 1) * TT], op=ALU.mult),
                    reads=[(tag + "qb", b), tag + "ctab"], writes=[(tag + "t1", b)])
                yield
                cx.op("dve", lambda e: e.tensor_tensor(
                    out=t2[b][:], in0=sw[:], in1=stab[:, tt * TT:(tt + 1) * TT], op=ALU.mult),
                    reads=[pk[bS], tag + "stab"], writes=[(tag + "t2", b)])
                yield
                j0 = tt * TT // d
                cx.op("dve", lambda e: e.tensor_tensor(
                    out=dst_v[:, j0:j0 + TT // d, :],
                    in0=t1[b][:, :].rearrange("p (j r) -> p j r", r=d),
                    in1=t2[b][:, :].rearrange("p (j r) -> p j r", r=d), op=ALU.add),
                    reads=[(tag + "t1", b), (tag + "t2", b)], writes=[(dkey, tt)])
                yield

            pieces = [(w_, wt_, dst_, dk_, tt) for (w_, wt_, dst_, dk_) in
                      (("q", wq, Qd, tag + "Qd"), ("k", wk, Kd, tag + "Kd")) for tt in range(NTT)]
            act_g, nx = [], [0]

            def start_r():
                if nx[0] < len(pieces) and len(act_g) < 3:
                    w_, wt_, dst_, dk_, tt = pieces[nx[0]]
                    act_g.append(rope_piece(w_, wt_, dst_, dk_, tt, cnt["r"]))
                    cnt["r"] += 1
                    nx[0] += 1
            start_r()
            while act_g:
                for g_ in list(act_g):
                    try:
                        v = next(g_)
                    except StopIteration:
                        act_g.remove(g_)
                        start_r()
                        continue
                    if v == "NEXT" and g_ is act_g[-1]:
                        start_r()
            for b4 in range(8):
                b = cnt["r"] % 2
                cnt["r"] += 1
                pp = ps[b]

                def mmv(e, s=s, b4=b4, pp=pp, d=d, Ls=Ls):
                    ins = None
                    for i in range(4):
                        B = b4 * 4 + i
                        r, n = divmod(B, Ls // 128)
                        t0 = n * 128 * d + r
                        for kc in range(KC):
                            ins = e.matmul(pp[:, i * 128:(i + 1) * 128],
                                           hT[:, kc, t0: t0 + 127 * d + 1: d], wv[s][:, kc, :],
                                           start=(kc == 0), stop=(kc == KC - 1))
                    return ins
                cx.op("pe", mmv, reads=[(tag + "wv", s)], writes=[pk[b]])
                cx.op("act", lambda e, b4=b4, pp=pp: e.activation(
                    Vd[:, b4 * 4:(b4 + 1) * 4, :], pp[:, :].rearrange("p (i e) -> p i e", i=4), AF.Copy),
                    reads=[pk[b]], writes=[(tag + "Vd", b4)])
            qk_keys = [(tag + "Qd", tt) for tt in range(NTT)] + [(tag + "Kd", tt) for tt in range(NTT)]

            def scores(B, nb=nb):
                sb = (4, 5, 2, 3)[B % 4]
                nq = 256 if (B + 1) % nb != 0 else 128
                st = ps[sb]

                def mm(e, B=B, nq=nq, st=st):
                    e.matmul(st[:, 0:nq], Kd[:, B * 128:(B + 1) * 128], Qd[:, B * 128:B * 128 + nq],
                             start=True, stop=False)
                    return e.matmul(st[:, 0:nq], ident[:], maskb[:, 0:nq], start=False, stop=True)
                cx.op("pe", mm, reads=qk_keys + [tag + "ident", tag + "maskb"], writes=[pk[sb]])
                cx.op("act", lambda e, B=B, nq=nq, st=st: e.activation(
                    pt[B % 4][:, 0:nq], st[:, 0:nq], AF.Exp, scale=SCALE_A),
                    reads=[pk[sb]], writes=[(tag + "pt", B % 4)])

            def pv(B, nb=nb, g=g, d=d, Ls=Ls):
                first = (B % nb == 0)
                col = (B % 4) * 128
                bN, bD = ((6, 7), (0, 1))[(B // 4) % 2]

                def mm(e, B=B, first=first, col=col, bN=bN, bD=bD):
                    ins = None
                    for dst, lhs_fn in ((ps[bN], lambda blk: Vd[:, blk, :]), (ps[bD], lambda blk: c["ones_bf"][:])):
                        if not first:
                            e.matmul(dst[:, col:col + 128], lhs_fn(B - 1), pt[(B - 1) % 4][:, 128:256],
                                     start=True, stop=False)
                        ins = e.matmul(dst[:, col:col + 128], lhs_fn(B), pt[B % 4][:, 0:128],
                                       start=first, stop=True)
                    return ins
                rk = [(tag + "pt", B % 4), (tag + "Vd", B // 4), "ones_bf"]
                if not first:
                    rk += [(tag + "pt", (B - 1) % 4), (tag + "Vd", (B - 1) // 4)]
                cx.op("pe", mm, reads=rk, writes=[pk[bN], pk[bD]])
                if B % 4 == 3:
                    u0 = (B - 3) * 128
                    if d == 1:
                        def nat(ap2d):
                            return ap2d[:, u0:u0 + 512]
                        def src(p):
                            return p[:, :]
                    else:
                        r0, j0 = divmod(u0, Ls)
                        nr = max(1, 512 // Ls)
                        nj = 512 // nr
                        def nat(ap2d, r0=r0, j0=j0, nr=nr, nj=nj, d=d):
                            return ap2d.rearrange("p (j r) -> p r j", r=d)[:, r0:r0 + nr, j0:j0 + nj]
                        def src(p, nr=nr):
                            return p[:, :].rearrange("p (r j) -> p r j", r=nr)
                    cx.op("act", lambda e, nat=nat, src=src, g=g, bN=bN: e.activation(
                        nat(numT[:, g, :]), src(ps[bN]), AF.Copy),
                        reads=[pk[bN]], writes=[(tag + "numT", g)])
                    if g == 0:
                        cx.op("dve", lambda e, nat=nat, src=src, bD=bD: e.tensor_copy(nat(dent[:, :]), src(ps[bD])),
                              reads=[pk[bD]], writes=[tag + "dent"])
                    else:
                        cx.op("dve", lambda e, nat=nat, src=src, bD=bD: e.tensor_tensor(
                            out=nat(dent[:, :]), in0=nat(dent[:, :]), in1=src(ps[bD]), op=ALU.add),
                            reads=[pk[bD], tag + "dent"], writes=[tag + "dent"])

            scores(0)
            scores(1)
            for B in range(32):
                if B + 2 < 32:
                    scores(B + 2)
                pv(B)
        cx.op("dve", lambda e: e.reciprocal(dent[:, :], dent[:, :]), reads=[tag + "dent"], writes=[tag + "dent"])
        for g in range(3):
            hd = g * 4 + hg
            for tt in range(NTT):
                b = cnt["o"] % 2
                cnt["o"] += 1
                cx.op("dve", lambda e, b=b, g=g, tt=tt: e.tensor_tensor(
                    out=osc[b][:], in0=numT[:, g, tt * TT:(tt + 1) * TT], in1=dent[:, tt * TT:(tt + 1) * TT],
                    op=ALU.mult), reads=[tag + "dent", (tag + "numT", g)], writes=[(tag + "osc", b)])
                cx.dma("sp", os_d[hd * 128:(hd + 1) * 128, tt * TT:(tt + 1) * TT], osc[b][:],
                       reads=[(tag + "osc", b)], writes=[(okey, hd, tt)], skey=("osc", b))
    cx.barrier()
    cx.pop()


def outproj_phase(cx, c, tag, x_in, x_out, os_d, w_out, nheads, xkey_in, xkey_out, okey):
    cx.push()
    HT = 2048
    NT = HT // TT
    WB = 256
    osb = cx.sbuf(tag + "osb", [128, nheads, HT], BF16)
    wo = [cx.sbuf(tag + "wo%d" % i, [128, nheads, WB], BF16) for i in range(2)]
    xr = [cx.sbuf(tag + "xr%d" % i, [128, TT], F32) for i in range(2)]
    xn = [cx.sbuf(tag + "xn%d" % i, [128, TT], F32) for i in range(2)]
    ps = [cx.psum(tag + "ps%d" % i, [128, TT], F32) for i in range(2)]
    pk = [tag + "ps%d" % i for i in range(2)]
    cx.psum_keys.update(pk)
    w_out_r = w_out.rearrange("(kc p) n -> p kc n", p=128)
    os_r = os_d.rearrange("(h p) t -> p h t", p=128)
    cnt = {"o": 0, "x": 0}
    for h in range(T // HT):
        t0 = h * HT
        for hd in range(nheads):
            cx.dma("sp", osb[:, hd, :], os_r[:, hd, t0:t0 + HT],
                   reads=[(okey, hd, (t0 // TT) + i) for i in range(NT)], writes=[(tag + "osb", hd)], skey=("osb", hd % 4))
        for mb in range(D // WB):
            s = cnt["o"] % 2
            cnt["o"] += 1
            cx.dma("pool", wo[s][:], w_out_r[:, :, mb * WB:(mb + 1) * WB], writes=[(tag + "wo", s)], skey=("wo", s))
            for mm_ in range(WB // 128):
                mc = mb * (WB // 128) + mm_
                for tt in range(NT):
                    gt = (t0 + tt * TT) // TT
                    b = cnt["x"] % 2
                    cnt["x"] += 1
                    py = ps[b]
                    cx.dma("sp", xr[b][:], x_in[mc * 128:(mc + 1) * 128, t0 + tt * TT: t0 + (tt + 1) * TT],
                           reads=[(xkey_in, gt, mc)], writes=[(tag + "xr", b)], skey=("xr", b))

                    def mm2(e, s=s, mm_=mm_, tt=tt, py=py):
                        ins = None
                        for kc in range(nheads):
                            ins = e.matmul(py[:], wo[s][:, kc, mm_ * 128:(mm_ + 1) * 128],
                                           osb[:, kc, tt * TT:(tt + 1) * TT], start=(kc == 0), stop=(kc == nheads - 1))
                        return ins
                    cx.op("pe", mm2, reads=[(tag + "wo", s)] + [(tag + "osb", hd) for hd in range(nheads)],
                          writes=[pk[b]])
                    cx.op("dve", lambda e, b=b, py=py: e.tensor_tensor(
                        out=xn[b][:], in0=py[:], in1=xr[b][:], op=ALU.add),
                        reads=[pk[b], (tag + "xr", b)], writes=[(tag + "xn", b)])
                    cx.dma("sp", x_out[mc * 128:(mc + 1) * 128, t0 + tt * TT: t0 + (tt + 1) * TT], xn[b][:],
                           reads=[(tag + "xn", b)], writes=[(xkey_out, gt, mc)], skey=("xn", b))
    cx.barrier()
    cx.pop()


def build_attn_prog():
    nc = bass.Bass("TRN2", target_bir_lowering=False)
    x = nc.dram_tensor("x", [D, T], F32, kind="ExternalInput").ap()
    nw = nc.dram_tensor("nw", [128, KC], F32, kind="ExternalInput").ap()
    w_in = nc.dram_tensor("w_in", [D, 3 * NH_A * 128], F32, kind="ExternalInput").ap()
    w_out = nc.dram_tensor("w_out", [NH_A * 128, D], F32, kind="ExternalInput").ap()
    ctab = nc.dram_tensor("ctab", [128, T], F32, kind="ExternalInput").ap()
    stab = nc.dram_tensor("stab", [128, T], F32, kind="ExternalInput").ap()
    cst = nc.dram_tensor("cst", [128, CST_W], F32, kind="ExternalInput").ap()
    os_d = nc.dram_tensor("os_scratch", [NH_A * 128, T], BF16).ap()
    y = nc.dram_tensor("y", [D, T], F32, kind="ExternalOutput").ap()
    cx = Ctx(nc)
    c = consts(cx)
    attn_phase(cx, c, "a_", x, os_d, nw, w_in, ctab, stab, cst, "xin", "os")
    outproj_phase(cx, c, "ao_", x, y, os_d, w_out, NH_A, "xin", "xout", "os")
    cx.finish()
    return nc


NVH = 16
NKH = 8
GP = 6176
NTILE = T // 128
DBG = {}


class _Stop(Exception):
    pass


def _chk(n):
    if DBG.get("stop", 99) <= n:
        raise _Stop()


def gdn_core_phase(*a):
    cx = a[0]
    depth = len(cx.scopes)
    try:
        _gdn_core_phase(*a)
    except _Stop:
        cx.barrier()
        while len(cx.scopes) > depth:
            cx.pop()


def _gdn_core_phase(cx, c, tag, x_in, oraw_d, nw_d, w_in, convw_d, alog_d, dtb_d, cst_d, xkey_in, okey):
    cx.push()
    hT = cx.sbuf(tag + "hT", [128, KC, T], BF16)
    ident = cx.sbuf(tag + "ident", [128, 128], BF16)
    id4 = cx.sbuf(tag + "id4", [128, 4, 128], BF16)
    minc4 = cx.sbuf(tag + "minc4", [128, 4, 128], BF16)
    mlow4 = cx.sbuf(tag + "mlow4", [128, 4, 128], BF16)
    cw = cx.sbuf(tag + "cw", [128, 32, 4], F32)
    beta = cx.sbuf(tag + "beta", [128, NTILE, NVH], F32)
    nbeta = cx.sbuf(tag + "nbeta", [128, NTILE, NVH], F32)
    gc = cx.sbuf(tag + "gc", [128, NTILE, NVH], F32)
    kbs = cx.sbuf(tag + "kbs", [128, NTILE, NVH], F32)
    egr = cx.sbuf(tag + "egr", [128, NTILE, NVH], F32)
    cd = cx.sbuf(tag + "cd", [128, NTILE, NVH], F32)
    gchi = cx.sbuf(tag + "gchi", [128, NTILE, NVH], BF16)
    gclo = cx.sbuf(tag + "gclo", [128, NTILE, NVH], BF16)
    ps = [cx.psum(tag + "ps%d" % i, [128, TT], F32) for i in range(8)]
    pk = [tag + "ps%d" % i for i in range(8)]
    cx.psum_keys.update(pk)
    K = lambda n: tag + n
    cx.push()
    alog = cx.sbuf(tag + "alog", [128, 512], F32)
    dtb = cx.sbuf(tag + "dtb", [128, 512], F32)
    gt = cx.sbuf(tag + "gt", [128, NTILE, NVH], F32)
    wab = cx.sbuf(tag + "wab", [128, KC, 32], BF16)
    ghi = cx.sbuf(tag + "ghi", [128, NTILE, NVH], BF16)
    glo = cx.sbuf(tag + "glo", [128, NTILE, NVH], BF16)
    trib = cx.sbuf(tag + "trib", [128, 128], BF16)
    trisb = cx.sbuf(tag + "trisb", [128, 128], BF16)
    cx.dma("pool", ident[:], cst_d[:, C_IDENT:C_IDENT + 128], writes=[K("ident")], skey="ident")
    for i in range(4):
        cx.dma("pool", id4[:, i, :], cst_d[:, C_IDENT:C_IDENT + 128], writes=[K("id4")], skey=("id4", i))
        cx.dma("pool", minc4[:, i, :], cst_d[:, C_MINC:C_MINC + 128], writes=[K("minc4")], skey=("minc4", i))
        cx.dma("pool", mlow4[:, i, :], cst_d[:, C_TRIS:C_TRIS + 128], writes=[K("mlow4")], skey=("mlow4", i))
    cx.dma("sp", cw[:], convw_d, writes=[K("cw")], skey="cw")
    cx.dma("sp", alog[:], alog_d, writes=[K("alog")], skey="alog")
    cx.dma("sp", dtb[:], dtb_d, writes=[K("dtb")], skey="dtb")
    w_in_r = w_in.rearrange("(kc p) n -> p kc n", p=128)
    cx.dma("pool", wab[:], w_in_r[:, :, 6144:6176], writes=[K("wab")], skey="wab")
    _chk(0)
    norm_stage(cx, c, tag, x_in, nw_d, hT, xkey_in, ps[6], pk[6])
    _chk(1)

    f2 = lambda t: t[:, :, :].rearrange("p c h -> p (c h)")
    for which in range(2):
        def mm(e, which=which):
            ins = None
            for ct in range(NTILE):
                for kc in range(KC):
                    ins = e.matmul(ps[which][:, ct * 16:(ct + 1) * 16], hT[:, kc, ct * 128:(ct + 1) * 128],
                                   wab[:, kc, which * 16:(which + 1) * 16], start=(kc == 0), stop=(kc == KC - 1))
            return ins
        cx.op("pe", mm, reads=[], writes=[pk[which]])
    _chk(1.05)
    cx.op("act", lambda e: e.activation(f2(beta), ps[0][:], AF.Sigmoid), reads=[pk[0]], writes=[K("beta")])
    _chk(1.1)
    cx.op("dve", lambda e: e.tensor_tensor(out=f2(gt), in0=ps[1][:], in1=dtb[:], op=ALU.add),
          reads=[pk[1], K("dtb")], writes=[K("gt")])
    _chk(1.2)
    cx.op("act", lambda e: e.activation(f2(gt), f2(gt), AF.Exp), reads=[K("gt")], writes=[K("gt")])
    _chk(1.4)
    cx.op("act", lambda e: e.activation(f2(gt), f2(gt), AF.Ln, bias=1.0), reads=[K("gt")], writes=[K("gt")])
    _chk(1.6)
    cx.op("act", lambda e: e.activation(alog[:], alog[:], AF.Exp), reads=[K("alog")], writes=[K("alog")])
    _chk(1.8)
    cx.op("dve", lambda e: e.scalar_tensor_tensor(out=f2(gt), in0=f2(gt), scalar=-1.0, in1=alog[:],
                                                   op0=ALU.mult, op1=ALU.mult),
          reads=[K("gt"), K("alog")], writes=[K("gt")])
    cx.op("dve", lambda e: e.tensor_scalar(f2(nbeta), f2(beta), -1.0, None, ALU.mult),
          reads=[K("beta")], writes=[K("nbeta")])
    _chk(2)

    cx.dma("pool", trib[:], cst_d[:, C_TRI:C_TRI + 128], writes=[K("trib")], skey="trib")
    cx.dma("pool", trisb[:], cst_d[:, C_TRIS:C_TRIS + 128], writes=[K("trisb")], skey="trisb")
    cx.op("dve", lambda e: e.tensor_copy(f2(ghi), f2(gt)), reads=[K("gt")], writes=[K("ghi")])
    cx.op("dve", lambda e: e.tensor_tensor(out=f2(glo), in0=f2(gt), in1=f2(ghi), op=ALU.subtract),
          reads=[K("gt"), K("ghi")], writes=[K("glo")])
    _chk(2.2)

    def mmg(e):
        ins = None
        for ct in range(NTILE):
            for dst, m in ((ps[2], trib), (ps[3], trisb), (ps[4], c["ones_bf"])):
                e.matmul(dst[:, ct * 16:(ct + 1) * 16], m[:], ghi[:, ct, :], start=True, stop=False)
                ins = e.matmul(dst[:, ct * 16:(ct + 1) * 16], m[:], glo[:, ct, :], start=False, stop=True)
        return ins
    cx.op("pe", mmg, reads=[K("ghi"), K("glo"), K("trib"), K("trisb"), "ones_bf"], writes=[pk[2], pk[3], pk[4]])
    _chk(2.4)
    cx.op("dve", lambda e: e.tensor_copy(f2(gchi), ps[2][:]), reads=[pk[2]], writes=[K("gchi")])
    cx.op("dve", lambda e: e.tensor_tensor(out=f2(gclo), in0=ps[2][:], in1=f2(gchi), op=ALU.subtract),
          reads=[pk[2], K("gchi")], writes=[K("gclo")])
    _chk(2.6)
    cx.op("dve", lambda e: e.tensor_copy(f2(gc), ps[2][:]), reads=[pk[2]], writes=[K("gc")])
    cx.op("act", lambda e: e.activation(f2(kbs), ps[2][:], AF.Exp), reads=[pk[2]], writes=[K("kbs")])
    _chk(2.7)
    cx.op("dve", lambda e: e.tensor_tensor(out=f2(kbs), in0=f2(kbs), in1=f2(beta), op=ALU.mult),
          reads=[K("kbs"), K("beta")], writes=[K("kbs")])
    _chk(2.8)
    cx.op("act", lambda e: e.activation(f2(egr), ps[3][:], AF.Exp), reads=[pk[3]], writes=[K("egr")])
    cx.op("act", lambda e: e.activation(f2(cd), ps[4][:], AF.Exp), reads=[pk[4]], writes=[K("cd")])
    _chk(2.9)
    cx.barrier()
    cx.pop()
    _chk(3)

    qT = cx.sbuf(tag + "qT", [128, 2, T], BF16)
    kT = cx.sbuf(tag + "kT", [128, 2, T], BF16)
    ktok = cx.sbuf(tag + "ktok", [128, NTILE, 2, 128], BF16)
    vtok = cx.sbuf(tag + "vtok", [128, NTILE, 4, 128], BF16)
    PW = 1024
    for hgp in range(DBG.get('hgps', 4)):
        cx.push()
        NB = 3
        wch = [cx.sbuf(tag + "wch%d" % i, [128, KC, 128], BF16) for i in range(2)]
        dg = [cx.sbuf(tag + "dg%d" % i, [128, 4, 128], BF16) for i in range(2)]
        xcb = [cx.sbuf(tag + "xcb%d" % i, [128, 4 + TT], BF16) for i in range(NB)]
        sil = [cx.sbuf(tag + "sil%d" % i, [128, TT], F32) for i in range(NB)]
        silb = [cx.sbuf(tag + "silb%d" % i, [128, TT], BF16) for i in range(NB)]
        sqb = [cx.sbuf(tag + "sqb%d" % i, [128, TT], BF16) for i in range(NB)]
        rn = [cx.sbuf(tag + "rn%d" % i, [128, TT], F32) for i in range(NB)]
        chunks = [("q", 0), ("q", 1), ("k", 0), ("k", 1), ("v", 0), ("v", 1), ("v", 2), ("v", 3)]

        def g1_piece(kind, li, gch, ws, tt, item):
            b = item % NB
            i2 = item % 2
            bP, bC, bS, bT = i2, 2 + i2, 4 + i2, 6 + i2
            XK = lambda i: (K("xcb"), i)
            if tt == 0:
                cx.op("act", lambda e: e.activation(xcb[b][:, 0:4], xcb[b][:, 0:4], AF.Copy, scale=0.0),
                      reads=[], writes=[(K("xh"), b)])

            def mm(e):
                ins = None
                for kc in range(KC):
                    ins = e.matmul(ps[bP][:], wch[ws][:, kc, :], hT[:, kc, tt * TT:(tt + 1) * TT],
                                   start=(kc == 0), stop=(kc == KC - 1))
                return ins
            cx.op("pe", mm, reads=[(K("wch"), ws)], writes=[pk[bP]])
            yield
            cx.op("act", lambda e: e.activation(xcb[b][:, 4:4 + TT], ps[bP][:], AF.Copy),
                  reads=[pk[bP]], writes=[XK(b)])
            if tt < NTT - 1:
                nb_ = (item + 1) % NB
                cx.op("act", lambda e: e.activation(xcb[nb_][:, 0:4], xcb[b][:, TT:TT + 4], AF.Copy),
                      reads=[XK(b)], writes=[(K("xh"), nb_)])
            yield "NEXT"

            def mmc(e):
                ins = None
                for j in range(4):
                    ins = e.matmul(ps[bC][:], dg[ws][:, j, :], xcb[b][:, 1 + j:1 + j + TT], start=(j == 0), stop=(j == 3))
                return ins
            cx.op("pe", mmc, reads=[XK(b), (K("xh"), b), (K("dg"), ws)], writes=[pk[bC]])
            yield
            tsl = slice(tt * TT, (tt + 1) * TT)
            if kind == "v":
                cx.op("act", lambda e: e.activation(silb[b][:], ps[bC][:], AF.Silu), reads=[pk[bC]], writes=[(K("silb"), b)])
                yield
            else:
                cx.op("act", lambda e: e.activation(sil[b][:], ps[bC][:], AF.Silu), reads=[pk[bC]], writes=[(K("sil"), b)])
                yield
                cx.op("dve", lambda e: e.tensor_tensor(out=sqb[b][:], in0=sil[b][:], in1=sil[b][:], op=ALU.mult),
                      reads=[(K("sil"), b)], writes=[(K("sqb"), b)])
                yield
                cx.op("pe", lambda e: e.matmul(ps[bS][:], c["ones_bf"][:], sqb[b][:], start=True, stop=True),
                      reads=[(K("sqb"), b), "ones_bf"], writes=[pk[bS]])
                yield
                cx.op("act", lambda e: e.activation(rn[b][:], ps[bS][:], AF.Sqrt, bias=1e-6, scale=1.0),
                      reads=[pk[bS]], writes=[(K("rn"), b)])
                yield
                cx.op("dve", lambda e: e.reciprocal(rn[b][:], rn[b][:]), reads=[(K("rn"), b)], writes=[(K("rn"), b)])
                dstT = qT if kind == "q" else kT
                sc = (128 ** -0.5) if kind == "q" else 1.0
                cx.op("dve", lambda e: e.scalar_tensor_tensor(
                    out=dstT[:, li, tsl], in0=sil[b][:], scalar=sc, in1=rn[b][:], op0=ALU.mult, op1=ALU.mult),
                    reads=[(K("sil"), b), (K("rn"), b)], writes=[(K(kind + "T"), li, tt)])
                yield
            if kind in ("k", "v"):
                ptv = ps[bT][:, :].bitcast(BF16)

                def mmt(e):
                    ins = None
                    for i in range(4):
                        if kind == "v":
                            src = silb[b][:, i * 128:(i + 1) * 128]
                        else:
                            src = kT[:, li, tt * TT + i * 128: tt * TT + (i + 1) * 128]
                        ins = e.transpose(ptv[:, i * 128:(i + 1) * 128], src, ident[:])
                    return ins
                rk = [(K("silb"), b)] if kind == "v" else [(K("kT"), li, tt)]
                cx.op("pe", mmt, reads=rk + [K("ident")], writes=[pk[bT]])
                yield
                dst = vtok if kind == "v" else ktok
                cx.op("act", lambda e: e.activation(
                    dst[:, tt * 4:tt * 4 + 4, li, :], ptv[:, 0:512].rearrange("p (i x) -> p i x", i=4), AF.Copy),
                    reads=[pk[bT]], writes=[(K(kind + "tok"), li, tt)])
                yield

        items = []
        wcnt = 0
        for kind, li in chunks:
            gch = {"q": 2 * hgp + li, "k": 8 + 2 * hgp + li, "v": 16 + 4 * hgp + li}[kind]
            ws = wcnt % 2
            wcnt += 1
            for tt in range(NTT):
                items.append((kind, li, gch, ws, tt))
        active, nxt_i = [], [0]

        def start1():
            while nxt_i[0] < len(items) and len(active) < NB:
                kind, li, gch, ws, tt = items[nxt_i[0]]
                if tt == 0:
                    cx.dma("pool", wch[ws][:], w_in_r[:, :, gch * 128:(gch + 1) * 128], writes=[(K("wch"), ws)],
                           skey=("wch", ws))
                    for j in range(4):
                        cx.op("dve", lambda e, ws=ws, gch=gch, j=j: e.tensor_scalar(
                            dg[ws][:, j, :], ident[:], cw[:, gch, j:j + 1], None, ALU.mult),
                            reads=[K("ident"), K("cw")], writes=[(K("dg"), ws)])
                active.append(g1_piece(kind, li, gch, ws, tt, nxt_i[0]))
                nxt_i[0] += 1
                return
        start1()
        while active:
            for g in list(active):
                try:
                    v = next(g)
                except StopIteration:
                    active.remove(g)
                    start1()
                    continue
                if v == "NEXT" and g is active[-1]:
                    start1()
        cx.barrier()
        cx.pop()
        _chk(5)

        cx.push()
        mk = lambda nm, dt_, n=2: [cx.sbuf(tag + nm + "%d" % i, [128, 4, 128], dt_) for i in range(n)]
        Abuf0, Abuf1, Atbuf0, Atbuf1 = mk("Aa", BF16), mk("Ab", BF16), mk("Ata", BF16), mk("Atb", BF16)
        A_O, AtO, Tt = mk("A_O", BF16), mk("AtO", BF16), mk("Tt", BF16)
        qkT, qdT, kd, vb, kb = mk("qkT", BF16), mk("qdT", BF16), mk("kd", BF16), mk("vb", BF16), mk("kb", BF16)
        dmT = cx.sbuf(tag + "dmT", [128, 4, 128], F32)
        dm2 = cx.sbuf(tag + "dm2", [128, 4, 128], F32)
        eB = cx.sbuf(tag + "eB", [128, 4, 128], F32)
        rr = cx.sbuf(tag + "rr", [128, 4, 128], F32)
        u = cx.sbuf(tag + "u", [128, 4, 128], F32)
        wT = cx.sbuf(tag + "wT", [128, 4, 128], BF16)
        vnew = cx.sbuf(tag + "vnew", [128, 4, 128], BF16)
        S32 = cx.sbuf(tag + "S32", [128, 4, 128], F32)
        S16 = cx.sbuf(tag + "S16", [128, 4, 128], BF16)
        ost = [cx.sbuf(tag + "ost%d" % i, [128, 4, 128], F32) for i in range(2)]
        fl = lambda t: t[:, :, :].rearrange("p h x -> p (h x)")
        cx.op("dve", lambda e: e.memset(fl(S32), 0.0), writes=[K("S32")])
        cx.op("dve", lambda e: e.memset(fl(S16), 0.0), writes=[K("S16")])
        hs = [4 * hgp + hl for hl in range(4)]
        B0, B1 = 0, 1

        def g2_tile(ct, hs, hgp=hgp):
            par = ct % 2
            bA, bAt, bTu = 2 + 3 * par, 3 + 3 * par, 4 + 3 * par
            tsl = slice(ct * 128, (ct + 1) * 128)
            P = lambda n: (K(n), par)
            AO, AtOp, Ttp = A_O[par], AtO[par], Tt[par]
            bufs = [(Abuf0[par], Atbuf0[par]), (Abuf1[par], Atbuf1[par]), (AO, AtOp)]

            def mm1(e):
                for hl in range(4):
                    e.matmul(ps[B0][:, hl * 128:(hl + 1) * 128], gchi[:, ct, hs[hl]:hs[hl] + 1].to_broadcast([128, 128]),
                             ident[:], start=True, stop=False)
                    e.matmul(ps[B0][:, hl * 128:(hl + 1) * 128], gclo[:, ct, hs[hl]:hs[hl] + 1].to_broadcast([128, 128]),
                             ident[:], start=False, stop=True)
                ins = None
                for khl in range(2):
                    e.matmul(ps[B1][:, khl * 128:(khl + 1) * 128], kT[:, khl, tsl], kT[:, khl, tsl], start=True, stop=True)
                    ins = e.matmul(ps[B1][:, (2 + khl) * 128:(3 + khl) * 128], kT[:, khl, tsl], qT[:, khl, tsl],
                                   start=True, stop=True)
                return ins
            cx.op("pe", mm1, reads=[K("ident")], writes=[pk[B0], pk[B1]])
            yield
            for hl in range(4):
                h = hs[hl]
                cx.op("dve", lambda e, hl=hl, h=h: e.tensor_scalar(
                    dmT[:, hl, :], ps[B0][:, hl * 128:(hl + 1) * 128], gc[:, ct, h:h + 1], 0.0, ALU.subtract, ALU.min),
                    reads=[pk[B0]], writes=[K("dmT")])
                cx.op("dve", lambda e, hl=hl, h=h: e.tensor_scalar(
                    dm2[:, hl, :], ps[B0][:, hl * 128:(hl + 1) * 128], gc[:, ct, h:h + 1], 0.0, ALU.subtract, ALU.max),
                    reads=[pk[B0]], writes=[K("dm2")])
                yield
            cx.op("act", lambda e: e.activation(fl(eB), ps[B0][:], AF.Exp), reads=[pk[B0]], writes=[K("eB")])
            cx.op("act", lambda e: e.activation(fl(dmT), fl(dmT), AF.Exp), reads=[K("dmT")], writes=[K("dmT")])
            cx.op("act", lambda e: e.activation(fl(dm2), fl(dm2), AF.Exp, scale=-1.0), reads=[K("dm2")], writes=[K("dm2")])
            yield
            cx.op("dve", lambda e: e.tensor_tensor(out=fl(dmT), in0=fl(dmT), in1=fl(minc4), op=ALU.mult),
                  reads=[K("dmT")], writes=[K("dmT")])
            cx.op("dve", lambda e: e.tensor_tensor(out=fl(dm2), in0=fl(dm2), in1=fl(mlow4), op=ALU.mult),
                  reads=[K("dm2")], writes=[K("dm2")])
            yield
            for hl in range(4):
                h = hs[hl]
                khl = hl // 2
                cx.op("dve", lambda e, hl=hl, h=h, khl=khl: e.scalar_tensor_tensor(
                    out=AO[:, hl, :], in0=ps[B1][:, khl * 128:(khl + 1) * 128], scalar=nbeta[:, ct, h:h + 1],
                    in1=dm2[:, hl, :], op0=ALU.mult, op1=ALU.mult), reads=[pk[B1], K("dm2")], writes=[(P("A"), 2)])
                cx.op("dve", lambda e, hl=hl, khl=khl: e.tensor_tensor(
                    out=qkT[par][:, hl, :], in0=ps[B1][:, (2 + khl) * 128:(3 + khl) * 128], in1=dmT[:, hl, :], op=ALU.mult),
                    reads=[pk[B1], K("dmT")], writes=[P("qkT")])
                cx.op("dve", lambda e, hl=hl, khl=khl: e.tensor_tensor(
                    out=qdT[par][:, hl, :], in0=qT[:, khl, tsl], in1=eB[:, hl, :], op=ALU.mult),
                    reads=[K("eB")], writes=[P("qdT")])
                cx.op("act", lambda e, hl=hl, h=h: e.activation(
                    vb[par][:, hl, :], vtok[:, ct, hl, :], AF.Copy, scale=beta[:, ct, h:h + 1]), writes=[P("vb")])
                cx.op("act", lambda e, hl=hl, h=h, khl=khl: e.activation(
                    kb[par][:, hl, :], ktok[:, ct, khl, :], AF.Copy, scale=kbs[:, ct, h:h + 1]), writes=[P("kb")])
                cx.op("act", lambda e, hl=hl, h=h, khl=khl: e.activation(
                    kd[par][:, hl, :], ktok[:, ct, khl, :], AF.Copy, scale=egr[:, ct, h:h + 1]), writes=[P("kd")])
                yield
            ptv = ps[bA][:, :].bitcast(BF16)

            def mmT(e):
                ins = None
                for hl in range(4):
                    ins = e.transpose(ptv[:, hl * 128:(hl + 1) * 128], AO[:, hl, :], ident[:])
                return ins
            cx.op("pe", mmT, reads=[(P("A"), 2), K("ident")], writes=[pk[bA]])
            yield
            cx.op("act", lambda e: e.activation(fl(AtOp), ptv[:, 0:512], AF.Copy),
                  reads=[pk[bA]], writes=[(P("At"), 2)])
            yield
            cx.op("dve", lambda e: e.tensor_tensor(out=fl(Ttp), in0=fl(AtOp), in1=fl(id4), op=ALU.add),
                  reads=[(P("At"), 2), K("id4")], writes=[P("Tt")])
            yield
            cur = 2
            NSTEP = 6
            for p in range(1, NSTEP + 1):
                nxt = 0 if cur == 2 else 1 - cur
                A, At = bufs[cur]
                An, Atn = bufs[nxt]
                last = (p == NSTEP)

                def mmsq(e, A=A, At=At, last=last):
                    ins = None
                    for hl in range(4):
                        ins = e.matmul(ps[bA][:, hl * 128:(hl + 1) * 128], At[:, hl, :], A[:, hl, :], start=True, stop=True)
                    if not last:
                        for hl in range(4):
                            ins = e.matmul(ps[bAt][:, hl * 128:(hl + 1) * 128], A[:, hl, :], At[:, hl, :],
                                           start=True, stop=True)
                    return ins
                cx.op("pe", mmsq, reads=[(P("A"), cur), (P("At"), cur)], writes=[pk[bA]] + ([] if last else [pk[bAt]]))
                yield
                cx.op("act", lambda e, An=An: e.activation(fl(An), ps[bA][:], AF.Copy),
                      reads=[pk[bA]], writes=[(P("A"), nxt)])
                if not last:
                    cx.op("act", lambda e, Atn=Atn: e.activation(fl(Atn), ps[bAt][:], AF.Copy),
                          reads=[pk[bAt]], writes=[(P("At"), nxt)])
                yield

                def mmtu(e, An=An):
                    ins = None
                    for hl in range(4):
                        ins = e.matmul(ps[bTu][:, hl * 128:(hl + 1) * 128], An[:, hl, :], Ttp[:, hl, :], start=True, stop=True)
                    return ins
                cx.op("pe", mmtu, reads=[(P("A"), nxt), P("Tt")], writes=[pk[bTu]])
                yield
                cx.op("dve", lambda e: e.tensor_tensor(out=fl(Ttp), in0=fl(Ttp), in1=ps[bTu][:], op=ALU.add),
                      reads=[pk[bTu], P("Tt")], writes=[P("Tt")])
                cur = nxt
                if p == 3:
                    yield "MID"
                else:
                    yield
            T0, Rb = Atbuf0[par], Abuf0[par]
            ptv3 = ps[bAt][:, :].bitcast(BF16)

            def mmT0(e):
                ins = None
                for hl in range(4):
                    ins = e.transpose(ptv3[:, hl * 128:(hl + 1) * 128], Ttp[:, hl, :], ident[:])
                return ins
            cx.op("pe", mmT0, reads=[P("Tt"), K("ident")], writes=[pk[bAt]])
            yield
            cx.op("act", lambda e: e.activation(fl(T0), ptv3[:, 0:512], AF.Copy),
                  reads=[pk[bAt]], writes=[(P("At"), 0)])
            yield

            def mmR(e):
                ins = None
                for hl in range(4):
                    ins = e.matmul(ps[bA][:, hl * 128:(hl + 1) * 128], AtOp[:, hl, :], T0[:, hl, :], start=True, stop=True)
                return ins
            cx.op("pe", mmR, reads=[(P("At"), 2), (P("At"), 0)], writes=[pk[bA]])
            yield
            cx.op("dve", lambda e: e.scalar_tensor_tensor(out=fl(rr), in0=fl(T0), scalar=-1.0, in1=ps[bA][:],
                                                           op0=ALU.mult, op1=ALU.add),
                  reads=[(P("At"), 0), pk[bA]], writes=[K("rr")])
            cx.op("dve", lambda e: e.tensor_tensor(out=fl(Rb), in0=fl(rr), in1=fl(id4), op=ALU.add),
                  reads=[K("rr"), K("id4")], writes=[(P("A"), 0)])
            yield

            def mmU(e):
                ins = None
                for hl in range(4):
                    ins = e.matmul(ps[bTu][:, hl * 128:(hl + 1) * 128], Rb[:, hl, :], Ttp[:, hl, :], start=True, stop=True)
                return ins
            cx.op("pe", mmU, reads=[(P("A"), 0), P("Tt")], writes=[pk[bTu]])
            yield
            cx.op("dve", lambda e: e.tensor_tensor(out=fl(Ttp), in0=fl(Ttp), in1=ps[bTu][:], op=ALU.add),
                  reads=[pk[bTu], P("Tt")], writes=[P("Tt")])
            yield

            def mmuw(e):
                ins = None
                for hl in range(4):
                    e.matmul(ps[bA][:, hl * 128:(hl + 1) * 128], Ttp[:, hl, :], vb[par][:, hl, :], start=True, stop=True)
                    ins = e.matmul(ps[bAt][:, hl * 128:(hl + 1) * 128], kb[par][:, hl, :], Ttp[:, hl, :], start=True, stop=True)
                return ins
            cx.op("pe", mmuw, reads=[P("Tt"), P("vb"), P("kb")], writes=[pk[bA], pk[bAt]])
            yield
            cx.op("act", lambda e: e.activation(fl(u), ps[bA][:], AF.Copy), reads=[pk[bA]], writes=[K("u")])
            cx.op("act", lambda e: e.activation(fl(wT), ps[bAt][:], AF.Copy), reads=[pk[bAt]], writes=[K("wT")])
            yield

            def mmvn(e):
                ins = None
                for hl in range(4):
                    ins = e.matmul(ps[bTu][:, hl * 128:(hl + 1) * 128], wT[:, hl, :], S16[:, hl, :], start=True, stop=True)
                return ins
            cx.op("pe", mmvn, reads=[K("wT"), K("S16")], writes=[pk[bTu]])
            yield
            cx.op("dve", lambda e: e.tensor_tensor(out=fl(vnew), in0=fl(u), in1=ps[bTu][:], op=ALU.subtract),
                  reads=[K("u"), pk[bTu]], writes=[K("vnew")])
            yield

            def mmo(e):
                ins = None
                for hl in range(4):
                    e.matmul(ps[bA][:, hl * 128:(hl + 1) * 128], S16[:, hl, :], qdT[par][:, hl, :], start=True, stop=False)
                    e.matmul(ps[bA][:, hl * 128:(hl + 1) * 128], vnew[:, hl, :], qkT[par][:, hl, :], start=False, stop=True)
                for hl in range(4):
                    ins = e.matmul(ps[bAt][:, hl * 128:(hl + 1) * 128], kd[par][:, hl, :], vnew[:, hl, :], start=True, stop=True)
                return ins
            cx.op("pe", mmo, reads=[K("S16"), P("qdT"), K("vnew"), P("qkT"), P("kd")], writes=[pk[bA], pk[bAt]])
            yield
            ob = ct % 2
            for hl in range(4):
                h = hs[hl]
                cx.op("dve", lambda e, hl=hl, h=h: e.scalar_tensor_tensor(
                    out=S32[:, hl, :], in0=S32[:, hl, :], scalar=cd[:, ct, h:h + 1], in1=ps[bAt][:, hl * 128:(hl + 1) * 128],
                    op0=ALU.mult, op1=ALU.add), reads=[pk[bAt], K("S32")], writes=[K("S32")])
            cx.op("act", lambda e: e.activation(fl(S16), fl(S32), AF.Copy), reads=[K("S32")], writes=[K("S16")])
            cx.op("act", lambda e: e.activation(fl(ost[ob]), ps[bA][:], AF.Copy),
                  reads=[pk[bA]], writes=[(K("ost"), ob)])
            for hl in range(4):
                h = hs[hl]
                cx.dma("sp", oraw_d[h * 128:(h + 1) * 128, ct * 128:(ct + 1) * 128], ost[ob][:, hl, :],
                       reads=[(K("ost"), ob)], writes=[(okey, h, ct)], skey=("ost", ob, hl))
            yield

        ntile = DBG.get('tiles', NTILE)
        active, nxt_t = [], [0]

        def start():
            if nxt_t[0] < ntile and len(active) < 2:
                active.append(g2_tile(nxt_t[0], list(hs)))
                nxt_t[0] += 1
        start()
        while active:
            for g in list(active):
                try:
                    v = next(g)
                except StopIteration:
                    active.remove(g)
                    start()
                    continue
                if v == "MID":
                    start()
        cx.barrier()
        cx.pop()
    cx.barrier()
    cx.pop()


def gdn_out_phase(cx, c, tag, x_in, x_out, oraw_d, nw_d, gnw_d, w_in, w_out, xkey_in, xkey_out, okey):
    cx.push()
    HT = 2048
    NT = HT // TT
    WB = 256
    nheads = NVH
    K = lambda n: tag + n
    hT = cx.sbuf(tag + "hT", [128, KC, HT], BF16)
    osb = cx.sbuf(tag + "osb", [128, nheads, HT], BF16)
    gnw = cx.sbuf(tag + "gnw", [128, 1], F32)
    nw = cx.sbuf(tag + "nw", [128, KC], F32)
    xt = [cx.sbuf(tag + "xt%d" % i, [128, KC, TT], F32) for i in range(1)]
    sq = cx.sbuf(tag + "sq", [128, KC, TT], BF16)
    rstd = cx.sbuf(tag + "rstd", [128, TT], F32)
    wz = [cx.sbuf(tag + "wz%d" % i, [128, KC, 128], BF16) for i in range(2)]
    orw = [cx.sbuf(tag + "orw%d" % i, [128, TT], F32) for i in range(3)]
    osq = [cx.sbuf(tag + "osq%d" % i, [128, TT], BF16) for i in range(3)]
    orn = [cx.sbuf(tag + "orn%d" % i, [128, TT], F32) for i in range(3)]
    sz = [cx.sbuf(tag + "sz%d" % i, [128, TT], F32) for i in range(3)]
    wo = [cx.sbuf(tag + "wo%d" % i, [128, nheads, WB], BF16) for i in range(2)]
    xr = [cx.sbuf(tag + "xr%d" % i, [128, TT], F32) for i in range(2)]
    xn = [cx.sbuf(tag + "xn%d" % i, [128, TT], F32) for i in range(2)]
    ps = [cx.psum(tag + "ps%d" % i, [128, TT], F32) for i in range(8)]
    pk = [tag + "ps%d" % i for i in range(8)]
    cx.psum_keys.update(pk)
    w_in_r = w_in.rearrange("(kc p) n -> p kc n", p=128)
    w_out_r = w_out.rearrange("(kc p) n -> p kc n", p=128)
    x_in_r = x_in.rearrange("(kc p) t -> p kc t", p=128)
    cx.dma("sp", nw[:], nw_d, writes=[K("nw")], skey="nw")
    cx.dma("sp", gnw[:], gnw_d, writes=[K("gnw")], skey="gnw")
    cnt = {"o": 0, "x": 0, "z": 0, "w": 0}
    for hf in range(T // HT):
        t0 = hf * HT
        for tt in range(NT):
            gt_ = (t0 + tt * TT) // TT
            cx.dma("sp", xt[0][:], x_in_r[:, :, t0 + tt * TT: t0 + (tt + 1) * TT],
                   reads=[(xkey_in, gt_, kc) for kc in range(KC)], writes=[K("xt")], skey="xt")
            rmsnorm_tile(cx, c, xt[0], nw, lambda kc, tt=tt: hT[:, kc, tt * TT:(tt + 1) * TT], tag,
                         [K("xt")], (K("hT"), tt), pk[6], ps[6], {"sq": sq, "rstd": rstd})
        def gate_piece(h, ws, tt, item):
            b = item % 3
            bZ, bQ = b, 3 + b
            gt_ = (t0 + tt * TT) // TT
            cx.dma("sp", orw[b][:], oraw_d[h * 128:(h + 1) * 128, t0 + tt * TT: t0 + (tt + 1) * TT],
                   reads=[(okey, h, gt_)], writes=[(K("orw"), b)], skey=("orw", b))

            def mmz(e):
                ins = None
                for kc in range(KC):
                    ins = e.matmul(ps[bZ][:], wz[ws][:, kc, :], hT[:, kc, tt * TT:(tt + 1) * TT],
                                   start=(kc == 0), stop=(kc == KC - 1))
                return ins
            cx.op("pe", mmz, reads=[(K("wz"), ws), (K("hT"), tt)], writes=[pk[bZ]])
            yield
            cx.op("act", lambda e: e.activation(osq[b][:], orw[b][:], AF.Square),
                  reads=[(K("orw"), b)], writes=[(K("osq"), b)])
            yield "NEXT"
            cx.op("pe", lambda e: e.matmul(ps[bQ][:], c["ones_bf"][:], osq[b][:], start=True, stop=True),
                  reads=[(K("osq"), b), "ones_bf"], writes=[pk[bQ]])
            yield
            cx.op("act", lambda e: e.activation(orn[b][:], ps[bQ][:], AF.Sqrt, bias=EPS, scale=1.0 / 128),
                  reads=[pk[bQ]], writes=[(K("orn"), b)])
            cx.op("act", lambda e: e.activation(sz[b][:], ps[bZ][:], AF.Silu), reads=[pk[bZ]],
                  writes=[(K("sz"), b)])
            yield
            cx.op("dve", lambda e: e.reciprocal(orn[b][:], orn[b][:]), reads=[(K("orn"), b)],
                  writes=[(K("orn"), b)])
            yield
            cx.op("dve", lambda e: e.scalar_tensor_tensor(
                out=orn[b][:], in0=orw[b][:], scalar=gnw[:, 0:1], in1=orn[b][:], op0=ALU.mult, op1=ALU.mult),
                reads=[(K("orw"), b), (K("orn"), b), K("gnw")], writes=[(K("orn"), b)])
            yield
            cx.op("dve", lambda e: e.tensor_tensor(
                out=osb[:, h, tt * TT:(tt + 1) * TT], in0=orn[b][:], in1=sz[b][:], op=ALU.mult),
                reads=[(K("orn"), b), (K("sz"), b)], writes=[(K("osb"), h)])
            yield

        gitems = [(h, tt) for h in range(nheads) for tt in range(NT)]
        act_g, nx = [], [0]

        def start_g():
            if nx[0] < len(gitems) and len(act_g) < 3:
                h, tt = gitems[nx[0]]
                if tt == 0:
                    ws = cnt["w"] % 2
                    cnt["w"] += 1
                    cx.dma("pool", wz[ws][:], w_in_r[:, :, 4096 + h * 128: 4096 + (h + 1) * 128],
                           writes=[(K("wz"), ws)], skey=("wz", ws))
                ws = (cnt["w"] - 1) % 2
                act_g.append(gate_piece(h, ws, tt, cnt["z"]))
                cnt["z"] += 1
                nx[0] += 1
        start_g()
        while act_g:
            for g_ in list(act_g):
                try:
                    v = next(g_)
                except StopIteration:
                    act_g.remove(g_)
                    start_g()
                    continue
                if v == "NEXT" and g_ is act_g[-1]:
                    start_g()
        for mb in range(D // WB):
            s = cnt["o"] % 2
            cnt["o"] += 1
            cx.dma("pool", wo[s][:], w_out_r[:, :, mb * WB:(mb + 1) * WB], writes=[(K("wo"), s)], skey=("wo", s))
            for mm_ in range(WB // 128):
                mc = mb * (WB // 128) + mm_
                for tt in range(NT):
                    gt_ = (t0 + tt * TT) // TT
                    b = cnt["x"] % 2
                    cnt["x"] += 1
                    py = ps[6 + b]
                    cx.dma("sp", xr[b][:], x_in[mc * 128:(mc + 1) * 128, t0 + tt * TT: t0 + (tt + 1) * TT],
                           reads=[(xkey_in, gt_, mc)], writes=[(K("xr"), b)], skey=("xr", b))

                    def mm2(e, s=s, mm_=mm_, tt=tt, py=py):
                        ins = None
                        for kc in range(nheads):
                            ins = e.matmul(py[:], wo[s][:, kc, mm_ * 128:(mm_ + 1) * 128],
                                           osb[:, kc, tt * TT:(tt + 1) * TT], start=(kc == 0), stop=(kc == nheads - 1))
                        return ins
                    cx.op("pe", mm2, reads=[(K("wo"), s)] + [(K("osb"), hd) for hd in range(nheads)],
                          writes=[pk[6 + b]])
                    cx.op("dve", lambda e, b=b, py=py: e.tensor_tensor(
                        out=xn[b][:], in0=py[:], in1=xr[b][:], op=ALU.add),
                        reads=[pk[6 + b], (K("xr"), b)], writes=[(K("xn"), b)])
                    cx.dma("sp", x_out[mc * 128:(mc + 1) * 128, t0 + tt * TT: t0 + (tt + 1) * TT], xn[b][:],
                           reads=[(K("xn"), b)], writes=[(xkey_out, gt_, mc)], skey=("xn", b))
    cx.barrier()
    cx.pop()


def build_gdn_prog():
    nc = bass.Bass("TRN2", target_bir_lowering=False)
    x = nc.dram_tensor("x", [D, T], F32, kind="ExternalInput").ap()
    nw = nc.dram_tensor("nw", [128, KC], F32, kind="ExternalInput").ap()
    w_in = nc.dram_tensor("w_in", [D, GP], F32, kind="ExternalInput").ap()
    w_out = nc.dram_tensor("w_out", [NVH * 128, D], F32, kind="ExternalInput").ap()
    convw = nc.dram_tensor("convw", [128, 32, 4], F32, kind="ExternalInput").ap()
    alog = nc.dram_tensor("alog", [128, 512], F32, kind="ExternalInput").ap()
    dtb = nc.dram_tensor("dtb", [128, 512], F32, kind="ExternalInput").ap()
    gnw = nc.dram_tensor("gnw", [128, 1], F32, kind="ExternalInput").ap()
    cst = nc.dram_tensor("cst", [128, CST_W], F32, kind="ExternalInput").ap()
    oraw = nc.dram_tensor("oraw_scratch", [NVH * 128, T], F32, kind=("ExternalOutput" if DBG.get("noout") else "Internal")).ap()
    y = nc.dram_tensor("y", [D, T], F32, kind="ExternalOutput").ap()
    cx = Ctx(nc)
    c = consts(cx)
    gdn_core_phase(cx, c, "g_", x, oraw, nw, w_in, convw, alog, dtb, cst, "xin", "or")
    if not DBG.get("noout"):
        gdn_out_phase(cx, c, "go_", x, y, oraw, nw, gnw, w_in, w_out, "xin", "xout", "or")
    cx.finish()
    return nc


def final_norm_phase(cx, c, tag, x_in, y_out, nw_d, xkey_in):
    cx.push()
    xt = [cx.sbuf(tag + "xt%d" % i, [128, KC, TT], F32) for i in range(2)]
    yo = [cx.sbuf(tag + "yo%d" % i, [128, KC, TT], F32) for i in range(2)]
    sq = cx.sbuf(tag + "sq", [128, KC, TT], BF16)
    rstd = cx.sbuf(tag + "rstd", [128, TT], F32)
    nw = cx.sbuf(tag + "nw", [128, KC], F32)
    ps = cx.psum(tag + "ps", [128, TT], F32)
    cx.psum_keys.add(tag + "ps")
    cx.dma("sp", nw[:], nw_d, writes=[tag + "nw"], skey="nw")
    x_in_r = x_in.rearrange("(kc p) t -> p kc t", p=128)
    y_out_r = y_out.rearrange("(kc p) t -> p kc t", p=128)
    for tt in range(NTT):
        b = tt % 2
        cx.dma("sp", xt[b][:], x_in_r[:, :, tt * TT:(tt + 1) * TT],
               reads=[(xkey_in, tt, kc) for kc in range(KC)], writes=[(tag + "xt", b)], skey=("xt", b))
        rmsnorm_tile(cx, c, xt[b], nw, lambda kc, b=b: yo[b][:, kc, :], tag,
                     [(tag + "xt", b)], (tag + "yo", b), tag + "ps", ps, {"sq": sq, "rstd": rstd})
        cx.dma("sp", y_out_r[:, :, tt * TT:(tt + 1) * TT], yo[b][:], reads=[(tag + "yo", b)],
               writes=[("yout", tt)], skey=("yo", b))
    cx.barrier()
    cx.pop()


def build_fin_prog():
    nc = bass.Bass("TRN2", target_bir_lowering=False)
    x = nc.dram_tensor("x", [D, T], F32, kind="ExternalInput").ap()
    nw = nc.dram_tensor("nw", [128, KC], F32, kind="ExternalInput").ap()
    y = nc.dram_tensor("y", [D, T], F32, kind="ExternalOutput").ap()
    cx = Ctx(nc)
    c = consts(cx)
    final_norm_phase(cx, c, "n_", x, y, nw, "xin")
    cx.finish()
    return nc


def build_full_prog():
    nc = bass.Bass("TRN2", target_bir_lowering=False)
    dt = lambda name, shape, kind="ExternalInput", dtype=F32: nc.dram_tensor(name, list(shape), dtype, kind=kind).ap()
    x = dt("x", [D, T])
    nws = dt("nws", [13, 128, KC])
    ffn_w_in = dt("ffn_w_in", [4, 2, D, 2 * DFF])
    ffn_w_out = dt("ffn_w_out", [4, 2, DFF, D])
    attn_w_in = dt("attn_w_in", [2, D, 3 * NH_A * 128])
    attn_w_out = dt("attn_w_out", [2, NH_A * 128, D])
    gdn_w_in = dt("gdn_w_in", [2, D, GP])
    gdn_w_out = dt("gdn_w_out", [2, NVH * 128, D])
    convw = dt("convw", [2, 128, 32, 4])
    alog = dt("alog", [2, 128, 512])
    dtb = dt("dtb", [2, 128, 512])
    gnw = dt("gnw", [2, 128, 1])
    ctab = dt("ctab", [128, T])
    stab = dt("stab", [128, T])
    cst = dt("cst", [128, CST_W])
    y = dt("y", [D, T], kind="ExternalOutput")
    xbuf = [nc.dram_tensor("xres%d" % i, [D, T], F32).ap() for i in range(2)]
    os_d = nc.dram_tensor("os_scratch", [NH_A * 128, T], BF16).ap()
    oraw = nc.dram_tensor("oraw_scratch", [NVH * 128, T], F32).ap()
    cx = Ctx(nc)
    c = consts(cx)
    cur, nb_, ph = x, 0, 0
    ia = ib = 0
    for i in range(4):
        ffn_phase(cx, c, "f%da_" % i, cur, xbuf[nb_], nws[ph], ffn_w_in[i, 0], ffn_w_out[i, 0], "xi", "xo")
        cur, nb_, ph = xbuf[nb_], 1 - nb_, ph + 1
        if i % 2 == 0:
            attn_phase(cx, c, "a%d_" % i, cur, os_d, nws[ph], attn_w_in[ia], ctab, stab, cst, "xi", "os")
            outproj_phase(cx, c, "ao%d_" % i, cur, xbuf[nb_], os_d, attn_w_out[ia], NH_A, "xi", "xo", "os")
            ia += 1
        else:
            gdn_core_phase(cx, c, "g%d_" % i, cur, oraw, nws[ph], gdn_w_in[ib], convw[ib], alog[ib], dtb[ib], cst, "xi", "or")
            gdn_out_phase(cx, c, "go%d_" % i, cur, xbuf[nb_], oraw, nws[ph], gnw[ib], gdn_w_in[ib], gdn_w_out[ib],
                          "xi", "xo", "or")
            ib += 1
        cur, nb_, ph = xbuf[nb_], 1 - nb_, ph + 1
        ffn_phase(cx, c, "f%db_" % i, cur, xbuf[nb_], nws[ph], ffn_w_in[i, 1], ffn_w_out[i, 1], "xi", "xo")
        cur, nb_, ph = xbuf[nb_], 1 - nb_, ph + 1
    final_norm_phase(cx, c, "n_", cur, y, nws[ph], "xi")
    cx.finish()
    return nc


_PROGS = {}


def _prog(name):
    if name not in _PROGS:
        _PROGS[name] = {"ffn": build_ffn_prog, "attn": build_attn_prog, "gdn": build_gdn_prog,
                        "fin": build_fin_prog}[name]()
    return _PROGS[name]


def _lay_nw(v):
    return np.ascontiguousarray(np.asarray(v, np.float32).reshape(KC, 128).T)


def _launch(name, xs, shared):
    nc = _prog(name)
    n = len(xs)
    in_maps = [dict(shared, x=xs[i]) for i in range(n)]
    res = run_bass_kernel_spmd(nc, in_maps, core_ids=list(range(n)))
    return [np.asarray(res.results[i]["y"]) for i in range(n)]


def kernel(x, norm_w, ffn_w_in, ffn_w_out, attn_w_in, attn_w_out, gdn_w_in, gdn_conv_w, gdn_a_log,
           gdn_dt_bias, gdn_norm_w, gdn_w_out, final_norm_w):
    f = lambda a: np.ascontiguousarray(np.asarray(a, np.float32))
    x = f(x)
    B = x.shape[0]
    ctab, stab = host_rope()
    nws = np.stack([_lay_nw(norm_w[i, j]) for i in range(4) for j in range(3)] + [_lay_nw(final_norm_w)], 0)
    shared = {
        "nws": np.ascontiguousarray(nws), "ffn_w_in": f(ffn_w_in), "ffn_w_out": f(ffn_w_out),
        "attn_w_in": f(attn_w_in), "attn_w_out": f(attn_w_out), "gdn_w_in": f(gdn_w_in), "gdn_w_out": f(gdn_w_out),
        "convw": np.ascontiguousarray(f(gdn_conv_w).reshape(2, 4, 32, 128).transpose(0, 3, 2, 1)),
        "alog": np.ascontiguousarray(np.tile(f(gdn_a_log)[:, None, :], (1, 128, 32))),
        "dtb": np.ascontiguousarray(np.tile(f(gdn_dt_bias)[:, None, :], (1, 128, 32))),
        "gnw": np.ascontiguousarray(f(gdn_norm_w).reshape(2, 128, 1)),
        "ctab": ctab, "stab": stab, "cst": host_consts(),
    }
    if "full" not in _PROGS:
        _PROGS["full"] = build_full_prog()
    nc = _PROGS["full"]
    in_maps = [dict(shared, x=np.ascontiguousarray(x[b].T)) for b in range(B)]
    res = run_bass_kernel_spmd(nc, in_maps, core_ids=list(range(B)))
    return np.stack([np.asarray(res.results[b]["y"]).T for b in range(B)], 0).astype(np.float32)
```

```python
from contextlib import ExitStack
import numpy as np
import concourse.bass as bass
import concourse.mybir as mybir
from concourse.bass_utils import run_bass_kernel_spmd

F32 = mybir.dt.float32
BF16 = mybir.dt.bfloat16
ALU = mybir.AluOpType
AF = mybir.ActivationFunctionType

ENGS = ("pe", "act", "dve", "pool", "sp")
NAMES = []


class _Op:
    __slots__ = ("eng", "fn", "deps", "idx", "is_dma", "dsem", "dval", "need_inc", "cnt")

    def __init__(self, eng, fn, is_dma):
        self.eng = eng
        self.fn = fn
        self.deps = []
        self.is_dma = is_dma
        self.dsem = None
        self.dval = 0
        self.need_inc = False
        self.cnt = 0


class Ctx:
    def __init__(self, nc):
        self.nc = nc
        self.q = {e: [] for e in ENGS}
        self.wr = {}
        self.rd = {}
        self.dsems = {}
        self.pending_dma = []
        self.sem_pool = {}
        self.psum_keys = set()
        self.stack = ExitStack()
        self.scopes = []
        self.esem = {e: nc.alloc_semaphore("sem_" + e) for e in ENGS}
        self.n_psum = 0

    def push(self):
        st = ExitStack()
        self.scopes.append(st)

    def pop(self):
        self.scopes.pop().close()

    def _scope(self):
        return self.scopes[-1] if self.scopes else self.stack

    def sbuf(self, name, shape, dtype):
        self.n_psum += 1
        NAMES.append("%s_%d" % (name, self.n_psum))
        return self._scope().enter_context(self.nc.sbuf_tensor("%s_%d" % (name, self.n_psum), list(shape), dtype))

    def psum(self, name, shape, dtype=F32):
        self.n_psum += 1
        return self._scope().enter_context(self.nc.psum_tensor("%s_%d" % (name, self.n_psum), list(shape), dtype))

    def _track(self, op, reads, writes):
        deps = op.deps
        for k in reads:
            for o in self.wr.get(k, {}).values():
                deps.append(o)
            if k in self.psum_keys:
                for ek2, o in self.rd.get(k, {}).items():
                    if o.eng != op.eng:
                        deps.append(o)
        for k in writes:
            for o in self.wr.get(k, {}).values():
                deps.append(o)
            for o in self.rd.get(k, {}).values():
                deps.append(o)
        ek = id(op) if op.is_dma else op.eng
        for k in reads:
            self.rd.setdefault(k, {})[ek] = op
        for k in writes:
            self.wr[k] = {ek: op}
            self.rd[k] = {}

    def op(self, eng, fn, reads=(), writes=()):
        o = _Op(eng, fn, False)
        self._track(o, reads, writes)
        o.idx = len(self.q[eng])
        self.q[eng].append(o)
        return o

    def dma(self, eng, out, in_, reads=(), writes=(), skey=None, **kw):
        key = ("dma", skey)
        o = _Op(eng, lambda e: e.dma_start(out=out, in_=in_, **kw), True)
        self._track(o, reads, writes)
        key = (eng == "pool", key)
        if key not in self.dsems:
            pool = self.sem_pool.setdefault(eng == "pool", [])
            i = sum(1 for k in self.dsems if k[0] == key[0])
            if i >= len(pool):
                pool.append([self.nc.alloc_semaphore("d%s%d" % ("s" if key[0] else "h", i)), 0])
            self.dsems[key] = pool[i]
        ds = self.dsems[key]
        ds[1] += 16
        o.dsem = ds[0]
        o.dval = ds[1]
        o.idx = len(self.q[eng])
        self.q[eng].append(o)
        self.pending_dma.append(o)
        return o

    def barrier(self):
        lasts = [self.q[e][-1] for e in ENGS if self.q[e] and not self.q[e][-1].is_dma]
        lasts += [o for e in ENGS for o in self.q[e][-1:] if False]
        dm = list(self.pending_dma)
        self.pending_dma = []
        for e in ENGS:
            o = _Op(e, None, False)
            for e2 in ENGS:
                for p in reversed(self.q[e2]):
                    if not p.is_dma and p.fn is not None:
                        o.deps.append(p)
                        break
            o.deps.extend(dm)
            o.idx = len(self.q[e])
            self.q[e].append(o)
        self.wr = {}
        self.rd = {}
        self.dsems = {}

    def finish(self):
        nc = self.nc
        self.barrier()
        for e in ENGS:
            for o in self.q[e]:
                for d in o.deps:
                    if not d.is_dma and (d.eng != o.eng or e != "pe"):
                        d.need_inc = True
        for e in ENGS:
            c = 0
            for o in self.q[e]:
                if o.need_inc:
                    c += 1
                o.cnt = c
        engobj = {"pe": "tensor", "act": "scalar", "dve": "vector", "pool": "gpsimd", "sp": "sync"}
        with nc.Block() as block:
            for e in ENGS:
                def body(eng, e=e):
                    known = {}
                    for o in self.q[e]:
                        waits = {}
                        for d in o.deps:
                            if d.is_dma:
                                s, v = d.dsem, d.dval
                            else:
                                if d.eng == e and e == "pe":
                                    continue
                                s, v = self.esem[d.eng], d.cnt
                            if v > waits.get(s, (None, 0))[1]:
                                waits[s] = (s, v)
                        for s, v in waits.values():
                            if known.get(s, 0) >= v:
                                continue
                            known[s] = v
                            eng.wait_ge(s, v)
                        if o.fn is None:
                            continue
                        ins = o.fn(eng)
                        if o.is_dma:
                            ins.then_inc(o.dsem, 16)
                        elif o.need_inc:
                            ins.then_inc(self.esem[e], 1)
                getattr(block, engobj[e])(body)
        self.stack.close()


D = 1024
T = 4096
KC = D // 128
DFF = 2816
JC = DFF // 128
TT = 512
NTT = T // TT
EPS = 1e-6


def consts(cx):
    nc = cx.nc
    c = {}
    c["ones_bf"] = cx.sbuf("ones_bf", [128, 128], BF16)
    cx.op("pool", lambda e: e.memset(c["ones_bf"][:], 1.0), writes=["ones_bf"])
    return c


def rmsnorm_tile(cx, c, xt, nw, hT_out, tag, rd_keys, wr_key, ps_key, ps, scr, part=None):
    sq, rstd = scr["sq"], scr["rstd"]
    if part in (None, "A"):
        cx.op("act", lambda e: e.activation(sq[:], xt[:], AF.Square), reads=rd_keys, writes=[tag + "sq"])
    if part == "A":
        return

    def mm(e):
        ins = None
        for kc in range(KC):
            ins = e.matmul(ps[:], c["ones_bf"][:], sq[:, kc, :], start=(kc == 0), stop=(kc == KC - 1))
        return ins
    cx.op("pe", mm, reads=[tag + "sq", "ones_bf"], writes=[ps_key])
    cx.op("act", lambda e: e.activation(rstd[:], ps[:], AF.Sqrt, bias=EPS, scale=1.0 / D),
          reads=[ps_key], writes=[tag + "rstd"])
    cx.op("dve", lambda e: e.reciprocal(rstd[:], rstd[:]), reads=[tag + "rstd"], writes=[tag + "rstd"])
    for kc in range(KC):
        cx.op("dve", lambda e, kc=kc: e.scalar_tensor_tensor(
            out=hT_out(kc), in0=xt[:, kc, :], scalar=nw[:, kc:kc + 1], in1=rstd[:],
            op0=ALU.mult, op1=ALU.mult), reads=rd_keys + [tag + "rstd", tag + "nw"], writes=[wr_key])


def ffn_phase(cx, c, tag, x_in, x_out, nw_d, w_in, w_out, xkey_in, xkey_out):
    nc = cx.nc
    cx.push()
    HT = 2048
    NH = T // HT
    NT = HT // TT
    hT = cx.sbuf(tag + "hT", [128, KC, HT], BF16)
    act = cx.sbuf(tag + "act", [128, JC, HT], BF16)
    xt = cx.sbuf(tag + "xt", [128, KC, TT], F32)
    sq = cx.sbuf(tag + "sq", [128, KC, TT], BF16)
    rstd = cx.sbuf(tag + "rstd", [128, TT], F32)
    nw = cx.sbuf(tag + "nw", [128, KC], F32)
    WB = 256
    wg = [cx.sbuf(tag + "wg%d" % i, [128, KC, WB], BF16) for i in range(2)]
    wu = [cx.sbuf(tag + "wu%d" % i, [128, KC, WB], BF16) for i in range(2)]
    wo = [cx.sbuf(tag + "wo%d" % i, [128, JC, WB], BF16) for i in range(2)]
    sg = [cx.sbuf(tag + "sg%d" % i, [128, TT], F32) for i in range(2)]
    xr = [cx.sbuf(tag + "xr%d" % i, [128, TT], F32) for i in range(2)]
    xn = [cx.sbuf(tag + "xn%d" % i, [128, TT], F32) for i in range(2)]
    ps = [cx.psum(tag + "ps%d" % i, [128, TT], F32) for i in range(7)]
    pk = [tag + "ps%d" % i for i in range(7)]
    cx.psum_keys.update(pk)

    cx.dma("sp", nw[:], nw_d, writes=[tag + "nw"], skey="nw")
    w_in_r = w_in.rearrange("(kc p) n -> p kc n", p=128)
    w_out_r = w_out.rearrange("(kc p) n -> p kc n", p=128)
    x_in_r = x_in.rearrange("(kc p) t -> p kc t", p=128)
    cnt = {"g": 0, "o": 0, "s": 0, "x": 0}
    def norm_tile_ops(h, tt, part=None):
        t0_ = h * HT
        gt = (t0_ + tt * TT) // TT
        if part in (None, "A"):
            cx.dma("sp", xt[:], x_in_r[:, :, t0_ + tt * TT: t0_ + (tt + 1) * TT],
                   reads=[(xkey_in, gt, kc) for kc in range(KC)], writes=[tag + "xt"], skey="xt")
        rmsnorm_tile(cx, c, xt, nw, lambda kc, tt=tt: hT[:, kc, tt * TT:(tt + 1) * TT], tag,
                     [tag + "xt"], (tag + "hT", tt), pk[6], ps[6], {"sq": sq, "rstd": rstd}, part=part)

    for h in range(NH):
        t0 = h * HT
        if h == 0:
            for tt in range(NT):
                norm_tile_ops(0, tt)
        for jb in range(JC * 128 // WB):
            s = cnt["g"] % 2
            cnt["g"] += 1
            cx.dma("pool", wg[s][:], w_in_r[:, :, jb * WB:(jb + 1) * WB], writes=[(tag + "wg", s)],
                   skey=("wg", s))
            cx.dma("pool", wu[s][:], w_in_r[:, :, DFF + jb * WB: DFF + (jb + 1) * WB], writes=[(tag + "wu", s)],
                   skey=("wu", s))
            for jj in range(WB // 128):
                j = jb * (WB // 128) + jj
                for tt in range(NT):
                    b = cnt["s"] % 2
                    cnt["s"] += 1
                    pg, pu = ps[b], ps[2 + b]

                    def mm(e, s=s, jj=jj, tt=tt, pg=pg, pu=pu):
                        ins = None
                        for kc in range(KC):
                            ins = e.matmul(pg[:], wg[s][:, kc, jj * 128:(jj + 1) * 128],
                                           hT[:, kc, tt * TT:(tt + 1) * TT], start=(kc == 0), stop=(kc == KC - 1))
                        for kc in range(KC):
                            ins = e.matmul(pu[:], wu[s][:, kc, jj * 128:(jj + 1) * 128],
                                           hT[:, kc, tt * TT:(tt + 1) * TT], start=(kc == 0), stop=(kc == KC - 1))
                        return ins
                    cx.op("pe", mm, reads=[(tag + "wg", s), (tag + "wu", s), (tag + "hT", tt)],
                          writes=[pk[b], pk[2 + b]])
                    cx.op("act", lambda e, b=b, pg=pg: e.activation(sg[b][:], pg[:], AF.Silu),
                          reads=[pk[b]], writes=[(tag + "sg", b)])
                    cx.op("dve", lambda e, b=b, pu=pu, j=j, tt=tt: e.tensor_tensor(
                        out=act[:, j, tt * TT:(tt + 1) * TT], in0=sg[b][:], in1=pu[:], op=ALU.mult),
                        reads=[(tag + "sg", b), pk[2 + b]], writes=[(tag + "act", j, tt)])
        for mb in range(D // WB):
            s = cnt["o"] % 2
            cnt["o"] += 1
            cx.dma("pool", wo[s][:], w_out_r[:, :, mb * WB:(mb + 1) * WB], writes=[(tag + "wo", s)],
                   skey=("wo", s))
            for mm_ in range(WB // 128):
                mc = mb * (WB // 128) + mm_
                for tt in range(NT):
                    gt = (t0 + tt * TT) // TT
                    b = cnt["x"] % 2
                    cnt["x"] += 1
                    py = ps[4 + b]
                    cx.dma("sp", xr[b][:], x_in[mc * 128:(mc + 1) * 128, t0 + tt * TT: t0 + (tt + 1) * TT],
                           reads=[(xkey_in, gt, mc)], writes=[(tag + "xr", b)], skey=("xr", b))

                    def mm2(e, s=s, mm_=mm_, tt=tt, py=py):
                        ins = None
                        for kc in range(JC):
                            ins = e.matmul(py[:], wo[s][:, kc, mm_ * 128:(mm_ + 1) * 128],
                                           act[:, kc, tt * TT:(tt + 1) * TT], start=(kc == 0), stop=(kc == JC - 1))
                        return ins
                    cx.op("pe", mm2, reads=[(tag + "wo", s)] + [(tag + "act", j, tt) for j in range(JC)],
                          writes=[pk[4 + b]])
                    cx.op("dve", lambda e, b=b, py=py: e.scalar_tensor_tensor(
                        out=xn[b][:], in0=py[:], scalar=0.5, in1=xr[b][:], op0=ALU.mult, op1=ALU.add),
                        reads=[pk[4 + b], (tag + "xr", b)], writes=[(tag + "xn", b)])
                    cx.dma("sp", x_out[mc * 128:(mc + 1) * 128, t0 + tt * TT: t0 + (tt + 1) * TT], xn[b][:],
                           reads=[(tag + "xn", b)], writes=[(xkey_out, gt, mc)], skey=("xn", b))
                if h + 1 < NH:
                    hk = mc
                    norm_tile_ops(h + 1, hk // 2, part=("A" if hk % 2 == 0 else "B"))
    cx.barrier()
    cx.pop()


def build_ffn_prog():
    nc = bass.Bass("TRN2", target_bir_lowering=False)
    x = nc.dram_tensor("x", [D, T], F32, kind="ExternalInput").ap()
    nw = nc.dram_tensor("nw", [128, KC], F32, kind="ExternalInput").ap()
    w_in = nc.dram_tensor("w_in", [D, 2 * DFF], F32, kind="ExternalInput").ap()
    w_out = nc.dram_tensor("w_out", [DFF, D], F32, kind="ExternalInput").ap()
    y = nc.dram_tensor("y", [D, T], F32, kind="ExternalOutput").ap()
    cx = Ctx(nc)
    c = consts(cx)
    ffn_phase(cx, c, "f_", x, y, nw, w_in, w_out, "xin", "xout")
    cx.finish()
    return nc


NH_A = 12
DIL = (1, 4, 16)
NEG = -30000.0
SCALE_A = 128 ** -0.5
C_IDENT = 0
C_PSWAP = 128
C_MASK = 256
C_TRI = 512
C_TRIS = 640
C_MINC = 768
C_MSTR = 896
C_LAST = 1024
CST_W = 1152


def host_consts():
    m = np.zeros((128, CST_W), np.float32)
    k = np.arange(128)[:, None]
    q = np.arange(128)[None, :]
    m[:, C_IDENT:C_IDENT + 128] = (k == q)
    m[:, C_PSWAP:C_PSWAP + 128] = (k == (q + 64) % 128)
    m[:, C_MASK:C_MASK + 128] = np.where(k <= q, 0.0, NEG)
    m[:, C_MASK + 128:C_MASK + 256] = np.where(k >= q, 0.0, NEG)
    m[:, C_TRI:C_TRI + 128] = (k <= q)
    m[:, C_TRIS:C_TRIS + 128] = (k > q)
    m[:, C_MINC:C_MINC + 128] = (k <= q)
    m[:, C_MSTR:C_MSTR + 128] = (k < q)
    m[:, C_LAST:C_LAST + 128] = (k == 127)
    return m


def host_rope():
    inv = (1.0 / (10000.0 ** (np.arange(0, 128, 2, dtype=np.float32) / np.float32(128)))).astype(np.float32)
    ang = (np.arange(T, dtype=np.float32)[None, :] * inv[:, None]).astype(np.float32)
    cs, sn = np.cos(ang).astype(np.float32), np.sin(ang).astype(np.float32)
    return np.concatenate([cs, cs], 0), np.concatenate([-sn, sn], 0)


def norm_stage(cx, c, tag, x_in, nw_d, hT, xkey_in, ps, pskey):
    cx.push()
    xt = [cx.sbuf(tag + "nxt%d" % i, [128, KC, TT], F32) for i in range(2)]
    sq = cx.sbuf(tag + "nsq", [128, KC, TT], BF16)
    rstd = cx.sbuf(tag + "nrstd", [128, TT], F32)
    nw = cx.sbuf(tag + "nnw", [128, KC], F32)
    cx.dma("sp", nw[:], nw_d, writes=[tag + "nw"], skey="nw")
    x_in_r = x_in.rearrange("(kc p) t -> p kc t", p=128)
    for tt in range(NTT):
        b = tt % 2
        cx.dma("sp", xt[b][:], x_in_r[:, :, tt * TT:(tt + 1) * TT],
               reads=[(xkey_in, tt, kc) for kc in range(KC)], writes=[(tag + "xt", b)], skey=("xt", b))
        rmsnorm_tile(cx, c, xt[b], nw, lambda kc, tt=tt: hT[:, kc, tt * TT:(tt + 1) * TT], tag,
                     [(tag + "xt", b)], (tag + "hT", tt), pskey, ps, {"sq": sq, "rstd": rstd})
    cx.barrier()
    cx.pop()


def attn_phase(cx, c, tag, x_in, os_d, nw_d, w_in, ctab_d, stab_d, cst_d, xkey_in, okey):
    nc = cx.nc
    cx.push()
    hT = cx.sbuf(tag + "hT", [128, KC, T], BF16)
    ctab = cx.sbuf(tag + "ctab", [128, T], F32)
    stab = cx.sbuf(tag + "stab", [128, T], F32)
    ident = cx.sbuf(tag + "ident", [128, 128], BF16)
    pswap = cx.sbuf(tag + "pswap", [128, 128], BF16)
    maskb = cx.sbuf(tag + "maskb", [128, 256], BF16)
    ps = [cx.psum(tag + "ps%d" % i, [128, TT], F32) for i in range(8)]
    pk = [tag + "ps%d" % i for i in range(8)]
    cx.psum_keys.update(pk)
    cx.dma("sp", ctab[:], ctab_d, writes=[tag + "ctab"], skey="ctab")
    cx.dma("sp", stab[:], stab_d, writes=[tag + "stab"], skey="stab")
    cx.dma("pool", ident[:], cst_d[:, C_IDENT:C_IDENT + 128], writes=[tag + "ident"], skey="ident")
    cx.dma("pool", pswap[:], cst_d[:, C_PSWAP:C_PSWAP + 128], writes=[tag + "pswap"], skey="pswap")
    cx.dma("pool", maskb[:], cst_d[:, C_MASK:C_MASK + 256], writes=[tag + "maskb"], skey="maskb")
    norm_stage(cx, c, tag, x_in, nw_d, hT, xkey_in, ps[6], pk[6])
    hkeys = []

    Qd = cx.sbuf(tag + "Qd", [128, T], BF16)
    Kd = cx.sbuf(tag + "Kd", [128, T], BF16)
    Vd = cx.sbuf(tag + "Vd", [128, 32, 128], BF16)
    numT = cx.sbuf(tag + "numT", [128, 3, T], BF16)
    dent = cx.sbuf(tag + "dent", [128, T], F32)
    wq = [cx.sbuf(tag + "wq%d" % i, [128, KC, 128], BF16) for i in range(2)]
    wk = [cx.sbuf(tag + "wk%d" % i, [128, KC, 128], BF16) for i in range(2)]
    wv = [cx.sbuf(tag + "wv%d" % i, [128, KC, 128], BF16) for i in range(2)]
    qb = [cx.sbuf(tag + "qb%d" % i, [128, TT], BF16) for i in range(3)]
    t1 = [cx.sbuf(tag + "t1%d" % i, [128, TT], F32) for i in range(3)]
    t2 = [cx.sbuf(tag + "t2%d" % i, [128, TT], F32) for i in range(3)]
    pt = [cx.sbuf(tag + "pt%d" % i, [128, 256], BF16) for i in range(4)]
    osc = [cx.sbuf(tag + "osc%d" % i, [128, TT], BF16) for i in range(2)]
    w_in_r = w_in.rearrange("(kc p) n -> p kc n", p=128)
    AW = NH_A * 128
    cnt = {"w": 0, "r": 0, "o": 0}

    def load_w(hd):
        s = cnt["w"] % 2
        cnt["w"] += 1
        for nm, wt, off in (("wq", wq, 0), ("wk", wk, AW), ("wv", wv, 2 * AW)):
            cx.dma("pool", wt[s][:], w_in_r[:, :, off + hd * 128: off + (hd + 1) * 128],
                   writes=[(tag + nm, s)], skey=(nm, s))
        return s

    for hg in range(4):
        for g in range(3):
            d = DIL[g]
            Ls = T // d
            nb = Ls // 128
            hd = g * 4 + hg
            s = load_w(hd)
            def rope_piece(which, wt, dst, dkey, tt, item, s=s, d=d):
                b = item % 3
                bP, bS = (0, 1, 4)[b], (2, 3, 5)[b]
                pp, sw = ps[bP], ps[bS]
                dst_v = dst[:, :].rearrange("p (r j) -> p j r", r=d)

                def mm(e):
                    ins = None
                    for kc in range(KC):
                        ins = e.matmul(pp[:], wt[s][:, kc, :], hT[:, kc, tt * TT:(tt + 1) * TT],
                                       start=(kc == 0), stop=(kc == KC - 1))
                    return ins
                cx.op("pe", mm, reads=[(tag + "w" + which, s)], writes=[pk[bP]])
                yield
                cx.op("act", lambda e: e.activation(qb[b][:], pp[:], AF.Copy),
                      reads=[pk[bP]], writes=[(tag + "qb", b)])
                yield "NEXT"
                cx.op("pe", lambda e: e.matmul(sw[:], pswap[:], qb[b][:], start=True, stop=True),
                      reads=[(tag + "qb", b), tag + "pswap"], writes=[pk[bS]])
                cx.op("dve", lambda e: e.tensor_tensor(
                    out=t1[b][:], in0=qb[b][:], in1=ctab[:, tt * TT:(tt + 1) * TT], op=ALU.mult),
                    reads=[(tag + "qb", b), tag + "ctab"], writes=[(tag + "t1", b)])
                yield
                cx.op("dve", lambda e: e.tensor_tensor(
                    out=t2[b][:], in0=sw[:], in1=stab[:, tt * TT:(tt + 1) * TT], op=ALU.mult),
                    reads=[pk[bS], tag + "stab"], writes=[(tag + "t2", b)])
                yield
                j0 = tt * TT // d
                cx.op("dve", lambda e: e.tensor_tensor(
                    out=dst_v[:, j0:j0 + TT // d, :],
                    in0=t1[b][:, :].rearrange("p (j r) -> p j r", r=d),
                    in1=t2[b][:, :].rearrange("p (j r) -> p j r", r=d), op=ALU.add),
                    reads=[(tag + "t1", b), (tag + "t2", b)], writes=[(dkey, tt)])
                yield

            pieces = [(w_, wt_, dst_, dk_, tt) for (w_, wt_, dst_, dk_) in
                      (("q", wq, Qd, tag + "Qd"), ("k", wk, Kd, tag + "Kd")) for tt in range(NTT)]
            act_g, nx = [], [0]

            def start_r():
                if nx[0] < len(pieces) and len(act_g) < 3:
                    w_, wt_, dst_, dk_, tt = pieces[nx[0]]
                    act_g.append(rope_piece(w_, wt_, dst_, dk_, tt, cnt["r"]))
                    cnt["r"] += 1
                    nx[0] += 1
            start_r()
            while act_g:
                for g_ in list(act_g):
                    try:
                        v = next(g_)
                    except StopIteration:
                        act_g.remove(g_)
                        start_r()
                        continue
                    if v == "NEXT" and g_ is act_g[-1]:
                        start_r()
            for b4 in range(8):
                b = cnt["r"] % 2
                cnt["r"] += 1
                pp = ps[b]

                def mmv(e, s=s, b4=b4, pp=pp, d=d, Ls=Ls):
                    ins = None
                    for i in range(4):
                        B = b4 * 4 + i
                        r, n = divmod(B, Ls // 128)
                        t0 = n * 128 * d + r
                        for kc in range(KC):
                            ins = e.matmul(pp[:, i * 128:(i + 1) * 128],
                                           hT[:, kc, t0: t0 + 127 * d + 1: d], wv[s][:, kc, :],
                                           start=(kc == 0), stop=(kc == KC - 1))
                    return ins
                cx.op("pe", mmv, reads=[(tag + "wv", s)], writes=[pk[b]])
                cx.op("act", lambda e, b4=b4, pp=pp: e.activation(
                    Vd[:, b4 * 4:(b4 + 1) * 4, :], pp[:, :].rearrange("p (i e) -> p i e", i=4), AF.Copy),
                    reads=[pk[b]], writes=[(tag + "Vd", b4)])
            qk_keys = [(tag + "Qd", tt) for tt in range(NTT)] + [(tag + "Kd", tt) for tt in range(NTT)]

            def scores(B, nb=nb):
                sb = (4, 5, 2, 3)[B % 4]
                nq = 256 if (B + 1) % nb != 0 else 128
                st = ps[sb]

                def mm(e, B=B, nq=nq, st=st):
                    e.matmul(st[:, 0:nq], Kd[:, B * 128:(B + 1) * 128], Qd[:, B * 128:B * 128 + nq],
                             start=True, stop=False)
                    return e.matmul(st[:, 0:nq], ident[:], maskb[:, 0:nq], start=False, stop=True)
                cx.op("pe", mm, reads=qk_keys + [tag + "ident", tag + "maskb"], writes=[pk[sb]])
                cx.op("act", lambda e, B=B, nq=nq, st=st: e.activation(
                    pt[B % 4][:, 0:nq], st[:, 0:nq], AF.Exp, scale=SCALE_A),
                    reads=[pk[sb]], writes=[(tag + "pt", B % 4)])

            def pv(B, nb=nb, g=g, d=d, Ls=Ls):
                first = (B % nb == 0)
                col = (B % 4) * 128
                bN, bD = ((6, 7), (0, 1))[(B // 4) % 2]

                def mm(e, B=B, first=first, col=col, bN=bN, bD=bD):
                    ins = None
                    for dst, lhs_fn in ((ps[bN], lambda blk: Vd[:, blk, :]), (ps[bD], lambda blk: c["ones_bf"][:])):
                        if not first:
                            e.matmul(dst[:, col:col + 128], lhs_fn(B - 1), pt[(B - 1) % 4][:, 128:256],
                                     start=True, stop=False)
                        ins = e.matmul(dst[:, col:col + 128], lhs_fn(B), pt[B % 4][:, 0:128],
                                       start=first, stop=True)
                    return ins
                rk = [(tag + "pt", B % 4), (tag + "Vd", B // 4), "ones_bf"]
                if not first:
                    rk += [(tag + "pt", (B - 1) % 4), (tag + "Vd", (B - 1) // 4)]
                cx.op("pe", mm, reads=rk, writes=[pk[bN], pk[bD]])
                if B % 4 == 3:
                    u0 = (B - 3) * 128
                    if d == 1:
                        def nat(ap2d):
                            return ap2d[:, u0:u0 + 512]
                        def src(p):
                            return p[:, :]
                    else:
                        r0, j0 = divmod(u0, Ls)
                        nr = max(1, 512 // Ls)
                        nj = 512 // nr
                        def nat(ap2d, r0=r0, j0=j0, nr=nr, nj=nj, d=d):
                            return ap2d.rearrange("p (j r) -> p r j", r=d)[:, r0:r0 + nr, j0:j0 + nj]
                        def src(p, nr=nr):
                            return p[:, :].rearrange("p (r j) -> p r j", r=nr)
                    cx.op("act", lambda e, nat=nat, src=src, g=g, bN=bN: e.activation(
                        nat(numT[:, g, :]), src(ps[bN]), AF.Copy),
                        reads=[pk[bN]], writes=[(tag + "numT", g)])
                    if g == 0:
                        cx.op("dve", lambda e, nat=nat, src=src, bD=bD: e.tensor_copy(nat(dent[:, :]), src(ps[bD])),
                              reads=[pk[bD]], writes=[tag + "dent"])
                    else:
                        cx.op("dve", lambda e, nat=nat, src=src, bD=bD: e.tensor_tensor(
                            out=nat(dent[:, :]), in0=nat(dent[:, :]), in1=src(ps[bD]), op=ALU.add),
                            reads=[pk[bD], tag + "dent"], writes=[tag + "dent"])

            scores(0)
            scores(1)
            for B in range(32):
                if B + 2 < 32:
                    scores(B + 2)
                pv(B)
        cx.op("dve", lambda e: e.reciprocal(dent[:, :], dent[:, :]), reads=[tag + "dent"], writes=[tag + "dent"])
        for g in range(3):
            hd = g * 4 + hg
            for tt in range(NTT):
                b = cnt["o"] % 2
                cnt["o"] += 1
                cx.op("dve", lambda e, b=b, g=g, tt=tt: e.tensor_tensor(
                    out=osc[b][:], in0=numT[:, g, tt * TT:(tt + 1) * TT], in1=dent[:, tt * TT:(tt + 1) * TT],
                    op=ALU.mult), reads=[tag + "dent", (tag + "numT", g)], writes=[(tag + "osc", b)])
                cx.dma("sp", os_d[hd * 128:(hd + 1) * 128, tt * TT:(tt + 1) * TT], osc[b][:],
                       reads=[(tag + "osc", b)], writes=[(okey, hd, tt)], skey=("osc", b))
    cx.barrier()
    cx.pop()


def outproj_phase(cx, c, tag, x_in, x_out, os_d, w_out, nheads, xkey_in, xkey_out, okey):
    cx.push()
    HT = 2048
    NT = HT // TT
    WB = 256
    osb = cx.sbuf(tag + "osb", [128, nheads, HT], BF16)
    wo = [cx.sbuf(tag + "wo%d" % i, [128, nheads, WB], BF16) for i in range(2)]
    xr = [cx.sbuf(tag + "xr%d" % i, [128, TT], F32) for i in range(2)]
    xn = [cx.sbuf(tag + "xn%d" % i, [128, TT], F32) for i in range(2)]
    ps = [cx.psum(tag + "ps%d" % i, [128, TT], F32) for i in range(2)]
    pk = [tag + "ps%d" % i for i in range(2)]
    cx.psum_keys.update(pk)
    w_out_r = w_out.rearrange("(kc p) n -> p kc n", p=128)
    os_r = os_d.rearrange("(h p) t -> p h t", p=128)
    cnt = {"o": 0, "x": 0}
    for h in range(T // HT):
        t0 = h * HT
        for hd in range(nheads):
            cx.dma("sp", osb[:, hd, :], os_r[:, hd, t0:t0 + HT],
                   reads=[(okey, hd, (t0 // TT) + i) for i in range(NT)], writes=[(tag + "osb", hd)], skey=("osb", hd % 4))
        for mb in range(D // WB):
            s = cnt["o"] % 2
            cnt["o"] += 1
            cx.dma("pool", wo[s][:], w_out_r[:, :, mb * WB:(mb + 1) * WB], writes=[(tag + "wo", s)], skey=("wo", s))
            for mm_ in range(WB // 128):
                mc = mb * (WB // 128) + mm_
                for tt in range(NT):
                    gt = (t0 + tt * TT) // TT
                    b = cnt["x"] % 2
                    cnt["x"] += 1
                    py = ps[b]
                    cx.dma("sp", xr[b][:], x_in[mc * 128:(mc + 1) * 128, t0 + tt * TT: t0 + (tt + 1) * TT],
                           reads=[(xkey_in, gt, mc)], writes=[(tag + "xr", b)], skey=("xr", b))

                    def mm2(e, s=s, mm_=mm_, tt=tt, py=py):
                        ins = None
                        for kc in range(nheads):
                            ins = e.matmul(py[:], wo[s][:, kc, mm_ * 128:(mm_ + 1) * 128],
                                           osb[:, kc, tt * TT:(tt + 1) * TT], start=(kc == 0), stop=(kc == nheads - 1))
                        return ins
                    cx.op("pe", mm2, reads=[(tag + "wo", s)] + [(tag + "osb", hd) for hd in range(nheads)],
                          writes=[pk[b]])
                    cx.op("dve", lambda e, b=b, py=py: e.tensor_tensor(
                        out=xn[b][:], in0=py[:], in1=xr[b][:], op=ALU.add),
                        reads=[pk[b], (tag + "xr", b)], writes=[(tag + "xn", b)])
                    cx.dma("sp", x_out[mc * 128:(mc + 1) * 128, t0 + tt * TT: t0 + (tt + 1) * TT], xn[b][:],
                           reads=[(tag + "xn", b)], writes=[(xkey_out, gt, mc)], skey=("xn", b))
    cx.barrier()
    cx.pop()


def build_attn_prog():
    nc = bass.Bass("TRN2", target_bir_lowering=False)
    x = nc.dram_tensor("x", [D, T], F32, kind="ExternalInput").ap()
    nw = nc.dram_tensor("nw", [128, KC], F32, kind="ExternalInput").ap()
    w_in = nc.dram_tensor("w_in", [D, 3 * NH_A * 128], F32, kind="ExternalInput").ap()
    w_out = nc.dram_tensor("w_out", [NH_A * 128, D], F32, kind="ExternalInput").ap()
    ctab = nc.dram_tensor("ctab", [128, T], F32, kind="ExternalInput").ap()
    stab = nc.dram_tensor("stab", [128, T], F32, kind="ExternalInput").ap()
    cst = nc.dram_tensor("cst", [128, CST_W], F32, kind="ExternalInput").ap()
    os_d = nc.dram_tensor("os_scratch", [NH_A * 128, T], BF16).ap()
    y = nc.dram_tensor("y", [D, T], F32, kind="ExternalOutput").ap()
    cx = Ctx(nc)
    c = consts(cx)
    attn_phase(cx, c, "a_", x, os_d, nw, w_in, ctab, stab, cst, "xin", "os")
    outproj_phase(cx, c, "ao_", x, y, os_d, w_out, NH_A, "xin", "xout", "os")
    cx.finish()
    return nc


NVH = 16
NKH = 8
GP = 6176
NTILE = T // 128
DBG = {}


class _Stop(Exception):
    pass


def _chk(n):
    if DBG.get("stop", 99) <= n:
        raise _Stop()


def gdn_core_phase(*a):
    cx = a[0]
    depth = len(cx.scopes)
    try:
        _gdn_core_phase(*a)
    except _Stop:
        cx.barrier()
        while len(cx.scopes) > depth:
            cx.pop()


def _gdn_core_phase(cx, c, tag, x_in, oraw_d, nw_d, w_in, convw_d, alog_d, dtb_d, cst_d, xkey_in, okey):
    cx.push()
    hT = cx.sbuf(tag + "hT", [128, KC, T], BF16)
    ident = cx.sbuf(tag + "ident", [128, 128], BF16)
    id4 = cx.sbuf(tag + "id4", [128, 4, 128], BF16)
    minc4 = cx.sbuf(tag + "minc4", [128, 4, 128], BF16)
    mlow4 = cx.sbuf(tag + "mlow4", [128, 4, 128], BF16)
    cw = cx.sbuf(tag + "cw", [128, 32, 4], F32)
    beta = cx.sbuf(tag + "beta", [128, NTILE, NVH], F32)
    nbeta = cx.sbuf(tag + "nbeta", [128, NTILE, NVH], F32)
    gc = cx.sbuf(tag + "gc", [128, NTILE, NVH], F32)
    kbs = cx.sbuf(tag + "kbs", [128, NTILE, NVH], F32)
    egr = cx.sbuf(tag + "egr", [128, NTILE, NVH], F32)
    cd = cx.sbuf(tag + "cd", [128, NTILE, NVH], F32)
    gchi = cx.sbuf(tag + "gchi", [128, NTILE, NVH], BF16)
    gclo = cx.sbuf(tag + "gclo", [128, NTILE, NVH], BF16)
    ps = [cx.psum(tag + "ps%d" % i, [128, TT], F32) for i in range(8)]
    pk = [tag + "ps%d" % i for i in range(8)]
    cx.psum_keys.update(pk)
    K = lambda n: tag + n
    cx.push()
    alog = cx.sbuf(tag + "alog", [128, 512], F32)
    dtb = cx.sbuf(tag + "dtb", [128, 512], F32)
    gt = cx.sbuf(tag + "gt", [128, NTILE, NVH], F32)
    wab = cx.sbuf(tag + "wab", [128, KC, 32], BF16)
    ghi = cx.sbuf(tag + "ghi", [128, NTILE, NVH], BF16)
    glo = cx.sbuf(tag + "glo", [128, NTILE, NVH], BF16)
    trib = cx.sbuf(tag + "trib", [128, 128], BF16)
    trisb = cx.sbuf(tag + "trisb", [128, 128], BF16)
    cx.dma("pool", ident[:], cst_d[:, C_IDENT:C_IDENT + 128], writes=[K("ident")], skey="ident")
    for i in range(4):
        cx.dma("pool", id4[:, i, :], cst_d[:, C_IDENT:C_IDENT + 128], writes=[K("id4")], skey=("id4", i))
        cx.dma("pool", minc4[:, i, :], cst_d[:, C_MINC:C_MINC + 128], writes=[K("minc4")], skey=("minc4", i))
        cx.dma("pool", mlow4[:, i, :], cst_d[:, C_TRIS:C_TRIS + 128], writes=[K("mlow4")], skey=("mlow4", i))
    cx.dma("sp", cw[:], convw_d, writes=[K("cw")], skey="cw")
    cx.dma("sp", alog[:], alog_d, writes=[K("alog")], skey="alog")
    cx.dma("sp", dtb[:], dtb_d, writes=[K("dtb")], skey="dtb")
    w_in_r = w_in.rearrange("(kc p) n -> p kc n", p=128)
    cx.dma("pool", wab[:], w_in_r[:, :, 6144:6176], writes=[K("wab")], skey="wab")
    _chk(0)
    norm_stage(cx, c, tag, x_in, nw_d, hT, xkey_in, ps[6], pk[6])
    _chk(1)

    f2 = lambda t: t[:, :, :].rearrange("p c h -> p (c h)")
    for which in range(2):
        def mm(e, which=which):
            ins = None
            for ct in range(NTILE):
                for kc in range(KC):
                    ins = e.matmul(ps[which][:, ct * 16:(ct + 1) * 16], hT[:, kc, ct * 128:(ct + 1) * 128],
                                   wab[:, kc, which * 16:(which + 1) * 16], start=(kc == 0), stop=(kc == KC - 1))
            return ins
        cx.op("pe", mm, reads=[], writes=[pk[which]])
    _chk(1.05)
    cx.op("act", lambda e: e.activation(f2(beta), ps[0][:], AF.Sigmoid), reads=[pk[0]], writes=[K("beta")])
    _chk(1.1)
    cx.op("dve", lambda e: e.tensor_tensor(out=f2(gt), in0=ps[1][:], in1=dtb[:], op=ALU.add),
          reads=[pk[1], K("dtb")], writes=[K("gt")])
    _chk(1.2)
    cx.op("act", lambda e: e.activation(f2(gt), f2(gt), AF.Exp), reads=[K("gt")], writes=[K("gt")])
    _chk(1.4)
    cx.op("act", lambda e: e.activation(f2(gt), f2(gt), AF.Ln, bias=1.0), reads=[K("gt")], writes=[K("gt")])
    _chk(1.6)
    cx.op("act", lambda e: e.activation(alog[:], alog[:], AF.Exp), reads=[K("alog")], writes=[K("alog")])
    _chk(1.8)
    cx.op("dve", lambda e: e.scalar_tensor_tensor(out=f2(gt), in0=f2(gt), scalar=-1.0, in1=alog[:],
                                                   op0=ALU.mult, op1=ALU.mult),
          reads=[K("gt"), K("alog")], writes=[K("gt")])
    cx.op("dve", lambda e: e.tensor_scalar(f2(nbeta), f2(beta), -1.0, None, ALU.mult),
          reads=[K("beta")], writes=[K("nbeta")])
    _chk(2)

    cx.dma("pool", trib[:], cst_d[:, C_TRI:C_TRI + 128], writes=[K("trib")], skey="trib")
    cx.dma("pool", trisb[:], cst_d[:, C_TRIS:C_TRIS + 128], writes=[K("trisb")], skey="trisb")
    cx.op("dve", lambda e: e.tensor_copy(f2(ghi), f2(gt)), reads=[K("gt")], writes=[K("ghi")])
    cx.op("dve", lambda e: e.tensor_tensor(out=f2(glo), in0=f2(gt), in1=f2(ghi), op=ALU.subtract),
          reads=[K("gt"), K("ghi")], writes=[K("glo")])
    _chk(2.2)

    def mmg(e):
        ins = None
        for ct in range(NTILE):
            for dst, m in ((ps[2], trib), (ps[3], trisb), (ps[4], c["ones_bf"])):
                e.matmul(dst[:, ct * 16:(ct + 1) * 16], m[:], ghi[:, ct, :], start=True, stop=False)
                ins = e.matmul(dst[:, ct * 16:(ct + 1) * 16], m[:], glo[:, ct, :], start=False, stop=True)
        return ins
    cx.op("pe", mmg, reads=[K("ghi"), K("glo"), K("trib"), K("trisb"), "ones_bf"], writes=[pk[2], pk[3], pk[4]])
    _chk(2.4)
    cx.op("dve", lambda e: e.tensor_copy(f2(gchi), ps[2][:]), reads=[pk[2]], writes=[K("gchi")])
    cx.op("dve", lambda e: e.tensor_tensor(out=f2(gclo), in0=ps[2][:], in1=f2(gchi), op=ALU.subtract),
          reads=[pk[2], K("gchi")], writes=[K("gclo")])
    _chk(2.6)
    cx.op("dve", lambda e: e.tensor_copy(f2(gc), ps[2][:]), reads=[pk[2]], writes=[K("gc")])
    cx.op("act", lambda e: e.activation(f2(kbs), ps[2][:], AF.Exp), reads=[pk[2]], writes=[K("kbs")])
    _chk(2.7)
    cx.op("dve", lambda e: e.tensor_tensor(out=f2(kbs), in0=f2(kbs), in1=f2(beta), op=ALU.mult),
          reads=[K("kbs"), K("beta")], writes=[K("kbs")])
    _chk(2.8)
    cx.op("act", lambda e: e.activation(f2(egr), ps[3][:], AF.Exp), reads=[pk[3]], writes=[K("egr")])
    cx.op("act", lambda e: e.activation(f2(cd), ps[4][:], AF.Exp), reads=[pk[4]], writes=[K("cd")])
    _chk(2.9)
    cx.barrier()
    cx.pop()
    _chk(3)

    qT = cx.sbuf(tag + "qT", [128, 2, T], BF16)
    kT = cx.sbuf(tag + "kT", [128, 2, T], BF16)
    ktok = cx.sbuf(tag + "ktok", [128, NTILE, 2, 128], BF16)
    vtok = cx.sbuf(tag + "vtok", [128, NTILE, 4, 128], BF16)
    PW = 1024
    for hgp in range(DBG.get('hgps', 4)):
        cx.push()
        NB = 3
        wch = [cx.sbuf(tag + "wch%d" % i, [128, KC, 128], BF16) for i in range(2)]
        dg = [cx.sbuf(tag + "dg%d" % i, [128, 4, 128], BF16) for i in range(2)]
        xcb = [cx.sbuf(tag + "xcb%d" % i, [128, 4 + TT], BF16) for i in range(NB)]
        sil = [cx.sbuf(tag + "sil%d" % i, [128, TT], F32) for i in range(NB)]
        silb = [cx.sbuf(tag + "silb%d" % i, [128, TT], BF16) for i in range(NB)]
        sqb = [cx.sbuf(tag + "sqb%d" % i, [128, TT], BF16) for i in range(NB)]
        rn = [cx.sbuf(tag + "rn%d" % i, [128, TT], F32) for i in range(NB)]
        chunks = [("q", 0), ("q", 1), ("k", 0), ("k", 1), ("v", 0), ("v", 1), ("v", 2), ("v", 3)]

        def g1_piece(kind, li, gch, ws, tt, item):
            b = item % NB
            i2 = item % 2
            bP, bC, bS, bT = i2, 2 + i2, 4 + i2, 6 + i2
            XK = lambda i: (K("xcb"), i)
            if tt == 0:
                cx.op("act", lambda e: e.activation(xcb[b][:, 0:4], xcb[b][:, 0:4], AF.Copy, scale=0.0),
                      reads=[], writes=[(K("xh"), b)])

            def mm(e):
                ins = None
                for kc in range(KC):
                    ins = e.matmul(ps[bP][:], wch[ws][:, kc, :], hT[:, kc, tt * TT:(tt + 1) * TT],
                                   start=(kc == 0), stop=(kc == KC - 1))
                return ins
            cx.op("pe", mm, reads=[(K("wch"), ws)], writes=[pk[bP]])
            yield
            cx.op("act", lambda e: e.activation(xcb[b][:, 4:4 + TT], ps[bP][:], AF.Copy),
                  reads=[pk[bP]], writes=[XK(b)])
            if tt < NTT - 1:
                nb_ = (item + 1) % NB
                cx.op("act", lambda e: e.activation(xcb[nb_][:, 0:4], xcb[b][:, TT:TT + 4], AF.Copy),
                      reads=[XK(b)], writes=[(K("xh"), nb_)])
            yield "NEXT"

            def mmc(e):
                ins = None
                for j in range(4):
                    ins = e.matmul(ps[bC][:], dg[ws][:, j, :], xcb[b][:, 1 + j:1 + j + TT], start=(j == 0), stop=(j == 3))
                return ins
            cx.op("pe", mmc, reads=[XK(b), (K("xh"), b), (K("dg"), ws)], writes=[pk[bC]])
            yield
            tsl = slice(tt * TT, (tt + 1) * TT)
            if kind == "v":
                cx.op("act", lambda e: e.activation(silb[b][:], ps[bC][:], AF.Silu), reads=[pk[bC]], writes=[(K("silb"), b)])
                yield
            else:
                cx.op("act", lambda e: e.activation(sil[b][:], ps[bC][:], AF.Silu), reads=[pk[bC]], writes=[(K("sil"), b)])
                yield
                cx.op("dve", lambda e: e.tensor_tensor(out=sqb[b][:], in0=sil[b][:], in1=sil[b][:], op=ALU.mult),
                      reads=[(K("sil"), b)], writes=[(K("sqb"), b)])
                yield
                cx.op("pe", lambda e: e.matmul(ps[bS][:], c["ones_bf"][:], sqb[b][:], start=True, stop=True),
                      reads=[(K("sqb"), b), "ones_bf"], writes=[pk[bS]])
                yield
                cx.op("act", lambda e: e.activation(rn[b][:], ps[bS][:], AF.Sqrt, bias=1e-6, scale=1.0),
                      reads=[pk[bS]], writes=[(K("rn"), b)])
                yield
                cx.op("dve", lambda e: e.reciprocal(rn[b][:], rn[b][:]), reads=[(K("rn"), b)], writes=[(K("rn"), b)])
                dstT = qT if kind == "q" else kT
                sc = (128 ** -0.5) if kind == "q" else 1.0
                cx.op("dve", lambda e: e.scalar_tensor_tensor(
                    out=dstT[:, li, tsl], in0=sil[b][:], scalar=sc, in1=rn[b][:], op0=ALU.mult, op1=ALU.mult),
                    reads=[(K("sil"), b), (K("rn"), b)], writes=[(K(kind + "T"), li, tt)])
                yield
            if kind in ("k", "v"):
                ptv = ps[bT][:, :].bitcast(BF16)

                def mmt(e):
                    ins = None
                    for i in range(4):
                        if kind == "v":
                            src = silb[b][:, i * 128:(i + 1) * 128]
                        else:
                            src = kT[:, li, tt * TT + i * 128: tt * TT + (i + 1) * 128]
                        ins = e.transpose(ptv[:, i * 128:(i + 1) * 128], src, ident[:])
                    return ins
                rk = [(K("silb"), b)] if kind == "v" else [(K("kT"), li, tt)]
                cx.op("pe", mmt, reads=rk + [K("ident")], writes=[pk[bT]])
                yield
                dst = vtok if kind == "v" else ktok
                cx.op("act", lambda e: e.activation(
                    dst[:, tt * 4:tt * 4 + 4, li, :], ptv[:, 0:512].rearrange("p (i x) -> p i x", i=4), AF.Copy),
                    reads=[pk[bT]], writes=[(K(kind + "tok"), li, tt)])
                yield

        items = []
        wcnt = 0
        for kind, li in chunks:
            gch = {"q": 2 * hgp + li, "k": 8 + 2 * hgp + li, "v": 16 + 4 * hgp + li}[kind]
            ws = wcnt % 2
            wcnt += 1
            for tt in range(NTT):
                items.append((kind, li, gch, ws, tt))
        active, nxt_i = [], [0]

        def start1():
            while nxt_i[0] < len(items) and len(active) < NB:
                kind, li, gch, ws, tt = items[nxt_i[0]]
                if tt == 0:
                    cx.dma("pool", wch[ws][:], w_in_r[:, :, gch * 128:(gch + 1) * 128], writes=[(K("wch"), ws)],
                           skey=("wch", ws))
                    for j in range(4):
                        cx.op("dve", lambda e, ws=ws, gch=gch, j=j: e.tensor_scalar(
                            dg[ws][:, j, :], ident[:], cw[:, gch, j:j + 1], None, ALU.mult),
                            reads=[K("ident"), K("cw")], writes=[(K("dg"), ws)])
                active.append(g1_piece(kind, li, gch, ws, tt, nxt_i[0]))
                nxt_i[0] += 1
                return
        start1()
        while active:
            for g in list(active):
                try:
                    v = next(g)
                except StopIteration:
                    active.remove(g)
                    start1()
                    continue
                if v == "NEXT" and g is active[-1]:
                    start1()
        cx.barrier()
        cx.pop()
        _chk(5)

        cx.push()
        mk = lambda nm, dt_, n=2: [cx.sbuf(tag + nm + "%d" % i, [128, 4, 128], dt_) for i in range(n)]
        Abuf0, Abuf1, Atbuf0, Atbuf1 = mk("Aa", BF16), mk("Ab", BF16), mk("Ata", BF16), mk("Atb", BF16)
        A_O, AtO, Tt = mk("A_O", BF16), mk("AtO", BF16), mk("Tt", BF16)
        qkT, qdT, kd, vb, kb = mk("qkT", BF16), mk("qdT", BF16), mk("kd", BF16), mk("vb", BF16), mk("kb", BF16)
        dmT = cx.sbuf(tag + "dmT", [128, 4, 128], F32)
        dm2 = cx.sbuf(tag + "dm2", [128, 4, 128], F32)
        eB = cx.sbuf(tag + "eB", [128, 4, 128], F32)
        rr = cx.sbuf(tag + "rr", [128, 4, 128], F32)
        u = cx.sbuf(tag + "u", [128, 4, 128], F32)
        wT = cx.sbuf(tag + "wT", [128, 4, 128], BF16)
        vnew = cx.sbuf(tag + "vnew", [128, 4, 128], BF16)
        S32 = cx.sbuf(tag + "S32", [128, 4, 128], F32)
        S16 = cx.sbuf(tag + "S16", [128, 4, 128], BF16)
        ost = [cx.sbuf(tag + "ost%d" % i, [128, 4, 128], F32) for i in range(2)]
        fl = lambda t: t[:, :, :].rearrange("p h x -> p (h x)")
        cx.op("dve", lambda e: e.memset(fl(S32), 0.0), writes=[K("S32")])
        cx.op("dve", lambda e: e.memset(fl(S16), 0.0), writes=[K("S16")])
        hs = [4 * hgp + hl for hl in range(4)]
        B0, B1 = 0, 1

        def g2_tile(ct, hs, hgp=hgp):
            par = ct % 2
            bA, bAt, bTu = 2 + 3 * par, 3 + 3 * par, 4 + 3 * par
            tsl = slice(ct * 128, (ct + 1) * 128)
            P = lambda n: (K(n), par)
            AO, AtOp, Ttp = A_O[par], AtO[par], Tt[par]
            bufs = [(Abuf0[par], Atbuf0[par]), (Abuf1[par], Atbuf1[par]), (AO, AtOp)]

            def mm1(e):
                for hl in range(4):
                    e.matmul(ps[B0][:, hl * 128:(hl + 1) * 128], gchi[:, ct, hs[hl]:hs[hl] + 1].to_broadcast([128, 128]),
                             ident[:], start=True, stop=False)
                    e.matmul(ps[B0][:, hl * 128:(hl + 1) * 128], gclo[:, ct, hs[hl]:hs[hl] + 1].to_broadcast([128, 128]),
                             ident[:], start=False, stop=True)
                ins = None
                for khl in range(2):
                    e.matmul(ps[B1][:, khl * 128:(khl + 1) * 128], kT[:, khl, tsl], kT[:, khl, tsl], start=True, stop=True)
                    ins = e.matmul(ps[B1][:, (2 + khl) * 128:(3 + khl) * 128], kT[:, khl, tsl], qT[:, khl, tsl],
                                   start=True, stop=True)
                return ins
            cx.op("pe", mm1, reads=[K("ident")], writes=[pk[B0], pk[B1]])
            yield
            for hl in range(4):
                h = hs[hl]
                cx.op("dve", lambda e, hl=hl, h=h: e.tensor_scalar(
                    dmT[:, hl, :], ps[B0][:, hl * 128:(hl + 1) * 128], gc[:, ct, h:h + 1], 0.0, ALU.subtract, ALU.min),
                    reads=[pk[B0]], writes=[K("dmT")])
                cx.op("dve", lambda e, hl=hl, h=h: e.tensor_scalar(
                    dm2[:, hl, :], ps[B0][:, hl * 128:(hl + 1) * 128], gc[:, ct, h:h + 1], 0.0, ALU.subtract, ALU.max),
                    reads=[pk[B0]], writes=[K("dm2")])
                yield
            cx.op("act", lambda e: e.activation(fl(eB), ps[B0][:], AF.Exp), reads=[pk[B0]], writes=[K("eB")])
            cx.op("act", lambda e: e.activation(fl(dmT), fl(dmT), AF.Exp), reads=[K("dmT")], writes=[K("dmT")])
            cx.op("act", lambda e: e.activation(fl(dm2), fl(dm2), AF.Exp, scale=-1.0), reads=[K("dm2")], writes=[K("dm2")])
            yield
            cx.op("dve", lambda e: e.tensor_tensor(out=fl(dmT), in0=fl(dmT), in1=fl(minc4), op=ALU.mult),
                  reads=[K("dmT")], writes=[K("dmT")])
            cx.op("dve", lambda e: e.tensor_tensor(out=fl(dm2), in0=fl(dm2), in1=fl(mlow4), op=ALU.mult),
                  reads=[K("dm2")], writes=[K("dm2")])
            yield
            for hl in range(4):
                h = hs[hl]
                khl = hl // 2
                cx.op("dve", lambda e, hl=hl, h=h, khl=khl: e.scalar_tensor_tensor(
                    out=AO[:, hl, :], in0=ps[B1][:, khl * 128:(khl + 1) * 128], scalar=nbeta[:, ct, h:h + 1],
                    in1=dm2[:, hl, :], op0=ALU.mult, op1=ALU.mult), reads=[pk[B1], K("dm2")], writes=[(P("A"), 2)])
                cx.op("dve", lambda e, hl=hl, khl=khl: e.tensor_tensor(
                    out=qkT[par][:, hl, :], in0=ps[B1][:, (2 + khl) * 128:(3 + khl) * 128], in1=dmT[:, hl, :], op=ALU.mult),
                    reads=[pk[B1], K("dmT")], writes=[P("qkT")])
                cx.op("dve", lambda e, hl=hl, khl=khl: e.tensor_tensor(
                    out=qdT[par][:, hl, :], in0=qT[:, khl, tsl], in1=eB[:, hl, :], op=ALU.mult),
                    reads=[K("eB")], writes=[P("qdT")])
                cx.op("act", lambda e, hl=hl, h=h: e.activation(
                    vb[par][:, hl, :], vtok[:, ct, hl, :], AF.Copy, scale=beta[:, ct, h:h + 1]), writes=[P("vb")])
                cx.op("act", lambda e, hl=hl, h=h, khl=khl: e.activation(
                    kb[par][:, hl, :], ktok[:, ct, khl, :], AF.Copy, scale=kbs[:, ct, h:h + 1]), writes=[P("kb")])
                cx.op("act", lambda e, hl=hl, h=h, khl=khl: e.activation(
                    kd[par][:, hl, :], ktok[:, ct, khl, :], AF.Copy, scale=egr[:, ct, h:h + 1]), writes=[P("kd")])
                yield
            ptv = ps[bA][:, :].bitcast(BF16)

            def mmT(e):
                ins = None
                for hl in range(4):
                    ins = e.transpose(ptv[:, hl * 128:(hl + 1) * 128], AO[:, hl, :], ident[:])
                return ins
            cx.op("pe", mmT, reads=[(P("A"), 2), K("ident")], writes=[pk[bA]])
            yield
            cx.op("act", lambda e: e.activation(fl(AtOp), ptv[:, 0:512], AF.Copy),
                  reads=[pk[bA]], writes=[(P("At"), 2)])
            yield
            cx.op("dve", lambda e: e.tensor_tensor(out=fl(Ttp), in0=fl(AtOp), in1=fl(id4), op=ALU.add),
                  reads=[(P("At"), 2), K("id4")], writes=[P("Tt")])
            yield
            cur = 2
            NSTEP = 6
            for p in range(1, NSTEP + 1):
                nxt = 0 if cur == 2 else 1 - cur
                A, At = bufs[cur]
                An, Atn = bufs[nxt]
                last = (p == NSTEP)

                def mmsq(e, A=A, At=At, last=last):
                    ins = None
                    for hl in range(4):
                        ins = e.matmul(ps[bA][:, hl * 128:(hl + 1) * 128], At[:, hl, :], A[:, hl, :], start=True, stop=True)
                    if not last:
                        for hl in range(4):
                            ins = e.matmul(ps[bAt][:, hl * 128:(hl + 1) * 128], A[:, hl, :], At[:, hl, :],
                                           start=True, stop=True)
                    return ins
                cx.op("pe", mmsq, reads=[(P("A"), cur), (P("At"), cur)], writes=[pk[bA]] + ([] if last else [pk[bAt]]))
                yield
                cx.op("act", lambda e, An=An: e.activation(fl(An), ps[bA][:], AF.Copy),
                      reads=[pk[bA]], writes=[(P("A"), nxt)])
                if not last:
                    cx.op("act", lambda e, Atn=Atn: e.activation(fl(Atn), ps[bAt][:], AF.Copy),
                          reads=[pk[bAt]], writes=[(P("At"), nxt)])
                yield

                def mmtu(e, An=An):
                    ins = None
                    for hl in range(4):
                        ins = e.matmul(ps[bTu][:, hl * 128:(hl + 1) * 128], An[:, hl, :], Ttp[:, hl, :], start=True, stop=True)
                    return ins
                cx.op("pe", mmtu, reads=[(P("A"), nxt), P("Tt")], writes=[pk[bTu]])
                yield
                cx.op("dve", lambda e: e.tensor_tensor(out=fl(Ttp), in0=fl(Ttp), in1=ps[bTu][:], op=ALU.add),
                      reads=[pk[bTu], P("Tt")], writes=[P("Tt")])
                cur = nxt
                if p == 3:
                    yield "MID"
                else:
                    yield
            T0, Rb = Atbuf0[par], Abuf0[par]
            ptv3 = ps[bAt][:, :].bitcast(BF16)

            def mmT0(e):
                ins = None
                for hl in range(4):
                    ins = e.transpose(ptv3[:, hl * 128:(hl + 1) * 128], Ttp[:, hl, :], ident[:])
                return ins
            cx.op("pe", mmT0, reads=[P("Tt"), K("ident")], writes=[pk[bAt]])
            yield
            cx.op("act", lambda e: e.activation(fl(T0), ptv3[:, 0:512], AF.Copy),
                  reads=[pk[bAt]], writes=[(P("At"), 0)])
            yield

            def mmR(e):
                ins = None
                for hl in range(4):
                    ins = e.matmul(ps[bA][:, hl * 128:(hl + 1) * 128], AtOp[:, hl, :], T0[:, hl, :], start=True, stop=True)
                return ins
            cx.op("pe", mmR, reads=[(P("At"), 2), (P("At"), 0)], writes=[pk[bA]])
            yield
            cx.op("dve", lambda e: e.scalar_tensor_tensor(out=fl(rr), in0=fl(T0), scalar=-1.0, in1=ps[bA][:],
                                                           op0=ALU.mult, op1=ALU.add),
                  reads=[(P("At"), 0), pk[bA]], writes=[K("rr")])
            cx.op("dve", lambda e: e.tensor_tensor(out=fl(Rb), in0=fl(rr), in1=fl(id4), op=ALU.add),
                  reads=[K("rr"), K("id4")], writes=[(P("A"), 0)])
            yield

            def mmU(e):
                ins = None
                for hl in range(4):
                    ins = e.matmul(ps[bTu][:, hl * 128:(hl + 1) * 128], Rb[:, hl, :], Ttp[:, hl, :], start=True, stop=True)
                return ins
            cx.op("pe", mmU, reads=[(P("A"), 0), P("Tt")], writes=[pk[bTu]])
            yield
            cx.op("dve", lambda e: e.tensor_tensor(out=fl(Ttp), in0=fl(Ttp), in1=ps[bTu][:], op=ALU.add),
                  reads=[pk[bTu], P("Tt")], writes=[P("Tt")])
            yield

            def mmuw(e):
                ins = None
                for hl in range(4):
                    e.matmul(ps[bA][:, hl * 128:(hl + 1) * 128], Ttp[:, hl, :], vb[par][:, hl, :], start=True, stop=True)
                    ins = e.matmul(ps[bAt][:, hl * 128:(hl + 1) * 128], kb[par][:, hl, :], Ttp[:, hl, :], start=True, stop=True)
                return ins
            cx.op("pe", mmuw, reads=[P("Tt"), P("vb"), P("kb")], writes=[pk[bA], pk[bAt]])
            yield
            cx.op("act", lambda e: e.activation(fl(u), ps[bA][:], AF.Copy), reads=[pk[bA]], writes=[K("u")])
            cx.op("act", lambda e: e.activation(fl(wT), ps[bAt][:], AF.Copy), reads=[pk[bAt]], writes=[K("wT")])
            yield

            def mmvn(e):
                ins = None
                for hl in range(4):
                    ins = e.matmul(ps[bTu][:, hl * 128:(hl + 1) * 128], wT[:, hl, :], S16[:, hl, :], start=True, stop=True)
                return ins
            cx.op("pe", mmvn, reads=[K("wT"), K("S16")], writes=[pk[bTu]])
            yield
            cx.op("dve", lambda e: e.tensor_tensor(out=fl(vnew), in0=fl(u), in1=ps[bTu][:], op=ALU.subtract),
                  reads=[K("u"), pk[bTu]], writes=[K("vnew")])
            yield

            def mmo(e):
                ins = None
                for hl in range(4):
                    e.matmul(ps[bA][:, hl * 128:(hl + 1) * 128], S16[:, hl, :], qdT[par][:, hl, :], start=True, stop=False)
                    e.matmul(ps[bA][:, hl * 128:(hl + 1) * 128], vnew[:, hl, :], qkT[par][:, hl, :], start=False, stop=True)
                for hl in range(4):
                    ins = e.matmul(ps[bAt][:, hl * 128:(hl + 1) * 128], kd[par][:, hl, :], vnew[:, hl, :], start=True, stop=True)
                return ins
            cx.op("pe", mmo, reads=[K("S16"), P("qdT"), K("vnew"), P("qkT"), P("kd")], writes=[pk[bA], pk[bAt]])
            yield
            ob = ct % 2
            for hl in range(4):
                h = hs[hl]
                cx.op("dve", lambda e, hl=hl, h=h: e.scalar_tensor_tensor(
                    out=S32[:, hl, :], in0=S32[:, hl, :], scalar=cd[:, ct, h:h + 1], in1=ps[bAt][:, hl * 128:(hl + 1) * 128],
                    op0=ALU.mult, op1=ALU.add), reads=[pk[bAt], K("S32")], writes=[K("S32")])
            cx.op("act", lambda e: e.activation(fl(S16), fl(S32), AF.Copy), reads=[K("S32")], writes=[K("S16")])
            cx.op("act", lambda e: e.activation(fl(ost[ob]), ps[bA][:], AF.Copy),
                  reads=[pk[bA]], writes=[(K("ost"), ob)])
            for hl in range(4):
                h = hs[hl]
                cx.dma("sp", oraw_d[h * 128:(h + 1) * 128, ct * 128:(ct + 1) * 128], ost[ob][:, hl, :],
                       reads=[(K("ost"), ob)], writes=[(okey, h, ct)], skey=("ost", ob, hl))
            yield

        ntile = DBG.get('tiles', NTILE)
        active, nxt_t = [], [0]

        def start():
            if nxt_t[0] < ntile and len(active) < 2:
                active.append(g2_tile(nxt_t[0], list(hs)))
                nxt_t[0] += 1
        start()
        while active:
            for g in list(active):
                try:
                    v = next(g)
                except StopIteration:
                    active.remove(g)
                    start()
                    continue
                if v == "MID":
                    start()
        cx.barrier()
        cx.pop()
    cx.barrier()
    cx.pop()


def gdn_out_phase(cx, c, tag, x_in, x_out, oraw_d, nw_d, gnw_d, w_in, w_out, xkey_in, xkey_out, okey):
    cx.push()
    HT = 2048
    NT = HT // TT
    WB = 256
    nheads = NVH
    K = lambda n: tag + n
    hT = cx.sbuf(tag + "hT", [128, KC, HT], BF16)
    osb = cx.sbuf(tag + "osb", [128, nheads, HT], BF16)
    gnw = cx.sbuf(tag + "gnw", [128, 1], F32)
    nw = cx.sbuf(tag + "nw", [128, KC], F32)
    xt = [cx.sbuf(tag + "xt%d" % i, [128, KC, TT], F32) for i in range(1)]
    sq = cx.sbuf(tag + "sq", [128, KC, TT], BF16)
    rstd = cx.sbuf(tag + "rstd", [128, TT], F32)
    wz = [cx.sbuf(tag + "wz%d" % i, [128, KC, 128], BF16) for i in range(2)]
    orw = [cx.sbuf(tag + "orw%d" % i, [128, TT], F32) for i in range(3)]
    osq = [cx.sbuf(tag + "osq%d" % i, [128, TT], BF16) for i in range(3)]
    orn = [cx.sbuf(tag + "orn%d" % i, [128, TT], F32) for i in range(3)]
    sz = [cx.sbuf(tag + "sz%d" % i, [128, TT], F32) for i in range(3)]
    wo = [cx.sbuf(tag + "wo%d" % i, [128, nheads, WB], BF16) for i in range(2)]
    xr = [cx.sbuf(tag + "xr%d" % i, [128, TT], F32) for i in range(2)]
    xn = [cx.sbuf(tag + "xn%d" % i, [128, TT], F32) for i in range(2)]
    ps = [cx.psum(tag + "ps%d" % i, [128, TT], F32) for i in range(8)]
    pk = [tag + "ps%d" % i for i in range(8)]
    cx.psum_keys.update(pk)
    w_in_r = w_in.rearrange("(kc p) n -> p kc n", p=128)
    w_out_r = w_out.rearrange("(kc p) n -> p kc n", p=128)
    x_in_r = x_in.rearrange("(kc p) t -> p kc t", p=128)
    cx.dma("sp", nw[:], nw_d, writes=[K("nw")], skey="nw")
    cx.dma("sp", gnw[:], gnw_d, writes=[K("gnw")], skey="gnw")
    cnt = {"o": 0, "x": 0, "z": 0, "w": 0}
    def norm_tile_ops(hf_, tt, part=None):
        t0_ = hf_ * HT
        gt_ = (t0_ + tt * TT) // TT
        if part in (None, "A"):
            cx.dma("sp", xt[0][:], x_in_r[:, :, t0_ + tt * TT: t0_ + (tt + 1) * TT],
                   reads=[(xkey_in, gt_, kc) for kc in range(KC)], writes=[K("xt")], skey="xt")
        rmsnorm_tile(cx, c, xt[0], nw, lambda kc, tt=tt: hT[:, kc, tt * TT:(tt + 1) * TT], tag,
                     [K("xt")], (K("hT"), tt), pk[5], ps[5], {"sq": sq, "rstd": rstd}, part=part)

    for hf in range(T // HT):
        t0 = hf * HT
        if hf == 0:
            for tt in range(NT):
                norm_tile_ops(0, tt)
        def gate_piece(h, ws, tt, item):
            b = item % 3
            bZ, bQ = b, 3 + b
            gt_ = (t0 + tt * TT) // TT
            cx.dma("sp", orw[b][:], oraw_d[h * 128:(h + 1) * 128, t0 + tt * TT: t0 + (tt + 1) * TT],
                   reads=[(okey, h, gt_)], writes=[(K("orw"), b)], skey=("orw", b))

            def mmz(e):
                ins = None
                for kc in range(KC):
                    ins = e.matmul(ps[bZ][:], wz[ws][:, kc, :], hT[:, kc, tt * TT:(tt + 1) * TT],
                                   start=(kc == 0), stop=(kc == KC - 1))
                return ins
            cx.op("pe", mmz, reads=[(K("wz"), ws), (K("hT"), tt)], writes=[pk[bZ]])
            yield
            cx.op("act", lambda e: e.activation(osq[b][:], orw[b][:], AF.Square),
                  reads=[(K("orw"), b)], writes=[(K("osq"), b)])
            yield "NEXT"
            cx.op("pe", lambda e: e.matmul(ps[bQ][:], c["ones_bf"][:], osq[b][:], start=True, stop=True),
                  reads=[(K("osq"), b), "ones_bf"], writes=[pk[bQ]])
            yield
            cx.op("act", lambda e: e.activation(orn[b][:], ps[bQ][:], AF.Sqrt, bias=EPS, scale=1.0 / 128),
                  reads=[pk[bQ]], writes=[(K("orn"), b)])
            cx.op("act", lambda e: e.activation(sz[b][:], ps[bZ][:], AF.Silu), reads=[pk[bZ]],
                  writes=[(K("sz"), b)])
            yield
            cx.op("dve", lambda e: e.reciprocal(orn[b][:], orn[b][:]), reads=[(K("orn"), b)],
                  writes=[(K("orn"), b)])
            yield
            cx.op("dve", lambda e: e.scalar_tensor_tensor(
                out=orn[b][:], in0=orw[b][:], scalar=gnw[:, 0:1], in1=orn[b][:], op0=ALU.mult, op1=ALU.mult),
                reads=[(K("orw"), b), (K("orn"), b), K("gnw")], writes=[(K("orn"), b)])
            yield
            cx.op("dve", lambda e: e.tensor_tensor(
                out=osb[:, h, tt * TT:(tt + 1) * TT], in0=orn[b][:], in1=sz[b][:], op=ALU.mult),
                reads=[(K("orn"), b), (K("sz"), b)], writes=[(K("osb"), h)])
            yield

        gitems = [(h, tt) for h in range(nheads) for tt in range(NT)]
        act_g, nx = [], [0]

        def start_g():
            if nx[0] < len(gitems) and len(act_g) < 3:
                h, tt = gitems[nx[0]]
                if tt == 0:
                    ws = cnt["w"] % 2
                    cnt["w"] += 1
                    cx.dma("pool", wz[ws][:], w_in_r[:, :, 4096 + h * 128: 4096 + (h + 1) * 128],
                           writes=[(K("wz"), ws)], skey=("wz", ws))
                ws = (cnt["w"] - 1) % 2
                act_g.append(gate_piece(h, ws, tt, cnt["z"]))
                cnt["z"] += 1
                nx[0] += 1
        start_g()
        while act_g:
            for g_ in list(act_g):
                try:
                    v = next(g_)
                except StopIteration:
                    act_g.remove(g_)
                    start_g()
                    continue
                if v == "NEXT" and g_ is act_g[-1]:
                    start_g()
        for mb in range(D // WB):
            s = cnt["o"] % 2
            cnt["o"] += 1
            cx.dma("pool", wo[s][:], w_out_r[:, :, mb * WB:(mb + 1) * WB], writes=[(K("wo"), s)], skey=("wo", s))
            for mm_ in range(WB // 128):
                mc = mb * (WB // 128) + mm_
                for tt in range(NT):
                    gt_ = (t0 + tt * TT) // TT
                    b = cnt["x"] % 2
                    cnt["x"] += 1
                    py = ps[6 + b]
                    cx.dma("sp", xr[b][:], x_in[mc * 128:(mc + 1) * 128, t0 + tt * TT: t0 + (tt + 1) * TT],
                           reads=[(xkey_in, gt_, mc)], writes=[(K("xr"), b)], skey=("xr", b))

                    def mm2(e, s=s, mm_=mm_, tt=tt, py=py):
                        ins = None
                        for kc in range(nheads):
                            ins = e.matmul(py[:], wo[s][:, kc, mm_ * 128:(mm_ + 1) * 128],
                                           osb[:, kc, tt * TT:(tt + 1) * TT], start=(kc == 0), stop=(kc == nheads - 1))
                        return ins
                    cx.op("pe", mm2, reads=[(K("wo"), s)] + [(K("osb"), hd) for hd in range(nheads)],
                          writes=[pk[6 + b]])
                    cx.op("dve", lambda e, b=b, py=py: e.tensor_tensor(
                        out=xn[b][:], in0=py[:], in1=xr[b][:], op=ALU.add),
                        reads=[pk[6 + b], (K("xr"), b)], writes=[(K("xn"), b)])
                    cx.dma("sp", x_out[mc * 128:(mc + 1) * 128, t0 + tt * TT: t0 + (tt + 1) * TT], xn[b][:],
                           reads=[(K("xn"), b)], writes=[(xkey_out, gt_, mc)], skey=("xn", b))
                if hf + 1 < T // HT:
                    norm_tile_ops(hf + 1, mc // 2, part=("A" if mc % 2 == 0 else "B"))
    cx.barrier()
    cx.pop()


def build_gdn_prog():
    nc = bass.Bass("TRN2", target_bir_lowering=False)
    x = nc.dram_tensor("x", [D, T], F32, kind="ExternalInput").ap()
    nw = nc.dram_tensor("nw", [128, KC], F32, kind="ExternalInput").ap()
    w_in = nc.dram_tensor("w_in", [D, GP], F32, kind="ExternalInput").ap()
    w_out = nc.dram_tensor("w_out", [NVH * 128, D], F32, kind="ExternalInput").ap()
    convw = nc.dram_tensor("convw", [128, 32, 4], F32, kind="ExternalInput").ap()
    alog = nc.dram_tensor("alog", [128, 512], F32, kind="ExternalInput").ap()
    dtb = nc.dram_tensor("dtb", [128, 512], F32, kind="ExternalInput").ap()
    gnw = nc.dram_tensor("gnw", [128, 1], F32, kind="ExternalInput").ap()
    cst = nc.dram_tensor("cst", [128, CST_W], F32, kind="ExternalInput").ap()
    oraw = nc.dram_tensor("oraw_scratch", [NVH * 128, T], F32, kind=("ExternalOutput" if DBG.get("noout") else "Internal")).ap()
    y = nc.dram_tensor("y", [D, T], F32, kind="ExternalOutput").ap()
    cx = Ctx(nc)
    c = consts(cx)
    gdn_core_phase(cx, c, "g_", x, oraw, nw, w_in, convw, alog, dtb, cst, "xin", "or")
    if not DBG.get("noout"):
        gdn_out_phase(cx, c, "go_", x, y, oraw, nw, gnw, w_in, w_out, "xin", "xout", "or")
    cx.finish()
    return nc


def final_norm_phase(cx, c, tag, x_in, y_out, nw_d, xkey_in):
    cx.push()
    xt = [cx.sbuf(tag + "xt%d" % i, [128, KC, TT], F32) for i in range(2)]
    yo = [cx.sbuf(tag + "yo%d" % i, [128, KC, TT], F32) for i in range(2)]
    sq = cx.sbuf(tag + "sq", [128, KC, TT], BF16)
    rstd = cx.sbuf(tag + "rstd", [128, TT], F32)
    nw = cx.sbuf(tag + "nw", [128, KC], F32)
    ps = cx.psum(tag + "ps", [128, TT], F32)
    cx.psum_keys.add(tag + "ps")
    cx.dma("sp", nw[:], nw_d, writes=[tag + "nw"], skey="nw")
    x_in_r = x_in.rearrange("(kc p) t -> p kc t", p=128)
    y_out_r = y_out.rearrange("(kc p) t -> p kc t", p=128)
    for tt in range(NTT):
        b = tt % 2
        cx.dma("sp", xt[b][:], x_in_r[:, :, tt * TT:(tt + 1) * TT],
               reads=[(xkey_in, tt, kc) for kc in range(KC)], writes=[(tag + "xt", b)], skey=("xt", b))
        rmsnorm_tile(cx, c, xt[b], nw, lambda kc, b=b: yo[b][:, kc, :], tag,
                     [(tag + "xt", b)], (tag + "yo", b), tag + "ps", ps, {"sq": sq, "rstd": rstd})
        cx.dma("sp", y_out_r[:, :, tt * TT:(tt + 1) * TT], yo[b][:], reads=[(tag + "yo", b)],
               writes=[("yout", tt)], skey=("yo", b))
    cx.barrier()
    cx.pop()


def build_fin_prog():
    nc = bass.Bass("TRN2", target_bir_lowering=False)
    x = nc.dram_tensor("x", [D, T], F32, kind="ExternalInput").ap()
    nw = nc.dram_tensor("nw", [128, KC], F32, kind="ExternalInput").ap()
    y = nc.dram_tensor("y", [D, T], F32, kind="ExternalOutput").ap()
    cx = Ctx(nc)
    c = consts(cx)
    final_norm_phase(cx, c, "n_", x, y, nw, "xin")
    cx.finish()
    return nc


def build_full_prog():
    nc = bass.Bass("TRN2", target_bir_lowering=False)
    dt = lambda name, shape, kind="ExternalInput", dtype=F32: nc.dram_tensor(name, list(shape), dtype, kind=kind).ap()
    x = dt("x", [D, T])
    nws = dt("nws", [13, 128, KC])
    ffn_w_in = dt("ffn_w_in", [4, 2, D, 2 * DFF])
    ffn_w_out = dt("ffn_w_out", [4, 2, DFF, D])
    attn_w_in = dt("attn_w_in", [2, D, 3 * NH_A * 128])
    attn_w_out = dt("attn_w_out", [2, NH_A * 128, D])
    gdn_w_in = dt("gdn_w_in", [2, D, GP])
    gdn_w_out = dt("gdn_w_out", [2, NVH * 128, D])
    convw = dt("convw", [2, 128, 32, 4])
    alog = dt("alog", [2, 128, 512])
    dtb = dt("dtb", [2, 128, 512])
    gnw = dt("gnw", [2, 128, 1])
    ctab = dt("ctab", [128, T])
    stab = dt("stab", [128, T])
    cst = dt("cst", [128, CST_W])
    y = dt("y", [D, T], kind="ExternalOutput")
    xbuf = [nc.dram_tensor("xres%d" % i, [D, T], F32).ap() for i in range(2)]
    os_d = nc.dram_tensor("os_scratch", [NH_A * 128, T], BF16).ap()
    oraw = nc.dram_tensor("oraw_scratch", [NVH * 128, T], F32).ap()
    cx = Ctx(nc)
    c = consts(cx)
    cur, nb_, ph = x, 0, 0
    ia = ib = 0
    for i in range(4):
        ffn_phase(cx, c, "f%da_" % i, cur, xbuf[nb_], nws[ph], ffn_w_in[i, 0], ffn_w_out[i, 0], "xi", "xo")
        cur, nb_, ph = xbuf[nb_], 1 - nb_, ph + 1
        if i % 2 == 0:
            attn_phase(cx, c, "a%d_" % i, cur, os_d, nws[ph], attn_w_in[ia], ctab, stab, cst, "xi", "os")
            outproj_phase(cx, c, "ao%d_" % i, cur, xbuf[nb_], os_d, attn_w_out[ia], NH_A, "xi", "xo", "os")
            ia += 1
        else:
            gdn_core_phase(cx, c, "g%d_" % i, cur, oraw, nws[ph], gdn_w_in[ib], convw[ib], alog[ib], dtb[ib], cst, "xi", "or")
            gdn_out_phase(cx, c, "go%d_" % i, cur, xbuf[nb_], oraw, nws[ph], gnw[ib], gdn_w_in[ib], gdn_w_out[ib],
                          "xi", "xo", "or")
            ib += 1
        cur, nb_, ph = xbuf[nb_], 1 - nb_, ph + 1
        ffn_phase(cx, c, "f%db_" % i, cur, xbuf[nb_], nws[ph], ffn_w_in[i, 1], ffn_w_out[i, 1], "xi", "xo")
        cur, nb_, ph = xbuf[nb_], 1 - nb_, ph + 1
    final_norm_phase(cx, c, "n_", cur, y, nws[ph], "xi")
    cx.finish()
    return nc


_PROGS = {}


def _prog(name):
    if name not in _PROGS:
        _PROGS[name] = {"ffn": build_ffn_prog, "attn": build_attn_prog, "gdn": build_gdn_prog,
                        "fin": build_fin_prog}[name]()
    return _PROGS[name]


def _lay_nw(v):
    return np.ascontiguousarray(np.asarray(v, np.float32).reshape(KC, 128).T)


def _launch(name, xs, shared):
    nc = _prog(name)
    n = len(xs)
    in_maps = [dict(shared, x=xs[i]) for i in range(n)]
    res = run_bass_kernel_spmd(nc, in_maps, core_ids=list(range(n)))
    return [np.asarray(res.results[i]["y"]) for i in range(n)]


def kernel(x, norm_w, ffn_w_in, ffn_w_out, attn_w_in, attn_w_out, gdn_w_in, gdn_conv_w, gdn_a_log,
           gdn_dt_bias, gdn_norm_w, gdn_w_out, final_norm_w):
    f = lambda a: np.ascontiguousarray(np.asarray(a, np.float32))
    x = f(x)
    B = x.shape[0]
    ctab, stab = host_rope()
    nws = np.stack([_lay_nw(norm_w[i, j]) for i in range(4) for j in range(3)] + [_lay_nw(final_norm_w)], 0)
    shared = {
        "nws": np.ascontiguousarray(nws), "ffn_w_in": f(ffn_w_in), "ffn_w_out": f(ffn_w_out),
        "attn_w_in": f(attn_w_in), "attn_w_out": f(attn_w_out), "gdn_w_in": f(gdn_w_in), "gdn_w_out": f(gdn_w_out),
        "convw": np.ascontiguousarray(f(gdn_conv_w).reshape(2, 4, 32, 128).transpose(0, 3, 2, 1)),
        "alog": np.ascontiguousarray(np.tile(f(gdn_a_log)[:, None, :], (1, 128, 32))),
        "dtb": np.ascontiguousarray(np.tile(f(gdn_dt_bias)[:, None, :], (1, 128, 32))),
        "gnw": np.ascontiguousarray(f(gdn_norm_w).reshape(2, 128, 1)),
        "ctab": ctab, "stab": stab, "cst": host_consts(),
    }
    if "full" not in _PROGS:
        _PROGS["full"] = build_full_prog()
    nc = _PROGS["full"]
    in_maps = [dict(shared, x=np.ascontiguousarray(x[b].T)) for b in range(B)]
    res = run_bass_kernel_spmd(nc, in_maps, core_ids=list(range(B)))
    return np.stack([np.asarray(res.results[b]["y"]).T for b in range(B)], 0).astype(np.float32)
```

```python
from contextlib import ExitStack
import numpy as np
import concourse.bass as bass
import concourse.mybir as mybir
from concourse.bass_utils import run_bass_kernel_spmd

F32 = mybir.dt.float32
BF16 = mybir.dt.bfloat16
ALU = mybir.AluOpType
AF = mybir.ActivationFunctionType

ENGS = ("pe", "act", "dve", "pool", "sp")
NAMES = []


class _Op:
    __slots__ = ("eng", "fn", "deps", "idx", "is_dma", "dsem", "dval", "need_inc", "cnt")

    def __init__(self, eng, fn, is_dma):
        self.eng = eng
        self.fn = fn
        self.deps = []
        self.is_dma = is_dma
        self.dsem = None
        self.dval = 0
        self.need_inc = False
        self.cnt = 0


class Ctx:
    def __init__(self, nc):
        self.nc = nc
        self.q = {e: [] for e in ENGS}
        self.wr = {}
        self.rd = {}
        self.dsems = {}
        self.pending_dma = []
        self.sem_pool = {}
        self.psum_keys = set()
        self.stack = ExitStack()
        self.scopes = []
        self.esem = {e: nc.alloc_semaphore("sem_" + e) for e in ENGS}
        self.n_psum = 0

    def push(self):
        st = ExitStack()
        self.scopes.append(st)

    def pop(self):
        self.scopes.pop().close()

    def _scope(self):
        return self.scopes[-1] if self.scopes else self.stack

    def sbuf(self, name, shape, dtype):
        self.n_psum += 1
        NAMES.append("%s_%d" % (name, self.n_psum))
        return self._scope().enter_context(self.nc.sbuf_tensor("%s_%d" % (name, self.n_psum), list(shape), dtype))

    def psum(self, name, shape, dtype=F32):
        self.n_psum += 1
        return self._scope().enter_context(self.nc.psum_tensor("%s_%d" % (name, self.n_psum), list(shape), dtype))

    def _track(self, op, reads, writes):
        deps = op.deps
        for k in reads:
            for o in self.wr.get(k, {}).values():
                deps.append(o)
            if k in self.psum_keys:
                for ek2, o in self.rd.get(k, {}).items():
                    if o.eng != op.eng:
                        deps.append(o)
        for k in writes:
            for o in self.wr.get(k, {}).values():
                deps.append(o)
            for o in self.rd.get(k, {}).values():
                deps.append(o)
        ek = id(op) if op.is_dma else op.eng
        for k in reads:
            self.rd.setdefault(k, {})[ek] = op
        for k in writes:
            self.wr[k] = {ek: op}
            self.rd[k] = {}

    def op(self, eng, fn, reads=(), writes=()):
        o = _Op(eng, fn, False)
        self._track(o, reads, writes)
        o.idx = len(self.q[eng])
        self.q[eng].append(o)
        return o

    def dma(self, eng, out, in_, reads=(), writes=(), skey=None, **kw):
        key = ("dma", skey)
        o = _Op(eng, lambda e: e.dma_start(out=out, in_=in_, **kw), True)
        self._track(o, reads, writes)
        key = (eng == "pool", key)
        if key not in self.dsems:
            pool = self.sem_pool.setdefault(eng == "pool", [])
            i = sum(1 for k in self.dsems if k[0] == key[0])
            if i >= len(pool):
                pool.append([self.nc.alloc_semaphore("d%s%d" % ("s" if key[0] else "h", i)), 0])
            self.dsems[key] = pool[i]
        ds = self.dsems[key]
        ds[1] += 16
        o.dsem = ds[0]
        o.dval = ds[1]
        o.idx = len(self.q[eng])
        self.q[eng].append(o)
        self.pending_dma.append(o)
        return o

    def barrier(self):
        lasts = [self.q[e][-1] for e in ENGS if self.q[e] and not self.q[e][-1].is_dma]
        lasts += [o for e in ENGS for o in self.q[e][-1:] if False]
        dm = list(self.pending_dma)
        self.pending_dma = []
        for e in ENGS:
            o = _Op(e, None, False)
            for e2 in ENGS:
                for p in reversed(self.q[e2]):
                    if not p.is_dma and p.fn is not None:
                        o.deps.append(p)
                        break
            o.deps.extend(dm)
            o.idx = len(self.q[e])
            self.q[e].append(o)
        self.wr = {}
        self.rd = {}
        self.dsems = {}

    def finish(self):
        nc = self.nc
        self.barrier()
        for e in ENGS:
            for o in self.q[e]:
                for d in o.deps:
                    if not d.is_dma and (d.eng != o.eng or e != "pe"):
                        d.need_inc = True
        for e in ENGS:
            c = 0
            for o in self.q[e]:
                if o.need_inc:
                    c += 1
                o.cnt = c
        engobj = {"pe": "tensor", "act": "scalar", "dve": "vector", "pool": "gpsimd", "sp": "sync"}
        with nc.Block() as block:
            for e in ENGS:
                def body(eng, e=e):
                    known = {}
                    for o in self.q[e]:
                        waits = {}
                        for d in o.deps:
                            if d.is_dma:
                                s, v = d.dsem, d.dval
                            else:
                                if d.eng == e and e == "pe":
                                    continue
                                s, v = self.esem[d.eng], d.cnt
                            if v > waits.get(s, (None, 0))[1]:
                                waits[s] = (s, v)
                        for s, v in waits.values():
                            if known.get(s, 0) >= v:
                                continue
                            known[s] = v
                            eng.wait_ge(s, v)
                        if o.fn is None:
                            continue
                        ins = o.fn(eng)
                        if o.is_dma:
                            ins.then_inc(o.dsem, 16)
                        elif o.need_inc:
                            ins.then_inc(self.esem[e], 1)
                getattr(block, engobj[e])(body)
        self.stack.close()


D = 1024
T = 4096
KC = D // 128
DFF = 2816
JC = DFF // 128
TT = 512
NTT = T // TT
EPS = 1e-6


def consts(cx):
    nc = cx.nc
    c = {}
    c["ones_bf"] = cx.sbuf("ones_bf", [128, 128], BF16)
    cx.op("pool", lambda e: e.memset(c["ones_bf"][:], 1.0), writes=["ones_bf"])
    return c


def rmsnorm_tile(cx, c, xt, nw, hT_out, tag, rd_keys, wr_key, ps_key, ps, scr, part=None):
    sq, rstd = scr["sq"], scr["rstd"]
    if part in (None, "A"):
        cx.op("act", lambda e: e.activation(sq[:], xt[:], AF.Square), reads=rd_keys, writes=[tag + "sq"])
    if part == "A":
        return

    def mm(e):
        ins = None
        for kc in range(KC):
            ins = e.matmul(ps[:], c["ones_bf"][:], sq[:, kc, :], start=(kc == 0), stop=(kc == KC - 1))
        return ins
    cx.op("pe", mm, reads=[tag + "sq", "ones_bf"], writes=[ps_key])
    cx.op("act", lambda e: e.activation(rstd[:], ps[:], AF.Sqrt, bias=EPS, scale=1.0 / D),
          reads=[ps_key], writes=[tag + "rstd"])
    cx.op("dve", lambda e: e.reciprocal(rstd[:], rstd[:]), reads=[tag + "rstd"], writes=[tag + "rstd"])
    for kc in range(KC):
        cx.op("dve", lambda e, kc=kc: e.scalar_tensor_tensor(
            out=hT_out(kc), in0=xt[:, kc, :], scalar=nw[:, kc:kc + 1], in1=rstd[:],
            op0=ALU.mult, op1=ALU.mult), reads=rd_keys + [tag + "rstd", tag + "nw"], writes=[wr_key])


def ffn_phase(cx, c, tag, x_in, x_out, nw_d, w_in, w_out, xkey_in, xkey_out):
    nc = cx.nc
    cx.push()
    HT = 2048
    NH = T // HT
    NT = HT // TT
    hT = cx.sbuf(tag + "hT", [128, KC, HT], BF16)
    act = cx.sbuf(tag + "act", [128, JC, HT], BF16)
    xt = cx.sbuf(tag + "xt", [128, KC, TT], F32)
    sq = cx.sbuf(tag + "sq", [128, KC, TT], BF16)
    rstd = cx.sbuf(tag + "rstd", [128, TT], F32)
    nw = cx.sbuf(tag + "nw", [128, KC], F32)
    WB = 256
    wg = [cx.sbuf(tag + "wg%d" % i, [128, KC, WB], BF16) for i in range(2)]
    wu = [cx.sbuf(tag + "wu%d" % i, [128, KC, WB], BF16) for i in range(2)]
    wo = [cx.sbuf(tag + "wo%d" % i, [128, JC, WB], BF16) for i in range(2)]
    sg = [cx.sbuf(tag + "sg%d" % i, [128, TT], F32) for i in range(2)]
    xr = [cx.sbuf(tag + "xr%d" % i, [128, TT], F32) for i in range(2)]
    xn = [cx.sbuf(tag + "xn%d" % i, [128, TT], F32) for i in range(2)]
    ps = [cx.psum(tag + "ps%d" % i, [128, TT], F32) for i in range(7)]
    pk = [tag + "ps%d" % i for i in range(7)]
    cx.psum_keys.update(pk)

    cx.dma("sp", nw[:], nw_d, writes=[tag + "nw"], skey="nw")
    w_in_r = w_in.rearrange("(kc p) n -> p kc n", p=128)
    w_out_r = w_out.rearrange("(kc p) n -> p kc n", p=128)
    x_in_r = x_in.rearrange("(kc p) t -> p kc t", p=128)
    cnt = {"g": 0, "o": 0, "s": 0, "x": 0}
    def norm_tile_ops(h, tt, part=None):
        t0_ = h * HT
        gt = (t0_ + tt * TT) // TT
        if part in (None, "A"):
            cx.dma("sp", xt[:], x_in_r[:, :, t0_ + tt * TT: t0_ + (tt + 1) * TT],
                   reads=[(xkey_in, gt, kc) for kc in range(KC)], writes=[tag + "xt"], skey="xt")
        rmsnorm_tile(cx, c, xt, nw, lambda kc, tt=tt: hT[:, kc, tt * TT:(tt + 1) * TT], tag,
                     [tag + "xt"], (tag + "hT", tt), pk[6], ps[6], {"sq": sq, "rstd": rstd}, part=part)

    for h in range(NH):
        t0 = h * HT
        if h == 0:
            for tt in range(NT):
                norm_tile_ops(0, tt)
        for jb in range(JC * 128 // WB):
            s = cnt["g"] % 2
            cnt["g"] += 1
            cx.dma("pool", wg[s][:], w_in_r[:, :, jb * WB:(jb + 1) * WB], writes=[(tag + "wg", s)],
                   skey=("wg", s))
            cx.dma("pool", wu[s][:], w_in_r[:, :, DFF + jb * WB: DFF + (jb + 1) * WB], writes=[(tag + "wu", s)],
                   skey=("wu", s))
            for jj in range(WB // 128):
                j = jb * (WB // 128) + jj
                for tt in range(NT):
                    b = cnt["s"] % 2
                    cnt["s"] += 1
                    pg, pu = ps[b], ps[2 + b]

                    def mm(e, s=s, jj=jj, tt=tt, pg=pg, pu=pu):
                        ins = None
                        for kc in range(KC):
                            ins = e.matmul(pg[:], wg[s][:, kc, jj * 128:(jj + 1) * 128],
                                           hT[:, kc, tt * TT:(tt + 1) * TT], start=(kc == 0), stop=(kc == KC - 1))
                        for kc in range(KC):
                            ins = e.matmul(pu[:], wu[s][:, kc, jj * 128:(jj + 1) * 128],
                                           hT[:, kc, tt * TT:(tt + 1) * TT], start=(kc == 0), stop=(kc == KC - 1))
                        return ins
                    cx.op("pe", mm, reads=[(tag + "wg", s), (tag + "wu", s), (tag + "hT", tt)],
                          writes=[pk[b], pk[2 + b]])
                    cx.op("act", lambda e, b=b, pg=pg: e.activation(sg[b][:], pg[:], AF.Silu),
                          reads=[pk[b]], writes=[(tag + "sg", b)])
                    cx.op("dve", lambda e, b=b, pu=pu, j=j, tt=tt: e.tensor_tensor(
                        out=act[:, j, tt * TT:(tt + 1) * TT], in0=sg[b][:], in1=pu[:], op=ALU.mult),
                        reads=[(tag + "sg", b), pk[2 + b]], writes=[(tag + "act", j, tt)])
        for mb in range(D // WB):
            s = cnt["o"] % 2
            cnt["o"] += 1
            cx.dma("pool", wo[s][:], w_out_r[:, :, mb * WB:(mb + 1) * WB], writes=[(tag + "wo", s)],
                   skey=("wo", s))
            for mm_ in range(WB // 128):
                mc = mb * (WB // 128) + mm_
                for tt in range(NT):
                    gt = (t0 + tt * TT) // TT
                    b = cnt["x"] % 2
                    cnt["x"] += 1
                    py = ps[4 + b]
                    cx.dma("sp", xr[b][:], x_in[mc * 128:(mc + 1) * 128, t0 + tt * TT: t0 + (tt + 1) * TT],
                           reads=[(xkey_in, gt, mc)], writes=[(tag + "xr", b)], skey=("xr", b))

                    def mm2(e, s=s, mm_=mm_, tt=tt, py=py):
                        ins = None
                        for kc in range(JC):
                            ins = e.matmul(py[:], wo[s][:, kc, mm_ * 128:(mm_ + 1) * 128],
                                           act[:, kc, tt * TT:(tt + 1) * TT], start=(kc == 0), stop=(kc == JC - 1))
                        return ins
                    cx.op("pe", mm2, reads=[(tag + "wo", s)] + [(tag + "act", j, tt) for j in range(JC)],
                          writes=[pk[4 + b]])
                    cx.op("dve", lambda e, b=b, py=py: e.scalar_tensor_tensor(
                        out=xn[b][:], in0=py[:], scalar=0.5, in1=xr[b][:], op0=ALU.mult, op1=ALU.add),
                        reads=[pk[4 + b], (tag + "xr", b)], writes=[(tag + "xn", b)])
                    cx.dma("sp", x_out[mc * 128:(mc + 1) * 128, t0 + tt * TT: t0 + (tt + 1) * TT], xn[b][:],
                           reads=[(tag + "xn", b)], writes=[(xkey_out, gt, mc)], skey=("xn", b))
                if h + 1 < NH:
                    hk = mc
                    norm_tile_ops(h + 1, hk // 2, part=("A" if hk % 2 == 0 else "B"))
    cx.barrier()
    cx.pop()


def build_ffn_prog():
    nc = bass.Bass("TRN2", target_bir_lowering=False)
    x = nc.dram_tensor("x", [D, T], F32, kind="ExternalInput").ap()
    nw = nc.dram_tensor("nw", [128, KC], F32, kind="ExternalInput").ap()
    w_in = nc.dram_tensor("w_in", [D, 2 * DFF], F32, kind="ExternalInput").ap()
    w_out = nc.dram_tensor("w_out", [DFF, D], F32, kind="ExternalInput").ap()
    y = nc.dram_tensor("y", [D, T], F32, kind="ExternalOutput").ap()
    cx = Ctx(nc)
    c = consts(cx)
    ffn_phase(cx, c, "f_", x, y, nw, w_in, w_out, "xin", "xout")
    cx.finish()
    return nc


NH_A = 12
DIL = (1, 4, 16)
NEG = -30000.0
SCALE_A = 128 ** -0.5
C_IDENT = 0
C_PSWAP = 128
C_MASK = 256
C_TRI = 512
C_TRIS = 640
C_MINC = 768
C_MSTR = 896
C_LAST = 1024
CST_W = 1152


def host_consts():
    m = np.zeros((128, CST_W), np.float32)
    k = np.arange(128)[:, None]
    q = np.arange(128)[None, :]
    m[:, C_IDENT:C_IDENT + 128] = (k == q)
    m[:, C_PSWAP:C_PSWAP + 128] = (k == (q + 64) % 128)
    m[:, C_MASK:C_MASK + 128] = np.where(k <= q, 0.0, NEG)
    m[:, C_MASK + 128:C_MASK + 256] = np.where(k >= q, 0.0, NEG)
    m[:, C_TRI:C_TRI + 128] = (k <= q)
    m[:, C_TRIS:C_TRIS + 128] = (k > q)
    m[:, C_MINC:C_MINC + 128] = (k <= q)
    m[:, C_MSTR:C_MSTR + 128] = (k < q)
    m[:, C_LAST:C_LAST + 128] = (k == 127)
    return m


def host_rope():
    inv = (1.0 / (10000.0 ** (np.arange(0, 128, 2, dtype=np.float32) / np.float32(128)))).astype(np.float32)
    ang = (np.arange(T, dtype=np.float32)[None, :] * inv[:, None]).astype(np.float32)
    cs, sn = np.cos(ang).astype(np.float32), np.sin(ang).astype(np.float32)
    return np.concatenate([cs, cs], 0), np.concatenate([-sn, sn], 0)


def norm_stage(cx, c, tag, x_in, nw_d, hT, xkey_in, ps, pskey):
    cx.push()
    xt = [cx.sbuf(tag + "nxt%d" % i, [128, KC, TT], F32) for i in range(2)]
    sq = cx.sbuf(tag + "nsq", [128, KC, TT], BF16)
    rstd = cx.sbuf(tag + "nrstd", [128, TT], F32)
    nw = cx.sbuf(tag + "nnw", [128, KC], F32)
    cx.dma("sp", nw[:], nw_d, writes=[tag + "nw"], skey="nw")
    x_in_r = x_in.rearrange("(kc p) t -> p kc t", p=128)
    for tt in range(NTT):
        b = tt % 2
        cx.dma("sp", xt[b][:], x_in_r[:, :, tt * TT:(tt + 1) * TT],
               reads=[(xkey_in, tt, kc) for kc in range(KC)], writes=[(tag + "xt", b)], skey=("xt", b))
        rmsnorm_tile(cx, c, xt[b], nw, lambda kc, tt=tt: hT[:, kc, tt * TT:(tt + 1) * TT], tag,
                     [(tag + "xt", b)], (tag + "hT", tt), pskey, ps, {"sq": sq, "rstd": rstd})
    cx.barrier()
    cx.pop()


def attn_phase(cx, c, tag, x_in, os_d, nw_d, w_in, ctab_d, stab_d, cst_d, xkey_in, okey):
    nc = cx.nc
    cx.push()
    hT = cx.sbuf(tag + "hT", [128, KC, T], BF16)
    ctab = cx.sbuf(tag + "ctab", [128, T], F32)
    stab = cx.sbuf(tag + "stab", [128, T], F32)
    ident = cx.sbuf(tag + "ident", [128, 128], BF16)
    pswap = cx.sbuf(tag + "pswap", [128, 128], BF16)
    maskb = cx.sbuf(tag + "maskb", [128, 256], BF16)
    ps = [cx.psum(tag + "ps%d" % i, [128, TT], F32) for i in range(8)]
    pk = [tag + "ps%d" % i for i in range(8)]
    cx.psum_keys.update(pk)
    cx.dma("sp", ctab[:], ctab_d, writes=[tag + "ctab"], skey="ctab")
    cx.dma("sp", stab[:], stab_d, writes=[tag + "stab"], skey="stab")
    cx.dma("pool", ident[:], cst_d[:, C_IDENT:C_IDENT + 128], writes=[tag + "ident"], skey="ident")
    cx.dma("pool", pswap[:], cst_d[:, C_PSWAP:C_PSWAP + 128], writes=[tag + "pswap"], skey="pswap")
    cx.dma("pool", maskb[:], cst_d[:, C_MASK:C_MASK + 256], writes=[tag + "maskb"], skey="maskb")
    norm_stage(cx, c, tag, x_in, nw_d, hT, xkey_in, ps[6], pk[6])
    hkeys = []

    Qd = cx.sbuf(tag + "Qd", [128, T], BF16)
    Kd = cx.sbuf(tag + "Kd", [128, T], BF16)
    Vd = cx.sbuf(tag + "Vd", [128, 32, 128], BF16)
    numT = cx.sbuf(tag + "numT", [128, 3, T], BF16)
    dent = cx.sbuf(tag + "dent", [128, T], F32)
    wq = [cx.sbuf(tag + "wq%d" % i, [128, KC, 128], BF16) for i in range(2)]
    wk = [cx.sbuf(tag + "wk%d" % i, [128, KC, 128], BF16) for i in range(2)]
    wv = [cx.sbuf(tag + "wv%d" % i, [128, KC, 128], BF16) for i in range(2)]
    qb = [cx.sbuf(tag + "qb%d" % i, [128, TT], BF16) for i in range(3)]
    t1 = [cx.sbuf(tag + "t1%d" % i, [128, TT], F32) for i in range(3)]
    t2 = [cx.sbuf(tag + "t2%d" % i, [128, TT], F32) for i in range(3)]
    pt = [cx.sbuf(tag + "pt%d" % i, [128, 256], BF16) for i in range(4)]
    osc = [cx.sbuf(tag + "osc%d" % i, [128, TT], BF16) for i in range(2)]
    w_in_r = w_in.rearrange("(kc p) n -> p kc n", p=128)
    AW = NH_A * 128
    cnt = {"w": 0, "r": 0, "o": 0}

    def load_w(hd):
        s = cnt["w"] % 2
        cnt["w"] += 1
        for nm, wt, off in (("wq", wq, 0), ("wk", wk, AW), ("wv", wv, 2 * AW)):
            cx.dma("pool", wt[s][:], w_in_r[:, :, off + hd * 128: off + (hd + 1) * 128],
                   writes=[(tag + nm, s)], skey=(nm, s))
        return s

    for hg in range(4):
        for g in range(3):
            d = DIL[g]
            Ls = T // d
            nb = Ls // 128
            hd = g * 4 + hg
            s = load_w(hd)
            def rope_piece(which, wt, dst, dkey, tt, item, s=s, d=d):
                b = item % 3
                bP, bS = (0, 1, 4)[b], (2, 3, 5)[b]
                pp, sw = ps[bP], ps[bS]
                dst_v = dst[:, :].rearrange("p (r j) -> p j r", r=d)

                def mm(e):
                    ins = None
                    for kc in range(KC):
                        ins = e.matmul(pp[:], wt[s][:, kc, :], hT[:, kc, tt * TT:(tt + 1) * TT],
                                       start=(kc == 0), stop=(kc == KC - 1))
                    return ins
                cx.op("pe", mm, reads=[(tag + "w" + which, s)], writes=[pk[bP]])
                yield
                cx.op("act", lambda e: e.activation(qb[b][:], pp[:], AF.Copy),
                      reads=[pk[bP]], writes=[(tag + "qb", b)])
                yield "NEXT"
                cx.op("pe", lambda e: e.matmul(sw[:], pswap[:], qb[b][:], start=True, stop=True),
                      reads=[(tag + "qb", b), tag + "pswap"], writes=[pk[bS]])
                cx.op("dve", lambda e: e.tensor_tensor(
                    out=t1[b][:], in0=qb[b][:], in1=ctab[:, tt * TT:(tt + 1) * TT], op=ALU.mult),
                    reads=[(tag + "qb", b), tag + "ctab"], writes=[(tag + "t1", b)])
                yield
                cx.op("dve", lambda e: e.tensor_tensor(
                    out=t2[b][:], in0=sw[:], in1=stab[:, tt * TT:(tt + 1) * TT], op=ALU.mult),
                    reads=[pk[bS], tag + "stab"], writes=[(tag + "t2", b)])
                yield
                j0 = tt * TT // d
                cx.op("dve", lambda e: e.tensor_tensor(
                    out=dst_v[:, j0:j0 + TT // d, :],
                    in0=t1[b][:, :].rearrange("p (j r) -> p j r", r=d),
                    in1=t2[b][:, :].rearrange("p (j r) -> p j r", r=d), op=ALU.add),
                    reads=[(tag + "t1", b), (tag + "t2", b)], writes=[(dkey, tt)])
                yield

            pieces = [(w_, wt_, dst_, dk_, tt) for (w_, wt_, dst_, dk_) in
                      (("q", wq, Qd, tag + "Qd"), ("k", wk, Kd, tag + "Kd")) for tt in range(NTT)]
            act_g, nx = [], [0]

            def start_r():
                if nx[0] < len(pieces) and len(act_g) < 3:
                    w_, wt_, dst_, dk_, tt = pieces[nx[0]]
                    act_g.append(rope_piece(w_, wt_, dst_, dk_, tt, cnt["r"]))
                    cnt["r"] += 1
                    nx[0] += 1
            start_r()
            while act_g:
                for g_ in list(act_g):
                    try:
                        v = next(g_)
                    except StopIteration:
                        act_g.remove(g_)
                        start_r()
                        continue
                    if v == "NEXT" and g_ is act_g[-1]:
                        start_r()
            for b4 in range(8):
                b = cnt["r"] % 2
                cnt["r"] += 1
                pp = ps[b]

                def mmv(e, s=s, b4=b4, pp=pp, d=d, Ls=Ls):
                    ins = None
                    for i in range(4):
                        B = b4 * 4 + i
                        r, n = divmod(B, Ls // 128)
                        t0 = n * 128 * d + r
                        for kc in range(KC):
                            ins = e.matmul(pp[:, i * 128:(i + 1) * 128],
                                           hT[:, kc, t0: t0 + 127 * d + 1: d], wv[s][:, kc, :],
                                           start=(kc == 0), stop=(kc == KC - 1))
                    return ins
                cx.op("pe", mmv, reads=[(tag + "wv", s)], writes=[pk[b]])
                cx.op("act", lambda e, b4=b4, pp=pp: e.activation(
                    Vd[:, b4 * 4:(b4 + 1) * 4, :], pp[:, :].rearrange("p (i e) -> p i e", i=4), AF.Copy),
                    reads=[pk[b]], writes=[(tag + "Vd", b4)])
            qk_keys = [(tag + "Qd", tt) for tt in range(NTT)] + [(tag + "Kd", tt) for tt in range(NTT)]

            def scores(B, nb=nb):
                sb = (4, 5, 2, 3)[B % 4]
                nq = 256 if (B + 1) % nb != 0 else 128
                st = ps[sb]

                def mm(e, B=B, nq=nq, st=st):
                    e.matmul(st[:, 0:nq], Kd[:, B * 128:(B + 1) * 128], Qd[:, B * 128:B * 128 + nq],
                             start=True, stop=False)
                    return e.matmul(st[:, 0:nq], ident[:], maskb[:, 0:nq], start=False, stop=True)
                cx.op("pe", mm, reads=qk_keys + [tag + "ident", tag + "maskb"], writes=[pk[sb]])
                cx.op("act", lambda e, B=B, nq=nq, st=st: e.activation(
                    pt[B % 4][:, 0:nq], st[:, 0:nq], AF.Exp, scale=SCALE_A),
                    reads=[pk[sb]], writes=[(tag + "pt", B % 4)])

            def pv(B, nb=nb, g=g, d=d, Ls=Ls):
                first = (B % nb == 0)
                col = (B % 4) * 128
                bN, bD = ((6, 7), (0, 1))[(B // 4) % 2]

                def mm(e, B=B, first=first, col=col, bN=bN, bD=bD):
                    ins = None
                    for dst, lhs_fn in ((ps[bN], lambda blk: Vd[:, blk, :]), (ps[bD], lambda blk: c["ones_bf"][:])):
                        if not first:
                            e.matmul(dst[:, col:col + 128], lhs_fn(B - 1), pt[(B - 1) % 4][:, 128:256],
                                     start=True, stop=False)
                        ins = e.matmul(dst[:, col:col + 128], lhs_fn(B), pt[B % 4][:, 0:128],
                                       start=first, stop=True)
                    return ins
                rk = [(tag + "pt", B % 4), (tag + "Vd", B // 4), "ones_bf"]
                if not first:
                    rk += [(tag + "pt", (B - 1) % 4), (tag + "Vd", (B - 1) // 4)]
                cx.op("pe", mm, reads=rk, writes=[pk[bN], pk[bD]])
                if B % 4 == 3:
                    u0 = (B - 3) * 128
                    if d == 1:
                        def nat(ap2d):
                            return ap2d[:, u0:u0 + 512]
                        def src(p):
                            return p[:, :]
                    else:
                        r0, j0 = divmod(u0, Ls)
                        nr = max(1, 512 // Ls)
                        nj = 512 // nr
                        def nat(ap2d, r0=r0, j0=j0, nr=nr, nj=nj, d=d):
                            return ap2d.rearrange("p (j r) -> p r j", r=d)[:, r0:r0 + nr, j0:j0 + nj]
                        def src(p, nr=nr):
                            return p[:, :].rearrange("p (r j) -> p r j", r=nr)
                    cx.op("act", lambda e, nat=nat, src=src, g=g, bN=bN: e.activation(
                        nat(numT[:, g, :]), src(ps[bN]), AF.Copy),
                        reads=[pk[bN]], writes=[(tag + "numT", g)])
                    if g == 0:
                        cx.op("dve", lambda e, nat=nat, src=src, bD=bD: e.tensor_copy(nat(dent[:, :]), src(ps[bD])),
                              reads=[pk[bD]], writes=[tag + "dent"])
                    else:
                        cx.op("dve", lambda e, nat=nat, src=src, bD=bD: e.tensor_tensor(
                            out=nat(dent[:, :]), in0=nat(dent[:, :]), in1=src(ps[bD]), op=ALU.add),
                            reads=[pk[bD], tag + "dent"], writes=[tag + "dent"])

            scores(0)
            scores(1)
            for B in range(32):
                if B + 2 < 32:
                    scores(B + 2)
                pv(B)
        cx.op("dve", lambda e: e.reciprocal(dent[:, :], dent[:, :]), reads=[tag + "dent"], writes=[tag + "dent"])
        for g in range(3):
            hd = g * 4 + hg
            for tt in range(NTT):
                b = cnt["o"] % 2
                cnt["o"] += 1
                cx.op("dve", lambda e, b=b, g=g, tt=tt: e.tensor_tensor(
                    out=osc[b][:], in0=numT[:, g, tt * TT:(tt + 1) * TT], in1=dent[:, tt * TT:(tt + 1) * TT],
                    op=ALU.mult), reads=[tag + "dent", (tag + "numT", g)], writes=[(tag + "osc", b)])
                cx.dma("sp", os_d[hd * 128:(hd + 1) * 128, tt * TT:(tt + 1) * TT], osc[b][:],
                       reads=[(tag + "osc", b)], writes=[(okey, hd, tt)], skey=("osc", b))
    cx.barrier()
    cx.pop()


def outproj_phase(cx, c, tag, x_in, x_out, os_d, w_out, nheads, xkey_in, xkey_out, okey):
    cx.push()
    HT = 2048
    NT = HT // TT
    WB = 256
    osb = cx.sbuf(tag + "osb", [128, nheads, HT], BF16)
    wo = [cx.sbuf(tag + "wo%d" % i, [128, nheads, WB], BF16) for i in range(2)]
    xr = [cx.sbuf(tag + "xr%d" % i, [128, TT], F32) for i in range(2)]
    xn = [cx.sbuf(tag + "xn%d" % i, [128, TT], F32) for i in range(2)]
    ps = [cx.psum(tag + "ps%d" % i, [128, TT], F32) for i in range(2)]
    pk = [tag + "ps%d" % i for i in range(2)]
    cx.psum_keys.update(pk)
    w_out_r = w_out.rearrange("(kc p) n -> p kc n", p=128)
    os_r = os_d.rearrange("(h p) t -> p h t", p=128)
    cnt = {"o": 0, "x": 0}
    for h in range(T // HT):
        t0 = h * HT
        for hd in range(nheads):
            cx.dma("sp", osb[:, hd, :], os_r[:, hd, t0:t0 + HT],
                   reads=[(okey, hd, (t0 // TT) + i) for i in range(NT)], writes=[(tag + "osb", hd)], skey=("osb", hd))
        for mb in range(D // WB):
            s = cnt["o"] % 2
            cnt["o"] += 1
            cx.dma("pool", wo[s][:], w_out_r[:, :, mb * WB:(mb + 1) * WB], writes=[(tag + "wo", s)], skey=("wo", s))
            for mm_ in range(WB // 128):
                mc = mb * (WB // 128) + mm_
                for tt in range(NT):
                    gt = (t0 + tt * TT) // TT
                    b = cnt["x"] % 2
                    cnt["x"] += 1
                    py = ps[b]
                    cx.dma("sp", xr[b][:], x_in[mc * 128:(mc + 1) * 128, t0 + tt * TT: t0 + (tt + 1) * TT],
                           reads=[(xkey_in, gt, mc)], writes=[(tag + "xr", b)], skey=("xr", b))

                    def mm2(e, s=s, mm_=mm_, tt=tt, py=py):
                        ins = None
                        for kc in range(nheads):
                            ins = e.matmul(py[:], wo[s][:, kc, mm_ * 128:(mm_ + 1) * 128],
                                           osb[:, kc, tt * TT:(tt + 1) * TT], start=(kc == 0), stop=(kc == nheads - 1))
                        return ins
                    cx.op("pe", mm2, reads=[(tag + "wo", s)] + [(tag + "osb", hd) for hd in range(nheads)],
                          writes=[pk[b]])
                    cx.op("dve", lambda e, b=b, py=py: e.tensor_tensor(
                        out=xn[b][:], in0=py[:], in1=xr[b][:], op=ALU.add),
                        reads=[pk[b], (tag + "xr", b)], writes=[(tag + "xn", b)])
                    cx.dma("sp", x_out[mc * 128:(mc + 1) * 128, t0 + tt * TT: t0 + (tt + 1) * TT], xn[b][:],
                           reads=[(tag + "xn", b)], writes=[(xkey_out, gt, mc)], skey=("xn", b))
    cx.barrier()
    cx.pop()


def build_attn_prog():
    nc = bass.Bass("TRN2", target_bir_lowering=False)
    x = nc.dram_tensor("x", [D, T], F32, kind="ExternalInput").ap()
    nw = nc.dram_tensor("nw", [128, KC], F32, kind="ExternalInput").ap()
    w_in = nc.dram_tensor("w_in", [D, 3 * NH_A * 128], F32, kind="ExternalInput").ap()
    w_out = nc.dram_tensor("w_out", [NH_A * 128, D], F32, kind="ExternalInput").ap()
    ctab = nc.dram_tensor("ctab", [128, T], F32, kind="ExternalInput").ap()
    stab = nc.dram_tensor("stab", [128, T], F32, kind="ExternalInput").ap()
    cst = nc.dram_tensor("cst", [128, CST_W], F32, kind="ExternalInput").ap()
    os_d = nc.dram_tensor("os_scratch", [NH_A * 128, T], BF16).ap()
    y = nc.dram_tensor("y", [D, T], F32, kind="ExternalOutput").ap()
    cx = Ctx(nc)
    c = consts(cx)
    attn_phase(cx, c, "a_", x, os_d, nw, w_in, ctab, stab, cst, "xin", "os")
    outproj_phase(cx, c, "ao_", x, y, os_d, w_out, NH_A, "xin", "xout", "os")
    cx.finish()
    return nc


NVH = 16
NKH = 8
GP = 6176
NTILE = T // 128
DBG = {}


class _Stop(Exception):
    pass


def _chk(n):
    if DBG.get("stop", 99) <= n:
        raise _Stop()


def gdn_core_phase(*a):
    cx = a[0]
    depth = len(cx.scopes)
    try:
        _gdn_core_phase(*a)
    except _Stop:
        cx.barrier()
        while len(cx.scopes) > depth:
            cx.pop()


def _gdn_core_phase(cx, c, tag, x_in, oraw_d, nw_d, w_in, convw_d, alog_d, dtb_d, cst_d, xkey_in, okey):
    cx.push()
    hT = cx.sbuf(tag + "hT", [128, KC, T], BF16)
    ident = cx.sbuf(tag + "ident", [128, 128], BF16)
    id4 = cx.sbuf(tag + "id4", [128, 4, 128], BF16)
    minc4 = cx.sbuf(tag + "minc4", [128, 4, 128], BF16)
    mlow4 = cx.sbuf(tag + "mlow4", [128, 4, 128], BF16)
    cw = cx.sbuf(tag + "cw", [128, 32, 4], F32)
    beta = cx.sbuf(tag + "beta", [128, NTILE, NVH], F32)
    nbeta = cx.sbuf(tag + "nbeta", [128, NTILE, NVH], F32)
    gc = cx.sbuf(tag + "gc", [128, NTILE, NVH], F32)
    kbs = cx.sbuf(tag + "kbs", [128, NTILE, NVH], F32)
    egr = cx.sbuf(tag + "egr", [128, NTILE, NVH], F32)
    cd = cx.sbuf(tag + "cd", [128, NTILE, NVH], F32)
    gchi = cx.sbuf(tag + "gchi", [128, NTILE, NVH], BF16)
    gclo = cx.sbuf(tag + "gclo", [128, NTILE, NVH], BF16)
    ps = [cx.psum(tag + "ps%d" % i, [128, TT], F32) for i in range(8)]
    pk = [tag + "ps%d" % i for i in range(8)]
    cx.psum_keys.update(pk)
    K = lambda n: tag + n
    cx.push()
    alog = cx.sbuf(tag + "alog", [128, 512], F32)
    dtb = cx.sbuf(tag + "dtb", [128, 512], F32)
    gt = cx.sbuf(tag + "gt", [128, NTILE, NVH], F32)
    wab = cx.sbuf(tag + "wab", [128, KC, 32], BF16)
    ghi = cx.sbuf(tag + "ghi", [128, NTILE, NVH], BF16)
    glo = cx.sbuf(tag + "glo", [128, NTILE, NVH], BF16)
    trib = cx.sbuf(tag + "trib", [128, 128], BF16)
    trisb = cx.sbuf(tag + "trisb", [128, 128], BF16)
    cx.dma("pool", ident[:], cst_d[:, C_IDENT:C_IDENT + 128], writes=[K("ident")], skey="ident")
    for i in range(4):
        cx.dma("pool", id4[:, i, :], cst_d[:, C_IDENT:C_IDENT + 128], writes=[K("id4")], skey=("id4", i))
        cx.dma("pool", minc4[:, i, :], cst_d[:, C_MINC:C_MINC + 128], writes=[K("minc4")], skey=("minc4", i))
        cx.dma("pool", mlow4[:, i, :], cst_d[:, C_TRIS:C_TRIS + 128], writes=[K("mlow4")], skey=("mlow4", i))
    cx.dma("sp", cw[:], convw_d, writes=[K("cw")], skey="cw")
    cx.dma("sp", alog[:], alog_d, writes=[K("alog")], skey="alog")
    cx.dma("sp", dtb[:], dtb_d, writes=[K("dtb")], skey="dtb")
    w_in_r = w_in.rearrange("(kc p) n -> p kc n", p=128)
    cx.dma("pool", wab[:], w_in_r[:, :, 6144:6176], writes=[K("wab")], skey="wab")
    _chk(0)
    norm_stage(cx, c, tag, x_in, nw_d, hT, xkey_in, ps[6], pk[6])
    _chk(1)

    f2 = lambda t: t[:, :, :].rearrange("p c h -> p (c h)")
    for which in range(2):
        def mm(e, which=which):
            ins = None
            for ct in range(NTILE):
                for kc in range(KC):
                    ins = e.matmul(ps[which][:, ct * 16:(ct + 1) * 16], hT[:, kc, ct * 128:(ct + 1) * 128],
                                   wab[:, kc, which * 16:(which + 1) * 16], start=(kc == 0), stop=(kc == KC - 1))
            return ins
        cx.op("pe", mm, reads=[], writes=[pk[which]])
    _chk(1.05)
    cx.op("act", lambda e: e.activation(f2(beta), ps[0][:], AF.Sigmoid), reads=[pk[0]], writes=[K("beta")])
    _chk(1.1)
    cx.op("dve", lambda e: e.tensor_tensor(out=f2(gt), in0=ps[1][:], in1=dtb[:], op=ALU.add),
          reads=[pk[1], K("dtb")], writes=[K("gt")])
    _chk(1.2)
    cx.op("act", lambda e: e.activation(f2(gt), f2(gt), AF.Exp), reads=[K("gt")], writes=[K("gt")])
    _chk(1.4)
    cx.op("act", lambda e: e.activation(f2(gt), f2(gt), AF.Ln, bias=1.0), reads=[K("gt")], writes=[K("gt")])
    _chk(1.6)
    cx.op("act", lambda e: e.activation(alog[:], alog[:], AF.Exp), reads=[K("alog")], writes=[K("alog")])
    _chk(1.8)
    cx.op("dve", lambda e: e.scalar_tensor_tensor(out=f2(gt), in0=f2(gt), scalar=-1.0, in1=alog[:],
                                                   op0=ALU.mult, op1=ALU.mult),
          reads=[K("gt"), K("alog")], writes=[K("gt")])
    cx.op("dve", lambda e: e.tensor_scalar(f2(nbeta), f2(beta), -1.0, None, ALU.mult),
          reads=[K("beta")], writes=[K("nbeta")])
    _chk(2)

    cx.dma("pool", trib[:], cst_d[:, C_TRI:C_TRI + 128], writes=[K("trib")], skey="trib")
    cx.dma("pool", trisb[:], cst_d[:, C_TRIS:C_TRIS + 128], writes=[K("trisb")], skey="trisb")
    cx.op("dve", lambda e: e.tensor_copy(f2(ghi), f2(gt)), reads=[K("gt")], writes=[K("ghi")])
    cx.op("dve", lambda e: e.tensor_tensor(out=f2(glo), in0=f2(gt), in1=f2(ghi), op=ALU.subtract),
          reads=[K("gt"), K("ghi")], writes=[K("glo")])
    _chk(2.2)

    def mmg(e):
        ins = None
        for ct in range(NTILE):
            for dst, m in ((ps[2], trib), (ps[3], trisb), (ps[4], c["ones_bf"])):
                e.matmul(dst[:, ct * 16:(ct + 1) * 16], m[:], ghi[:, ct, :], start=True, stop=False)
                ins = e.matmul(dst[:, ct * 16:(ct + 1) * 16], m[:], glo[:, ct, :], start=False, stop=True)
        return ins
    cx.op("pe", mmg, reads=[K("ghi"), K("glo"), K("trib"), K("trisb"), "ones_bf"], writes=[pk[2], pk[3], pk[4]])
    _chk(2.4)
    cx.op("dve", lambda e: e.tensor_copy(f2(gchi), ps[2][:]), reads=[pk[2]], writes=[K("gchi")])
    cx.op("dve", lambda e: e.tensor_tensor(out=f2(gclo), in0=ps[2][:], in1=f2(gchi), op=ALU.subtract),
          reads=[pk[2], K("gchi")], writes=[K("gclo")])
    _chk(2.6)
    cx.op("dve", lambda e: e.tensor_copy(f2(gc), ps[2][:]), reads=[pk[2]], writes=[K("gc")])
    cx.op("act", lambda e: e.activation(f2(kbs), ps[2][:], AF.Exp), reads=[pk[2]], writes=[K("kbs")])
    _chk(2.7)
    cx.op("dve", lambda e: e.tensor_tensor(out=f2(kbs), in0=f2(kbs), in1=f2(beta), op=ALU.mult),
          reads=[K("kbs"), K("beta")], writes=[K("kbs")])
    _chk(2.8)
    cx.op("act", lambda e: e.activation(f2(egr), ps[3][:], AF.Exp), reads=[pk[3]], writes=[K("egr")])
    cx.op("act", lambda e: e.activation(f2(cd), ps[4][:], AF.Exp), reads=[pk[4]], writes=[K("cd")])
    _chk(2.9)
    cx.barrier()
    cx.pop()
    _chk(3)

    qT = cx.sbuf(tag + "qT", [128, 2, T], BF16)
    kT = cx.sbuf(tag + "kT", [128, 2, T], BF16)
    ktok = cx.sbuf(tag + "ktok", [128, NTILE, 2, 128], BF16)
    vtok = cx.sbuf(tag + "vtok", [128, NTILE, 4, 128], BF16)
    PW = 1024
    for hgp in range(DBG.get('hgps', 4)):
        cx.push()
        NB = 3
        wch = [cx.sbuf(tag + "wch%d" % i, [128, KC, 128], BF16) for i in range(2)]
        dg = [cx.sbuf(tag + "dg%d" % i, [128, 4, 128], BF16) for i in range(2)]
        xcb = [cx.sbuf(tag + "xcb%d" % i, [128, 4 + TT], BF16) for i in range(NB)]
        sil = [cx.sbuf(tag + "sil%d" % i, [128, TT], F32) for i in range(NB)]
        silb = [cx.sbuf(tag + "silb%d" % i, [128, TT], BF16) for i in range(NB)]
        sqb = [cx.sbuf(tag + "sqb%d" % i, [128, TT], BF16) for i in range(NB)]
        rn = [cx.sbuf(tag + "rn%d" % i, [128, TT], F32) for i in range(NB)]
        chunks = [("q", 0), ("q", 1), ("k", 0), ("k", 1), ("v", 0), ("v", 1), ("v", 2), ("v", 3)]

        def g1_piece(kind, li, gch, ws, tt, item):
            b = item % NB
            i2 = item % 2
            bP, bC, bS, bT = i2, 2 + i2, 4 + i2, 6 + i2
            XK = lambda i: (K("xcb"), i)
            if tt == 0:
                cx.op("act", lambda e: e.activation(xcb[b][:, 0:4], xcb[b][:, 0:4], AF.Copy, scale=0.0),
                      reads=[], writes=[(K("xh"), b)])

            def mm(e):
                ins = None
                for kc in range(KC):
                    ins = e.matmul(ps[bP][:], wch[ws][:, kc, :], hT[:, kc, tt * TT:(tt + 1) * TT],
                                   start=(kc == 0), stop=(kc == KC - 1))
                return ins
            cx.op("pe", mm, reads=[(K("wch"), ws)], writes=[pk[bP]])
            yield
            cx.op("act", lambda e: e.activation(xcb[b][:, 4:4 + TT], ps[bP][:], AF.Copy),
                  reads=[pk[bP]], writes=[XK(b)])
            if tt < NTT - 1:
                nb_ = (item + 1) % NB
                cx.op("act", lambda e: e.activation(xcb[nb_][:, 0:4], xcb[b][:, TT:TT + 4], AF.Copy),
                      reads=[XK(b)], writes=[(K("xh"), nb_)])
            yield "NEXT"

            def mmc(e):
                ins = None
                for j in range(4):
                    ins = e.matmul(ps[bC][:], dg[ws][:, j, :], xcb[b][:, 1 + j:1 + j + TT], start=(j == 0), stop=(j == 3))
                return ins
            cx.op("pe", mmc, reads=[XK(b), (K("xh"), b), (K("dg"), ws)], writes=[pk[bC]])
            yield
            tsl = slice(tt * TT, (tt + 1) * TT)
            if kind == "v":
                cx.op("act", lambda e: e.activation(silb[b][:], ps[bC][:], AF.Silu), reads=[pk[bC]], writes=[(K("silb"), b)])
                yield
            else:
                cx.op("act", lambda e: e.activation(sil[b][:], ps[bC][:], AF.Silu), reads=[pk[bC]], writes=[(K("sil"), b)])
                yield
                cx.op("dve", lambda e: e.tensor_tensor(out=sqb[b][:], in0=sil[b][:], in1=sil[b][:], op=ALU.mult),
                      reads=[(K("sil"), b)], writes=[(K("sqb"), b)])
                yield
                cx.op("pe", lambda e: e.matmul(ps[bS][:], c["ones_bf"][:], sqb[b][:], start=True, stop=True),
                      reads=[(K("sqb"), b), "ones_bf"], writes=[pk[bS]])
                yield
                cx.op("act", lambda e: e.activation(rn[b][:], ps[bS][:], AF.Sqrt, bias=1e-6, scale=1.0),
                      reads=[pk[bS]], writes=[(K("rn"), b)])
                yield
                cx.op("dve", lambda e: e.reciprocal(rn[b][:], rn[b][:]), reads=[(K("rn"), b)], writes=[(K("rn"), b)])
                dstT = qT if kind == "q" else kT
                sc = (128 ** -0.5) if kind == "q" else 1.0
                cx.op("dve", lambda e: e.scalar_tensor_tensor(
                    out=dstT[:, li, tsl], in0=sil[b][:], scalar=sc, in1=rn[b][:], op0=ALU.mult, op1=ALU.mult),
                    reads=[(K("sil"), b), (K("rn"), b)], writes=[(K(kind + "T"), li, tt)])
                yield
            if kind in ("k", "v"):
                ptv = ps[bT][:, :].bitcast(BF16)

                def mmt(e):
                    ins = None
                    for i in range(4):
                        if kind == "v":
                            src = silb[b][:, i * 128:(i + 1) * 128]
                        else:
                            src = kT[:, li, tt * TT + i * 128: tt * TT + (i + 1) * 128]
                        ins = e.transpose(ptv[:, i * 128:(i + 1) * 128], src, ident[:])
                    return ins
                rk = [(K("silb"), b)] if kind == "v" else [(K("kT"), li, tt)]
                cx.op("pe", mmt, reads=rk + [K("ident")], writes=[pk[bT]])
                yield
                dst = vtok if kind == "v" else ktok
                cx.op("act", lambda e: e.activation(
                    dst[:, tt * 4:tt * 4 + 4, li, :], ptv[:, 0:512].rearrange("p (i x) -> p i x", i=4), AF.Copy),
                    reads=[pk[bT]], writes=[(K(kind + "tok"), li, tt)])
                yield

        items = []
        wcnt = 0
        for kind, li in chunks:
            gch = {"q": 2 * hgp + li, "k": 8 + 2 * hgp + li, "v": 16 + 4 * hgp + li}[kind]
            ws = wcnt % 2
            wcnt += 1
            for tt in range(NTT):
                items.append((kind, li, gch, ws, tt))
        active, nxt_i = [], [0]

        def start1():
            while nxt_i[0] < len(items) and len(active) < NB:
                kind, li, gch, ws, tt = items[nxt_i[0]]
                if tt == 0:
                    cx.dma("pool", wch[ws][:], w_in_r[:, :, gch * 128:(gch + 1) * 128], writes=[(K("wch"), ws)],
                           skey=("wch", ws))
                    for j in range(4):
                        cx.op("dve", lambda e, ws=ws, gch=gch, j=j: e.tensor_scalar(
                            dg[ws][:, j, :], ident[:], cw[:, gch, j:j + 1], None, ALU.mult),
                            reads=[K("ident"), K("cw")], writes=[(K("dg"), ws)])
                active.append(g1_piece(kind, li, gch, ws, tt, nxt_i[0]))
                nxt_i[0] += 1
                return
        start1()
        while active:
            for g in list(active):
                try:
                    v = next(g)
                except StopIteration:
                    active.remove(g)
                    start1()
                    continue
                if v == "NEXT" and g is active[-1]:
                    start1()
        cx.barrier()
        cx.pop()
        _chk(5)

        cx.push()
        mk = lambda nm, dt_, n=2: [cx.sbuf(tag + nm + "%d" % i, [128, 4, 128], dt_) for i in range(n)]
        Abuf0, Abuf1, Atbuf0, Atbuf1 = mk("Aa", BF16), mk("Ab", BF16), mk("Ata", BF16), mk("Atb", BF16)
        A_O, AtO, Tt = mk("A_O", BF16), mk("AtO", BF16), mk("Tt", BF16)
        qkT, qdT, kd, vb, kb = mk("qkT", BF16), mk("qdT", BF16), mk("kd", BF16), mk("vb", BF16), mk("kb", BF16)
        dmT = cx.sbuf(tag + "dmT", [128, 4, 128], F32)
        dm2 = cx.sbuf(tag + "dm2", [128, 4, 128], F32)
        eB = cx.sbuf(tag + "eB", [128, 4, 128], F32)
        rr = cx.sbuf(tag + "rr", [128, 4, 128], F32)
        u = cx.sbuf(tag + "u", [128, 4, 128], F32)
        wT = cx.sbuf(tag + "wT", [128, 4, 128], BF16)
        vnew = cx.sbuf(tag + "vnew", [128, 4, 128], BF16)
        S32 = cx.sbuf(tag + "S32", [128, 4, 128], F32)
        S16 = cx.sbuf(tag + "S16", [128, 4, 128], BF16)
        ost = [cx.sbuf(tag + "ost%d" % i, [128, 4, 128], F32) for i in range(2)]
        fl = lambda t: t[:, :, :].rearrange("p h x -> p (h x)")
        cx.op("dve", lambda e: e.memset(fl(S32), 0.0), writes=[K("S32")])
        cx.op("dve", lambda e: e.memset(fl(S16), 0.0), writes=[K("S16")])
        hs = [4 * hgp + hl for hl in range(4)]
        B0, B1 = 0, 1

        def g2_tile(ct, hs, hgp=hgp):
            par = ct % 2
            bA, bAt, bTu = 2 + 3 * par, 3 + 3 * par, 4 + 3 * par
            tsl = slice(ct * 128, (ct + 1) * 128)
            P = lambda n: (K(n), par)
            AO, AtOp, Ttp = A_O[par], AtO[par], Tt[par]
            bufs = [(Abuf0[par], Atbuf0[par]), (Abuf1[par], Atbuf1[par]), (AO, AtOp)]

            def mm1(e):
                for hl in range(4):
                    e.matmul(ps[B0][:, hl * 128:(hl + 1) * 128], gchi[:, ct, hs[hl]:hs[hl] + 1].to_broadcast([128, 128]),
                             ident[:], start=True, stop=False)
                    e.matmul(ps[B0][:, hl * 128:(hl + 1) * 128], gclo[:, ct, hs[hl]:hs[hl] + 1].to_broadcast([128, 128]),
                             ident[:], start=False, stop=True)
                ins = None
                for khl in range(2):
                    e.matmul(ps[B1][:, khl * 128:(khl + 1) * 128], kT[:, khl, tsl], kT[:, khl, tsl], start=True, stop=True)
                    ins = e.matmul(ps[B1][:, (2 + khl) * 128:(3 + khl) * 128], kT[:, khl, tsl], qT[:, khl, tsl],
                                   start=True, stop=True)
                return ins
            cx.op("pe", mm1, reads=[K("ident")], writes=[pk[B0], pk[B1]])
            yield
            for hl in range(4):
                h = hs[hl]
                cx.op("dve", lambda e, hl=hl, h=h: e.tensor_scalar(
                    dmT[:, hl, :], ps[B0][:, hl * 128:(hl + 1) * 128], gc[:, ct, h:h + 1], 0.0, ALU.subtract, ALU.min),
                    reads=[pk[B0]], writes=[K("dmT")])
                cx.op("dve", lambda e, hl=hl, h=h: e.tensor_scalar(
                    dm2[:, hl, :], ps[B0][:, hl * 128:(hl + 1) * 128], gc[:, ct, h:h + 1], 0.0, ALU.subtract, ALU.max),
                    reads=[pk[B0]], writes=[K("dm2")])
                yield
            cx.op("act", lambda e: e.activation(fl(eB), ps[B0][:], AF.Exp), reads=[pk[B0]], writes=[K("eB")])
            cx.op("act", lambda e: e.activation(fl(dmT), fl(dmT), AF.Exp), reads=[K("dmT")], writes=[K("dmT")])
            cx.op("act", lambda e: e.activation(fl(dm2), fl(dm2), AF.Exp, scale=-1.0), reads=[K("dm2")], writes=[K("dm2")])
            yield
            cx.op("dve", lambda e: e.tensor_tensor(out=fl(dmT), in0=fl(dmT), in1=fl(minc4), op=ALU.mult),
                  reads=[K("dmT")], writes=[K("dmT")])
            cx.op("dve", lambda e: e.tensor_tensor(out=fl(dm2), in0=fl(dm2), in1=fl(mlow4), op=ALU.mult),
                  reads=[K("dm2")], writes=[K("dm2")])
            yield
            for hl in range(4):
                h = hs[hl]
                khl = hl // 2
                cx.op("dve", lambda e, hl=hl, h=h, khl=khl: e.scalar_tensor_tensor(
                    out=AO[:, hl, :], in0=ps[B1][:, khl * 128:(khl + 1) * 128], scalar=nbeta[:, ct, h:h + 1],
                    in1=dm2[:, hl, :], op0=ALU.mult, op1=ALU.mult), reads=[pk[B1], K("dm2")], writes=[(P("A"), 2)])
                cx.op("dve", lambda e, hl=hl, khl=khl: e.tensor_tensor(
                    out=qkT[par][:, hl, :], in0=ps[B1][:, (2 + khl) * 128:(3 + khl) * 128], in1=dmT[:, hl, :], op=ALU.mult),
                    reads=[pk[B1], K("dmT")], writes=[P("qkT")])
                cx.op("dve", lambda e, hl=hl, khl=khl: e.tensor_tensor(
                    out=qdT[par][:, hl, :], in0=qT[:, khl, tsl], in1=eB[:, hl, :], op=ALU.mult),
                    reads=[K("eB")], writes=[P("qdT")])
                cx.op("act", lambda e, hl=hl, h=h: e.activation(
                    vb[par][:, hl, :], vtok[:, ct, hl, :], AF.Copy, scale=beta[:, ct, h:h + 1]), writes=[P("vb")])
                cx.op("act", lambda e, hl=hl, h=h, khl=khl: e.activation(
                    kb[par][:, hl, :], ktok[:, ct, khl, :], AF.Copy, scale=kbs[:, ct, h:h + 1]), writes=[P("kb")])
                cx.op("act", lambda e, hl=hl, h=h, khl=khl: e.activation(
                    kd[par][:, hl, :], ktok[:, ct, khl, :], AF.Copy, scale=egr[:, ct, h:h + 1]), writes=[P("kd")])
                yield
            ptv = ps[bA][:, :].bitcast(BF16)

            def mmT(e):
                ins = None
                for hl in range(4):
                    ins = e.transpose(ptv[:, hl * 128:(hl + 1) * 128], AO[:, hl, :], ident[:])
                return ins
            cx.op("pe", mmT, reads=[(P("A"), 2), K("ident")], writes=[pk[bA]])
            yield
            cx.op("act", lambda e: e.activation(fl(AtOp), ptv[:, 0:512], AF.Copy),
                  reads=[pk[bA]], writes=[(P("At"), 2)])
            yield
            cx.op("dve", lambda e: e.tensor_tensor(out=fl(Ttp), in0=fl(AtOp), in1=fl(id4), op=ALU.add),
                  reads=[(P("At"), 2), K("id4")], writes=[P("Tt")])
            yield
            cur = 2
            NSTEP = 6
            for p in range(1, NSTEP + 1):
                nxt = 0 if cur == 2 else 1 - cur
                A, At = bufs[cur]
                An, Atn = bufs[nxt]
                last = (p == NSTEP)

                def mmsq(e, A=A, At=At, last=last):
                    ins = None
                    for hl in range(4):
                        ins = e.matmul(ps[bA][:, hl * 128:(hl + 1) * 128], At[:, hl, :], A[:, hl, :], start=True, stop=True)
                    if not last:
                        for hl in range(4):
                            ins = e.matmul(ps[bAt][:, hl * 128:(hl + 1) * 128], A[:, hl, :], At[:, hl, :],
                                           start=True, stop=True)
                    return ins
                cx.op("pe", mmsq, reads=[(P("A"), cur), (P("At"), cur)], writes=[pk[bA]] + ([] if last else [pk[bAt]]))
                yield
                cx.op("act", lambda e, An=An: e.activation(fl(An), ps[bA][:], AF.Copy),
                      reads=[pk[bA]], writes=[(P("A"), nxt)])
                if not last:
                    cx.op("act", lambda e, Atn=Atn: e.activation(fl(Atn), ps[bAt][:], AF.Copy),
                          reads=[pk[bAt]], writes=[(P("At"), nxt)])
                yield

                def mmtu(e, An=An):
                    ins = None
                    for hl in range(4):
                        ins = e.matmul(ps[bTu][:, hl * 128:(hl + 1) * 128], An[:, hl, :], Ttp[:, hl, :], start=True, stop=True)
                    return ins
                cx.op("pe", mmtu, reads=[(P("A"), nxt), P("Tt")], writes=[pk[bTu]])
                yield
                cx.op("dve", lambda e: e.tensor_tensor(out=fl(Ttp), in0=fl(Ttp), in1=ps[bTu][:], op=ALU.add),
                      reads=[pk[bTu], P("Tt")], writes=[P("Tt")])
                cur = nxt
                if p == 3:
                    yield "MID"
                else:
                    yield
            T0, Rb = Atbuf0[par], Abuf0[par]
            ptv3 = ps[bAt][:, :].bitcast(BF16)

            def mmT0(e):
                ins = None
                for hl in range(4):
                    ins = e.transpose(ptv3[:, hl * 128:(hl + 1) * 128], Ttp[:, hl, :], ident[:])
                return ins
            cx.op("pe", mmT0, reads=[P("Tt"), K("ident")], writes=[pk[bAt]])
            yield
            cx.op("act", lambda e: e.activation(fl(T0), ptv3[:, 0:512], AF.Copy),
                  reads=[pk[bAt]], writes=[(P("At"), 0)])
            yield

            def mmR(e):
                ins = None
                for hl in range(4):
                    ins = e.matmul(ps[bA][:, hl * 128:(hl + 1) * 128], AtOp[:, hl, :], T0[:, hl, :], start=True, stop=True)
                return ins
            cx.op("pe", mmR, reads=[(P("At"), 2), (P("At"), 0)], writes=[pk[bA]])
            yield
            cx.op("dve", lambda e: e.scalar_tensor_tensor(out=fl(rr), in0=fl(T0), scalar=-1.0, in1=ps[bA][:],
                                                           op0=ALU.mult, op1=ALU.add),
                  reads=[(P("At"), 0), pk[bA]], writes=[K("rr")])
            cx.op("dve", lambda e: e.tensor_tensor(out=fl(Rb), in0=fl(rr), in1=fl(id4), op=ALU.add),
                  reads=[K("rr"), K("id4")], writes=[(P("A"), 0)])
            yield

            def mmU(e):
                ins = None
                for hl in range(4):
                    ins = e.matmul(ps[bTu][:, hl * 128:(hl + 1) * 128], Rb[:, hl, :], Ttp[:, hl, :], start=True, stop=True)
                return ins
            cx.op("pe", mmU, reads=[(P("A"), 0), P("Tt")], writes=[pk[bTu]])
            yield
            cx.op("dve", lambda e: e.tensor_tensor(out=fl(Ttp), in0=fl(Ttp), in1=ps[bTu][:], op=ALU.add),
                  reads=[pk[bTu], P("Tt")], writes=[P("Tt")])
            yield

            def mmuw(e):
                ins = None
                for hl in range(4):
                    e.matmul(ps[bA][:, hl * 128:(hl + 1) * 128], Ttp[:, hl, :], vb[par][:, hl, :], start=True, stop=True)
                    ins = e.matmul(ps[bAt][:, hl * 128:(hl + 1) * 128], kb[par][:, hl, :], Ttp[:, hl, :], start=True, stop=True)
                return ins
            cx.op("pe", mmuw, reads=[P("Tt"), P("vb"), P("kb")], writes=[pk[bA], pk[bAt]])
            yield
            cx.op("act", lambda e: e.activation(fl(u), ps[bA][:], AF.Copy), reads=[pk[bA]], writes=[K("u")])
            cx.op("act", lambda e: e.activation(fl(wT), ps[bAt][:], AF.Copy), reads=[pk[bAt]], writes=[K("wT")])
            yield

            def mmvn(e):
                ins = None
                for hl in range(4):
                    ins = e.matmul(ps[bTu][:, hl * 128:(hl + 1) * 128], wT[:, hl, :], S16[:, hl, :], start=True, stop=True)
                return ins
            cx.op("pe", mmvn, reads=[K("wT"), K("S16")], writes=[pk[bTu]])
            yield
            cx.op("dve", lambda e: e.tensor_tensor(out=fl(vnew), in0=fl(u), in1=ps[bTu][:], op=ALU.subtract),
                  reads=[K("u"), pk[bTu]], writes=[K("vnew")])
            yield

            def mmo(e):
                ins = None
                for hl in range(4):
                    e.matmul(ps[bA][:, hl * 128:(hl + 1) * 128], S16[:, hl, :], qdT[par][:, hl, :], start=True, stop=False)
                    e.matmul(ps[bA][:, hl * 128:(hl + 1) * 128], vnew[:, hl, :], qkT[par][:, hl, :], start=False, stop=True)
                for hl in range(4):
                    ins = e.matmul(ps[bAt][:, hl * 128:(hl + 1) * 128], kd[par][:, hl, :], vnew[:, hl, :], start=True, stop=True)
                return ins
            cx.op("pe", mmo, reads=[K("S16"), P("qdT"), K("vnew"), P("qkT"), P("kd")], writes=[pk[bA], pk[bAt]])
            yield
            ob = ct % 2
            for hl in range(4):
                h = hs[hl]
                cx.op("dve", lambda e, hl=hl, h=h: e.scalar_tensor_tensor(
                    out=S32[:, hl, :], in0=S32[:, hl, :], scalar=cd[:, ct, h:h + 1], in1=ps[bAt][:, hl * 128:(hl + 1) * 128],
                    op0=ALU.mult, op1=ALU.add), reads=[pk[bAt], K("S32")], writes=[K("S32")])
            cx.op("act", lambda e: e.activation(fl(S16), fl(S32), AF.Copy), reads=[K("S32")], writes=[K("S16")])
            cx.op("act", lambda e: e.activation(fl(ost[ob]), ps[bA][:], AF.Copy),
                  reads=[pk[bA]], writes=[(K("ost"), ob)])
            for hl in range(4):
                h = hs[hl]
                cx.dma("sp", oraw_d[h * 128:(h + 1) * 128, ct * 128:(ct + 1) * 128], ost[ob][:, hl, :],
                       reads=[(K("ost"), ob)], writes=[(okey, h, ct)], skey=("ost", ob, hl))
            yield

        ntile = DBG.get('tiles', NTILE)
        active, nxt_t = [], [0]

        def start():
            if nxt_t[0] < ntile and len(active) < 2:
                active.append(g2_tile(nxt_t[0], list(hs)))
                nxt_t[0] += 1
        start()
        while active:
            for g in list(active):
                try:
                    v = next(g)
                except StopIteration:
                    active.remove(g)
                    start()
                    continue
                if v == "MID":
                    start()
        cx.barrier()
        cx.pop()
    cx.barrier()
    cx.pop()


def gdn_out_phase(cx, c, tag, x_in, x_out, oraw_d, nw_d, gnw_d, w_in, w_out, xkey_in, xkey_out, okey):
    cx.push()
    HT = 2048
    NT = HT // TT
    WB = 256
    nheads = NVH
    K = lambda n: tag + n
    hT = cx.sbuf(tag + "hT", [128, KC, HT], BF16)
    osb = cx.sbuf(tag + "osb", [128, nheads, HT], BF16)
    gnw = cx.sbuf(tag + "gnw", [128, 1], F32)
    nw = cx.sbuf(tag + "nw", [128, KC], F32)
    xt = [cx.sbuf(tag + "xt%d" % i, [128, KC, TT], F32) for i in range(1)]
    sq = cx.sbuf(tag + "sq", [128, KC, TT], BF16)
    rstd = cx.sbuf(tag + "rstd", [128, TT], F32)
    wz = [cx.sbuf(tag + "wz%d" % i, [128, KC, 128], BF16) for i in range(2)]
    orw = [cx.sbuf(tag + "orw%d" % i, [128, TT], F32) for i in range(3)]
    osq = [cx.sbuf(tag + "osq%d" % i, [128, TT], BF16) for i in range(3)]
    orn = [cx.sbuf(tag + "orn%d" % i, [128, TT], F32) for i in range(3)]
    sz = [cx.sbuf(tag + "sz%d" % i, [128, TT], F32) for i in range(3)]
    wo = [cx.sbuf(tag + "wo%d" % i, [128, nheads, WB], BF16) for i in range(2)]
    xr = [cx.sbuf(tag + "xr%d" % i, [128, TT], F32) for i in range(2)]
    xn = [cx.sbuf(tag + "xn%d" % i, [128, TT], F32) for i in range(2)]
    ps = [cx.psum(tag + "ps%d" % i, [128, TT], F32) for i in range(8)]
    pk = [tag + "ps%d" % i for i in range(8)]
    cx.psum_keys.update(pk)
    w_in_r = w_in.rearrange("(kc p) n -> p kc n", p=128)
    w_out_r = w_out.rearrange("(kc p) n -> p kc n", p=128)
    x_in_r = x_in.rearrange("(kc p) t -> p kc t", p=128)
    cx.dma("sp", nw[:], nw_d, writes=[K("nw")], skey="nw")
    cx.dma("sp", gnw[:], gnw_d, writes=[K("gnw")], skey="gnw")
    cnt = {"o": 0, "x": 0, "z": 0, "w": 0}
    def norm_tile_ops(hf_, tt, part=None):
        t0_ = hf_ * HT
        gt_ = (t0_ + tt * TT) // TT
        if part in (None, "A"):
            cx.dma("sp", xt[0][:], x_in_r[:, :, t0_ + tt * TT: t0_ + (tt + 1) * TT],
                   reads=[(xkey_in, gt_, kc) for kc in range(KC)], writes=[K("xt")], skey="xt")
        rmsnorm_tile(cx, c, xt[0], nw, lambda kc, tt=tt: hT[:, kc, tt * TT:(tt + 1) * TT], tag,
                     [K("xt")], (K("hT"), tt), pk[5], ps[5], {"sq": sq, "rstd": rstd}, part=part)

    for hf in range(T // HT):
        t0 = hf * HT
        if hf == 0:
            for tt in range(NT):
                norm_tile_ops(0, tt)
        def gate_piece(h, ws, tt, item):
            b = item % 3
            bZ, bQ = b, 3 + b
            gt_ = (t0 + tt * TT) // TT
            cx.dma("sp", orw[b][:], oraw_d[h * 128:(h + 1) * 128, t0 + tt * TT: t0 + (tt + 1) * TT],
                   reads=[(okey, h, gt_)], writes=[(K("orw"), b)], skey=("orw", b))

            def mmz(e):
                ins = None
                for kc in range(KC):
                    ins = e.matmul(ps[bZ][:], wz[ws][:, kc, :], hT[:, kc, tt * TT:(tt + 1) * TT],
                                   start=(kc == 0), stop=(kc == KC - 1))
                return ins
            cx.op("pe", mmz, reads=[(K("wz"), ws), (K("hT"), tt)], writes=[pk[bZ]])
            yield
            cx.op("act", lambda e: e.activation(osq[b][:], orw[b][:], AF.Square),
                  reads=[(K("orw"), b)], writes=[(K("osq"), b)])
            yield "NEXT"
            cx.op("pe", lambda e: e.matmul(ps[bQ][:], c["ones_bf"][:], osq[b][:], start=True, stop=True),
                  reads=[(K("osq"), b), "ones_bf"], writes=[pk[bQ]])
            yield
            cx.op("act", lambda e: e.activation(orn[b][:], ps[bQ][:], AF.Sqrt, bias=EPS, scale=1.0 / 128),
                  reads=[pk[bQ]], writes=[(K("orn"), b)])
            cx.op("act", lambda e: e.activation(sz[b][:], ps[bZ][:], AF.Silu), reads=[pk[bZ]],
                  writes=[(K("sz"), b)])
            yield
            cx.op("dve", lambda e: e.reciprocal(orn[b][:], orn[b][:]), reads=[(K("orn"), b)],
                  writes=[(K("orn"), b)])
            yield
            cx.op("dve", lambda e: e.scalar_tensor_tensor(
                out=orn[b][:], in0=orw[b][:], scalar=gnw[:, 0:1], in1=orn[b][:], op0=ALU.mult, op1=ALU.mult),
                reads=[(K("orw"), b), (K("orn"), b), K("gnw")], writes=[(K("orn"), b)])
            yield
            cx.op("dve", lambda e: e.tensor_tensor(
                out=osb[:, h, tt * TT:(tt + 1) * TT], in0=orn[b][:], in1=sz[b][:], op=ALU.mult),
                reads=[(K("orn"), b), (K("sz"), b)], writes=[(K("osb"), h)])
            yield

        gitems = [(h, tt) for h in range(nheads) for tt in range(NT)]
        act_g, nx = [], [0]

        def start_g():
            if nx[0] < len(gitems) and len(act_g) < 3:
                h, tt = gitems[nx[0]]
                if tt == 0:
                    ws = cnt["w"] % 2
                    cnt["w"] += 1
                    cx.dma("pool", wz[ws][:], w_in_r[:, :, 4096 + h * 128: 4096 + (h + 1) * 128],
                           writes=[(K("wz"), ws)], skey=("wz", ws))
                ws = (cnt["w"] - 1) % 2
                act_g.append(gate_piece(h, ws, tt, cnt["z"]))
                cnt["z"] += 1
                nx[0] += 1
        start_g()
        while act_g:
            for g_ in list(act_g):
                try:
                    v = next(g_)
                except StopIteration:
                    act_g.remove(g_)
                    start_g()
                    continue
                if v == "NEXT" and g_ is act_g[-1]:
                    start_g()
        for mb in range(D // WB):
            s = cnt["o"] % 2
            cnt["o"] += 1
            cx.dma("pool", wo[s][:], w_out_r[:, :, mb * WB:(mb + 1) * WB], writes=[(K("wo"), s)], skey=("wo", s))
            for mm_ in range(WB // 128):
                mc = mb * (WB // 128) + mm_
                for tt in range(NT):
                    gt_ = (t0 + tt * TT) // TT
                    b = cnt["x"] % 2
                    cnt["x"] += 1
                    py = ps[6 + b]
                    cx.dma("sp", xr[b][:], x_in[mc * 128:(mc + 1) * 128, t0 + tt * TT: t0 + (tt + 1) * TT],
                           reads=[(xkey_in, gt_, mc)], writes=[(K("xr"), b)], skey=("xr", b))

                    def mm2(e, s=s, mm_=mm_, tt=tt, py=py):
                        ins = None
                        for kc in range(nheads):
                            ins = e.matmul(py[:], wo[s][:, kc, mm_ * 128:(mm_ + 1) * 128],
                                           osb[:, kc, tt * TT:(tt + 1) * TT], start=(kc == 0), stop=(kc == nheads - 1))
                        return ins
                    cx.op("pe", mm2, reads=[(K("wo"), s)] + [(K("osb"), hd) for hd in range(nheads)],
                          writes=[pk[6 + b]])
                    cx.op("dve", lambda e, b=b, py=py: e.tensor_tensor(
                        out=xn[b][:], in0=py[:], in1=xr[b][:], op=ALU.add),
                        reads=[pk[6 + b], (K("xr"), b)], writes=[(K("xn"), b)])
                    cx.dma("sp", x_out[mc * 128:(mc + 1) * 128, t0 + tt * TT: t0 + (tt + 1) * TT], xn[b][:],
                           reads=[(K("xn"), b)], writes=[(xkey_out, gt_, mc)], skey=("xn", b))
                if hf + 1 < T // HT:
                    norm_tile_ops(hf + 1, mc // 2, part=("A" if mc % 2 == 0 else "B"))
    cx.barrier()
    cx.pop()


def build_gdn_prog():
    nc = bass.Bass("TRN2", target_bir_lowering=False)
    x = nc.dram_tensor("x", [D, T], F32, kind="ExternalInput").ap()
    nw = nc.dram_tensor("nw", [128, KC], F32, kind="ExternalInput").ap()
    w_in = nc.dram_tensor("w_in", [D, GP], F32, kind="ExternalInput").ap()
    w_out = nc.dram_tensor("w_out", [NVH * 128, D], F32, kind="ExternalInput").ap()
    convw = nc.dram_tensor("convw", [128, 32, 4], F32, kind="ExternalInput").ap()
    alog = nc.dram_tensor("alog", [128, 512], F32, kind="ExternalInput").ap()
    dtb = nc.dram_tensor("dtb", [128, 512], F32, kind="ExternalInput").ap()
    gnw = nc.dram_tensor("gnw", [128, 1], F32, kind="ExternalInput").ap()
    cst = nc.dram_tensor("cst", [128, CST_W], F32, kind="ExternalInput").ap()
    oraw = nc.dram_tensor("oraw_scratch", [NVH * 128, T], F32, kind=("ExternalOutput" if DBG.get("noout") else "Internal")).ap()
    y = nc.dram_tensor("y", [D, T], F32, kind="ExternalOutput").ap()
    cx = Ctx(nc)
    c = consts(cx)
    gdn_core_phase(cx, c, "g_", x, oraw, nw, w_in, convw, alog, dtb, cst, "xin", "or")
    if not DBG.get("noout"):
        gdn_out_phase(cx, c, "go_", x, y, oraw, nw, gnw, w_in, w_out, "xin", "xout", "or")
    cx.finish()
    return nc


def final_norm_phase(cx, c, tag, x_in, y_out, nw_d, xkey_in):
    cx.push()
    xt = [cx.sbuf(tag + "xt%d" % i, [128, KC, TT], F32) for i in range(2)]
    yo = [cx.sbuf(tag + "yo%d" % i, [128, KC, TT], F32) for i in range(2)]
    sq = cx.sbuf(tag + "sq", [128, KC, TT], BF16)
    rstd = cx.sbuf(tag + "rstd", [128, TT], F32)
    nw = cx.sbuf(tag + "nw", [128, KC], F32)
    ps = cx.psum(tag + "ps", [128, TT], F32)
    cx.psum_keys.add(tag + "ps")
    cx.dma("sp", nw[:], nw_d, writes=[tag + "nw"], skey="nw")
    x_in_r = x_in.rearrange("(kc p) t -> p kc t", p=128)
    y_out_r = y_out.rearrange("(kc p) t -> p kc t", p=128)
    for tt in range(NTT):
        b = tt % 2
        cx.dma("sp", xt[b][:], x_in_r[:, :, tt * TT:(tt + 1) * TT],
               reads=[(xkey_in, tt, kc) for kc in range(KC)], writes=[(tag + "xt", b)], skey=("xt", b))
        rmsnorm_tile(cx, c, xt[b], nw, lambda kc, b=b: yo[b][:, kc, :], tag,
                     [(tag + "xt", b)], (tag + "yo", b), tag + "ps", ps, {"sq": sq, "rstd": rstd})
        cx.dma("sp", y_out_r[:, :, tt * TT:(tt + 1) * TT], yo[b][:], reads=[(tag + "yo", b)],
               writes=[("yout", tt)], skey=("yo", b))
    cx.barrier()
    cx.pop()


def build_fin_prog():
    nc = bass.Bass("TRN2", target_bir_lowering=False)
    x = nc.dram_tensor("x", [D, T], F32, kind="ExternalInput").ap()
    nw = nc.dram_tensor("nw", [128, KC], F32, kind="ExternalInput").ap()
    y = nc.dram_tensor("y", [D, T], F32, kind="ExternalOutput").ap()
    cx = Ctx(nc)
    c = consts(cx)
    final_norm_phase(cx, c, "n_", x, y, nw, "xin")
    cx.finish()
    return nc


def build_full_prog():
    nc = bass.Bass("TRN2", target_bir_lowering=False)
    dt = lambda name, shape, kind="ExternalInput", dtype=F32: nc.dram_tensor(name, list(shape), dtype, kind=kind).ap()
    x = dt("x", [D, T])
    nws = dt("nws", [13, 128, KC])
    ffn_w_in = dt("ffn_w_in", [4, 2, D, 2 * DFF])
    ffn_w_out = dt("ffn_w_out", [4, 2, DFF, D])
    attn_w_in = dt("attn_w_in", [2, D, 3 * NH_A * 128])
    attn_w_out = dt("attn_w_out", [2, NH_A * 128, D])
    gdn_w_in = dt("gdn_w_in", [2, D, GP])
    gdn_w_out = dt("gdn_w_out", [2, NVH * 128, D])
    convw = dt("convw", [2, 128, 32, 4])
    alog = dt("alog", [2, 128, 512])
    dtb = dt("dtb", [2, 128, 512])
    gnw = dt("gnw", [2, 128, 1])
    ctab = dt("ctab", [128, T])
    stab = dt("stab", [128, T])
    cst = dt("cst", [128, CST_W])
    y = dt("y", [D, T], kind="ExternalOutput")
    xbuf = [nc.dram_tensor("xres%d" % i, [D, T], F32).ap() for i in range(2)]
    os_d = nc.dram_tensor("os_scratch", [NH_A * 128, T], BF16).ap()
    oraw = nc.dram_tensor("oraw_scratch", [NVH * 128, T], F32).ap()
    cx = Ctx(nc)
    c = consts(cx)
    cur, nb_, ph = x, 0, 0
    ia = ib = 0
    for i in range(4):
        ffn_phase(cx, c, "f%da_" % i, cur, xbuf[nb_], nws[ph], ffn_w_in[i, 0], ffn_w_out[i, 0], "xi", "xo")
        cur, nb_, ph = xbuf[nb_], 1 - nb_, ph + 1
        if i % 2 == 0:
            attn_phase(cx, c, "a%d_" % i, cur, os_d, nws[ph], attn_w_in[ia], ctab, stab, cst, "xi", "os")
            outproj_phase(cx, c, "ao%d_" % i, cur, xbuf[nb_], os_d, attn_w_out[ia], NH_A, "xi", "xo", "os")
            ia += 1
        else:
            gdn_core_phase(cx, c, "g%d_" % i, cur, oraw, nws[ph], gdn_w_in[ib], convw[ib], alog[ib], dtb[ib], cst, "xi", "or")
            gdn_out_phase(cx, c, "go%d_" % i, cur, xbuf[nb_], oraw, nws[ph], gnw[ib], gdn_w_in[ib], gdn_w_out[ib],
                          "xi", "xo", "or")
            ib += 1
        cur, nb_, ph = xbuf[nb_], 1 - nb_, ph + 1
        ffn_phase(cx, c, "f%db_" % i, cur, xbuf[nb_], nws[ph], ffn_w_in[i, 1], ffn_w_out[i, 1], "xi", "xo")
        cur, nb_, ph = xbuf[nb_], 1 - nb_, ph + 1
    final_norm_phase(cx, c, "n_", cur, y, nws[ph], "xi")
    cx.finish()
    return nc


_PROGS = {}


def _prog(name):
    if name not in _PROGS:
        _PROGS[name] = {"ffn": build_ffn_prog, "attn": build_attn_prog, "gdn": build_gdn_prog,
                        "fin": build_fin_prog}[name]()
    return _PROGS[name]


def _lay_nw(v):
    return np.ascontiguousarray(np.asarray(v, np.float32).reshape(KC, 128).T)


def _launch(name, xs, shared):
    nc = _prog(name)
    n = len(xs)
    in_maps = [dict(shared, x=xs[i]) for i in range(n)]
    res = run_bass_kernel_spmd(nc, in_maps, core_ids=list(range(n)))
    return [np.asarray(res.results[i]["y"]) for i in range(n)]


def kernel(x, norm_w, ffn_w_in, ffn_w_out, attn_w_in, attn_w_out, gdn_w_in, gdn_conv_w, gdn_a_log,
           gdn_dt_bias, gdn_norm_w, gdn_w_out, final_norm_w):
    f = lambda a: np.ascontiguousarray(np.asarray(a, np.float32))
    x = f(x)
    B = x.shape[0]
    ctab, stab = host_rope()
    nws = np.stack([_lay_nw(norm_w[i, j]) for i in range(4) for j in range(3)] + [_lay_nw(final_norm_w)], 0)
    shared = {
        "nws": np.ascontiguousarray(nws), "ffn_w_in": f(ffn_w_in), "ffn_w_out": f(ffn_w_out),
        "attn_w_in": f(attn_w_in), "attn_w_out": f(attn_w_out), "gdn_w_in": f(gdn_w_in), "gdn_w_out": f(gdn_w_out),
        "convw": np.ascontiguousarray(f(gdn_conv_w).reshape(2, 4, 32, 128).transpose(0, 3, 2, 1)),
        "alog": np.ascontiguousarray(np.tile(f(gdn_a_log)[:, None, :], (1, 128, 32))),
        "dtb": np.ascontiguousarray(np.tile(f(gdn_dt_bias)[:, None, :], (1, 128, 32))),
        "gnw": np.ascontiguousarray(f(gdn_norm_w).reshape(2, 128, 1)),
        "ctab": ctab, "stab": stab, "cst": host_consts(),
    }
    if "full" not in _PROGS:
        _PROGS["full"] = build_full_prog()
    nc = _PROGS["full"]
    in_maps = [dict(shared, x=np.ascontiguousarray(x[b].T)) for b in range(B)]
    res = run_bass_kernel_spmd(nc, in_maps, core_ids=list(range(B)))
    return np.stack([np.asarray(res.results[b]["y"]).T for b in range(B)], 0).astype(np.float32)
```

```python
from contextlib import ExitStack
import numpy as np
import concourse.bass as bass
import concourse.mybir as mybir
from concourse.bass_utils import run_bass_kernel_spmd

F32 = mybir.dt.float32
BF16 = mybir.dt.bfloat16
ALU = mybir.AluOpType
AF = mybir.ActivationFunctionType

ENGS = ("pe", "act", "dve", "pool", "sp")
NAMES = []


class _Op:
    __slots__ = ("eng", "fn", "deps", "idx", "is_dma", "dsem", "dval", "need_inc", "cnt")

    def __init__(self, eng, fn, is_dma):
        self.eng = eng
        self.fn = fn
        self.deps = []
        self.is_dma = is_dma
        self.dsem = None
        self.dval = 0
        self.need_inc = False
        self.cnt = 0


class Ctx:
    def __init__(self, nc):
        self.nc = nc
        self.q = {e: [] for e in ENGS}
        self.wr = {}
        self.rd = {}
        self.dsems = {}
        self.pending_dma = []
        self.sem_pool = {}
        self.psum_keys = set()
        self.stack = ExitStack()
        self.scopes = []
        self.esem = {e: nc.alloc_semaphore("sem_" + e) for e in ENGS}
        self.n_psum = 0

    def push(self):
        st = ExitStack()
        self.scopes.append(st)

    def pop(self):
        self.scopes.pop().close()

    def _scope(self):
        return self.scopes[-1] if self.scopes else self.stack

    def sbuf(self, name, shape, dtype):
        self.n_psum += 1
        NAMES.append("%s_%d" % (name, self.n_psum))
        return self._scope().enter_context(self.nc.sbuf_tensor("%s_%d" % (name, self.n_psum), list(shape), dtype))

    def psum(self, name, shape, dtype=F32):
        self.n_psum += 1
        return self._scope().enter_context(self.nc.psum_tensor("%s_%d" % (name, self.n_psum), list(shape), dtype))

    def _track(self, op, reads, writes):
        deps = op.deps
        for k in reads:
            for o in self.wr.get(k, {}).values():
                deps.append(o)
            if k in self.psum_keys:
                for ek2, o in self.rd.get(k, {}).items():
                    if o.eng != op.eng:
                        deps.append(o)
        for k in writes:
            for o in self.wr.get(k, {}).values():
                deps.append(o)
            for o in self.rd.get(k, {}).values():
                deps.append(o)
        ek = id(op) if op.is_dma else op.eng
        for k in reads:
            self.rd.setdefault(k, {})[ek] = op
        for k in writes:
            self.wr[k] = {ek: op}
            self.rd[k] = {}

    def op(self, eng, fn, reads=(), writes=()):
        o = _Op(eng, fn, False)
        self._track(o, reads, writes)
        o.idx = len(self.q[eng])
        self.q[eng].append(o)
        return o

    def dma(self, eng, out, in_, reads=(), writes=(), skey=None, **kw):
        key = ("dma", skey)
        o = _Op(eng, lambda e: e.dma_start(out=out, in_=in_, **kw), True)
        self._track(o, reads, writes)
        key = (eng == "pool", key)
        if key not in self.dsems:
            pool = self.sem_pool.setdefault(eng == "pool", [])
            i = sum(1 for k in self.dsems if k[0] == key[0])
            if i >= len(pool):
                pool.append([self.nc.alloc_semaphore("d%s%d" % ("s" if key[0] else "h", i)), 0])
            self.dsems[key] = pool[i]
        ds = self.dsems[key]
        ds[1] += 16
        o.dsem = ds[0]
        o.dval = ds[1]
        o.idx = len(self.q[eng])
        self.q[eng].append(o)
        self.pending_dma.append(o)
        return o

    def barrier(self):
        lasts = [self.q[e][-1] for e in ENGS if self.q[e] and not self.q[e][-1].is_dma]
        lasts += [o for e in ENGS for o in self.q[e][-1:] if False]
        dm = list(self.pending_dma)
        self.pending_dma = []
        for e in ENGS:
            o = _Op(e, None, False)
            for e2 in ENGS:
                for p in reversed(self.q[e2]):
                    if not p.is_dma and p.fn is not None:
                        o.deps.append(p)
                        break
            o.deps.extend(dm)
            o.idx = len(self.q[e])
            self.q[e].append(o)
        self.wr = {}
        self.rd = {}
        self.dsems = {}

    def finish(self):
        nc = self.nc
        self.barrier()
        for e in ENGS:
            for o in self.q[e]:
                for d in o.deps:
                    if not d.is_dma and (d.eng != o.eng or e != "pe"):
                        d.need_inc = True
        for e in ENGS:
            c = 0
            for o in self.q[e]:
                if o.need_inc:
                    c += 1
                o.cnt = c
        engobj = {"pe": "tensor", "act": "scalar", "dve": "vector", "pool": "gpsimd", "sp": "sync"}
        with nc.Block() as block:
            for e in ENGS:
                def body(eng, e=e):
                    known = {}
                    for o in self.q[e]:
                        waits = {}
                        for d in o.deps:
                            if d.is_dma:
                                s, v = d.dsem, d.dval
                            else:
                                if d.eng == e and e == "pe":
                                    continue
                                s, v = self.esem[d.eng], d.cnt
                            if v > waits.get(s, (None, 0))[1]:
                                waits[s] = (s, v)
                        for s, v in waits.values():
                            if known.get(s, 0) >= v:
                                continue
                            known[s] = v
                            eng.wait_ge(s, v)
                        if o.fn is None:
                            continue
                        ins = o.fn(eng)
                        if o.is_dma:
                            ins.then_inc(o.dsem, 16)
                        elif o.need_inc:
                            ins.then_inc(self.esem[e], 1)
                getattr(block, engobj[e])(body)
        self.stack.close()


D = 1024
T = 4096
KC = D // 128
DFF = 2816
JC = DFF // 128
TT = 512
NTT = T // TT
EPS = 1e-6


def consts(cx):
    nc = cx.nc
    c = {}
    c["ones_bf"] = cx.sbuf("ones_bf", [128, 128], BF16)
    cx.op("pool", lambda e: e.memset(c["ones_bf"][:], 1.0), writes=["ones_bf"])
    return c


def rmsnorm_tile(cx, c, xt, nw, hT_out, tag, rd_keys, wr_key, ps_key, ps, scr, part=None):
    sq, rstd = scr["sq"], scr["rstd"]
    if part in (None, "A"):
        cx.op("act", lambda e: e.activation(sq[:], xt[:], AF.Square), reads=rd_keys, writes=[tag + "sq"])
    if part == "A":
        return

    def mm(e):
        ins = None
        for kc in range(KC):
            ins = e.matmul(ps[:], c["ones_bf"][:], sq[:, kc, :], start=(kc == 0), stop=(kc == KC - 1))
        return ins
    cx.op("pe", mm, reads=[tag + "sq", "ones_bf"], writes=[ps_key])
    cx.op("act", lambda e: e.activation(rstd[:], ps[:], AF.Sqrt, bias=EPS, scale=1.0 / D),
          reads=[ps_key], writes=[tag + "rstd"])
    cx.op("dve", lambda e: e.reciprocal(rstd[:], rstd[:]), reads=[tag + "rstd"], writes=[tag + "rstd"])
    for kc in range(KC):
        cx.op("dve", lambda e, kc=kc: e.scalar_tensor_tensor(
            out=hT_out(kc), in0=xt[:, kc, :], scalar=nw[:, kc:kc + 1], in1=rstd[:],
            op0=ALU.mult, op1=ALU.mult), reads=rd_keys + [tag + "rstd", tag + "nw"], writes=[wr_key])


def ffn_phase(cx, c, tag, x_in, x_out, nw_d, w_in, w_out, xkey_in, xkey_out):
    nc = cx.nc
    cx.push()
    HT = 2048
    NH = T // HT
    NT = HT // TT
    hT = cx.sbuf(tag + "hT", [128, KC, HT], BF16)
    act = cx.sbuf(tag + "act", [128, JC, HT], BF16)
    xt = cx.sbuf(tag + "xt", [128, KC, TT], F32)
    sq = cx.sbuf(tag + "sq", [128, KC, TT], BF16)
    rstd = cx.sbuf(tag + "rstd", [128, TT], F32)
    nw = cx.sbuf(tag + "nw", [128, KC], F32)
    WB = 256
    wg = [cx.sbuf(tag + "wg%d" % i, [128, KC, WB], BF16) for i in range(2)]
    wu = [cx.sbuf(tag + "wu%d" % i, [128, KC, WB], BF16) for i in range(2)]
    wo = [cx.sbuf(tag + "wo%d" % i, [128, JC, WB], BF16) for i in range(2)]
    sg = [cx.sbuf(tag + "sg%d" % i, [128, TT], F32) for i in range(2)]
    xr = [cx.sbuf(tag + "xr%d" % i, [128, TT], F32) for i in range(2)]
    xn = [cx.sbuf(tag + "xn%d" % i, [128, TT], F32) for i in range(2)]
    ps = [cx.psum(tag + "ps%d" % i, [128, TT], F32) for i in range(7)]
    pk = [tag + "ps%d" % i for i in range(7)]
    cx.psum_keys.update(pk)

    cx.dma("sp", nw[:], nw_d, writes=[tag + "nw"], skey="nw")
    w_in_r = w_in.rearrange("(kc p) n -> p kc n", p=128)
    w_out_r = w_out.rearrange("(kc p) n -> p kc n", p=128)
    x_in_r = x_in.rearrange("(kc p) t -> p kc t", p=128)
    cnt = {"g": 0, "o": 0, "s": 0, "x": 0}
    def norm_tile_ops(h, tt, part=None):
        t0_ = h * HT
        gt = (t0_ + tt * TT) // TT
        if part in (None, "A"):
            cx.dma("sp", xt[:], x_in_r[:, :, t0_ + tt * TT: t0_ + (tt + 1) * TT],
                   reads=[(xkey_in, gt, kc) for kc in range(KC)], writes=[tag + "xt"], skey="xt")
        rmsnorm_tile(cx, c, xt, nw, lambda kc, tt=tt: hT[:, kc, tt * TT:(tt + 1) * TT], tag,
                     [tag + "xt"], (tag + "hT", tt), pk[6], ps[6], {"sq": sq, "rstd": rstd}, part=part)

    for h in range(NH):
        t0 = h * HT
        if h == 0:
            for tt in range(NT):
                norm_tile_ops(0, tt)
        for jb in range(JC * 128 // WB):
            s = cnt["g"] % 2
            cnt["g"] += 1
            cx.dma("pool", wg[s][:], w_in_r[:, :, jb * WB:(jb + 1) * WB], writes=[(tag + "wg", s)],
                   skey=("wg", s))
            cx.dma("pool", wu[s][:], w_in_r[:, :, DFF + jb * WB: DFF + (jb + 1) * WB], writes=[(tag + "wu", s)],
                   skey=("wu", s))
            for jj in range(WB // 128):
                j = jb * (WB // 128) + jj
                for tt in range(NT):
                    b = cnt["s"] % 2
                    cnt["s"] += 1
                    pg, pu = ps[b], ps[2 + b]

                    def mm(e, s=s, jj=jj, tt=tt, pg=pg, pu=pu):
                        ins = None
                        for kc in range(KC):
                            ins = e.matmul(pg[:], wg[s][:, kc, jj * 128:(jj + 1) * 128],
                                           hT[:, kc, tt * TT:(tt + 1) * TT], start=(kc == 0), stop=(kc == KC - 1))
                        for kc in range(KC):
                            ins = e.matmul(pu[:], wu[s][:, kc, jj * 128:(jj + 1) * 128],
                                           hT[:, kc, tt * TT:(tt + 1) * TT], start=(kc == 0), stop=(kc == KC - 1))
                        return ins
                    cx.op("pe", mm, reads=[(tag + "wg", s), (tag + "wu", s), (tag + "hT", tt)],
                          writes=[pk[b], pk[2 + b]])
                    cx.op("act", lambda e, b=b, pg=pg: e.activation(sg[b][:], pg[:], AF.Silu),
                          reads=[pk[b]], writes=[(tag + "sg", b)])
                    cx.op("dve", lambda e, b=b, pu=pu, j=j, tt=tt: e.tensor_tensor(
                        out=act[:, j, tt * TT:(tt + 1) * TT], in0=sg[b][:], in1=pu[:], op=ALU.mult),
                        reads=[(tag + "sg", b), pk[2 + b]], writes=[(tag + "act", j, tt)])
        for mb in range(D // WB):
            s = cnt["o"] % 2
            cnt["o"] += 1
            cx.dma("pool", wo[s][:], w_out_r[:, :, mb * WB:(mb + 1) * WB], writes=[(tag + "wo", s)],
                   skey=("wo", s))
            for mm_ in range(WB // 128):
                mc = mb * (WB // 128) + mm_
                for tt in range(NT):
                    gt = (t0 + tt * TT) // TT
                    b = cnt["x"] % 2
                    cnt["x"] += 1
                    py = ps[4 + b]
                    cx.dma("sp", xr[b][:], x_in[mc * 128:(mc + 1) * 128, t0 + tt * TT: t0 + (tt + 1) * TT],
                           reads=[(xkey_in, gt, mc)], writes=[(tag + "xr", b)], skey=("xr", b))

                    def mm2(e, s=s, mm_=mm_, tt=tt, py=py):
                        ins = None
                        for kc in range(JC):
                            ins = e.matmul(py[:], wo[s][:, kc, mm_ * 128:(mm_ + 1) * 128],
                                           act[:, kc, tt * TT:(tt + 1) * TT], start=(kc == 0), stop=(kc == JC - 1))
                        return ins
                    cx.op("pe", mm2, reads=[(tag + "wo", s)] + [(tag + "act", j, tt) for j in range(JC)],
                          writes=[pk[4 + b]])
                    cx.op("dve", lambda e, b=b, py=py: e.scalar_tensor_tensor(
                        out=xn[b][:], in0=py[:], scalar=0.5, in1=xr[b][:], op0=ALU.mult, op1=ALU.add),
                        reads=[pk[4 + b], (tag + "xr", b)], writes=[(tag + "xn", b)])
                    cx.dma("sp", x_out[mc * 128:(mc + 1) * 128, t0 + tt * TT: t0 + (tt + 1) * TT], xn[b][:],
                           reads=[(tag + "xn", b)], writes=[(xkey_out, gt, mc)], skey=("xn", b))
                if h + 1 < NH:
                    hk = mc
                    norm_tile_ops(h + 1, hk // 2, part=("A" if hk % 2 == 0 else "B"))
    cx.barrier()
    cx.pop()


def build_ffn_prog():
    nc = bass.Bass("TRN2", target_bir_lowering=False)
    x = nc.dram_tensor("x", [D, T], F32, kind="ExternalInput").ap()
    nw = nc.dram_tensor("nw", [128, KC], F32, kind="ExternalInput").ap()
    w_in = nc.dram_tensor("w_in", [D, 2 * DFF], F32, kind="ExternalInput").ap()
    w_out = nc.dram_tensor("w_out", [DFF, D], F32, kind="ExternalInput").ap()
    y = nc.dram_tensor("y", [D, T], F32, kind="ExternalOutput").ap()
    cx = Ctx(nc)
    c = consts(cx)
    ffn_phase(cx, c, "f_", x, y, nw, w_in, w_out, "xin", "xout")
    cx.finish()
    return nc


NH_A = 12
DIL = (1, 4, 16)
NEG = -30000.0
SCALE_A = 128 ** -0.5
C_IDENT = 0
C_PSWAP = 128
C_MASK = 256
C_TRI = 512
C_TRIS = 640
C_MINC = 768
C_MSTR = 896
C_LAST = 1024
CST_W = 1152


def host_consts():
    m = np.zeros((128, CST_W), np.float32)
    k = np.arange(128)[:, None]
    q = np.arange(128)[None, :]
    m[:, C_IDENT:C_IDENT + 128] = (k == q)
    m[:, C_PSWAP:C_PSWAP + 128] = (k == (q + 64) % 128)
    m[:, C_MASK:C_MASK + 128] = np.where(k <= q, 0.0, NEG)
    m[:, C_MASK + 128:C_MASK + 256] = np.where(k >= q, 0.0, NEG)
    m[:, C_TRI:C_TRI + 128] = (k <= q)
    m[:, C_TRIS:C_TRIS + 128] = (k > q)
    m[:, C_MINC:C_MINC + 128] = (k <= q)
    m[:, C_MSTR:C_MSTR + 128] = (k < q)
    m[:, C_LAST:C_LAST + 128] = (k == 127)
    return m


def host_rope():
    inv = (1.0 / (10000.0 ** (np.arange(0, 128, 2, dtype=np.float32) / np.float32(128)))).astype(np.float32)
    ang = (np.arange(T, dtype=np.float32)[None, :] * inv[:, None]).astype(np.float32)
    cs, sn = np.cos(ang).astype(np.float32), np.sin(ang).astype(np.float32)
    return np.concatenate([cs, cs], 0), np.concatenate([-sn, sn], 0)


def norm_stage(cx, c, tag, x_in, nw_d, hT, xkey_in, ps, pskey):
    cx.push()
    xt = [cx.sbuf(tag + "nxt%d" % i, [128, KC, TT], F32) for i in range(2)]
    sq = cx.sbuf(tag + "nsq", [128, KC, TT], BF16)
    rstd = cx.sbuf(tag + "nrstd", [128, TT], F32)
    nw = cx.sbuf(tag + "nnw", [128, KC], F32)
    cx.dma("sp", nw[:], nw_d, writes=[tag + "nw"], skey="nw")
    x_in_r = x_in.rearrange("(kc p) t -> p kc t", p=128)
    for tt in range(NTT):
        b = tt % 2
        cx.dma("sp", xt[b][:], x_in_r[:, :, tt * TT:(tt + 1) * TT],
               reads=[(xkey_in, tt, kc) for kc in range(KC)], writes=[(tag + "xt", b)], skey=("xt", b))
        rmsnorm_tile(cx, c, xt[b], nw, lambda kc, tt=tt: hT[:, kc, tt * TT:(tt + 1) * TT], tag,
                     [(tag + "xt", b)], (tag + "hT", tt), pskey, ps, {"sq": sq, "rstd": rstd})
    cx.barrier()
    cx.pop()


def attn_phase(cx, c, tag, x_in, os_d, nw_d, w_in, ctab_d, stab_d, cst_d, xkey_in, okey):
    nc = cx.nc
    cx.push()
    hT = cx.sbuf(tag + "hT", [128, KC, T], BF16)
    ctab = cx.sbuf(tag + "ctab", [128, T], F32)
    stab = cx.sbuf(tag + "stab", [128, T], F32)
    ident = cx.sbuf(tag + "ident", [128, 128], BF16)
    pswap = cx.sbuf(tag + "pswap", [128, 128], BF16)
    maskb = cx.sbuf(tag + "maskb", [128, 256], BF16)
    ps = [cx.psum(tag + "ps%d" % i, [128, TT], F32) for i in range(8)]
    pk = [tag + "ps%d" % i for i in range(8)]
    cx.psum_keys.update(pk)
    cx.dma("sp", ctab[:], ctab_d, writes=[tag + "ctab"], skey="ctab")
    cx.dma("sp", stab[:], stab_d, writes=[tag + "stab"], skey="stab")
    cx.dma("pool", ident[:], cst_d[:, C_IDENT:C_IDENT + 128], writes=[tag + "ident"], skey="ident")
    cx.dma("pool", pswap[:], cst_d[:, C_PSWAP:C_PSWAP + 128], writes=[tag + "pswap"], skey="pswap")
    cx.dma("pool", maskb[:], cst_d[:, C_MASK:C_MASK + 256], writes=[tag + "maskb"], skey="maskb")
    norm_stage(cx, c, tag, x_in, nw_d, hT, xkey_in, ps[6], pk[6])
    hkeys = []

    Qd = cx.sbuf(tag + "Qd", [128, T], BF16)
    Kd = cx.sbuf(tag + "Kd", [128, T], BF16)
    Vd = cx.sbuf(tag + "Vd", [128, 32, 128], BF16)
    numT = cx.sbuf(tag + "numT", [128, 3, T], BF16)
    dent = cx.sbuf(tag + "dent", [128, T], F32)
    wq = [cx.sbuf(tag + "wq%d" % i, [128, KC, 128], BF16) for i in range(2)]
    wk = [cx.sbuf(tag + "wk%d" % i, [128, KC, 128], BF16) for i in range(2)]
    wv = [cx.sbuf(tag + "wv%d" % i, [128, KC, 128], BF16) for i in range(2)]
    qb = [cx.sbuf(tag + "qb%d" % i, [128, TT], BF16) for i in range(3)]
    t1 = [cx.sbuf(tag + "t1%d" % i, [128, TT], F32) for i in range(3)]
    t2 = [cx.sbuf(tag + "t2%d" % i, [128, TT], F32) for i in range(3)]
    pt = [cx.sbuf(tag + "pt%d" % i, [128, 256], BF16) for i in range(4)]
    osc = [cx.sbuf(tag + "osc%d" % i, [128, TT], BF16) for i in range(2)]
    w_in_r = w_in.rearrange("(kc p) n -> p kc n", p=128)
    AW = NH_A * 128
    cnt = {"w": 0, "r": 0, "o": 0}

    def load_w(hd):
        s = cnt["w"] % 2
        cnt["w"] += 1
        for nm, wt, off in (("wq", wq, 0), ("wk", wk, AW), ("wv", wv, 2 * AW)):
            cx.dma("pool", wt[s][:], w_in_r[:, :, off + hd * 128: off + (hd + 1) * 128],
                   writes=[(tag + nm, s)], skey=(nm, s))
        return s

    for hg in range(4):
        for g in range(3):
            d = DIL[g]
            Ls = T // d
            nb = Ls // 128
            hd = g * 4 + hg
            s = load_w(hd)
            def rope_piece(which, wt, dst, dkey, tt, item, s=s, d=d):
                b = item % 3
                bP, bS = (0, 1, 4)[b], (2, 3, 5)[b]
                pp, sw = ps[bP], ps[bS]
                dst_v = dst[:, :].rearrange("p (r j) -> p j r", r=d)

                def mm(e):
                    ins = None
                    for kc in range(KC):
                        ins = e.matmul(pp[:], wt[s][:, kc, :], hT[:, kc, tt * TT:(tt + 1) * TT],
                                       start=(kc == 0), stop=(kc == KC - 1))
                    return ins
                cx.op("pe", mm, reads=[(tag + "w" + which, s)], writes=[pk[bP]])
                yield
                cx.op("act", lambda e: e.activation(qb[b][:], pp[:], AF.Copy),
                      reads=[pk[bP]], writes=[(tag + "qb", b)])
                yield "NEXT"
                cx.op("pe", lambda e: e.matmul(sw[:], pswap[:], qb[b][:], start=True, stop=True),
                      reads=[(tag + "qb", b), tag + "pswap"], writes=[pk[bS]])
                cx.op("dve", lambda e: e.tensor_tensor(
                    out=t1[b][:], in0=qb[b][:], in1=ctab[:, tt * TT:(tt + 1) * TT], op=ALU.mult),
                    reads=[(tag + "qb", b), tag + "ctab"], writes=[(tag + "t1", b)])
                yield
                cx.op("dve", lambda e: e.tensor_tensor(
                    out=t2[b][:], in0=sw[:], in1=stab[:, tt * TT:(tt + 1) * TT], op=ALU.mult),
                    reads=[pk[bS], tag + "stab"], writes=[(tag + "t2", b)])
                yield
                j0 = tt * TT // d
                cx.op("dve", lambda e: e.tensor_tensor(
                    out=dst_v[:, j0:j0 + TT // d, :],
                    in0=t1[b][:, :].rearrange("p (j r) -> p j r", r=d),
                    in1=t2[b][:, :].rearrange("p (j r) -> p j r", r=d), op=ALU.add),
                    reads=[(tag + "t1", b), (tag + "t2", b)], writes=[(dkey, tt)])
                yield

            pieces = [(w_, wt_, dst_, dk_, tt) for (w_, wt_, dst_, dk_) in
                      (("q", wq, Qd, tag + "Qd"), ("k", wk, Kd, tag + "Kd")) for tt in range(NTT)]
            act_g, nx = [], [0]

            def start_r():
                if nx[0] < len(pieces) and len(act_g) < 3:
                    w_, wt_, dst_, dk_, tt = pieces[nx[0]]
                    act_g.append(rope_piece(w_, wt_, dst_, dk_, tt, cnt["r"]))
                    cnt["r"] += 1
                    nx[0] += 1
            start_r()
            while act_g:
                for g_ in list(act_g):
                    try:
                        v = next(g_)
                    except StopIteration:
                        act_g.remove(g_)
                        start_r()
                        continue
                    if v == "NEXT" and g_ is act_g[-1]:
                        start_r()
            for b4 in range(8):
                b = cnt["r"] % 2
                cnt["r"] += 1
                pp = ps[b]

                def mmv(e, s=s, b4=b4, pp=pp, d=d, Ls=Ls):
                    ins = None
                    for i in range(4):
                        B = b4 * 4 + i
                        r, n = divmod(B, Ls // 128)
                        t0 = n * 128 * d + r
                        for kc in range(KC):
                            ins = e.matmul(pp[:, i * 128:(i + 1) * 128],
                                           hT[:, kc, t0: t0 + 127 * d + 1: d], wv[s][:, kc, :],
                                           start=(kc == 0), stop=(kc == KC - 1))
                    return ins
                cx.op("pe", mmv, reads=[(tag + "wv", s)], writes=[pk[b]])
                cx.op("act", lambda e, b4=b4, pp=pp: e.activation(
                    Vd[:, b4 * 4:(b4 + 1) * 4, :], pp[:, :].rearrange("p (i e) -> p i e", i=4), AF.Copy),
                    reads=[pk[b]], writes=[(tag + "Vd", b4)])
            qk_keys = [(tag + "Qd", tt) for tt in range(NTT)] + [(tag + "Kd", tt) for tt in range(NTT)]

            def scores(B, nb=nb):
                sb = (4, 5, 2, 3)[B % 4]
                nq = 256 if (B + 1) % nb != 0 else 128
                st = ps[sb]

                def mm(e, B=B, nq=nq, st=st):
                    e.matmul(st[:, 0:nq], Kd[:, B * 128:(B + 1) * 128], Qd[:, B * 128:B * 128 + nq],
                             start=True, stop=False)
                    return e.matmul(st[:, 0:nq], ident[:], maskb[:, 0:nq], start=False, stop=True)
                cx.op("pe", mm, reads=qk_keys + [tag + "ident", tag + "maskb"], writes=[pk[sb]])
                cx.op("act", lambda e, B=B, nq=nq, st=st: e.activation(
                    pt[B % 4][:, 0:nq], st[:, 0:nq], AF.Exp, scale=SCALE_A),
                    reads=[pk[sb]], writes=[(tag + "pt", B % 4)])

            def pv(B, nb=nb, g=g, d=d, Ls=Ls):
                first = (B % nb == 0)
                col = (B % 4) * 128
                bN, bD = ((6, 7), (0, 1))[(B // 4) % 2]

                def mm(e, B=B, first=first, col=col, bN=bN, bD=bD):
                    ins = None
                    for dst, lhs_fn in ((ps[bN], lambda blk: Vd[:, blk, :]), (ps[bD], lambda blk: c["ones_bf"][:])):
                        if not first:
                            e.matmul(dst[:, col:col + 128], lhs_fn(B - 1), pt[(B - 1) % 4][:, 128:256],
                                     start=True, stop=False)
                        ins = e.matmul(dst[:, col:col + 128], lhs_fn(B), pt[B % 4][:, 0:128],
                                       start=first, stop=True)
                    return ins
                rk = [(tag + "pt", B % 4), (tag + "Vd", B // 4), "ones_bf"]
                if not first:
                    rk += [(tag + "pt", (B - 1) % 4), (tag + "Vd", (B - 1) // 4)]
                cx.op("pe", mm, reads=rk, writes=[pk[bN], pk[bD]])
                if B % 4 == 3:
                    u0 = (B - 3) * 128
                    if d == 1:
                        def nat(ap2d):
                            return ap2d[:, u0:u0 + 512]
                        def src(p):
                            return p[:, :]
                    else:
                        r0, j0 = divmod(u0, Ls)
                        nr = max(1, 512 // Ls)
                        nj = 512 // nr
                        def nat(ap2d, r0=r0, j0=j0, nr=nr, nj=nj, d=d):
                            return ap2d.rearrange("p (j r) -> p r j", r=d)[:, r0:r0 + nr, j0:j0 + nj]
                        def src(p, nr=nr):
                            return p[:, :].rearrange("p (r j) -> p r j", r=nr)
                    cx.op("act", lambda e, nat=nat, src=src, g=g, bN=bN: e.activation(
                        nat(numT[:, g, :]), src(ps[bN]), AF.Copy),
                        reads=[pk[bN]], writes=[(tag + "numT", g)])
                    if g == 0:
                        cx.op("dve", lambda e, nat=nat, src=src, bD=bD: e.tensor_copy(nat(dent[:, :]), src(ps[bD])),
                              reads=[pk[bD]], writes=[tag + "dent"])
                    else:
                        cx.op("dve", lambda e, nat=nat, src=src, bD=bD: e.tensor_tensor(
                            out=nat(dent[:, :]), in0=nat(dent[:, :]), in1=src(ps[bD]), op=ALU.add),
                            reads=[pk[bD], tag + "dent"], writes=[tag + "dent"])

            scores(0)
            scores(1)
            for B in range(32):
                if B + 2 < 32:
                    scores(B + 2)
                pv(B)
        cx.op("dve", lambda e: e.reciprocal(dent[:, :], dent[:, :]), reads=[tag + "dent"], writes=[tag + "dent"])
        for g in range(3):
            hd = g * 4 + hg
            for tt in range(NTT):
                b = cnt["o"] % 2
                cnt["o"] += 1
                cx.op("dve", lambda e, b=b, g=g, tt=tt: e.tensor_tensor(
                    out=osc[b][:], in0=numT[:, g, tt * TT:(tt + 1) * TT], in1=dent[:, tt * TT:(tt + 1) * TT],
                    op=ALU.mult), reads=[tag + "dent", (tag + "numT", g)], writes=[(tag + "osc", b)])
                cx.dma("sp", os_d[hd * 128:(hd + 1) * 128, tt * TT:(tt + 1) * TT], osc[b][:],
                       reads=[(tag + "osc", b)], writes=[(okey, hd, tt)], skey=("osc", b))
    cx.barrier()
    cx.pop()


def outproj_phase(cx, c, tag, x_in, x_out, os_d, w_out, nheads, xkey_in, xkey_out, okey):
    cx.push()
    HT = 2048
    NT = HT // TT
    WB = 256
    osb = cx.sbuf(tag + "osb", [128, nheads, HT], BF16)
    wo = [cx.sbuf(tag + "wo%d" % i, [128, nheads, WB], BF16) for i in range(2)]
    xr = [cx.sbuf(tag + "xr%d" % i, [128, TT], F32) for i in range(2)]
    xn = [cx.sbuf(tag + "xn%d" % i, [128, TT], F32) for i in range(2)]
    ps = [cx.psum(tag + "ps%d" % i, [128, TT], F32) for i in range(2)]
    pk = [tag + "ps%d" % i for i in range(2)]
    cx.psum_keys.update(pk)
    w_out_r = w_out.rearrange("(kc p) n -> p kc n", p=128)
    os_r = os_d.rearrange("(h p) t -> p h t", p=128)
    cnt = {"o": 0, "x": 0}
    for h in range(T // HT):
        t0 = h * HT
        for hd in range(nheads):
            cx.dma("sp", osb[:, hd, :], os_r[:, hd, t0:t0 + HT],
                   reads=[(okey, hd, (t0 // TT) + i) for i in range(NT)], writes=[(tag + "osb", hd)], skey=("osb", hd))
        for mb in range(D // WB):
            s = cnt["o"] % 2
            cnt["o"] += 1
            cx.dma("pool", wo[s][:], w_out_r[:, :, mb * WB:(mb + 1) * WB], writes=[(tag + "wo", s)], skey=("wo", s))
            for mm_ in range(WB // 128):
                mc = mb * (WB // 128) + mm_
                for tt in range(NT):
                    gt = (t0 + tt * TT) // TT
                    b = cnt["x"] % 2
                    cnt["x"] += 1
                    py = ps[b]
                    cx.dma("sp", xr[b][:], x_in[mc * 128:(mc + 1) * 128, t0 + tt * TT: t0 + (tt + 1) * TT],
                           reads=[(xkey_in, gt, mc)], writes=[(tag + "xr", b)], skey=("xr", b))

                    def mm2(e, s=s, mm_=mm_, tt=tt, py=py):
                        ins = None
                        for kc in range(nheads):
                            ins = e.matmul(py[:], wo[s][:, kc, mm_ * 128:(mm_ + 1) * 128],
                                           osb[:, kc, tt * TT:(tt + 1) * TT], start=(kc == 0), stop=(kc == nheads - 1))
                        return ins
                    cx.op("pe", mm2, reads=[(tag + "wo", s)] + [(tag + "osb", hd) for hd in range(nheads)],
                          writes=[pk[b]])
                    cx.op("dve", lambda e, b=b, py=py: e.tensor_tensor(
                        out=xn[b][:], in0=py[:], in1=xr[b][:], op=ALU.add),
                        reads=[pk[b], (tag + "xr", b)], writes=[(tag + "xn", b)])
                    cx.dma("sp", x_out[mc * 128:(mc + 1) * 128, t0 + tt * TT: t0 + (tt + 1) * TT], xn[b][:],
                           reads=[(tag + "xn", b)], writes=[(xkey_out, gt, mc)], skey=("xn", b))
    cx.barrier()
    cx.pop()


def build_attn_prog():
    nc = bass.Bass("TRN2", target_bir_lowering=False)
    x = nc.dram_tensor("x", [D, T], F32, kind="ExternalInput").ap()
    nw = nc.dram_tensor("nw", [128, KC], F32, kind="ExternalInput").ap()
    w_in = nc.dram_tensor("w_in", [D, 3 * NH_A * 128], F32, kind="ExternalInput").ap()
    w_out = nc.dram_tensor("w_out", [NH_A * 128, D], F32, kind="ExternalInput").ap()
    ctab = nc.dram_tensor("ctab", [128, T], F32, kind="ExternalInput").ap()
    stab = nc.dram_tensor("stab", [128, T], F32, kind="ExternalInput").ap()
    cst = nc.dram_tensor("cst", [128, CST_W], F32, kind="ExternalInput").ap()
    os_d = nc.dram_tensor("os_scratch", [NH_A * 128, T], BF16).ap()
    y = nc.dram_tensor("y", [D, T], F32, kind="ExternalOutput").ap()
    cx = Ctx(nc)
    c = consts(cx)
    attn_phase(cx, c, "a_", x, os_d, nw, w_in, ctab, stab, cst, "xin", "os")
    outproj_phase(cx, c, "ao_", x, y, os_d, w_out, NH_A, "xin", "xout", "os")
    cx.finish()
    return nc


NVH = 16
NKH = 8
GP = 6176
NTILE = T // 128
DBG = {}


class _Stop(Exception):
    pass


def _chk(n):
    if DBG.get("stop", 99) <= n:
        raise _Stop()


def gdn_core_phase(*a):
    cx = a[0]
    depth = len(cx.scopes)
    try:
        _gdn_core_phase(*a)
    except _Stop:
        cx.barrier()
        while len(cx.scopes) > depth:
            cx.pop()


def _gdn_core_phase(cx, c, tag, x_in, oraw_d, nw_d, w_in, convw_d, alog_d, dtb_d, cst_d, xkey_in, okey):
    cx.push()
    hT = cx.sbuf(tag + "hT", [128, KC, T], BF16)
    ident = cx.sbuf(tag + "ident", [128, 128], BF16)
    id4 = cx.sbuf(tag + "id4", [128, 4, 128], BF16)
    minc4 = cx.sbuf(tag + "minc4", [128, 4, 128], BF16)
    mlow4 = cx.sbuf(tag + "mlow4", [128, 4, 128], BF16)
    cw = cx.sbuf(tag + "cw", [128, 32, 4], F32)
    beta = cx.sbuf(tag + "beta", [128, NTILE, NVH], F32)
    nbeta = cx.sbuf(tag + "nbeta", [128, NTILE, NVH], F32)
    gc = cx.sbuf(tag + "gc", [128, NTILE, NVH], F32)
    kbs = cx.sbuf(tag + "kbs", [128, NTILE, NVH], F32)
    egr = cx.sbuf(tag + "egr", [128, NTILE, NVH], F32)
    cd = cx.sbuf(tag + "cd", [128, NTILE, NVH], F32)
    gchi = cx.sbuf(tag + "gchi", [128, NTILE, NVH], BF16)
    gclo = cx.sbuf(tag + "gclo", [128, NTILE, NVH], BF16)
    ps = [cx.psum(tag + "ps%d" % i, [128, TT], F32) for i in range(8)]
    pk = [tag + "ps%d" % i for i in range(8)]
    cx.psum_keys.update(pk)
    K = lambda n: tag + n
    cx.push()
    alog = cx.sbuf(tag + "alog", [128, 512], F32)
    dtb = cx.sbuf(tag + "dtb", [128, 512], F32)
    gt = cx.sbuf(tag + "gt", [128, NTILE, NVH], F32)
    wab = cx.sbuf(tag + "wab", [128, KC, 32], BF16)
    ghi = cx.sbuf(tag + "ghi", [128, NTILE, NVH], BF16)
    glo = cx.sbuf(tag + "glo", [128, NTILE, NVH], BF16)
    trib = cx.sbuf(tag + "trib", [128, 128], BF16)
    trisb = cx.sbuf(tag + "trisb", [128, 128], BF16)
    cx.dma("pool", ident[:], cst_d[:, C_IDENT:C_IDENT + 128], writes=[K("ident")], skey="ident")
    for i in range(4):
        cx.dma("pool", id4[:, i, :], cst_d[:, C_IDENT:C_IDENT + 128], writes=[K("id4")], skey=("id4", i))
        cx.dma("pool", minc4[:, i, :], cst_d[:, C_MINC:C_MINC + 128], writes=[K("minc4")], skey=("minc4", i))
        cx.dma("pool", mlow4[:, i, :], cst_d[:, C_TRIS:C_TRIS + 128], writes=[K("mlow4")], skey=("mlow4", i))
    cx.dma("sp", cw[:], convw_d, writes=[K("cw")], skey="cw")
    cx.dma("sp", alog[:], alog_d, writes=[K("alog")], skey="alog")
    cx.dma("sp", dtb[:], dtb_d, writes=[K("dtb")], skey="dtb")
    w_in_r = w_in.rearrange("(kc p) n -> p kc n", p=128)
    cx.dma("pool", wab[:], w_in_r[:, :, 6144:6176], writes=[K("wab")], skey="wab")
    _chk(0)
    norm_stage(cx, c, tag, x_in, nw_d, hT, xkey_in, ps[6], pk[6])
    _chk(1)

    f2 = lambda t: t[:, :, :].rearrange("p c h -> p (c h)")
    for which in range(2):
        def mm(e, which=which):
            ins = None
            for ct in range(NTILE):
                for kc in range(KC):
                    ins = e.matmul(ps[which][:, ct * 16:(ct + 1) * 16], hT[:, kc, ct * 128:(ct + 1) * 128],
                                   wab[:, kc, which * 16:(which + 1) * 16], start=(kc == 0), stop=(kc == KC - 1))
            return ins
        cx.op("pe", mm, reads=[], writes=[pk[which]])
    _chk(1.05)
    cx.op("act", lambda e: e.activation(f2(beta), ps[0][:], AF.Sigmoid), reads=[pk[0]], writes=[K("beta")])
    _chk(1.1)
    cx.op("dve", lambda e: e.tensor_tensor(out=f2(gt), in0=ps[1][:], in1=dtb[:], op=ALU.add),
          reads=[pk[1], K("dtb")], writes=[K("gt")])
    _chk(1.2)
    cx.op("act", lambda e: e.activation(f2(gt), f2(gt), AF.Exp), reads=[K("gt")], writes=[K("gt")])
    _chk(1.4)
    cx.op("act", lambda e: e.activation(f2(gt), f2(gt), AF.Ln, bias=1.0), reads=[K("gt")], writes=[K("gt")])
    _chk(1.6)
    cx.op("act", lambda e: e.activation(alog[:], alog[:], AF.Exp), reads=[K("alog")], writes=[K("alog")])
    _chk(1.8)
    cx.op("dve", lambda e: e.scalar_tensor_tensor(out=f2(gt), in0=f2(gt), scalar=-1.0, in1=alog[:],
                                                   op0=ALU.mult, op1=ALU.mult),
          reads=[K("gt"), K("alog")], writes=[K("gt")])
    cx.op("dve", lambda e: e.tensor_scalar(f2(nbeta), f2(beta), -1.0, None, ALU.mult),
          reads=[K("beta")], writes=[K("nbeta")])
    _chk(2)

    cx.dma("pool", trib[:], cst_d[:, C_TRI:C_TRI + 128], writes=[K("trib")], skey="trib")
    cx.dma("pool", trisb[:], cst_d[:, C_TRIS:C_TRIS + 128], writes=[K("trisb")], skey="trisb")
    cx.op("dve", lambda e: e.tensor_copy(f2(ghi), f2(gt)), reads=[K("gt")], writes=[K("ghi")])
    cx.op("dve", lambda e: e.tensor_tensor(out=f2(glo), in0=f2(gt), in1=f2(ghi), op=ALU.subtract),
          reads=[K("gt"), K("ghi")], writes=[K("glo")])
    _chk(2.2)

    def mmg(e):
        ins = None
        for ct in range(NTILE):
            for dst, m in ((ps[2], trib), (ps[3], trisb), (ps[4], c["ones_bf"])):
                e.matmul(dst[:, ct * 16:(ct + 1) * 16], m[:], ghi[:, ct, :], start=True, stop=False)
                ins = e.matmul(dst[:, ct * 16:(ct + 1) * 16], m[:], glo[:, ct, :], start=False, stop=True)
        return ins
    cx.op("pe", mmg, reads=[K("ghi"), K("glo"), K("trib"), K("trisb"), "ones_bf"], writes=[pk[2], pk[3], pk[4]])
    _chk(2.4)
    cx.op("dve", lambda e: e.tensor_copy(f2(gchi), ps[2][:]), reads=[pk[2]], writes=[K("gchi")])
    cx.op("dve", lambda e: e.tensor_tensor(out=f2(gclo), in0=ps[2][:], in1=f2(gchi), op=ALU.subtract),
          reads=[pk[2], K("gchi")], writes=[K("gclo")])
    _chk(2.6)
    cx.op("dve", lambda e: e.tensor_copy(f2(gc), ps[2][:]), reads=[pk[2]], writes=[K("gc")])
    cx.op("act", lambda e: e.activation(f2(kbs), ps[2][:], AF.Exp), reads=[pk[2]], writes=[K("kbs")])
    _chk(2.7)
    cx.op("dve", lambda e: e.tensor_tensor(out=f2(kbs), in0=f2(kbs), in1=f2(beta), op=ALU.mult),
          reads=[K("kbs"), K("beta")], writes=[K("kbs")])
    _chk(2.8)
    cx.op("act", lambda e: e.activation(f2(egr), ps[3][:], AF.Exp), reads=[pk[3]], writes=[K("egr")])
    cx.op("act", lambda e: e.activation(f2(cd), ps[4][:], AF.Exp), reads=[pk[4]], writes=[K("cd")])
    _chk(2.9)
    cx.barrier()
    cx.pop()
    _chk(3)

    qT = cx.sbuf(tag + "qT", [128, 2, T], BF16)
    kT = cx.sbuf(tag + "kT", [128, 2, T], BF16)
    ktok = cx.sbuf(tag + "ktok", [128, NTILE, 2, 128], BF16)
    vtok = cx.sbuf(tag + "vtok", [128, NTILE, 4, 128], BF16)
    PW = 1024
    for hgp in range(DBG.get('hgps', 4)):
        cx.push()
        NB = 3
        wch = [cx.sbuf(tag + "wch%d" % i, [128, KC, 128], BF16) for i in range(2)]
        dg = [cx.sbuf(tag + "dg%d" % i, [128, 4, 128], BF16) for i in range(2)]
        xcb = [cx.sbuf(tag + "xcb%d" % i, [128, 4 + TT], BF16) for i in range(NB)]
        sil = [cx.sbuf(tag + "sil%d" % i, [128, TT], F32) for i in range(NB)]
        silb = [cx.sbuf(tag + "silb%d" % i, [128, TT], BF16) for i in range(NB)]
        sqb = [cx.sbuf(tag + "sqb%d" % i, [128, TT], BF16) for i in range(NB)]
        rn = [cx.sbuf(tag + "rn%d" % i, [128, TT], F32) for i in range(NB)]
        chunks = [("q", 0), ("q", 1), ("k", 0), ("k", 1), ("v", 0), ("v", 1), ("v", 2), ("v", 3)]

        def g1_piece(kind, li, gch, ws, tt, item):
            b = item % NB
            i2 = item % 2
            bP, bC, bS, bT = i2, 2 + i2, 4 + i2, 6 + i2
            XK = lambda i: (K("xcb"), i)
            if tt == 0:
                cx.op("act", lambda e: e.activation(xcb[b][:, 0:4], xcb[b][:, 0:4], AF.Copy, scale=0.0),
                      reads=[], writes=[(K("xh"), b)])

            def mm(e):
                ins = None
                for kc in range(KC):
                    ins = e.matmul(ps[bP][:], wch[ws][:, kc, :], hT[:, kc, tt * TT:(tt + 1) * TT],
                                   start=(kc == 0), stop=(kc == KC - 1))
                return ins
            cx.op("pe", mm, reads=[(K("wch"), ws)], writes=[pk[bP]])
            yield
            cx.op("act", lambda e: e.activation(xcb[b][:, 4:4 + TT], ps[bP][:], AF.Copy),
                  reads=[pk[bP]], writes=[XK(b)])
            if tt < NTT - 1:
                nb_ = (item + 1) % NB
                cx.op("act", lambda e: e.activation(xcb[nb_][:, 0:4], xcb[b][:, TT:TT + 4], AF.Copy),
                      reads=[XK(b)], writes=[(K("xh"), nb_)])
            yield "NEXT"

            def mmc(e):
                ins = None
                for j in range(4):
                    ins = e.matmul(ps[bC][:], dg[ws][:, j, :], xcb[b][:, 1 + j:1 + j + TT], start=(j == 0), stop=(j == 3))
                return ins
            cx.op("pe", mmc, reads=[XK(b), (K("xh"), b), (K("dg"), ws)], writes=[pk[bC]])
            yield
            tsl = slice(tt * TT, (tt + 1) * TT)
            if kind == "v":
                cx.op("act", lambda e: e.activation(silb[b][:], ps[bC][:], AF.Silu), reads=[pk[bC]], writes=[(K("silb"), b)])
                yield
            else:
                cx.op("act", lambda e: e.activation(sil[b][:], ps[bC][:], AF.Silu), reads=[pk[bC]], writes=[(K("sil"), b)])
                yield
                cx.op("dve", lambda e: e.tensor_tensor(out=sqb[b][:], in0=sil[b][:], in1=sil[b][:], op=ALU.mult),
                      reads=[(K("sil"), b)], writes=[(K("sqb"), b)])
                yield
                cx.op("pe", lambda e: e.matmul(ps[bS][:], c["ones_bf"][:], sqb[b][:], start=True, stop=True),
                      reads=[(K("sqb"), b), "ones_bf"], writes=[pk[bS]])
                yield
                cx.op("act", lambda e: e.activation(rn[b][:], ps[bS][:], AF.Sqrt, bias=1e-6, scale=1.0),
                      reads=[pk[bS]], writes=[(K("rn"), b)])
                yield
                cx.op("dve", lambda e: e.reciprocal(rn[b][:], rn[b][:]), reads=[(K("rn"), b)], writes=[(K("rn"), b)])
                dstT = qT if kind == "q" else kT
                sc = (128 ** -0.5) if kind == "q" else 1.0
                cx.op("dve", lambda e: e.scalar_tensor_tensor(
                    out=dstT[:, li, tsl], in0=sil[b][:], scalar=sc, in1=rn[b][:], op0=ALU.mult, op1=ALU.mult),
                    reads=[(K("sil"), b), (K("rn"), b)], writes=[(K(kind + "T"), li, tt)])
                yield
            if kind in ("k", "v"):
                ptv = ps[bT][:, :].bitcast(BF16)

                def mmt(e):
                    ins = None
                    for i in range(4):
                        if kind == "v":
                            src = silb[b][:, i * 128:(i + 1) * 128]
                        else:
                            src = kT[:, li, tt * TT + i * 128: tt * TT + (i + 1) * 128]
                        ins = e.transpose(ptv[:, i * 128:(i + 1) * 128], src, ident[:])
                    return ins
                rk = [(K("silb"), b)] if kind == "v" else [(K("kT"), li, tt)]
                cx.op("pe", mmt, reads=rk + [K("ident")], writes=[pk[bT]])
                yield
                dst = vtok if kind == "v" else ktok
                cx.op("act", lambda e: e.activation(
                    dst[:, tt * 4:tt * 4 + 4, li, :], ptv[:, 0:512].rearrange("p (i x) -> p i x", i=4), AF.Copy),
                    reads=[pk[bT]], writes=[(K(kind + "tok"), li, tt)])
                yield

        items = []
        wcnt = 0
        for kind, li in chunks:
            gch = {"q": 2 * hgp + li, "k": 8 + 2 * hgp + li, "v": 16 + 4 * hgp + li}[kind]
            ws = wcnt % 2
            wcnt += 1
            for tt in range(NTT):
                items.append((kind, li, gch, ws, tt))
        active, nxt_i = [], [0]

        def start1():
            while nxt_i[0] < len(items) and len(active) < NB:
                kind, li, gch, ws, tt = items[nxt_i[0]]
                if tt == 0:
                    cx.dma("pool", wch[ws][:], w_in_r[:, :, gch * 128:(gch + 1) * 128], writes=[(K("wch"), ws)],
                           skey=("wch", ws))
                    for j in range(4):
                        cx.op("dve", lambda e, ws=ws, gch=gch, j=j: e.tensor_scalar(
                            dg[ws][:, j, :], ident[:], cw[:, gch, j:j + 1], None, ALU.mult),
                            reads=[K("ident"), K("cw")], writes=[(K("dg"), ws)])
                active.append(g1_piece(kind, li, gch, ws, tt, nxt_i[0]))
                nxt_i[0] += 1
                return
        start1()
        while active:
            for g in list(active):
                try:
                    v = next(g)
                except StopIteration:
                    active.remove(g)
                    start1()
                    continue
                if v == "NEXT" and g is active[-1]:
                    start1()
        cx.barrier()
        cx.pop()
        _chk(5)

        cx.push()
        mk = lambda nm, dt_, n=2: [cx.sbuf(tag + nm + "%d" % i, [128, 4, 128], dt_) for i in range(n)]
        Abuf0, Abuf1, Atbuf0, Atbuf1 = mk("Aa", BF16), mk("Ab", BF16), mk("Ata", BF16), mk("Atb", BF16)
        A_O, AtO, Tt = mk("A_O", BF16), mk("AtO", BF16), mk("Tt", BF16)
        qkT, qdT, kd, vb, kb = mk("qkT", BF16), mk("qdT", BF16), mk("kd", BF16), mk("vb", BF16), mk("kb", BF16)
        dmT = cx.sbuf(tag + "dmT", [128, 4, 128], F32)
        dm2 = cx.sbuf(tag + "dm2", [128, 4, 128], F32)
        eB = cx.sbuf(tag + "eB", [128, 4, 128], F32)
        rr = cx.sbuf(tag + "rr", [128, 4, 128], F32)
        u = cx.sbuf(tag + "u", [128, 4, 128], F32)
        wT = cx.sbuf(tag + "wT", [128, 4, 128], BF16)
        vnew = cx.sbuf(tag + "vnew", [128, 4, 128], BF16)
        S32 = cx.sbuf(tag + "S32", [128, 4, 128], F32)
        S16 = cx.sbuf(tag + "S16", [128, 4, 128], BF16)
        ost = [cx.sbuf(tag + "ost%d" % i, [128, 4, 128], F32) for i in range(2)]
        fl = lambda t: t[:, :, :].rearrange("p h x -> p (h x)")
        cx.op("dve", lambda e: e.memset(fl(S32), 0.0), writes=[K("S32")])
        cx.op("dve", lambda e: e.memset(fl(S16), 0.0), writes=[K("S16")])
        hs = [4 * hgp + hl for hl in range(4)]
        B0, B1 = 0, 1

        def g2_tile(ct, hs, hgp=hgp):
            par = ct % 2
            bA, bAt, bTu = 2 + 3 * par, 3 + 3 * par, 4 + 3 * par
            tsl = slice(ct * 128, (ct + 1) * 128)
            P = lambda n: (K(n), par)
            AO, AtOp, Ttp = A_O[par], AtO[par], Tt[par]
            bufs = [(Abuf0[par], Atbuf0[par]), (Abuf1[par], Atbuf1[par]), (AO, AtOp)]

            def mm1(e):
                for hl in range(4):
                    e.matmul(ps[B0][:, hl * 128:(hl + 1) * 128], gchi[:, ct, hs[hl]:hs[hl] + 1].to_broadcast([128, 128]),
                             ident[:], start=True, stop=False)
                    e.matmul(ps[B0][:, hl * 128:(hl + 1) * 128], gclo[:, ct, hs[hl]:hs[hl] + 1].to_broadcast([128, 128]),
                             ident[:], start=False, stop=True)
                ins = None
                for khl in range(2):
                    e.matmul(ps[B1][:, khl * 128:(khl + 1) * 128], kT[:, khl, tsl], kT[:, khl, tsl], start=True, stop=True)
                    ins = e.matmul(ps[B1][:, (2 + khl) * 128:(3 + khl) * 128], kT[:, khl, tsl], qT[:, khl, tsl],
                                   start=True, stop=True)
                return ins
            cx.op("pe", mm1, reads=[K("ident")], writes=[pk[B0], pk[B1]])
            yield
            for hl in range(4):
                h = hs[hl]
                cx.op("dve", lambda e, hl=hl, h=h: e.tensor_scalar(
                    dmT[:, hl, :], ps[B0][:, hl * 128:(hl + 1) * 128], gc[:, ct, h:h + 1], 0.0, ALU.subtract, ALU.min),
                    reads=[pk[B0]], writes=[K("dmT")])
                cx.op("dve", lambda e, hl=hl, h=h: e.tensor_scalar(
                    dm2[:, hl, :], ps[B0][:, hl * 128:(hl + 1) * 128], gc[:, ct, h:h + 1], 0.0, ALU.subtract, ALU.max),
                    reads=[pk[B0]], writes=[K("dm2")])
                yield
            cx.op("act", lambda e: e.activation(fl(eB), ps[B0][:], AF.Exp), reads=[pk[B0]], writes=[K("eB")])
            cx.op("act", lambda e: e.activation(fl(dmT), fl(dmT), AF.Exp), reads=[K("dmT")], writes=[K("dmT")])
            cx.op("act", lambda e: e.activation(fl(dm2), fl(dm2), AF.Exp, scale=-1.0), reads=[K("dm2")], writes=[K("dm2")])
            yield
            cx.op("dve", lambda e: e.tensor_tensor(out=fl(dmT), in0=fl(dmT), in1=fl(minc4), op=ALU.mult),
                  reads=[K("dmT")], writes=[K("dmT")])
            cx.op("dve", lambda e: e.tensor_tensor(out=fl(dm2), in0=fl(dm2), in1=fl(mlow4), op=ALU.mult),
                  reads=[K("dm2")], writes=[K("dm2")])
            yield
            for hl in range(4):
                h = hs[hl]
                khl = hl // 2
                cx.op("dve", lambda e, hl=hl, h=h, khl=khl: e.scalar_tensor_tensor(
                    out=AO[:, hl, :], in0=ps[B1][:, khl * 128:(khl + 1) * 128], scalar=nbeta[:, ct, h:h + 1],
                    in1=dm2[:, hl, :], op0=ALU.mult, op1=ALU.mult), reads=[pk[B1], K("dm2")], writes=[(P("A"), 2)])
                cx.op("dve", lambda e, hl=hl, khl=khl: e.tensor_tensor(
                    out=qkT[par][:, hl, :], in0=ps[B1][:, (2 + khl) * 128:(3 + khl) * 128], in1=dmT[:, hl, :], op=ALU.mult),
                    reads=[pk[B1], K("dmT")], writes=[P("qkT")])
                cx.op("dve", lambda e, hl=hl, khl=khl: e.tensor_tensor(
                    out=qdT[par][:, hl, :], in0=qT[:, khl, tsl], in1=eB[:, hl, :], op=ALU.mult),
                    reads=[K("eB")], writes=[P("qdT")])
                cx.op("pool", lambda e, hl=hl, h=h: e.tensor_scalar(
                    vb[par][:, hl, :], vtok[:, ct, hl, :], beta[:, ct, h:h + 1], None, ALU.mult), writes=[P("vb")])
                cx.op("act", lambda e, hl=hl, h=h, khl=khl: e.activation(
                    kb[par][:, hl, :], ktok[:, ct, khl, :], AF.Copy, scale=kbs[:, ct, h:h + 1]), writes=[P("kb")])
                cx.op("pool", lambda e, hl=hl, h=h, khl=khl: e.tensor_scalar(
                    kd[par][:, hl, :], ktok[:, ct, khl, :], egr[:, ct, h:h + 1], None, ALU.mult), writes=[P("kd")])
                yield
            ptv = ps[bA][:, :].bitcast(BF16)

            def mmT(e):
                ins = None
                for hl in range(4):
                    ins = e.transpose(ptv[:, hl * 128:(hl + 1) * 128], AO[:, hl, :], ident[:])
                return ins
            cx.op("pe", mmT, reads=[(P("A"), 2), K("ident")], writes=[pk[bA]])
            yield
            cx.op("act", lambda e: e.activation(fl(AtOp), ptv[:, 0:512], AF.Copy),
                  reads=[pk[bA]], writes=[(P("At"), 2)])
            yield
            cx.op("dve", lambda e: e.tensor_tensor(out=fl(Ttp), in0=fl(AtOp), in1=fl(id4), op=ALU.add),
                  reads=[(P("At"), 2), K("id4")], writes=[P("Tt")])
            yield
            cur = 2
            NSTEP = 6
            for p in range(1, NSTEP + 1):
                nxt = 0 if cur == 2 else 1 - cur
                A, At = bufs[cur]
                An, Atn = bufs[nxt]
                last = (p == NSTEP)

                def mmsq(e, A=A, At=At, last=last):
                    ins = None
                    for hl in range(4):
                        ins = e.matmul(ps[bA][:, hl * 128:(hl + 1) * 128], At[:, hl, :], A[:, hl, :], start=True, stop=True)
                    if not last:
                        for hl in range(4):
                            ins = e.matmul(ps[bAt][:, hl * 128:(hl + 1) * 128], A[:, hl, :], At[:, hl, :],
                                           start=True, stop=True)
                    return ins
                cx.op("pe", mmsq, reads=[(P("A"), cur), (P("At"), cur)], writes=[pk[bA]] + ([] if last else [pk[bAt]]))
                yield
                cx.op("act", lambda e, An=An: e.activation(fl(An), ps[bA][:], AF.Copy),
                      reads=[pk[bA]], writes=[(P("A"), nxt)])
                if not last:
                    cx.op("dve", lambda e, Atn=Atn: e.tensor_copy(fl(Atn), ps[bAt][:]),
                          reads=[pk[bAt]], writes=[(P("At"), nxt)])
                yield

                def mmtu(e, An=An):
                    ins = None
                    for hl in range(4):
                        ins = e.matmul(ps[bTu][:, hl * 128:(hl + 1) * 128], An[:, hl, :], Ttp[:, hl, :], start=True, stop=True)
                    return ins
                cx.op("pe", mmtu, reads=[(P("A"), nxt), P("Tt")], writes=[pk[bTu]])
                yield
                cx.op("dve", lambda e: e.tensor_tensor(out=fl(Ttp), in0=fl(Ttp), in1=ps[bTu][:], op=ALU.add),
                      reads=[pk[bTu], P("Tt")], writes=[P("Tt")])
                cur = nxt
                if p == 3:
                    yield "MID"
                else:
                    yield
            T0, Rb = Atbuf0[par], Abuf0[par]
            ptv3 = ps[bAt][:, :].bitcast(BF16)

            def mmT0(e):
                ins = None
                for hl in range(4):
                    ins = e.transpose(ptv3[:, hl * 128:(hl + 1) * 128], Ttp[:, hl, :], ident[:])
                return ins
            cx.op("pe", mmT0, reads=[P("Tt"), K("ident")], writes=[pk[bAt]])
            yield
            cx.op("act", lambda e: e.activation(fl(T0), ptv3[:, 0:512], AF.Copy),
                  reads=[pk[bAt]], writes=[(P("At"), 0)])
            yield

            def mmR(e):
                ins = None
                for hl in range(4):
                    ins = e.matmul(ps[bA][:, hl * 128:(hl + 1) * 128], AtOp[:, hl, :], T0[:, hl, :], start=True, stop=True)
                return ins
            cx.op("pe", mmR, reads=[(P("At"), 2), (P("At"), 0)], writes=[pk[bA]])
            yield
            cx.op("dve", lambda e: e.scalar_tensor_tensor(out=fl(rr), in0=fl(T0), scalar=-1.0, in1=ps[bA][:],
                                                           op0=ALU.mult, op1=ALU.add),
                  reads=[(P("At"), 0), pk[bA]], writes=[K("rr")])
            cx.op("dve", lambda e: e.tensor_tensor(out=fl(Rb), in0=fl(rr), in1=fl(id4), op=ALU.add),
                  reads=[K("rr"), K("id4")], writes=[(P("A"), 0)])
            yield

            def mmU(e):
                ins = None
                for hl in range(4):
                    ins = e.matmul(ps[bTu][:, hl * 128:(hl + 1) * 128], Rb[:, hl, :], Ttp[:, hl, :], start=True, stop=True)
                return ins
            cx.op("pe", mmU, reads=[(P("A"), 0), P("Tt")], writes=[pk[bTu]])
            yield
            cx.op("dve", lambda e: e.tensor_tensor(out=fl(Ttp), in0=fl(Ttp), in1=ps[bTu][:], op=ALU.add),
                  reads=[pk[bTu], P("Tt")], writes=[P("Tt")])
            yield

            def mmuw(e):
                ins = None
                for hl in range(4):
                    e.matmul(ps[bA][:, hl * 128:(hl + 1) * 128], Ttp[:, hl, :], vb[par][:, hl, :], start=True, stop=True)
                    ins = e.matmul(ps[bAt][:, hl * 128:(hl + 1) * 128], kb[par][:, hl, :], Ttp[:, hl, :], start=True, stop=True)
                return ins
            cx.op("pe", mmuw, reads=[P("Tt"), P("vb"), P("kb")], writes=[pk[bA], pk[bAt]])
            yield
            cx.op("act", lambda e: e.activation(fl(u), ps[bA][:], AF.Copy), reads=[pk[bA]], writes=[K("u")])
            cx.op("act", lambda e: e.activation(fl(wT), ps[bAt][:], AF.Copy), reads=[pk[bAt]], writes=[K("wT")])
            yield

            def mmvn(e):
                ins = None
                for hl in range(4):
                    ins = e.matmul(ps[bTu][:, hl * 128:(hl + 1) * 128], wT[:, hl, :], S16[:, hl, :], start=True, stop=True)
                return ins
            cx.op("pe", mmvn, reads=[K("wT"), K("S16")], writes=[pk[bTu]])
            yield
            cx.op("dve", lambda e: e.tensor_tensor(out=fl(vnew), in0=fl(u), in1=ps[bTu][:], op=ALU.subtract),
                  reads=[K("u"), pk[bTu]], writes=[K("vnew")])
            yield

            def mmo(e):
                ins = None
                for hl in range(4):
                    e.matmul(ps[bA][:, hl * 128:(hl + 1) * 128], S16[:, hl, :], qdT[par][:, hl, :], start=True, stop=False)
                    e.matmul(ps[bA][:, hl * 128:(hl + 1) * 128], vnew[:, hl, :], qkT[par][:, hl, :], start=False, stop=True)
                for hl in range(4):
                    ins = e.matmul(ps[bAt][:, hl * 128:(hl + 1) * 128], kd[par][:, hl, :], vnew[:, hl, :], start=True, stop=True)
                return ins
            cx.op("pe", mmo, reads=[K("S16"), P("qdT"), K("vnew"), P("qkT"), P("kd")], writes=[pk[bA], pk[bAt]])
            yield
            ob = ct % 2
            for hl in range(4):
                h = hs[hl]
                cx.op("dve", lambda e, hl=hl, h=h: e.scalar_tensor_tensor(
                    out=S32[:, hl, :], in0=S32[:, hl, :], scalar=cd[:, ct, h:h + 1], in1=ps[bAt][:, hl * 128:(hl + 1) * 128],
                    op0=ALU.mult, op1=ALU.add), reads=[pk[bAt], K("S32")], writes=[K("S32")])
            cx.op("act", lambda e: e.activation(fl(S16), fl(S32), AF.Copy), reads=[K("S32")], writes=[K("S16")])
            cx.op("act", lambda e: e.activation(fl(ost[ob]), ps[bA][:], AF.Copy),
                  reads=[pk[bA]], writes=[(K("ost"), ob)])
            for hl in range(4):
                h = hs[hl]
                cx.dma("sp", oraw_d[h * 128:(h + 1) * 128, ct * 128:(ct + 1) * 128], ost[ob][:, hl, :],
                       reads=[(K("ost"), ob)], writes=[(okey, h, ct)], skey=("ost", ob, hl))
            yield

        ntile = DBG.get('tiles', NTILE)
        active, nxt_t = [], [0]

        def start():
            if nxt_t[0] < ntile and len(active) < 2:
                active.append(g2_tile(nxt_t[0], list(hs)))
                nxt_t[0] += 1
        start()
        while active:
            for g in list(active):
                try:
                    v = next(g)
                except StopIteration:
                    active.remove(g)
                    start()
                    continue
                if v == "MID":
                    start()
        cx.barrier()
        cx.pop()
    cx.barrier()
    cx.pop()


def gdn_out_phase(cx, c, tag, x_in, x_out, oraw_d, nw_d, gnw_d, w_in, w_out, xkey_in, xkey_out, okey):
    cx.push()
    HT = 2048
    NT = HT // TT
    WB = 256
    nheads = NVH
    K = lambda n: tag + n
    hT = cx.sbuf(tag + "hT", [128, KC, HT], BF16)
    osb = cx.sbuf(tag + "osb", [128, nheads, HT], BF16)
    gnw = cx.sbuf(tag + "gnw", [128, 1], F32)
    nw = cx.sbuf(tag + "nw", [128, KC], F32)
    xt = [cx.sbuf(tag + "xt%d" % i, [128, KC, TT], F32) for i in range(1)]
    sq = cx.sbuf(tag + "sq", [128, KC, TT], BF16)
    rstd = cx.sbuf(tag + "rstd", [128, TT], F32)
    wz = [cx.sbuf(tag + "wz%d" % i, [128, KC, 128], BF16) for i in range(2)]
    orw = [cx.sbuf(tag + "orw%d" % i, [128, TT], F32) for i in range(3)]
    osq = [cx.sbuf(tag + "osq%d" % i, [128, TT], BF16) for i in range(3)]
    orn = [cx.sbuf(tag + "orn%d" % i, [128, TT], F32) for i in range(3)]
    sz = [cx.sbuf(tag + "sz%d" % i, [128, TT], F32) for i in range(3)]
    wo = [cx.sbuf(tag + "wo%d" % i, [128, nheads, WB], BF16) for i in range(2)]
    xr = [cx.sbuf(tag + "xr%d" % i, [128, TT], F32) for i in range(2)]
    xn = [cx.sbuf(tag + "xn%d" % i, [128, TT], F32) for i in range(2)]
    ps = [cx.psum(tag + "ps%d" % i, [128, TT], F32) for i in range(8)]
    pk = [tag + "ps%d" % i for i in range(8)]
    cx.psum_keys.update(pk)
    w_in_r = w_in.rearrange("(kc p) n -> p kc n", p=128)
    w_out_r = w_out.rearrange("(kc p) n -> p kc n", p=128)
    x_in_r = x_in.rearrange("(kc p) t -> p kc t", p=128)
    cx.dma("sp", nw[:], nw_d, writes=[K("nw")], skey="nw")
    cx.dma("sp", gnw[:], gnw_d, writes=[K("gnw")], skey="gnw")
    cnt = {"o": 0, "x": 0, "z": 0, "w": 0}
    def norm_tile_ops(hf_, tt, part=None):
        t0_ = hf_ * HT
        gt_ = (t0_ + tt * TT) // TT
        if part in (None, "A"):
            cx.dma("sp", xt[0][:], x_in_r[:, :, t0_ + tt * TT: t0_ + (tt + 1) * TT],
                   reads=[(xkey_in, gt_, kc) for kc in range(KC)], writes=[K("xt")], skey="xt")
        rmsnorm_tile(cx, c, xt[0], nw, lambda kc, tt=tt: hT[:, kc, tt * TT:(tt + 1) * TT], tag,
                     [K("xt")], (K("hT"), tt), pk[5], ps[5], {"sq": sq, "rstd": rstd}, part=part)

    for hf in range(T // HT):
        t0 = hf * HT
        if hf == 0:
            for tt in range(NT):
                norm_tile_ops(0, tt)
        def gate_piece(h, ws, tt, item):
            b = item % 3
            bZ, bQ = b, 3 + b
            gt_ = (t0 + tt * TT) // TT
            cx.dma("sp", orw[b][:], oraw_d[h * 128:(h + 1) * 128, t0 + tt * TT: t0 + (tt + 1) * TT],
                   reads=[(okey, h, gt_)], writes=[(K("orw"), b)], skey=("orw", b))

            def mmz(e):
                ins = None
                for kc in range(KC):
                    ins = e.matmul(ps[bZ][:], wz[ws][:, kc, :], hT[:, kc, tt * TT:(tt + 1) * TT],
                                   start=(kc == 0), stop=(kc == KC - 1))
                return ins
            cx.op("pe", mmz, reads=[(K("wz"), ws), (K("hT"), tt)], writes=[pk[bZ]])
            yield
            cx.op("act", lambda e: e.activation(osq[b][:], orw[b][:], AF.Square),
                  reads=[(K("orw"), b)], writes=[(K("osq"), b)])
            yield "NEXT"
            cx.op("pe", lambda e: e.matmul(ps[bQ][:], c["ones_bf"][:], osq[b][:], start=True, stop=True),
                  reads=[(K("osq"), b), "ones_bf"], writes=[pk[bQ]])
            yield
            cx.op("act", lambda e: e.activation(orn[b][:], ps[bQ][:], AF.Sqrt, bias=EPS, scale=1.0 / 128),
                  reads=[pk[bQ]], writes=[(K("orn"), b)])
            cx.op("act", lambda e: e.activation(sz[b][:], ps[bZ][:], AF.Silu), reads=[pk[bZ]],
                  writes=[(K("sz"), b)])
            yield
            cx.op("dve", lambda e: e.reciprocal(orn[b][:], orn[b][:]), reads=[(K("orn"), b)],
                  writes=[(K("orn"), b)])
            yield
            cx.op("dve", lambda e: e.scalar_tensor_tensor(
                out=orn[b][:], in0=orw[b][:], scalar=gnw[:, 0:1], in1=orn[b][:], op0=ALU.mult, op1=ALU.mult),
                reads=[(K("orw"), b), (K("orn"), b), K("gnw")], writes=[(K("orn"), b)])
            yield
            cx.op("dve", lambda e: e.tensor_tensor(
                out=osb[:, h, tt * TT:(tt + 1) * TT], in0=orn[b][:], in1=sz[b][:], op=ALU.mult),
                reads=[(K("orn"), b), (K("sz"), b)], writes=[(K("osb"), h)])
            yield

        gitems = [(h, tt) for h in range(nheads) for tt in range(NT)]
        act_g, nx = [], [0]

        def start_g():
            if nx[0] < len(gitems) and len(act_g) < 3:
                h, tt = gitems[nx[0]]
                if tt == 0:
                    ws = cnt["w"] % 2
                    cnt["w"] += 1
                    cx.dma("pool", wz[ws][:], w_in_r[:, :, 4096 + h * 128: 4096 + (h + 1) * 128],
                           writes=[(K("wz"), ws)], skey=("wz", ws))
                ws = (cnt["w"] - 1) % 2
                act_g.append(gate_piece(h, ws, tt, cnt["z"]))
                cnt["z"] += 1
                nx[0] += 1
        start_g()
        while act_g:
            for g_ in list(act_g):
                try:
                    v = next(g_)
                except StopIteration:
                    act_g.remove(g_)
                    start_g()
                    continue
                if v == "NEXT" and g_ is act_g[-1]:
                    start_g()
        for mb in range(D // WB):
            s = cnt["o"] % 2
            cnt["o"] += 1
            cx.dma("pool", wo[s][:], w_out_r[:, :, mb * WB:(mb + 1) * WB], writes=[(K("wo"), s)], skey=("wo", s))
            for mm_ in range(WB // 128):
                mc = mb * (WB // 128) + mm_
                for tt in range(NT):
                    gt_ = (t0 + tt * TT) // TT
                    b = cnt["x"] % 2
                    cnt["x"] += 1
                    py = ps[6 + b]
                    cx.dma("sp", xr[b][:], x_in[mc * 128:(mc + 1) * 128, t0 + tt * TT: t0 + (tt + 1) * TT],
                           reads=[(xkey_in, gt_, mc)], writes=[(K("xr"), b)], skey=("xr", b))

                    def mm2(e, s=s, mm_=mm_, tt=tt, py=py):
                        ins = None
                        for kc in range(nheads):
                            ins = e.matmul(py[:], wo[s][:, kc, mm_ * 128:(mm_ + 1) * 128],
                                           osb[:, kc, tt * TT:(tt + 1) * TT], start=(kc == 0), stop=(kc == nheads - 1))
                        return ins
                    cx.op("pe", mm2, reads=[(K("wo"), s)] + [(K("osb"), hd) for hd in range(nheads)],
                          writes=[pk[6 + b]])
                    cx.op("dve", lambda e, b=b, py=py: e.tensor_tensor(
                        out=xn[b][:], in0=py[:], in1=xr[b][:], op=ALU.add),
                        reads=[pk[6 + b], (K("xr"), b)], writes=[(K("xn"), b)])
                    cx.dma("sp", x_out[mc * 128:(mc + 1) * 128, t0 + tt * TT: t0 + (tt + 1) * TT], xn[b][:],
                           reads=[(K("xn"), b)], writes=[(xkey_out, gt_, mc)], skey=("xn", b))
                if hf + 1 < T // HT:
                    norm_tile_ops(hf + 1, mc // 2, part=("A" if mc % 2 == 0 else "B"))
    cx.barrier()
    cx.pop()


def build_gdn_prog():
    nc = bass.Bass("TRN2", target_bir_lowering=False)
    x = nc.dram_tensor("x", [D, T], F32, kind="ExternalInput").ap()
    nw = nc.dram_tensor("nw", [128, KC], F32, kind="ExternalInput").ap()
    w_in = nc.dram_tensor("w_in", [D, GP], F32, kind="ExternalInput").ap()
    w_out = nc.dram_tensor("w_out", [NVH * 128, D], F32, kind="ExternalInput").ap()
    convw = nc.dram_tensor("convw", [128, 32, 4], F32, kind="ExternalInput").ap()
    alog = nc.dram_tensor("alog", [128, 512], F32, kind="ExternalInput").ap()
    dtb = nc.dram_tensor("dtb", [128, 512], F32, kind="ExternalInput").ap()
    gnw = nc.dram_tensor("gnw", [128, 1], F32, kind="ExternalInput").ap()
    cst = nc.dram_tensor("cst", [128, CST_W], F32, kind="ExternalInput").ap()
    oraw = nc.dram_tensor("oraw_scratch", [NVH * 128, T], F32, kind=("ExternalOutput" if DBG.get("noout") else "Internal")).ap()
    y = nc.dram_tensor("y", [D, T], F32, kind="ExternalOutput").ap()
    cx = Ctx(nc)
    c = consts(cx)
    gdn_core_phase(cx, c, "g_", x, oraw, nw, w_in, convw, alog, dtb, cst, "xin", "or")
    if not DBG.get("noout"):
        gdn_out_phase(cx, c, "go_", x, y, oraw, nw, gnw, w_in, w_out, "xin", "xout", "or")
    cx.finish()
    return nc


def final_norm_phase(cx, c, tag, x_in, y_out, nw_d, xkey_in):
    cx.push()
    xt = [cx.sbuf(tag + "xt%d" % i, [128, KC, TT], F32) for i in range(2)]
    yo = [cx.sbuf(tag + "yo%d" % i, [128, KC, TT], F32) for i in range(2)]
    sq = cx.sbuf(tag + "sq", [128, KC, TT], BF16)
    rstd = cx.sbuf(tag + "rstd", [128, TT], F32)
    nw = cx.sbuf(tag + "nw", [128, KC], F32)
    ps = cx.psum(tag + "ps", [128, TT], F32)
    cx.psum_keys.add(tag + "ps")
    cx.dma("sp", nw[:], nw_d, writes=[tag + "nw"], skey="nw")
    x_in_r = x_in.rearrange("(kc p) t -> p kc t", p=128)
    y_out_r = y_out.rearrange("(kc p) t -> p kc t", p=128)
    for tt in range(NTT):
        b = tt % 2
        cx.dma("sp", xt[b][:], x_in_r[:, :, tt * TT:(tt + 1) * TT],
               reads=[(xkey_in, tt, kc) for kc in range(KC)], writes=[(tag + "xt", b)], skey=("xt", b))
        rmsnorm_tile(cx, c, xt[b], nw, lambda kc, b=b: yo[b][:, kc, :], tag,
                     [(tag + "xt", b)], (tag + "yo", b), tag + "ps", ps, {"sq": sq, "rstd": rstd})
        cx.dma("sp", y_out_r[:, :, tt * TT:(tt + 1) * TT], yo[b][:], reads=[(tag + "yo", b)],
               writes=[("yout", tt)], skey=("yo", b))
    cx.barrier()
    cx.pop()


def build_fin_prog():
    nc = bass.Bass("TRN2", target_bir_lowering=False)
    x = nc.dram_tensor("x", [D, T], F32, kind="ExternalInput").ap()
    nw = nc.dram_tensor("nw", [128, KC], F32, kind="ExternalInput").ap()
    y = nc.dram_tensor("y", [D, T], F32, kind="ExternalOutput").ap()
    cx = Ctx(nc)
    c = consts(cx)
    final_norm_phase(cx, c, "n_", x, y, nw, "xin")
    cx.finish()
    return nc


def build_full_prog():
    nc = bass.Bass("TRN2", target_bir_lowering=False)
    dt = lambda name, shape, kind="ExternalInput", dtype=F32: nc.dram_tensor(name, list(shape), dtype, kind=kind).ap()
    x = dt("x", [D, T])
    nws = dt("nws", [13, 128, KC])
    ffn_w_in = dt("ffn_w_in", [4, 2, D, 2 * DFF])
    ffn_w_out = dt("ffn_w_out", [4, 2, DFF, D])
    attn_w_in = dt("attn_w_in", [2, D, 3 * NH_A * 128])
    attn_w_out = dt("attn_w_out", [2, NH_A * 128, D])
    gdn_w_in = dt("gdn_w_in", [2, D, GP])
    gdn_w_out = dt("gdn_w_out", [2, NVH * 128, D])
    convw = dt("convw", [2, 128, 32, 4])
    alog = dt("alog", [2, 128, 512])
    dtb = dt("dtb", [2, 128, 512])
    gnw = dt("gnw", [2, 128, 1])
    ctab = dt("ctab", [128, T])
    stab = dt("stab", [128, T])
    cst = dt("cst", [128, CST_W])
    y = dt("y", [D, T], kind="ExternalOutput")
    xbuf = [nc.dram_tensor("xres%d" % i, [D, T], F32).ap() for i in range(2)]
    os_d = nc.dram_tensor("os_scratch", [NH_A * 128, T], BF16).ap()
    oraw = nc.dram_tensor("oraw_scratch", [NVH * 128, T], F32).ap()
    cx = Ctx(nc)
    c = consts(cx)
    cur, nb_, ph = x, 0, 0
    ia = ib = 0
    for i in range(4):
        ffn_phase(cx, c, "f%da_" % i, cur, xbuf[nb_], nws[ph], ffn_w_in[i, 0], ffn_w_out[i, 0], "xi", "xo")
        cur, nb_, ph = xbuf[nb_], 1 - nb_, ph + 1
        if i % 2 == 0:
            attn_phase(cx, c, "a%d_" % i, cur, os_d, nws[ph], attn_w_in[ia], ctab, stab, cst, "xi", "os")
            outproj_phase(cx, c, "ao%d_" % i, cur, xbuf[nb_], os_d, attn_w_out[ia], NH_A, "xi", "xo", "os")
            ia += 1
        else:
            gdn_core_phase(cx, c, "g%d_" % i, cur, oraw, nws[ph], gdn_w_in[ib], convw[ib], alog[ib], dtb[ib], cst, "xi", "or")
            gdn_out_phase(cx, c, "go%d_" % i, cur, xbuf[nb_], oraw, nws[ph], gnw[ib], gdn_w_in[ib], gdn_w_out[ib],
                          "xi", "xo", "or")
            ib += 1
        cur, nb_, ph = xbuf[nb_], 1 - nb_, ph + 1
        ffn_phase(cx, c, "f%db_" % i, cur, xbuf[nb_], nws[ph], ffn_w_in[i, 1], ffn_w_out[i, 1], "xi", "xo")
        cur, nb_, ph = xbuf[nb_], 1 - nb_, ph + 1
    final_norm_phase(cx, c, "n_", cur, y, nws[ph], "xi")
    cx.finish()
    return nc


_PROGS = {}


def _prog(name):
    if name not in _PROGS:
        _PROGS[name] = {"ffn": build_ffn_prog, "attn": build_attn_prog, "gdn": build_gdn_prog,
                        "fin": build_fin_prog}[name]()
    return _PROGS[name]


def _lay_nw(v):
    return np.ascontiguousarray(np.asarray(v, np.float32).reshape(KC, 128).T)


def _launch(name, xs, shared):
    nc = _prog(name)
    n = len(xs)
    in_maps = [dict(shared, x=xs[i]) for i in range(n)]
    res = run_bass_kernel_spmd(nc, in_maps, core_ids=list(range(n)))
    return [np.asarray(res.results[i]["y"]) for i in range(n)]


def kernel(x, norm_w, ffn_w_in, ffn_w_out, attn_w_in, attn_w_out, gdn_w_in, gdn_conv_w, gdn_a_log,
           gdn_dt_bias, gdn_norm_w, gdn_w_out, final_norm_w):
    f = lambda a: np.ascontiguousarray(np.asarray(a, np.float32))
    x = f(x)
    B = x.shape[0]
    ctab, stab = host_rope()
    nws = np.stack([_lay_nw(norm_w[i, j]) for i in range(4) for j in range(3)] + [_lay_nw(final_norm_w)], 0)
    shared = {
        "nws": np.ascontiguousarray(nws), "ffn_w_in": f(ffn_w_in), "ffn_w_out": f(ffn_w_out),
        "attn_w_in": f(attn_w_in), "attn_w_out": f(attn_w_out), "gdn_w_in": f(gdn_w_in), "gdn_w_out": f(gdn_w_out),
        "convw": np.ascontiguousarray(f(gdn_conv_w).reshape(2, 4, 32, 128).transpose(0, 3, 2, 1)),
        "alog": np.ascontiguousarray(np.tile(f(gdn_a_log)[:, None, :], (1, 128, 32))),
        "dtb": np.ascontiguousarray(np.tile(f(gdn_dt_bias)[:, None, :], (1, 128, 32))),
        "gnw": np.ascontiguousarray(f(gdn_norm_w).reshape(2, 128, 1)),
        "ctab": ctab, "stab": stab, "cst": host_consts(),
    }
    if "full" not in _PROGS:
        _PROGS["full"] = build_full_prog()
    nc = _PROGS["full"]
    in_maps = [dict(shared, x=np.ascontiguousarray(x[b].T)) for b in range(B)]
    res = run_bass_kernel_spmd(nc, in_maps, core_ids=list(range(B)))
    return np.stack([np.asarray(res.results[b]["y"]).T for b in range(B)], 0).astype(np.float32)
```

```python
from contextlib import ExitStack
import numpy as np
import concourse.bass as bass
import concourse.mybir as mybir
from concourse.bass_utils import run_bass_kernel_spmd

F32 = mybir.dt.float32
BF16 = mybir.dt.bfloat16
ALU = mybir.AluOpType
AF = mybir.ActivationFunctionType

ENGS = ("pe", "act", "dve", "pool", "sp")
NAMES = []


class _Op:
    __slots__ = ("eng", "fn", "deps", "idx", "is_dma", "dsem", "dval", "need_inc", "cnt")

    def __init__(self, eng, fn, is_dma):
        self.eng = eng
        self.fn = fn
        self.deps = []
        self.is_dma = is_dma
        self.dsem = None
        self.dval = 0
        self.need_inc = False
        self.cnt = 0


class Ctx:
    def __init__(self, nc):
        self.nc = nc
        self.q = {e: [] for e in ENGS}
        self.wr = {}
        self.rd = {}
        self.dsems = {}
        self.pending_dma = []
        self.sem_pool = {}
        self.psum_keys = set()
        self.stack = ExitStack()
        self.scopes = []
        self.esem = {e: nc.alloc_semaphore("sem_" + e) for e in ENGS}
        self.n_psum = 0

    def push(self):
        st = ExitStack()
        self.scopes.append(st)

    def pop(self):
        self.scopes.pop().close()

    def _scope(self):
        return self.scopes[-1] if self.scopes else self.stack

    def sbuf(self, name, shape, dtype):
        self.n_psum += 1
        NAMES.append("%s_%d" % (name, self.n_psum))
        return self._scope().enter_context(self.nc.sbuf_tensor("%s_%d" % (name, self.n_psum), list(shape), dtype))

    def psum(self, name, shape, dtype=F32):
        self.n_psum += 1
        return self._scope().enter_context(self.nc.psum_tensor("%s_%d" % (name, self.n_psum), list(shape), dtype))

    def _track(self, op, reads, writes):
        deps = op.deps
        for k in reads:
            for o in self.wr.get(k, {}).values():
                deps.append(o)
            if k in self.psum_keys:
                for ek2, o in self.rd.get(k, {}).items():
                    if o.eng != op.eng:
                        deps.append(o)
        for k in writes:
            for o in self.wr.get(k, {}).values():
                deps.append(o)
            for o in self.rd.get(k, {}).values():
                deps.append(o)
        ek = id(op) if op.is_dma else op.eng
        for k in reads:
            self.rd.setdefault(k, {})[ek] = op
        for k in writes:
            self.wr[k] = {ek: op}
            self.rd[k] = {}

    def op(self, eng, fn, reads=(), writes=()):
        o = _Op(eng, fn, False)
        self._track(o, reads, writes)
        o.idx = len(self.q[eng])
        self.q[eng].append(o)
        return o

    def dma(self, eng, out, in_, reads=(), writes=(), skey=None, **kw):
        key = ("dma", skey)
        o = _Op(eng, lambda e: e.dma_start(out=out, in_=in_, **kw), True)
        self._track(o, reads, writes)
        key = (eng == "pool", key)
        if key not in self.dsems:
            pool = self.sem_pool.setdefault(eng == "pool", [])
            i = sum(1 for k in self.dsems if k[0] == key[0])
            if i >= len(pool):
                pool.append([self.nc.alloc_semaphore("d%s%d" % ("s" if key[0] else "h", i)), 0])
            self.dsems[key] = pool[i]
        ds = self.dsems[key]
        ds[1] += 16
        o.dsem = ds[0]
        o.dval = ds[1]
        o.idx = len(self.q[eng])
        self.q[eng].append(o)
        self.pending_dma.append(o)
        return o

    def barrier(self):
        lasts = [self.q[e][-1] for e in ENGS if self.q[e] and not self.q[e][-1].is_dma]
        lasts += [o for e in ENGS for o in self.q[e][-1:] if False]
        dm = list(self.pending_dma)
        self.pending_dma = []
        for e in ENGS:
            o = _Op(e, None, False)
            for e2 in ENGS:
                for p in reversed(self.q[e2]):
                    if not p.is_dma and p.fn is not None:
                        o.deps.append(p)
                        break
            o.deps.extend(dm)
            o.idx = len(self.q[e])
            self.q[e].append(o)
        self.wr = {}
        self.rd = {}
        self.dsems = {}

    def finish(self):
        nc = self.nc
        self.barrier()
        for e in ENGS:
            for o in self.q[e]:
                for d in o.deps:
                    if not d.is_dma and (d.eng != o.eng or e != "pe"):
                        d.need_inc = True
        for e in ENGS:
            c = 0
            for o in self.q[e]:
                if o.need_inc:
                    c += 1
                o.cnt = c
        engobj = {"pe": "tensor", "act": "scalar", "dve": "vector", "pool": "gpsimd", "sp": "sync"}
        with nc.Block() as block:
            for e in ENGS:
                def body(eng, e=e):
                    known = {}
                    for o in self.q[e]:
                        waits = {}
                        for d in o.deps:
                            if d.is_dma:
                                s, v = d.dsem, d.dval
                            else:
                                if d.eng == e and e == "pe":
                                    continue
                                s, v = self.esem[d.eng], d.cnt
                            if v > waits.get(s, (None, 0))[1]:
                                waits[s] = (s, v)
                        for s, v in waits.values():
                            if known.get(s, 0) >= v:
                                continue
                            known[s] = v
                            eng.wait_ge(s, v)
                        if o.fn is None:
                            continue
                        ins = o.fn(eng)
                        if o.is_dma:
                            ins.then_inc(o.dsem, 16)
                        elif o.need_inc:
                            ins.then_inc(self.esem[e], 1)
                getattr(block, engobj[e])(body)
        self.stack.close()


D = 1024
T = 4096
KC = D // 128
DFF = 2816
JC = DFF // 128
TT = 512
NTT = T // TT
EPS = 1e-6


def consts(cx):
    nc = cx.nc
    c = {}
    c["ones_bf"] = cx.sbuf("ones_bf", [128, 128], BF16)
    cx.op("pool", lambda e: e.memset(c["ones_bf"][:], 1.0), writes=["ones_bf"])
    return c


def rmsnorm_tile(cx, c, xt, nw, hT_out, tag, rd_keys, wr_key, ps_key, ps, scr, part=None):
    sq, rstd = scr["sq"], scr["rstd"]
    if part in (None, "A"):
        cx.op("act", lambda e: e.activation(sq[:], xt[:], AF.Square), reads=rd_keys, writes=[tag + "sq"])
    if part == "A":
        return

    def mm(e):
        ins = None
        for kc in range(KC):
            ins = e.matmul(ps[:], c["ones_bf"][:], sq[:, kc, :], start=(kc == 0), stop=(kc == KC - 1))
        return ins
    cx.op("pe", mm, reads=[tag + "sq", "ones_bf"], writes=[ps_key])
    cx.op("act", lambda e: e.activation(rstd[:], ps[:], AF.Sqrt, bias=EPS, scale=1.0 / D),
          reads=[ps_key], writes=[tag + "rstd"])
    cx.op("dve", lambda e: e.reciprocal(rstd[:], rstd[:]), reads=[tag + "rstd"], writes=[tag + "rstd"])
    for kc in range(KC):
        cx.op("dve", lambda e, kc=kc: e.scalar_tensor_tensor(
            out=hT_out(kc), in0=xt[:, kc, :], scalar=nw[:, kc:kc + 1], in1=rstd[:],
            op0=ALU.mult, op1=ALU.mult), reads=rd_keys + [tag + "rstd", tag + "nw"], writes=[wr_key])


def ffn_phase(cx, c, tag, x_in, x_out, nw_d, w_in, w_out, xkey_in, xkey_out):
    nc = cx.nc
    cx.push()
    HT = 2048
    NH = T // HT
    NT = HT // TT
    hT = cx.sbuf(tag + "hT", [128, KC, HT], BF16)
    act = cx.sbuf(tag + "act", [128, JC, HT], BF16)
    xt = cx.sbuf(tag + "xt", [128, KC, TT], F32)
    sq = cx.sbuf(tag + "sq", [128, KC, TT], BF16)
    rstd = cx.sbuf(tag + "rstd", [128, TT], F32)
    nw = cx.sbuf(tag + "nw", [128, KC], F32)
    WB = 256
    wg = [cx.sbuf(tag + "wg%d" % i, [128, KC, WB], BF16) for i in range(2)]
    wu = [cx.sbuf(tag + "wu%d" % i, [128, KC, WB], BF16) for i in range(2)]
    wo = [cx.sbuf(tag + "wo%d" % i, [128, JC, WB], BF16) for i in range(2)]
    sg = [cx.sbuf(tag + "sg%d" % i, [128, TT], F32) for i in range(2)]
    xr = [cx.sbuf(tag + "xr%d" % i, [128, TT], F32) for i in range(2)]
    xn = [cx.sbuf(tag + "xn%d" % i, [128, TT], F32) for i in range(2)]
    ps = [cx.psum(tag + "ps%d" % i, [128, TT], F32) for i in range(7)]
    pk = [tag + "ps%d" % i for i in range(7)]
    cx.psum_keys.update(pk)

    cx.dma("sp", nw[:], nw_d, writes=[tag + "nw"], skey="nw")
    w_in_r = w_in.rearrange("(kc p) n -> p kc n", p=128)
    w_out_r = w_out.rearrange("(kc p) n -> p kc n", p=128)
    x_in_r = x_in.rearrange("(kc p) t -> p kc t", p=128)
    cnt = {"g": 0, "o": 0, "s": 0, "x": 0}
    def norm_tile_ops(h, tt, part=None):
        t0_ = h * HT
        gt = (t0_ + tt * TT) // TT
        if part in (None, "A"):
            cx.dma("sp", xt[:], x_in_r[:, :, t0_ + tt * TT: t0_ + (tt + 1) * TT],
                   reads=[(xkey_in, gt, kc) for kc in range(KC)], writes=[tag + "xt"], skey="xt")
        rmsnorm_tile(cx, c, xt, nw, lambda kc, tt=tt: hT[:, kc, tt * TT:(tt + 1) * TT], tag,
                     [tag + "xt"], (tag + "hT", tt), pk[6], ps[6], {"sq": sq, "rstd": rstd}, part=part)

    for h in range(NH):
        t0 = h * HT
        if h == 0:
            for tt in range(NT):
                norm_tile_ops(0, tt)
        for jb in range(JC * 128 // WB):
            s = cnt["g"] % 2
            cnt["g"] += 1
            cx.dma("pool", wg[s][:], w_in_r[:, :, jb * WB:(jb + 1) * WB], writes=[(tag + "wg", s)],
                   skey=("wg", s))
            cx.dma("pool", wu[s][:], w_in_r[:, :, DFF + jb * WB: DFF + (jb + 1) * WB], writes=[(tag + "wu", s)],
                   skey=("wu", s))
            for jj in range(WB // 128):
                j = jb * (WB // 128) + jj
                for tt in range(NT):
                    b = cnt["s"] % 2
                    cnt["s"] += 1
                    pg, pu = ps[b], ps[2 + b]

                    def mm(e, s=s, jj=jj, tt=tt, pg=pg, pu=pu):
                        ins = None
                        for kc in range(KC):
                            ins = e.matmul(pg[:], wg[s][:, kc, jj * 128:(jj + 1) * 128],
                                           hT[:, kc, tt * TT:(tt + 1) * TT], start=(kc == 0), stop=(kc == KC - 1))
                        for kc in range(KC):
                            ins = e.matmul(pu[:], wu[s][:, kc, jj * 128:(jj + 1) * 128],
                                           hT[:, kc, tt * TT:(tt + 1) * TT], start=(kc == 0), stop=(kc == KC - 1))
                        return ins
                    cx.op("pe", mm, reads=[(tag + "wg", s), (tag + "wu", s), (tag + "hT", tt)],
                          writes=[pk[b], pk[2 + b]])
                    cx.op("act", lambda e, b=b, pg=pg: e.activation(sg[b][:], pg[:], AF.Silu),
                          reads=[pk[b]], writes=[(tag + "sg", b)])
                    cx.op("dve", lambda e, b=b, pu=pu, j=j, tt=tt: e.tensor_tensor(
                        out=act[:, j, tt * TT:(tt + 1) * TT], in0=sg[b][:], in1=pu[:], op=ALU.mult),
                        reads=[(tag + "sg", b), pk[2 + b]], writes=[(tag + "act", j, tt)])
        for mb in range(D // WB):
            s = cnt["o"] % 2
            cnt["o"] += 1
            cx.dma("pool", wo[s][:], w_out_r[:, :, mb * WB:(mb + 1) * WB], writes=[(tag + "wo", s)],
                   skey=("wo", s))
            for mm_ in range(WB // 128):
                mc = mb * (WB // 128) + mm_
                for tt in range(NT):
                    gt = (t0 + tt * TT) // TT
                    b = cnt["x"] % 2
                    cnt["x"] += 1
                    py = ps[4 + b]
                    cx.dma("sp", xr[b][:], x_in[mc * 128:(mc + 1) * 128, t0 + tt * TT: t0 + (tt + 1) * TT],
                           reads=[(xkey_in, gt, mc)], writes=[(tag + "xr", b)], skey=("xr", b))

                    def mm2(e, s=s, mm_=mm_, tt=tt, py=py):
                        ins = None
                        for kc in range(JC):
                            ins = e.matmul(py[:], wo[s][:, kc, mm_ * 128:(mm_ + 1) * 128],
                                           act[:, kc, tt * TT:(tt + 1) * TT], start=(kc == 0), stop=(kc == JC - 1))
                        return ins
                    cx.op("pe", mm2, reads=[(tag + "wo", s)] + [(tag + "act", j, tt) for j in range(JC)],
                          writes=[pk[4 + b]])
                    cx.op("dve", lambda e, b=b, py=py: e.scalar_tensor_tensor(
                        out=xn[b][:], in0=py[:], scalar=0.5, in1=xr[b][:], op0=ALU.mult, op1=ALU.add),
                        reads=[pk[4 + b], (tag + "xr", b)], writes=[(tag + "xn", b)])
                    cx.dma("sp", x_out[mc * 128:(mc + 1) * 128, t0 + tt * TT: t0 + (tt + 1) * TT], xn[b][:],
                           reads=[(tag + "xn", b)], writes=[(xkey_out, gt, mc)], skey=("xn", b))
                if h + 1 < NH:
                    hk = mc
                    norm_tile_ops(h + 1, hk // 2, part=("A" if hk % 2 == 0 else "B"))
    cx.barrier()
    cx.pop()


def build_ffn_prog():
    nc = bass.Bass("TRN2", target_bir_lowering=False)
    x = nc.dram_tensor("x", [D, T], F32, kind="ExternalInput").ap()
    nw = nc.dram_tensor("nw", [128, KC], F32, kind="ExternalInput").ap()
    w_in = nc.dram_tensor("w_in", [D, 2 * DFF], F32, kind="ExternalInput").ap()
    w_out = nc.dram_tensor("w_out", [DFF, D], F32, kind="ExternalInput").ap()
    y = nc.dram_tensor("y", [D, T], F32, kind="ExternalOutput").ap()
    cx = Ctx(nc)
    c = consts(cx)
    ffn_phase(cx, c, "f_", x, y, nw, w_in, w_out, "xin", "xout")
    cx.finish()
    return nc


NH_A = 12
DIL = (1, 4, 16)
NEG = -30000.0
SCALE_A = 128 ** -0.5
C_IDENT = 0
C_PSWAP = 128
C_MASK = 256
C_TRI = 512
C_TRIS = 640
C_MINC = 768
C_MSTR = 896
C_LAST = 1024
CST_W = 1152


def host_consts():
    m = np.zeros((128, CST_W), np.float32)
    k = np.arange(128)[:, None]
    q = np.arange(128)[None, :]
    m[:, C_IDENT:C_IDENT + 128] = (k == q)
    m[:, C_PSWAP:C_PSWAP + 128] = (k == (q + 64) % 128)
    m[:, C_MASK:C_MASK + 128] = np.where(k <= q, 0.0, NEG)
    m[:, C_MASK + 128:C_MASK + 256] = np.where(k >= q, 0.0, NEG)
    m[:, C_TRI:C_TRI + 128] = (k <= q)
    m[:, C_TRIS:C_TRIS + 128] = (k > q)
    m[:, C_MINC:C_MINC + 128] = (k <= q)
    m[:, C_MSTR:C_MSTR + 128] = (k < q)
    m[:, C_LAST:C_LAST + 128] = (k == 127)
    return m


def host_rope():
    inv = (1.0 / (10000.0 ** (np.arange(0, 128, 2, dtype=np.float32) / np.float32(128)))).astype(np.float32)
    ang = (np.arange(T, dtype=np.float32)[None, :] * inv[:, None]).astype(np.float32)
    cs, sn = np.cos(ang).astype(np.float32), np.sin(ang).astype(np.float32)
    return np.concatenate([cs, cs], 0), np.concatenate([-sn, sn], 0)


def norm_stage(cx, c, tag, x_in, nw_d, hT, xkey_in, ps, pskey):
    cx.push()
    xt = [cx.sbuf(tag + "nxt%d" % i, [128, KC, TT], F32) for i in range(2)]
    sq = cx.sbuf(tag + "nsq", [128, KC, TT], BF16)
    rstd = cx.sbuf(tag + "nrstd", [128, TT], F32)
    nw = cx.sbuf(tag + "nnw", [128, KC], F32)
    cx.dma("sp", nw[:], nw_d, writes=[tag + "nw"], skey="nw")
    x_in_r = x_in.rearrange("(kc p) t -> p kc t", p=128)
    for tt in range(NTT):
        b = tt % 2
        cx.dma("sp", xt[b][:], x_in_r[:, :, tt * TT:(tt + 1) * TT],
               reads=[(xkey_in, tt, kc) for kc in range(KC)], writes=[(tag + "xt", b)], skey=("xt", b))
        rmsnorm_tile(cx, c, xt[b], nw, lambda kc, tt=tt: hT[:, kc, tt * TT:(tt + 1) * TT], tag,
                     [(tag + "xt", b)], (tag + "hT", tt), pskey, ps, {"sq": sq, "rstd": rstd})
    cx.barrier()
    cx.pop()


def attn_phase(cx, c, tag, x_in, os_d, nw_d, w_in, ctab_d, stab_d, cst_d, xkey_in, okey):
    nc = cx.nc
    cx.push()
    hT = cx.sbuf(tag + "hT", [128, KC, T], BF16)
    ctab = cx.sbuf(tag + "ctab", [128, T], F32)
    stab = cx.sbuf(tag + "stab", [128, T], F32)
    ident = cx.sbuf(tag + "ident", [128, 128], BF16)
    pswap = cx.sbuf(tag + "pswap", [128, 128], BF16)
    maskb = cx.sbuf(tag + "maskb", [128, 256], BF16)
    ps = [cx.psum(tag + "ps%d" % i, [128, TT], F32) for i in range(8)]
    pk = [tag + "ps%d" % i for i in range(8)]
    cx.psum_keys.update(pk)
    cx.dma("sp", ctab[:], ctab_d, writes=[tag + "ctab"], skey="ctab")
    cx.dma("sp", stab[:], stab_d, writes=[tag + "stab"], skey="stab")
    cx.dma("pool", ident[:], cst_d[:, C_IDENT:C_IDENT + 128], writes=[tag + "ident"], skey="ident")
    cx.dma("pool", pswap[:], cst_d[:, C_PSWAP:C_PSWAP + 128], writes=[tag + "pswap"], skey="pswap")
    cx.dma("pool", maskb[:], cst_d[:, C_MASK:C_MASK + 256], writes=[tag + "maskb"], skey="maskb")
    norm_stage(cx, c, tag, x_in, nw_d, hT, xkey_in, ps[6], pk[6])
    hkeys = []

    Qd = cx.sbuf(tag + "Qd", [128, T], BF16)
    Kd = cx.sbuf(tag + "Kd", [128, T], BF16)
    Vd = cx.sbuf(tag + "Vd", [128, 32, 128], BF16)
    numT = cx.sbuf(tag + "numT", [128, 3, T], BF16)
    dent = cx.sbuf(tag + "dent", [128, T], F32)
    wq = [cx.sbuf(tag + "wq%d" % i, [128, KC, 128], BF16) for i in range(2)]
    wk = [cx.sbuf(tag + "wk%d" % i, [128, KC, 128], BF16) for i in range(2)]
    wv = [cx.sbuf(tag + "wv%d" % i, [128, KC, 128], BF16) for i in range(2)]
    qb = [cx.sbuf(tag + "qb%d" % i, [128, TT], BF16) for i in range(3)]
    t1 = [cx.sbuf(tag + "t1%d" % i, [128, TT], F32) for i in range(3)]
    t2 = [cx.sbuf(tag + "t2%d" % i, [128, TT], F32) for i in range(3)]
    pt = [cx.sbuf(tag + "pt%d" % i, [128, 256], BF16) for i in range(4)]
    osc = [cx.sbuf(tag + "osc%d" % i, [128, TT], BF16) for i in range(2)]
    w_in_r = w_in.rearrange("(kc p) n -> p kc n", p=128)
    AW = NH_A * 128
    cnt = {"w": 0, "r": 0, "o": 0}

    def load_w(hd):
        s = cnt["w"] % 2
        cnt["w"] += 1
        for nm, wt, off in (("wq", wq, 0), ("wk", wk, AW), ("wv", wv, 2 * AW)):
            cx.dma("pool", wt[s][:], w_in_r[:, :, off + hd * 128: off + (hd + 1) * 128],
                   writes=[(tag + nm, s)], skey=(nm, s))
        return s

    for hg in range(4):
        for g in range(3):
            d = DIL[g]
            Ls = T // d
            nb = Ls // 128
            hd = g * 4 + hg
            s = load_w(hd)
            def rope_piece(which, wt, dst, dkey, tt, item, s=s, d=d):
                b = item % 3
                bP, bS = (0, 1, 4)[b], (2, 3, 5)[b]
                pp, sw = ps[bP], ps[bS]
                dst_v = dst[:, :].rearrange("p (r j) -> p j r", r=d)

                def mm(e):
                    ins = None
                    for kc in range(KC):
                        ins = e.matmul(pp[:], wt[s][:, kc, :], hT[:, kc, tt * TT:(tt + 1) * TT],
                                       start=(kc == 0), stop=(kc == KC - 1))
                    return ins
                cx.op("pe", mm, reads=[(tag + "w" + which, s)], writes=[pk[bP]])
                yield
                cx.op("act", lambda e: e.activation(qb[b][:], pp[:], AF.Copy),
                      reads=[pk[bP]], writes=[(tag + "qb", b)])
                yield "NEXT"
                cx.op("pe", lambda e: e.matmul(sw[:], pswap[:], qb[b][:], start=True, stop=True),
                      reads=[(tag + "qb", b), tag + "pswap"], writes=[pk[bS]])
                cx.op("dve", lambda e: e.tensor_tensor(
                    out=t1[b][:], in0=qb[b][:], in1=ctab[:, tt * TT:(tt + 1) * TT], op=ALU.mult),
                    reads=[(tag + "qb", b), tag + "ctab"], writes=[(tag + "t1", b)])
                yield
                cx.op("dve", lambda e: e.tensor_tensor(
                    out=t2[b][:], in0=sw[:], in1=stab[:, tt * TT:(tt + 1) * TT], op=ALU.mult),
                    reads=[pk[bS], tag + "stab"], writes=[(tag + "t2", b)])
                yield
                j0 = tt * TT // d
                cx.op("dve", lambda e: e.tensor_tensor(
                    out=dst_v[:, j0:j0 + TT // d, :],
                    in0=t1[b][:, :].rearrange("p (j r) -> p j r", r=d),
                    in1=t2[b][:, :].rearrange("p (j r) -> p j r", r=d), op=ALU.add),
                    reads=[(tag + "t1", b), (tag + "t2", b)], writes=[(dkey, tt)])
                yield

            pieces = [(w_, wt_, dst_, dk_, tt) for (w_, wt_, dst_, dk_) in
                      (("q", wq, Qd, tag + "Qd"), ("k", wk, Kd, tag + "Kd")) for tt in range(NTT)]
            act_g, nx = [], [0]

            def start_r():
                if nx[0] < len(pieces) and len(act_g) < 3:
                    w_, wt_, dst_, dk_, tt = pieces[nx[0]]
                    act_g.append(rope_piece(w_, wt_, dst_, dk_, tt, cnt["r"]))
                    cnt["r"] += 1
                    nx[0] += 1
            start_r()
            while act_g:
                for g_ in list(act_g):
                    try:
                        v = next(g_)
                    except StopIteration:
                        act_g.remove(g_)
                        start_r()
                        continue
                    if v == "NEXT" and g_ is act_g[-1]:
                        start_r()
            for b4 in range(8):
                b = cnt["r"] % 2
                cnt["r"] += 1
                pp = ps[b]

                def mmv(e, s=s, b4=b4, pp=pp, d=d, Ls=Ls):
                    ins = None
                    for i in range(4):
                        B = b4 * 4 + i
                        r, n = divmod(B, Ls // 128)
                        t0 = n * 128 * d + r
                        for kc in range(KC):
                            ins = e.matmul(pp[:, i * 128:(i + 1) * 128],
                                           hT[:, kc, t0: t0 + 127 * d + 1: d], wv[s][:, kc, :],
                                           start=(kc == 0), stop=(kc == KC - 1))
                    return ins
                cx.op("pe", mmv, reads=[(tag + "wv", s)], writes=[pk[b]])
                cx.op("act", lambda e, b4=b4, pp=pp: e.activation(
                    Vd[:, b4 * 4:(b4 + 1) * 4, :], pp[:, :].rearrange("p (i e) -> p i e", i=4), AF.Copy),
                    reads=[pk[b]], writes=[(tag + "Vd", b4)])
            qk_keys = [(tag + "Qd", tt) for tt in range(NTT)] + [(tag + "Kd", tt) for tt in range(NTT)]

            def scores(B, nb=nb):
                sb = (4, 5, 2, 3)[B % 4]
                nq = 256 if (B + 1) % nb != 0 else 128
                st = ps[sb]

                def mm(e, B=B, nq=nq, st=st):
                    e.matmul(st[:, 0:nq], Kd[:, B * 128:(B + 1) * 128], Qd[:, B * 128:B * 128 + nq],
                             start=True, stop=False)
                    return e.matmul(st[:, 0:nq], ident[:], maskb[:, 0:nq], start=False, stop=True)
                cx.op("pe", mm, reads=qk_keys + [tag + "ident", tag + "maskb"], writes=[pk[sb]])
                cx.op("act", lambda e, B=B, nq=nq, st=st: e.activation(
                    pt[B % 4][:, 0:nq], st[:, 0:nq], AF.Exp, scale=SCALE_A),
                    reads=[pk[sb]], writes=[(tag + "pt", B % 4)])

            def pv(B, nb=nb, g=g, d=d, Ls=Ls):
                first = (B % nb == 0)
                col = (B % 4) * 128
                bN, bD = ((6, 7), (0, 1))[(B // 4) % 2]

                def mm(e, B=B, first=first, col=col, bN=bN, bD=bD):
                    ins = None
                    for dst, lhs_fn in ((ps[bN], lambda blk: Vd[:, blk, :]), (ps[bD], lambda blk: c["ones_bf"][:])):
                        if not first:
                            e.matmul(dst[:, col:col + 128], lhs_fn(B - 1), pt[(B - 1) % 4][:, 128:256],
                                     start=True, stop=False)
                        ins = e.matmul(dst[:, col:col + 128], lhs_fn(B), pt[B % 4][:, 0:128],
                                       start=first, stop=True)
                    return ins
                rk = [(tag + "pt", B % 4), (tag + "Vd", B // 4), "ones_bf"]
                if not first:
                    rk += [(tag + "pt", (B - 1) % 4), (tag + "Vd", (B - 1) // 4)]
                cx.op("pe", mm, reads=rk, writes=[pk[bN], pk[bD]])
                if B % 4 == 3:
                    u0 = (B - 3) * 128
                    if d == 1:
                        def nat(ap2d):
                            return ap2d[:, u0:u0 + 512]
                        def src(p):
                            return p[:, :]
                    else:
                        r0, j0 = divmod(u0, Ls)
                        nr = max(1, 512 // Ls)
                        nj = 512 // nr
                        def nat(ap2d, r0=r0, j0=j0, nr=nr, nj=nj, d=d):
                            return ap2d.rearrange("p (j r) -> p r j", r=d)[:, r0:r0 + nr, j0:j0 + nj]
                        def src(p, nr=nr):
                            return p[:, :].rearrange("p (r j) -> p r j", r=nr)
                    cx.op("act", lambda e, nat=nat, src=src, g=g, bN=bN: e.activation(
                        nat(numT[:, g, :]), src(ps[bN]), AF.Copy),
                        reads=[pk[bN]], writes=[(tag + "numT", g)])
                    if g == 0:
                        cx.op("dve", lambda e, nat=nat, src=src, bD=bD: e.tensor_copy(nat(dent[:, :]), src(ps[bD])),
                              reads=[pk[bD]], writes=[tag + "dent"])
                    else:
                        cx.op("dve", lambda e, nat=nat, src=src, bD=bD: e.tensor_tensor(
                            out=nat(dent[:, :]), in0=nat(dent[:, :]), in1=src(ps[bD]), op=ALU.add),
                            reads=[pk[bD], tag + "dent"], writes=[tag + "dent"])

            scores(0)
            scores(1)
            for B in range(32):
                if B + 2 < 32:
                    scores(B + 2)
                pv(B)
        cx.op("dve", lambda e: e.reciprocal(dent[:, :], dent[:, :]), reads=[tag + "dent"], writes=[tag + "dent"])
        for g in range(3):
            hd = g * 4 + hg
            for tt in range(NTT):
                b = cnt["o"] % 2
                cnt["o"] += 1
                cx.op("dve", lambda e, b=b, g=g, tt=tt: e.tensor_tensor(
                    out=osc[b][:], in0=numT[:, g, tt * TT:(tt + 1) * TT], in1=dent[:, tt * TT:(tt + 1) * TT],
                    op=ALU.mult), reads=[tag + "dent", (tag + "numT", g)], writes=[(tag + "osc", b)])
                cx.dma("sp", os_d[hd * 128:(hd + 1) * 128, tt * TT:(tt + 1) * TT], osc[b][:],
                       reads=[(tag + "osc", b)], writes=[(okey, hd, tt)], skey=("osc", b))
    cx.barrier()
    cx.pop()


def outproj_phase(cx, c, tag, x_in, x_out, os_d, w_out, nheads, xkey_in, xkey_out, okey):
    cx.push()
    HT = 2048
    NT = HT // TT
    WB = 256
    osb = cx.sbuf(tag + "osb", [128, nheads, HT], BF16)
    wo = [cx.sbuf(tag + "wo%d" % i, [128, nheads, WB], BF16) for i in range(2)]
    xr = [cx.sbuf(tag + "xr%d" % i, [128, TT], F32) for i in range(2)]
    xn = [cx.sbuf(tag + "xn%d" % i, [128, TT], F32) for i in range(2)]
    ps = [cx.psum(tag + "ps%d" % i, [128, TT], F32) for i in range(2)]
    pk = [tag + "ps%d" % i for i in range(2)]
    cx.psum_keys.update(pk)
    w_out_r = w_out.rearrange("(kc p) n -> p kc n", p=128)
    os_r = os_d.rearrange("(h p) t -> p h t", p=128)
    cnt = {"o": 0, "x": 0}
    for h in range(T // HT):
        t0 = h * HT
        for hd in range(nheads):
            cx.dma("sp", osb[:, hd, :], os_r[:, hd, t0:t0 + HT],
                   reads=[(okey, hd, (t0 // TT) + i) for i in range(NT)], writes=[(tag + "osb", hd)], skey=("osb", hd))
        for mb in range(D // WB):
            s = cnt["o"] % 2
            cnt["o"] += 1
            cx.dma("pool", wo[s][:], w_out_r[:, :, mb * WB:(mb + 1) * WB], writes=[(tag + "wo", s)], skey=("wo", s))
            for mm_ in range(WB // 128):
                mc = mb * (WB // 128) + mm_
                for tt in range(NT):
                    gt = (t0 + tt * TT) // TT
                    b = cnt["x"] % 2
                    cnt["x"] += 1
                    py = ps[b]
                    cx.dma("sp", xr[b][:], x_in[mc * 128:(mc + 1) * 128, t0 + tt * TT: t0 + (tt + 1) * TT],
                           reads=[(xkey_in, gt, mc)], writes=[(tag + "xr", b)], skey=("xr", b))

                    def mm2(e, s=s, mm_=mm_, tt=tt, py=py):
                        ins = None
                        for kc in range(nheads):
                            ins = e.matmul(py[:], wo[s][:, kc, mm_ * 128:(mm_ + 1) * 128],
                                           osb[:, kc, tt * TT:(tt + 1) * TT], start=(kc == 0), stop=(kc == nheads - 1))
                        return ins
                    cx.op("pe", mm2, reads=[(tag + "wo", s)] + [(tag + "osb", hd) for hd in range(nheads)],
                          writes=[pk[b]])
                    cx.op("dve", lambda e, b=b, py=py: e.tensor_tensor(
                        out=xn[b][:], in0=py[:], in1=xr[b][:], op=ALU.add),
                        reads=[pk[b], (tag + "xr", b)], writes=[(tag + "xn", b)])
                    cx.dma("sp", x_out[mc * 128:(mc + 1) * 128, t0 + tt * TT: t0 + (tt + 1) * TT], xn[b][:],
                           reads=[(tag + "xn", b)], writes=[(xkey_out, gt, mc)], skey=("xn", b))
    cx.barrier()
    cx.pop()


def build_attn_prog():
    nc = bass.Bass("TRN2", target_bir_lowering=False)
    x = nc.dram_tensor("x", [D, T], F32, kind="ExternalInput").ap()
    nw = nc.dram_tensor("nw", [128, KC], F32, kind="ExternalInput").ap()
    w_in = nc.dram_tensor("w_in", [D, 3 * NH_A * 128], F32, kind="ExternalInput").ap()
    w_out = nc.dram_tensor("w_out", [NH_A * 128, D], F32, kind="ExternalInput").ap()
    ctab = nc.dram_tensor("ctab", [128, T], F32, kind="ExternalInput").ap()
    stab = nc.dram_tensor("stab", [128, T], F32, kind="ExternalInput").ap()
    cst = nc.dram_tensor("cst", [128, CST_W], F32, kind="ExternalInput").ap()
    os_d = nc.dram_tensor("os_scratch", [NH_A * 128, T], BF16).ap()
    y = nc.dram_tensor("y", [D, T], F32, kind="ExternalOutput").ap()
    cx = Ctx(nc)
    c = consts(cx)
    attn_phase(cx, c, "a_", x, os_d, nw, w_in, ctab, stab, cst, "xin", "os")
    outproj_phase(cx, c, "ao_", x, y, os_d, w_out, NH_A, "xin", "xout", "os")
    cx.finish()
    return nc


NVH = 16
NKH = 8
GP = 6176
NTILE = T // 128
DBG = {}


class _Stop(Exception):
    pass


def _chk(n):
    if DBG.get("stop", 99) <= n:
        raise _Stop()


def gdn_core_phase(*a):
    cx = a[0]
    depth = len(cx.scopes)
    try:
        _gdn_core_phase(*a)
    except _Stop:
        cx.barrier()
        while len(cx.scopes) > depth:
            cx.pop()


def _gdn_core_phase(cx, c, tag, x_in, oraw_d, nw_d, w_in, convw_d, alog_d, dtb_d, cst_d, xkey_in, okey):
    cx.push()
    hT = cx.sbuf(tag + "hT", [128, KC, T], BF16)
    ident = cx.sbuf(tag + "ident", [128, 128], BF16)
    id4 = cx.sbuf(tag + "id4", [128, 4, 128], BF16)
    minc4 = cx.sbuf(tag + "minc4", [128, 4, 128], BF16)
    mlow4 = cx.sbuf(tag + "mlow4", [128, 4, 128], BF16)
    cw = cx.sbuf(tag + "cw", [128, 32, 4], F32)
    beta = cx.sbuf(tag + "beta", [128, NTILE, NVH], F32)
    nbeta = cx.sbuf(tag + "nbeta", [128, NTILE, NVH], F32)
    gc = cx.sbuf(tag + "gc", [128, NTILE, NVH], F32)
    kbs = cx.sbuf(tag + "kbs", [128, NTILE, NVH], F32)
    egr = cx.sbuf(tag + "egr", [128, NTILE, NVH], F32)
    cd = cx.sbuf(tag + "cd", [128, NTILE, NVH], F32)
    gchi = cx.sbuf(tag + "gchi", [128, NTILE, NVH], BF16)
    gclo = cx.sbuf(tag + "gclo", [128, NTILE, NVH], BF16)
    ps = [cx.psum(tag + "ps%d" % i, [128, TT], F32) for i in range(8)]
    pk = [tag + "ps%d" % i for i in range(8)]
    cx.psum_keys.update(pk)
    K = lambda n: tag + n
    cx.push()
    alog = cx.sbuf(tag + "alog", [128, 512], F32)
    dtb = cx.sbuf(tag + "dtb", [128, 512], F32)
    gt = cx.sbuf(tag + "gt", [128, NTILE, NVH], F32)
    wab = cx.sbuf(tag + "wab", [128, KC, 32], BF16)
    ghi = cx.sbuf(tag + "ghi", [128, NTILE, NVH], BF16)
    glo = cx.sbuf(tag + "glo", [128, NTILE, NVH], BF16)
    trib = cx.sbuf(tag + "trib", [128, 128], BF16)
    trisb = cx.sbuf(tag + "trisb", [128, 128], BF16)
    cx.dma("pool", ident[:], cst_d[:, C_IDENT:C_IDENT + 128], writes=[K("ident")], skey="ident")
    for i in range(4):
        cx.dma("pool", id4[:, i, :], cst_d[:, C_IDENT:C_IDENT + 128], writes=[K("id4")], skey=("id4", i))
        cx.dma("pool", minc4[:, i, :], cst_d[:, C_MINC:C_MINC + 128], writes=[K("minc4")], skey=("minc4", i))
        cx.dma("pool", mlow4[:, i, :], cst_d[:, C_TRIS:C_TRIS + 128], writes=[K("mlow4")], skey=("mlow4", i))
    cx.dma("sp", cw[:], convw_d, writes=[K("cw")], skey="cw")
    cx.dma("sp", alog[:], alog_d, writes=[K("alog")], skey="alog")
    cx.dma("sp", dtb[:], dtb_d, writes=[K("dtb")], skey="dtb")
    w_in_r = w_in.rearrange("(kc p) n -> p kc n", p=128)
    cx.dma("pool", wab[:], w_in_r[:, :, 6144:6176], writes=[K("wab")], skey="wab")
    _chk(0)
    norm_stage(cx, c, tag, x_in, nw_d, hT, xkey_in, ps[6], pk[6])
    _chk(1)

    f2 = lambda t: t[:, :, :].rearrange("p c h -> p (c h)")
    for which in range(2):
        def mm(e, which=which):
            ins = None
            for ct in range(NTILE):
                for kc in range(KC):
                    ins = e.matmul(ps[which][:, ct * 16:(ct + 1) * 16], hT[:, kc, ct * 128:(ct + 1) * 128],
                                   wab[:, kc, which * 16:(which + 1) * 16], start=(kc == 0), stop=(kc == KC - 1))
            return ins
        cx.op("pe", mm, reads=[], writes=[pk[which]])
    _chk(1.05)
    cx.op("act", lambda e: e.activation(f2(beta), ps[0][:], AF.Sigmoid), reads=[pk[0]], writes=[K("beta")])
    _chk(1.1)
    cx.op("dve", lambda e: e.tensor_tensor(out=f2(gt), in0=ps[1][:], in1=dtb[:], op=ALU.add),
          reads=[pk[1], K("dtb")], writes=[K("gt")])
    _chk(1.2)
    cx.op("act", lambda e: e.activation(f2(gt), f2(gt), AF.Exp), reads=[K("gt")], writes=[K("gt")])
    _chk(1.4)
    cx.op("act", lambda e: e.activation(f2(gt), f2(gt), AF.Ln, bias=1.0), reads=[K("gt")], writes=[K("gt")])
    _chk(1.6)
    cx.op("act", lambda e: e.activation(alog[:], alog[:], AF.Exp), reads=[K("alog")], writes=[K("alog")])
    _chk(1.8)
    cx.op("dve", lambda e: e.scalar_tensor_tensor(out=f2(gt), in0=f2(gt), scalar=-1.0, in1=alog[:],
                                                   op0=ALU.mult, op1=ALU.mult),
          reads=[K("gt"), K("alog")], writes=[K("gt")])
    cx.op("dve", lambda e: e.tensor_scalar(f2(nbeta), f2(beta), -1.0, None, ALU.mult),
          reads=[K("beta")], writes=[K("nbeta")])
    _chk(2)

    cx.dma("pool", trib[:], cst_d[:, C_TRI:C_TRI + 128], writes=[K("trib")], skey="trib")
    cx.dma("pool", trisb[:], cst_d[:, C_TRIS:C_TRIS + 128], writes=[K("trisb")], skey="trisb")
    cx.op("dve", lambda e: e.tensor_copy(f2(ghi), f2(gt)), reads=[K("gt")], writes=[K("ghi")])
    cx.op("dve", lambda e: e.tensor_tensor(out=f2(glo), in0=f2(gt), in1=f2(ghi), op=ALU.subtract),
          reads=[K("gt"), K("ghi")], writes=[K("glo")])
    _chk(2.2)

    def mmg(e):
        ins = None
        for ct in range(NTILE):
            for dst, m in ((ps[2], trib), (ps[3], trisb), (ps[4], c["ones_bf"])):
                e.matmul(dst[:, ct * 16:(ct + 1) * 16], m[:], ghi[:, ct, :], start=True, stop=False)
                ins = e.matmul(dst[:, ct * 16:(ct + 1) * 16], m[:], glo[:, ct, :], start=False, stop=True)
        return ins
    cx.op("pe", mmg, reads=[K("ghi"), K("glo"), K("trib"), K("trisb"), "ones_bf"], writes=[pk[2], pk[3], pk[4]])
    _chk(2.4)
    cx.op("dve", lambda e: e.tensor_copy(f2(gchi), ps[2][:]), reads=[pk[2]], writes=[K("gchi")])
    cx.op("dve", lambda e: e.tensor_tensor(out=f2(gclo), in0=ps[2][:], in1=f2(gchi), op=ALU.subtract),
          reads=[pk[2], K("gchi")], writes=[K("gclo")])
    _chk(2.6)
    cx.op("dve", lambda e: e.tensor_copy(f2(gc), ps[2][:]), reads=[pk[2]], writes=[K("gc")])
    cx.op("act", lambda e: e.activation(f2(kbs), ps[2][:], AF.Exp), reads=[pk[2]], writes=[K("kbs")])
    _chk(2.7)
    cx.op("dve", lambda e: e.tensor_tensor(out=f2(kbs), in0=f2(kbs), in1=f2(beta), op=ALU.mult),
          reads=[K("kbs"), K("beta")], writes=[K("kbs")])
    _chk(2.8)
    cx.op("act", lambda e: e.activation(f2(egr), ps[3][:], AF.Exp), reads=[pk[3]], writes=[K("egr")])
    cx.op("act", lambda e: e.activation(f2(cd), ps[4][:], AF.Exp), reads=[pk[4]], writes=[K("cd")])
    _chk(2.9)
    cx.barrier()
    cx.pop()
    _chk(3)

    qT = cx.sbuf(tag + "qT", [128, 2, T], BF16)
    kT = cx.sbuf(tag + "kT", [128, 2, T], BF16)
    ktok = cx.sbuf(tag + "ktok", [128, NTILE, 2, 128], BF16)
    vtok = cx.sbuf(tag + "vtok", [128, NTILE, 4, 128], BF16)
    PW = 1024
    for hgp in range(DBG.get('hgps', 4)):
        cx.push()
        NB = 3
        wch = [cx.sbuf(tag + "wch%d" % i, [128, KC, 128], BF16) for i in range(2)]
        dg = [cx.sbuf(tag + "dg%d" % i, [128, 4, 128], BF16) for i in range(2)]
        xcb = [cx.sbuf(tag + "xcb%d" % i, [128, 4 + TT], BF16) for i in range(NB)]
        sil = [cx.sbuf(tag + "sil%d" % i, [128, TT], F32) for i in range(NB)]
        silb = [cx.sbuf(tag + "silb%d" % i, [128, TT], BF16) for i in range(NB)]
        sqb = [cx.sbuf(tag + "sqb%d" % i, [128, TT], BF16) for i in range(NB)]
        rn = [cx.sbuf(tag + "rn%d" % i, [128, TT], F32) for i in range(NB)]
        chunks = [("q", 0), ("q", 1), ("k", 0), ("k", 1), ("v", 0), ("v", 1), ("v", 2), ("v", 3)]

        def g1_piece(kind, li, gch, ws, tt, item):
            b = item % NB
            i2 = item % 2
            bP, bC, bS, bT = i2, 2 + i2, 4 + i2, 6 + i2
            XK = lambda i: (K("xcb"), i)
            if tt == 0:
                cx.op("act", lambda e: e.activation(xcb[b][:, 0:4], xcb[b][:, 0:4], AF.Copy, scale=0.0),
                      reads=[], writes=[(K("xh"), b)])

            def mm(e):
                ins = None
                for kc in range(KC):
                    ins = e.matmul(ps[bP][:], wch[ws][:, kc, :], hT[:, kc, tt * TT:(tt + 1) * TT],
                                   start=(kc == 0), stop=(kc == KC - 1))
                return ins
            cx.op("pe", mm, reads=[(K("wch"), ws)], writes=[pk[bP]])
            yield
            cx.op("act", lambda e: e.activation(xcb[b][:, 4:4 + TT], ps[bP][:], AF.Copy),
                  reads=[pk[bP]], writes=[XK(b)])
            if tt < NTT - 1:
                nb_ = (item + 1) % NB
                cx.op("act", lambda e: e.activation(xcb[nb_][:, 0:4], xcb[b][:, TT:TT + 4], AF.Copy),
                      reads=[XK(b)], writes=[(K("xh"), nb_)])
            yield "NEXT"

            def mmc(e):
                ins = None
                for j in range(4):
                    ins = e.matmul(ps[bC][:], dg[ws][:, j, :], xcb[b][:, 1 + j:1 + j + TT], start=(j == 0), stop=(j == 3))
                return ins
            cx.op("pe", mmc, reads=[XK(b), (K("xh"), b), (K("dg"), ws)], writes=[pk[bC]])
            yield
            tsl = slice(tt * TT, (tt + 1) * TT)
            if kind == "v":
                cx.op("act", lambda e: e.activation(silb[b][:], ps[bC][:], AF.Silu), reads=[pk[bC]], writes=[(K("silb"), b)])
                yield
            else:
                cx.op("act", lambda e: e.activation(sil[b][:], ps[bC][:], AF.Silu), reads=[pk[bC]], writes=[(K("sil"), b)])
                yield
                cx.op("dve", lambda e: e.tensor_tensor(out=sqb[b][:], in0=sil[b][:], in1=sil[b][:], op=ALU.mult),
                      reads=[(K("sil"), b)], writes=[(K("sqb"), b)])
                yield
                cx.op("pe", lambda e: e.matmul(ps[bS][:], c["ones_bf"][:], sqb[b][:], start=True, stop=True),
                      reads=[(K("sqb"), b), "ones_bf"], writes=[pk[bS]])
                yield
                cx.op("act", lambda e: e.activation(rn[b][:], ps[bS][:], AF.Sqrt, bias=1e-6, scale=1.0),
                      reads=[pk[bS]], writes=[(K("rn"), b)])
                yield
                cx.op("dve", lambda e: e.reciprocal(rn[b][:], rn[b][:]), reads=[(K("rn"), b)], writes=[(K("rn"), b)])
                dstT = qT if kind == "q" else kT
                sc = (128 ** -0.5) if kind == "q" else 1.0
                cx.op("dve", lambda e: e.scalar_tensor_tensor(
                    out=dstT[:, li, tsl], in0=sil[b][:], scalar=sc, in1=rn[b][:], op0=ALU.mult, op1=ALU.mult),
                    reads=[(K("sil"), b), (K("rn"), b)], writes=[(K(kind + "T"), li, tt)])
                yield
            if kind in ("k", "v"):
                ptv = ps[bT][:, :].bitcast(BF16)

                def mmt(e):
                    ins = None
                    for i in range(4):
                        if kind == "v":
                            src = silb[b][:, i * 128:(i + 1) * 128]
                        else:
                            src = kT[:, li, tt * TT + i * 128: tt * TT + (i + 1) * 128]
                        ins = e.transpose(ptv[:, i * 128:(i + 1) * 128], src, ident[:])
                    return ins
                rk = [(K("silb"), b)] if kind == "v" else [(K("kT"), li, tt)]
                cx.op("pe", mmt, reads=rk + [K("ident")], writes=[pk[bT]])
                yield
                dst = vtok if kind == "v" else ktok
                cx.op("act", lambda e: e.activation(
                    dst[:, tt * 4:tt * 4 + 4, li, :], ptv[:, 0:512].rearrange("p (i x) -> p i x", i=4), AF.Copy),
                    reads=[pk[bT]], writes=[(K(kind + "tok"), li, tt)])
                yield

        items = []
        wcnt = 0
        for kind, li in chunks:
            gch = {"q": 2 * hgp + li, "k": 8 + 2 * hgp + li, "v": 16 + 4 * hgp + li}[kind]
            ws = wcnt % 2
            wcnt += 1
            for tt in range(NTT):
                items.append((kind, li, gch, ws, tt))
        active, nxt_i = [], [0]

        def start1():
            while nxt_i[0] < len(items) and len(active) < NB:
                kind, li, gch, ws, tt = items[nxt_i[0]]
                if tt == 0:
                    cx.dma("pool", wch[ws][:], w_in_r[:, :, gch * 128:(gch + 1) * 128], writes=[(K("wch"), ws)],
                           skey=("wch", ws))
                    for j in range(4):
                        cx.op("dve", lambda e, ws=ws, gch=gch, j=j: e.tensor_scalar(
                            dg[ws][:, j, :], ident[:], cw[:, gch, j:j + 1], None, ALU.mult),
                            reads=[K("ident"), K("cw")], writes=[(K("dg"), ws)])
                active.append(g1_piece(kind, li, gch, ws, tt, nxt_i[0]))
                nxt_i[0] += 1
                return
        start1()
        while active:
            for g in list(active):
                try:
                    v = next(g)
                except StopIteration:
                    active.remove(g)
                    start1()
                    continue
                if v == "NEXT" and g is active[-1]:
                    start1()
        cx.barrier()
        cx.pop()
        _chk(5)

        cx.push()
        mk = lambda nm, dt_, n=2: [cx.sbuf(tag + nm + "%d" % i, [128, 4, 128], dt_) for i in range(n)]
        Abuf0, Abuf1, Atbuf0, Atbuf1 = mk("Aa", BF16), mk("Ab", BF16), mk("Ata", BF16), mk("Atb", BF16)
        A_O, AtO, Tt = mk("A_O", BF16), mk("AtO", BF16), mk("Tt", BF16)
        qkT, qdT, kd, vb, kb = mk("qkT", BF16), mk("qdT", BF16), mk("kd", BF16), mk("vb", BF16), mk("kb", BF16)
        dmT = cx.sbuf(tag + "dmT", [128, 4, 128], F32)
        dm2 = cx.sbuf(tag + "dm2", [128, 4, 128], F32)
        eB = cx.sbuf(tag + "eB", [128, 4, 128], F32)
        rr = cx.sbuf(tag + "rr", [128, 4, 128], F32)
        u = cx.sbuf(tag + "u", [128, 4, 128], F32)
        wT = cx.sbuf(tag + "wT", [128, 4, 128], BF16)
        vnew = cx.sbuf(tag + "vnew", [128, 4, 128], BF16)
        S32 = cx.sbuf(tag + "S32", [128, 4, 128], F32)
        S16 = cx.sbuf(tag + "S16", [128, 4, 128], BF16)
        ost = [cx.sbuf(tag + "ost%d" % i, [128, 4, 128], F32) for i in range(2)]
        fl = lambda t: t[:, :, :].rearrange("p h x -> p (h x)")
        cx.op("dve", lambda e: e.memset(fl(S32), 0.0), writes=[K("S32")])
        cx.op("dve", lambda e: e.memset(fl(S16), 0.0), writes=[K("S16")])
        hs = [4 * hgp + hl for hl in range(4)]
        B0, B1 = 0, 1

        def g2_tile(ct, hs, hgp=hgp):
            par = ct % 2
            bA, bAt, bTu = 2 + 3 * par, 3 + 3 * par, 4 + 3 * par
            tsl = slice(ct * 128, (ct + 1) * 128)
            P = lambda n: (K(n), par)
            AO, AtOp, Ttp = A_O[par], AtO[par], Tt[par]
            bufs = [(Abuf0[par], Atbuf0[par]), (Abuf1[par], Atbuf1[par]), (AO, AtOp)]

            def mm1(e):
                for hl in range(4):
                    e.matmul(ps[B0][:, hl * 128:(hl + 1) * 128], gchi[:, ct, hs[hl]:hs[hl] + 1].to_broadcast([128, 128]),
                             ident[:], start=True, stop=False)
                    e.matmul(ps[B0][:, hl * 128:(hl + 1) * 128], gclo[:, ct, hs[hl]:hs[hl] + 1].to_broadcast([128, 128]),
                             ident[:], start=False, stop=True)
                ins = None
                for khl in range(2):
                    e.matmul(ps[B1][:, khl * 128:(khl + 1) * 128], kT[:, khl, tsl], kT[:, khl, tsl], start=True, stop=True)
                    ins = e.matmul(ps[B1][:, (2 + khl) * 128:(3 + khl) * 128], kT[:, khl, tsl], qT[:, khl, tsl],
                                   start=True, stop=True)
                return ins
            cx.op("pe", mm1, reads=[K("ident")], writes=[pk[B0], pk[B1]])
            yield
            for hl in range(4):
                h = hs[hl]
                cx.op("dve", lambda e, hl=hl, h=h: e.tensor_scalar(
                    dmT[:, hl, :], ps[B0][:, hl * 128:(hl + 1) * 128], gc[:, ct, h:h + 1], 0.0, ALU.subtract, ALU.min),
                    reads=[pk[B0]], writes=[K("dmT")])
                cx.op("dve", lambda e, hl=hl, h=h: e.tensor_scalar(
                    dm2[:, hl, :], ps[B0][:, hl * 128:(hl + 1) * 128], gc[:, ct, h:h + 1], 0.0, ALU.subtract, ALU.max),
                    reads=[pk[B0]], writes=[K("dm2")])
                yield
            cx.op("act", lambda e: e.activation(fl(eB), ps[B0][:], AF.Exp), reads=[pk[B0]], writes=[K("eB")])
            cx.op("act", lambda e: e.activation(fl(dmT), fl(dmT), AF.Exp), reads=[K("dmT")], writes=[K("dmT")])
            cx.op("act", lambda e: e.activation(fl(dm2), fl(dm2), AF.Exp, scale=-1.0), reads=[K("dm2")], writes=[K("dm2")])
            yield
            cx.op("dve", lambda e: e.tensor_tensor(out=fl(dmT), in0=fl(dmT), in1=fl(minc4), op=ALU.mult),
                  reads=[K("dmT")], writes=[K("dmT")])
            cx.op("dve", lambda e: e.tensor_tensor(out=fl(dm2), in0=fl(dm2), in1=fl(mlow4), op=ALU.mult),
                  reads=[K("dm2")], writes=[K("dm2")])
            yield
            for hl in range(4):
                h = hs[hl]
                khl = hl // 2
                cx.op("dve", lambda e, hl=hl, h=h, khl=khl: e.scalar_tensor_tensor(
                    out=AO[:, hl, :], in0=ps[B1][:, khl * 128:(khl + 1) * 128], scalar=nbeta[:, ct, h:h + 1],
                    in1=dm2[:, hl, :], op0=ALU.mult, op1=ALU.mult), reads=[pk[B1], K("dm2")], writes=[(P("A"), 2)])
                cx.op("dve", lambda e, hl=hl, khl=khl: e.tensor_tensor(
                    out=qkT[par][:, hl, :], in0=ps[B1][:, (2 + khl) * 128:(3 + khl) * 128], in1=dmT[:, hl, :], op=ALU.mult),
                    reads=[pk[B1], K("dmT")], writes=[P("qkT")])
                cx.op("dve", lambda e, hl=hl, khl=khl: e.tensor_tensor(
                    out=qdT[par][:, hl, :], in0=qT[:, khl, tsl], in1=eB[:, hl, :], op=ALU.mult),
                    reads=[K("eB")], writes=[P("qdT")])
                cx.op("pool", lambda e, hl=hl, h=h: e.tensor_scalar(
                    vb[par][:, hl, :], vtok[:, ct, hl, :], beta[:, ct, h:h + 1], None, ALU.mult), writes=[P("vb")])
                cx.op("act", lambda e, hl=hl, h=h, khl=khl: e.activation(
                    kb[par][:, hl, :], ktok[:, ct, khl, :], AF.Copy, scale=kbs[:, ct, h:h + 1]), writes=[P("kb")])
                cx.op("pool", lambda e, hl=hl, h=h, khl=khl: e.tensor_scalar(
                    kd[par][:, hl, :], ktok[:, ct, khl, :], egr[:, ct, h:h + 1], None, ALU.mult), writes=[P("kd")])
                yield
            ptv = ps[bA][:, :].bitcast(BF16)

            def mmT(e):
                ins = None
                for hl in range(4):
                    ins = e.transpose(ptv[:, hl * 128:(hl + 1) * 128], AO[:, hl, :], ident[:])
                return ins
            cx.op("pe", mmT, reads=[(P("A"), 2), K("ident")], writes=[pk[bA]])
            yield
            cx.op("act", lambda e: e.activation(fl(AtOp), ptv[:, 0:512], AF.Copy),
                  reads=[pk[bA]], writes=[(P("At"), 2)])
            yield
            cx.op("dve", lambda e: e.tensor_tensor(out=fl(Ttp), in0=fl(AtOp), in1=fl(id4), op=ALU.add),
                  reads=[(P("At"), 2), K("id4")], writes=[P("Tt")])
            yield
            cur = 2
            NSTEP = 6
            for p in range(1, NSTEP + 1):
                nxt = 0 if cur == 2 else 1 - cur
                A, At = bufs[cur]
                An, Atn = bufs[nxt]
                last = (p == NSTEP)

                def mmsq(e, A=A, At=At, last=last):
                    ins = None
                    for hl in range(4):
                        ins = e.matmul(ps[bA][:, hl * 128:(hl + 1) * 128], At[:, hl, :], A[:, hl, :], start=True, stop=True)
                    if not last:
                        for hl in range(4):
                            ins = e.matmul(ps[bAt][:, hl * 128:(hl + 1) * 128], A[:, hl, :], At[:, hl, :],
                                           start=True, stop=True)
                    return ins
                cx.op("pe", mmsq, reads=[(P("A"), cur), (P("At"), cur)], writes=[pk[bA]] + ([] if last else [pk[bAt]]))
                yield
                cx.op("act", lambda e, An=An: e.activation(fl(An), ps[bA][:], AF.Copy),
                      reads=[pk[bA]], writes=[(P("A"), nxt)])
                if not last:
                    cx.op("dve", lambda e, Atn=Atn: e.tensor_copy(fl(Atn), ps[bAt][:]),
                          reads=[pk[bAt]], writes=[(P("At"), nxt)])
                yield

                def mmtu(e, An=An):
                    ins = None
                    for hl in range(4):
                        ins = e.matmul(ps[bTu][:, hl * 128:(hl + 1) * 128], An[:, hl, :], Ttp[:, hl, :], start=True, stop=True)
                    return ins
                cx.op("pe", mmtu, reads=[(P("A"), nxt), P("Tt")], writes=[pk[bTu]])
                yield
                cx.op("dve", lambda e: e.tensor_tensor(out=fl(Ttp), in0=fl(Ttp), in1=ps[bTu][:], op=ALU.add),
                      reads=[pk[bTu], P("Tt")], writes=[P("Tt")])
                cur = nxt
                if p == 3:
                    yield "MID"
                else:
                    yield
            T0, Rb = Atbuf0[par], Abuf0[par]
            ptv3 = ps[bAt][:, :].bitcast(BF16)

            def mmT0(e):
                ins = None
                for hl in range(4):
                    ins = e.transpose(ptv3[:, hl * 128:(hl + 1) * 128], Ttp[:, hl, :], ident[:])
                return ins
            cx.op("pe", mmT0, reads=[P("Tt"), K("ident")], writes=[pk[bAt]])
            yield
            cx.op("act", lambda e: e.activation(fl(T0), ptv3[:, 0:512], AF.Copy),
                  reads=[pk[bAt]], writes=[(P("At"), 0)])
            yield

            def mmR(e):
                ins = None
                for hl in range(4):
                    e.matmul(ps[bA][:, hl * 128:(hl + 1) * 128], AtOp[:, hl, :], T0[:, hl, :], start=True, stop=False)
                    ins = e.matmul(ps[bA][:, hl * 128:(hl + 1) * 128], ident[:], ident[:], start=False, stop=True)
                return ins
            cx.op("pe", mmR, reads=[(P("At"), 2), (P("At"), 0), K("ident")], writes=[pk[bA]])
            yield
            cx.op("dve", lambda e: e.scalar_tensor_tensor(out=fl(Rb), in0=fl(T0), scalar=-1.0, in1=ps[bA][:],
                                                           op0=ALU.mult, op1=ALU.add),
                  reads=[(P("At"), 0), pk[bA]], writes=[(P("A"), 0)])
            yield

            def mmU(e):
                ins = None
                for hl in range(4):
                    ins = e.matmul(ps[bTu][:, hl * 128:(hl + 1) * 128], Rb[:, hl, :], Ttp[:, hl, :], start=True, stop=True)
                return ins
            cx.op("pe", mmU, reads=[(P("A"), 0), P("Tt")], writes=[pk[bTu]])
            yield
            cx.op("dve", lambda e: e.tensor_tensor(out=fl(Ttp), in0=fl(Ttp), in1=ps[bTu][:], op=ALU.add),
                  reads=[pk[bTu], P("Tt")], writes=[P("Tt")])
            yield

            def mmuw(e):
                ins = None
                for hl in range(4):
                    e.matmul(ps[bA][:, hl * 128:(hl + 1) * 128], Ttp[:, hl, :], vb[par][:, hl, :], start=True, stop=True)
                    ins = e.matmul(ps[bAt][:, hl * 128:(hl + 1) * 128], kb[par][:, hl, :], Ttp[:, hl, :], start=True, stop=True)
                return ins
            cx.op("pe", mmuw, reads=[P("Tt"), P("vb"), P("kb")], writes=[pk[bA], pk[bAt]])
            yield
            cx.op("act", lambda e: e.activation(fl(u), ps[bA][:], AF.Copy), reads=[pk[bA]], writes=[K("u")])
            cx.op("act", lambda e: e.activation(fl(wT), ps[bAt][:], AF.Copy), reads=[pk[bAt]], writes=[K("wT")])
            yield

            def mmvn(e):
                ins = None
                for hl in range(4):
                    ins = e.matmul(ps[bTu][:, hl * 128:(hl + 1) * 128], wT[:, hl, :], S16[:, hl, :], start=True, stop=True)
                return ins
            cx.op("pe", mmvn, reads=[K("wT"), K("S16")], writes=[pk[bTu]])
            yield
            cx.op("dve", lambda e: e.tensor_tensor(out=fl(vnew), in0=fl(u), in1=ps[bTu][:], op=ALU.subtract),
                  reads=[K("u"), pk[bTu]], writes=[K("vnew")])
            yield

            def mmo(e):
                ins = None
                for hl in range(4):
                    e.matmul(ps[bA][:, hl * 128:(hl + 1) * 128], S16[:, hl, :], qdT[par][:, hl, :], start=True, stop=False)
                    e.matmul(ps[bA][:, hl * 128:(hl + 1) * 128], vnew[:, hl, :], qkT[par][:, hl, :], start=False, stop=True)
                for hl in range(4):
                    ins = e.matmul(ps[bAt][:, hl * 128:(hl + 1) * 128], kd[par][:, hl, :], vnew[:, hl, :], start=True, stop=True)
                return ins
            cx.op("pe", mmo, reads=[K("S16"), P("qdT"), K("vnew"), P("qkT"), P("kd")], writes=[pk[bA], pk[bAt]])
            yield
            ob = ct % 2
            for hl in range(4):
                h = hs[hl]
                cx.op("dve", lambda e, hl=hl, h=h: e.scalar_tensor_tensor(
                    out=S32[:, hl, :], in0=S32[:, hl, :], scalar=cd[:, ct, h:h + 1], in1=ps[bAt][:, hl * 128:(hl + 1) * 128],
                    op0=ALU.mult, op1=ALU.add), reads=[pk[bAt], K("S32")], writes=[K("S32")])
            cx.op("act", lambda e: e.activation(fl(S16), fl(S32), AF.Copy), reads=[K("S32")], writes=[K("S16")])
            cx.op("act", lambda e: e.activation(fl(ost[ob]), ps[bA][:], AF.Copy),
                  reads=[pk[bA]], writes=[(K("ost"), ob)])
            for hl in range(4):
                h = hs[hl]
                cx.dma("sp", oraw_d[h * 128:(h + 1) * 128, ct * 128:(ct + 1) * 128], ost[ob][:, hl, :],
                       reads=[(K("ost"), ob)], writes=[(okey, h, ct)], skey=("ost", ob, hl))
            yield

        ntile = DBG.get('tiles', NTILE)
        active, nxt_t = [], [0]

        def start():
            if nxt_t[0] < ntile and len(active) < 2:
                active.append(g2_tile(nxt_t[0], list(hs)))
                nxt_t[0] += 1
        start()
        while active:
            for g in list(active):
                try:
                    v = next(g)
                except StopIteration:
                    active.remove(g)
                    start()
                    continue
                if v == "MID":
                    start()
        cx.barrier()
        cx.pop()
    cx.barrier()
    cx.pop()


def gdn_out_phase(cx, c, tag, x_in, x_out, oraw_d, nw_d, gnw_d, w_in, w_out, xkey_in, xkey_out, okey):
    cx.push()
    HT = 2048
    NT = HT // TT
    WB = 256
    nheads = NVH
    K = lambda n: tag + n
    hT = cx.sbuf(tag + "hT", [128, KC, HT], BF16)
    osb = cx.sbuf(tag + "osb", [128, nheads, HT], BF16)
    gnw = cx.sbuf(tag + "gnw", [128, 1], F32)
    nw = cx.sbuf(tag + "nw", [128, KC], F32)
    xt = [cx.sbuf(tag + "xt%d" % i, [128, KC, TT], F32) for i in range(1)]
    sq = cx.sbuf(tag + "sq", [128, KC, TT], BF16)
    rstd = cx.sbuf(tag + "rstd", [128, TT], F32)
    wz = [cx.sbuf(tag + "wz%d" % i, [128, KC, 128], BF16) for i in range(2)]
    orw = [cx.sbuf(tag + "orw%d" % i, [128, TT], F32) for i in range(3)]
    osq = [cx.sbuf(tag + "osq%d" % i, [128, TT], BF16) for i in range(3)]
    orn = [cx.sbuf(tag + "orn%d" % i, [128, TT], F32) for i in range(3)]
    sz = [cx.sbuf(tag + "sz%d" % i, [128, TT], F32) for i in range(3)]
    wo = [cx.sbuf(tag + "wo%d" % i, [128, nheads, WB], BF16) for i in range(2)]
    xr = [cx.sbuf(tag + "xr%d" % i, [128, TT], F32) for i in range(2)]
    xn = [cx.sbuf(tag + "xn%d" % i, [128, TT], F32) for i in range(2)]
    ps = [cx.psum(tag + "ps%d" % i, [128, TT], F32) for i in range(8)]
    pk = [tag + "ps%d" % i for i in range(8)]
    cx.psum_keys.update(pk)
    w_in_r = w_in.rearrange("(kc p) n -> p kc n", p=128)
    w_out_r = w_out.rearrange("(kc p) n -> p kc n", p=128)
    x_in_r = x_in.rearrange("(kc p) t -> p kc t", p=128)
    cx.dma("sp", nw[:], nw_d, writes=[K("nw")], skey="nw")
    cx.dma("sp", gnw[:], gnw_d, writes=[K("gnw")], skey="gnw")
    cnt = {"o": 0, "x": 0, "z": 0, "w": 0}
    def norm_tile_ops(hf_, tt, part=None):
        t0_ = hf_ * HT
        gt_ = (t0_ + tt * TT) // TT
        if part in (None, "A"):
            cx.dma("sp", xt[0][:], x_in_r[:, :, t0_ + tt * TT: t0_ + (tt + 1) * TT],
                   reads=[(xkey_in, gt_, kc) for kc in range(KC)], writes=[K("xt")], skey="xt")
        rmsnorm_tile(cx, c, xt[0], nw, lambda kc, tt=tt: hT[:, kc, tt * TT:(tt + 1) * TT], tag,
                     [K("xt")], (K("hT"), tt), pk[5], ps[5], {"sq": sq, "rstd": rstd}, part=part)

    for hf in range(T // HT):
        t0 = hf * HT
        if hf == 0:
            for tt in range(NT):
                norm_tile_ops(0, tt)
        def gate_piece(h, ws, tt, item):
            b = item % 3
            bZ, bQ = b, 3 + b
            gt_ = (t0 + tt * TT) // TT
            cx.dma("sp", orw[b][:], oraw_d[h * 128:(h + 1) * 128, t0 + tt * TT: t0 + (tt + 1) * TT],
                   reads=[(okey, h, gt_)], writes=[(K("orw"), b)], skey=("orw", b))

            def mmz(e):
                ins = None
                for kc in range(KC):
                    ins = e.matmul(ps[bZ][:], wz[ws][:, kc, :], hT[:, kc, tt * TT:(tt + 1) * TT],
                                   start=(kc == 0), stop=(kc == KC - 1))
                return ins
            cx.op("pe", mmz, reads=[(K("wz"), ws), (K("hT"), tt)], writes=[pk[bZ]])
            yield
            cx.op("act", lambda e: e.activation(osq[b][:], orw[b][:], AF.Square),
                  reads=[(K("orw"), b)], writes=[(K("osq"), b)])
            yield "NEXT"
            cx.op("pe", lambda e: e.matmul(ps[bQ][:], c["ones_bf"][:], osq[b][:], start=True, stop=True),
                  reads=[(K("osq"), b), "ones_bf"], writes=[pk[bQ]])
            yield
            cx.op("act", lambda e: e.activation(orn[b][:], ps[bQ][:], AF.Sqrt, bias=EPS, scale=1.0 / 128),
                  reads=[pk[bQ]], writes=[(K("orn"), b)])
            cx.op("act", lambda e: e.activation(sz[b][:], ps[bZ][:], AF.Silu), reads=[pk[bZ]],
                  writes=[(K("sz"), b)])
            yield
            cx.op("dve", lambda e: e.reciprocal(orn[b][:], orn[b][:]), reads=[(K("orn"), b)],
                  writes=[(K("orn"), b)])
            yield
            cx.op("dve", lambda e: e.scalar_tensor_tensor(
                out=orn[b][:], in0=orw[b][:], scalar=gnw[:, 0:1], in1=orn[b][:], op0=ALU.mult, op1=ALU.mult),
                reads=[(K("orw"), b), (K("orn"), b), K("gnw")], writes=[(K("orn"), b)])
            yield
            cx.op("dve", lambda e: e.tensor_tensor(
                out=osb[:, h, tt * TT:(tt + 1) * TT], in0=orn[b][:], in1=sz[b][:], op=ALU.mult),
                reads=[(K("orn"), b), (K("sz"), b)], writes=[(K("osb"), h)])
            yield

        gitems = [(h, tt) for h in range(nheads) for tt in range(NT)]
        act_g, nx = [], [0]

        def start_g():
            if nx[0] < len(gitems) and len(act_g) < 3:
                h, tt = gitems[nx[0]]
                if tt == 0:
                    ws = cnt["w"] % 2
                    cnt["w"] += 1
                    cx.dma("pool", wz[ws][:], w_in_r[:, :, 4096 + h * 128: 4096 + (h + 1) * 128],
                           writes=[(K("wz"), ws)], skey=("wz", ws))
                ws = (cnt["w"] - 1) % 2
                act_g.append(gate_piece(h, ws, tt, cnt["z"]))
                cnt["z"] += 1
                nx[0] += 1
        start_g()
        while act_g:
            for g_ in list(act_g):
                try:
                    v = next(g_)
                except StopIteration:
                    act_g.remove(g_)
                    start_g()
                    continue
                if v == "NEXT" and g_ is act_g[-1]:
                    start_g()
        for mb in range(D // WB):
            s = cnt["o"] % 2
            cnt["o"] += 1
            cx.dma("pool", wo[s][:], w_out_r[:, :, mb * WB:(mb + 1) * WB], writes=[(K("wo"), s)], skey=("wo", s))
            for mm_ in range(WB // 128):
                mc = mb * (WB // 128) + mm_
                for tt in range(NT):
                    gt_ = (t0 + tt * TT) // TT
                    b = cnt["x"] % 2
                    cnt["x"] += 1
                    py = ps[6 + b]
                    cx.dma("sp", xr[b][:], x_in[mc * 128:(mc + 1) * 128, t0 + tt * TT: t0 + (tt + 1) * TT],
                           reads=[(xkey_in, gt_, mc)], writes=[(K("xr"), b)], skey=("xr", b))

                    def mm2(e, s=s, mm_=mm_, tt=tt, py=py):
                        ins = None
                        for kc in range(nheads):
                            ins = e.matmul(py[:], wo[s][:, kc, mm_ * 128:(mm_ + 1) * 128],
                                           osb[:, kc, tt * TT:(tt + 1) * TT], start=(kc == 0), stop=(kc == nheads - 1))
                        return ins
                    cx.op("pe", mm2, reads=[(K("wo"), s)] + [(K("osb"), hd) for hd in range(nheads)],
                          writes=[pk[6 + b]])
                    cx.op("dve", lambda e, b=b, py=py: e.tensor_tensor(
                        out=xn[b][:], in0=py[:], in1=xr[b][:], op=ALU.add),
                        reads=[pk[6 + b], (K("xr"), b)], writes=[(K("xn"), b)])
                    cx.dma("sp", x_out[mc * 128:(mc + 1) * 128, t0 + tt * TT: t0 + (tt + 1) * TT], xn[b][:],
                           reads=[(K("xn"), b)], writes=[(xkey_out, gt_, mc)], skey=("xn", b))
                if hf + 1 < T // HT:
                    norm_tile_ops(hf + 1, mc // 2, part=("A" if mc % 2 == 0 else "B"))
    cx.barrier()
    cx.pop()


def build_gdn_prog():
    nc = bass.Bass("TRN2", target_bir_lowering=False)
    x = nc.dram_tensor("x", [D, T], F32, kind="ExternalInput").ap()
    nw = nc.dram_tensor("nw", [128, KC], F32, kind="ExternalInput").ap()
    w_in = nc.dram_tensor("w_in", [D, GP], F32, kind="ExternalInput").ap()
    w_out = nc.dram_tensor("w_out", [NVH * 128, D], F32, kind="ExternalInput").ap()
    convw = nc.dram_tensor("convw", [128, 32, 4], F32, kind="ExternalInput").ap()
    alog = nc.dram_tensor("alog", [128, 512], F32, kind="ExternalInput").ap()
    dtb = nc.dram_tensor("dtb", [128, 512], F32, kind="ExternalInput").ap()
    gnw = nc.dram_tensor("gnw", [128, 1], F32, kind="ExternalInput").ap()
    cst = nc.dram_tensor("cst", [128, CST_W], F32, kind="ExternalInput").ap()
    oraw = nc.dram_tensor("oraw_scratch", [NVH * 128, T], F32, kind=("ExternalOutput" if DBG.get("noout") else "Internal")).ap()
    y = nc.dram_tensor("y", [D, T], F32, kind="ExternalOutput").ap()
    cx = Ctx(nc)
    c = consts(cx)
    gdn_core_phase(cx, c, "g_", x, oraw, nw, w_in, convw, alog, dtb, cst, "xin", "or")
    if not DBG.get("noout"):
        gdn_out_phase(cx, c, "go_", x, y, oraw, nw, gnw, w_in, w_out, "xin", "xout", "or")
    cx.finish()
    return nc


def final_norm_phase(cx, c, tag, x_in, y_out, nw_d, xkey_in):
    cx.push()
    xt = [cx.sbuf(tag + "xt%d" % i, [128, KC, TT], F32) for i in range(2)]
    yo = [cx.sbuf(tag + "yo%d" % i, [128, KC, TT], F32) for i in range(2)]
    sq = cx.sbuf(tag + "sq", [128, KC, TT], BF16)
    rstd = cx.sbuf(tag + "rstd", [128, TT], F32)
    nw = cx.sbuf(tag + "nw", [128, KC], F32)
    ps = cx.psum(tag + "ps", [128, TT], F32)
    cx.psum_keys.add(tag + "ps")
    cx.dma("sp", nw[:], nw_d, writes=[tag + "nw"], skey="nw")
    x_in_r = x_in.rearrange("(kc p) t -> p kc t", p=128)
    y_out_r = y_out.rearrange("(kc p) t -> p kc t", p=128)
    for tt in range(NTT):
        b = tt % 2
        cx.dma("sp", xt[b][:], x_in_r[:, :, tt * TT:(tt + 1) * TT],
               reads=[(xkey_in, tt, kc) for kc in range(KC)], writes=[(tag + "xt", b)], skey=("xt", b))
        rmsnorm_tile(cx, c, xt[b], nw, lambda kc, b=b: yo[b][:, kc, :], tag,
                     [(tag + "xt", b)], (tag + "yo", b), tag + "ps", ps, {"sq": sq, "rstd": rstd})
        cx.dma("sp", y_out_r[:, :, tt * TT:(tt + 1) * TT], yo[b][:], reads=[(tag + "yo", b)],
               writes=[("yout", tt)], skey=("yo", b))
    cx.barrier()
    cx.pop()


def build_fin_prog():
    nc = bass.Bass("TRN2", target_bir_lowering=False)
    x = nc.dram_tensor("x", [D, T], F32, kind="ExternalInput").ap()
    nw = nc.dram_tensor("nw", [128, KC], F32, kind="ExternalInput").ap()
    y = nc.dram_tensor("y", [D, T], F32, kind="ExternalOutput").ap()
    cx = Ctx(nc)
    c = consts(cx)
    final_norm_phase(cx, c, "n_", x, y, nw, "xin")
    cx.finish()
    return nc


def build_full_prog():
    nc = bass.Bass("TRN2", target_bir_lowering=False)
    dt = lambda name, shape, kind="ExternalInput", dtype=F32: nc.dram_tensor(name, list(shape), dtype, kind=kind).ap()
    x = dt("x", [D, T])
    nws = dt("nws", [13, 128, KC])
    ffn_w_in = dt("ffn_w_in", [4, 2, D, 2 * DFF])
    ffn_w_out = dt("ffn_w_out", [4, 2, DFF, D])
    attn_w_in = dt("attn_w_in", [2, D, 3 * NH_A * 128])
    attn_w_out = dt("attn_w_out", [2, NH_A * 128, D])
    gdn_w_in = dt("gdn_w_in", [2, D, GP])
    gdn_w_out = dt("gdn_w_out", [2, NVH * 128, D])
    convw = dt("convw", [2, 128, 32, 4])
    alog = dt("alog", [2, 128, 512])
    dtb = dt("dtb", [2, 128, 512])
    gnw = dt("gnw", [2, 128, 1])
    ctab = dt("ctab", [128, T])
    stab = dt("stab", [128, T])
    cst = dt("cst", [128, CST_W])
    y = dt("y", [D, T], kind="ExternalOutput")
    xbuf = [nc.dram_tensor("xres%d" % i, [D, T], F32).ap() for i in range(2)]
    os_d = nc.dram_tensor("os_scratch", [NH_A * 128, T], BF16).ap()
    oraw = nc.dram_tensor("oraw_scratch", [NVH * 128, T], F32).ap()
    cx = Ctx(nc)
    c = consts(cx)
    cur, nb_, ph = x, 0, 0
    ia = ib = 0
    for i in range(4):
        ffn_phase(cx, c, "f%da_" % i, cur, xbuf[nb_], nws[ph], ffn_w_in[i, 0], ffn_w_out[i, 0], "xi", "xo")
        cur, nb_, ph = xbuf[nb_], 1 - nb_, ph + 1
        if i % 2 == 0:
            attn_phase(cx, c, "a%d_" % i, cur, os_d, nws[ph], attn_w_in[ia], ctab, stab, cst, "xi", "os")
            outproj_phase(cx, c, "ao%d_" % i, cur, xbuf[nb_], os_d, attn_w_out[ia], NH_A, "xi", "xo", "os")
            ia += 1
        else:
            gdn_core_phase(cx, c, "g%d_" % i, cur, oraw, nws[ph], gdn_w_in[ib], convw[ib], alog[ib], dtb[ib], cst, "xi", "or")
            gdn_out_phase(cx, c, "go%d_" % i, cur, xbuf[nb_], oraw, nws[ph], gnw[ib], gdn_w_in[ib], gdn_w_out[ib],
                          "xi", "xo", "or")
            ib += 1
        cur, nb_, ph = xbuf[nb_], 1 - nb_, ph + 1
        ffn_phase(cx, c, "f%db_" % i, cur, xbuf[nb_], nws[ph], ffn_w_in[i, 1], ffn_w_out[i, 1], "xi", "xo")
        cur, nb_, ph = xbuf[nb_], 1 - nb_, ph + 1
    final_norm_phase(cx, c, "n_", cur, y, nws[ph], "xi")
    cx.finish()
    return nc


_PROGS = {}


def _prog(name):
    if name not in _PROGS:
        _PROGS[name] = {"ffn": build_ffn_prog, "attn": build_attn_prog, "gdn": build_gdn_prog,
                        "fin": build_fin_prog}[name]()
    return _PROGS[name]


def _lay_nw(v):
    return np.ascontiguousarray(np.asarray(v, np.float32).reshape(KC, 128).T)


def _launch(name, xs, shared):
    nc = _prog(name)
    n = len(xs)
    in_maps = [dict(shared, x=xs[i]) for i in range(n)]
    res = run_bass_kernel_spmd(nc, in_maps, core_ids=list(range(n)))
    return [np.asarray(res.results[i]["y"]) for i in range(n)]


def kernel(x, norm_w, ffn_w_in, ffn_w_out, attn_w_in, attn_w_out, gdn_w_in, gdn_conv_w, gdn_a_log,
           gdn_dt_bias, gdn_norm_w, gdn_w_out, final_norm_w):
    f = lambda a: np.ascontiguousarray(np.asarray(a, np.float32))
    x = f(x)
    B = x.shape[0]
    ctab, stab = host_rope()
    nws = np.stack([_lay_nw(norm_w[i, j]) for i in range(4) for j in range(3)] + [_lay_nw(final_norm_w)], 0)
    shared = {
        "nws": np.ascontiguousarray(nws), "ffn_w_in": f(ffn_w_in), "ffn_w_out": f(ffn_w_out),
        "attn_w_in": f(attn_w_in), "attn_w_out": f(attn_w_out), "gdn_w_in": f(gdn_w_in), "gdn_w_out": f(gdn_w_out),
        "convw": np.ascontiguousarray(f(gdn_conv_w).reshape(2, 4, 32, 128).transpose(0, 3, 2, 1)),
        "alog": np.ascontiguousarray(np.tile(f(gdn_a_log)[:, None, :], (1, 128, 32))),
        "dtb": np.ascontiguousarray(np.tile(f(gdn_dt_bias)[:, None, :], (1, 128, 32))),
        "gnw": np.ascontiguousarray(f(gdn_norm_w).reshape(2, 128, 1)),
        "ctab": ctab, "stab": stab, "cst": host_consts(),
    }
    if "full" not in _PROGS:
        _PROGS["full"] = build_full_prog()
    nc = _PROGS["full"]
    in_maps = [dict(shared, x=np.ascontiguousarray(x[b].T)) for b in range(B)]
    res = run_bass_kernel_spmd(nc, in_maps, core_ids=list(range(B)))
    return np.stack([np.asarray(res.results[b]["y"]).T for b in range(B)], 0).astype(np.float32)
```

```python
from contextlib import ExitStack
import numpy as np
import concourse.bass as bass
import concourse.mybir as mybir
from concourse.bass_utils import run_bass_kernel_spmd

F32 = mybir.dt.float32
BF16 = mybir.dt.bfloat16
ALU = mybir.AluOpType
AF = mybir.ActivationFunctionType

ENGS = ("pe", "act", "dve", "pool", "sp")
NAMES = []


class _Op:
    __slots__ = ("eng", "fn", "deps", "idx", "is_dma", "dsem", "dval", "need_inc", "cnt")

    def __init__(self, eng, fn, is_dma):
        self.eng = eng
        self.fn = fn
        self.deps = []
        self.is_dma = is_dma
        self.dsem = None
        self.dval = 0
        self.need_inc = False
        self.cnt = 0


class Ctx:
    def __init__(self, nc):
        self.nc = nc
        self.q = {e: [] for e in ENGS}
        self.wr = {}
        self.rd = {}
        self.dsems = {}
        self.pending_dma = []
        self.sem_pool = {}
        self.psum_keys = set()
        self.stack = ExitStack()
        self.scopes = []
        self.esem = {e: nc.alloc_semaphore("sem_" + e) for e in ENGS}
        self.n_psum = 0

    def push(self):
        st = ExitStack()
        self.scopes.append(st)

    def pop(self):
        self.scopes.pop().close()

    def _scope(self):
        return self.scopes[-1] if self.scopes else self.stack

    def sbuf(self, name, shape, dtype):
        self.n_psum += 1
        NAMES.append("%s_%d" % (name, self.n_psum))
        return self._scope().enter_context(self.nc.sbuf_tensor("%s_%d" % (name, self.n_psum), list(shape), dtype))

    def psum(self, name, shape, dtype=F32):
        self.n_psum += 1
        return self._scope().enter_context(self.nc.psum_tensor("%s_%d" % (name, self.n_psum), list(shape), dtype))

    def _track(self, op, reads, writes):
        deps = op.deps
        for k in reads:
            for o in self.wr.get(k, {}).values():
                deps.append(o)
            if k in self.psum_keys:
                for ek2, o in self.rd.get(k, {}).items():
                    if o.eng != op.eng:
                        deps.append(o)
        for k in writes:
            for o in self.wr.get(k, {}).values():
                deps.append(o)
            for o in self.rd.get(k, {}).values():
                deps.append(o)
        ek = id(op) if op.is_dma else op.eng
        for k in reads:
            self.rd.setdefault(k, {})[ek] = op
        for k in writes:
            self.wr[k] = {ek: op}
            self.rd[k] = {}

    def op(self, eng, fn, reads=(), writes=()):
        o = _Op(eng, fn, False)
        self._track(o, reads, writes)
        o.idx = len(self.q[eng])
        self.q[eng].append(o)
        return o

    def dma(self, eng, out, in_, reads=(), writes=(), skey=None, **kw):
        key = ("dma", skey)
        o = _Op(eng, lambda e: e.dma_start(out=out, in_=in_, **kw), True)
        self._track(o, reads, writes)
        key = (eng == "pool", key)
        if key not in self.dsems:
            pool = self.sem_pool.setdefault(eng == "pool", [])
            i = sum(1 for k in self.dsems if k[0] == key[0])
            if i >= len(pool):
                pool.append([self.nc.alloc_semaphore("d%s%d" % ("s" if key[0] else "h", i)), 0])
            self.dsems[key] = pool[i]
        ds = self.dsems[key]
        ds[1] += 16
        o.dsem = ds[0]
        o.dval = ds[1]
        o.idx = len(self.q[eng])
        self.q[eng].append(o)
        self.pending_dma.append(o)
        return o

    def barrier(self):
        lasts = [self.q[e][-1] for e in ENGS if self.q[e] and not self.q[e][-1].is_dma]
        lasts += [o for e in ENGS for o in self.q[e][-1:] if False]
        dm = list(self.pending_dma)
        self.pending_dma = []
        for e in ENGS:
            o = _Op(e, None, False)
            for e2 in ENGS:
                for p in reversed(self.q[e2]):
                    if not p.is_dma and p.fn is not None:
                        o.deps.append(p)
                        break
            o.deps.extend(dm)
            o.idx = len(self.q[e])
            self.q[e].append(o)
        self.wr = {}
        self.rd = {}
        self.dsems = {}

    def finish(self):
        nc = self.nc
        self.barrier()
        for e in ENGS:
            for o in self.q[e]:
                for d in o.deps:
                    if not d.is_dma and (d.eng != o.eng or e != "pe"):
                        d.need_inc = True
        for e in ENGS:
            c = 0
            for o in self.q[e]:
                if o.need_inc:
                    c += 1
                o.cnt = c
        engobj = {"pe": "tensor", "act": "scalar", "dve": "vector", "pool": "gpsimd", "sp": "sync"}
        with nc.Block() as block:
            for e in ENGS:
                def body(eng, e=e):
                    known = {}
                    for o in self.q[e]:
                        waits = {}
                        for d in o.deps:
                            if d.is_dma:
                                s, v = d.dsem, d.dval
                            else:
                                if d.eng == e and e == "pe":
                                    continue
                                s, v = self.esem[d.eng], d.cnt
                            if v > waits.get(s, (None, 0))[1]:
                                waits[s] = (s, v)
                        for s, v in waits.values():
                            if known.get(s, 0) >= v:
                                continue
                            known[s] = v
                            eng.wait_ge(s, v)
                        if o.fn is None:
                            continue
                        ins = o.fn(eng)
                        if o.is_dma:
                            ins.then_inc(o.dsem, 16)
                        elif o.need_inc:
                            ins.then_inc(self.esem[e], 1)
                getattr(block, engobj[e])(body)
        self.stack.close()


D = 1024
T = 4096
KC = D // 128
DFF = 2816
JC = DFF // 128
TT = 512
NTT = T // TT
EPS = 1e-6


def consts(cx):
    nc = cx.nc
    c = {}
    c["ones_bf"] = cx.sbuf("ones_bf", [128, 128], BF16)
    cx.op("pool", lambda e: e.memset(c["ones_bf"][:], 1.0), writes=["ones_bf"])
    return c


def rmsnorm_tile(cx, c, xt, nw, hT_out, tag, rd_keys, wr_key, ps_key, ps, scr, part=None):
    sq, rstd = scr["sq"], scr["rstd"]
    if part in (None, "A"):
        cx.op("act", lambda e: e.activation(sq[:], xt[:], AF.Square), reads=rd_keys, writes=[tag + "sq"])
    if part == "A":
        return

    def mm(e):
        ins = None
        for kc in range(KC):
            ins = e.matmul(ps[:], c["ones_bf"][:], sq[:, kc, :], start=(kc == 0), stop=(kc == KC - 1))
        return ins
    cx.op("pe", mm, reads=[tag + "sq", "ones_bf"], writes=[ps_key])
    cx.op("act", lambda e: e.activation(rstd[:], ps[:], AF.Sqrt, bias=EPS, scale=1.0 / D),
          reads=[ps_key], writes=[tag + "rstd"])
    cx.op("dve", lambda e: e.reciprocal(rstd[:], rstd[:]), reads=[tag + "rstd"], writes=[tag + "rstd"])
    for kc in range(KC):
        cx.op("dve", lambda e, kc=kc: e.scalar_tensor_tensor(
            out=hT_out(kc), in0=xt[:, kc, :], scalar=nw[:, kc:kc + 1], in1=rstd[:],
            op0=ALU.mult, op1=ALU.mult), reads=rd_keys + [tag + "rstd", tag + "nw"], writes=[wr_key])


def ffn_phase(cx, c, tag, x_in, x_out, nw_d, w_in, w_out, xkey_in, xkey_out):
    nc = cx.nc
    cx.push()
    HT = 2048
    NH = T // HT
    NT = HT // TT
    hT = cx.sbuf(tag + "hT", [128, KC, HT], BF16)
    act = cx.sbuf(tag + "act", [128, JC, HT], BF16)
    xt = cx.sbuf(tag + "xt", [128, KC, TT], F32)
    sq = cx.sbuf(tag + "sq", [128, KC, TT], BF16)
    rstd = cx.sbuf(tag + "rstd", [128, TT], F32)
    nw = cx.sbuf(tag + "nw", [128, KC], F32)
    WB = 256
    wg = [cx.sbuf(tag + "wg%d" % i, [128, KC, WB], BF16) for i in range(2)]
    wu = [cx.sbuf(tag + "wu%d" % i, [128, KC, WB], BF16) for i in range(2)]
    wo = [cx.sbuf(tag + "wo%d" % i, [128, JC, WB], BF16) for i in range(2)]
    sg = [cx.sbuf(tag + "sg%d" % i, [128, TT], F32) for i in range(2)]
    xr = [cx.sbuf(tag + "xr%d" % i, [128, TT], F32) for i in range(2)]
    xn = [cx.sbuf(tag + "xn%d" % i, [128, TT], F32) for i in range(2)]
    ps = [cx.psum(tag + "ps%d" % i, [128, TT], F32) for i in range(7)]
    pk = [tag + "ps%d" % i for i in range(7)]
    cx.psum_keys.update(pk)

    cx.dma("sp", nw[:], nw_d, writes=[tag + "nw"], skey="nw")
    w_in_r = w_in.rearrange("(kc p) n -> p kc n", p=128)
    w_out_r = w_out.rearrange("(kc p) n -> p kc n", p=128)
    x_in_r = x_in.rearrange("(kc p) t -> p kc t", p=128)
    cnt = {"g": 0, "o": 0, "s": 0, "x": 0}
    def norm_tile_ops(h, tt, part=None):
        t0_ = h * HT
        gt = (t0_ + tt * TT) // TT
        if part in (None, "A"):
            cx.dma("sp", xt[:], x_in_r[:, :, t0_ + tt * TT: t0_ + (tt + 1) * TT],
                   reads=[(xkey_in, gt, kc) for kc in range(KC)], writes=[tag + "xt"], skey="xt")
        rmsnorm_tile(cx, c, xt, nw, lambda kc, tt=tt: hT[:, kc, tt * TT:(tt + 1) * TT], tag,
                     [tag + "xt"], (tag + "hT", tt), pk[6], ps[6], {"sq": sq, "rstd": rstd}, part=part)

    for h in range(NH):
        t0 = h * HT
        if h == 0:
            for tt in range(NT):
                norm_tile_ops(0, tt)
        for jb in range(JC * 128 // WB):
            s = cnt["g"] % 2
            cnt["g"] += 1
            cx.dma("pool", wg[s][:], w_in_r[:, :, jb * WB:(jb + 1) * WB], writes=[(tag + "wg", s)],
                   skey=("wg", s))
            cx.dma("pool", wu[s][:], w_in_r[:, :, DFF + jb * WB: DFF + (jb + 1) * WB], writes=[(tag + "wu", s)],
                   skey=("wu", s))
            for jj in range(WB // 128):
                j = jb * (WB // 128) + jj
                for tt in range(NT):
                    b = cnt["s"] % 2
                    cnt["s"] += 1
                    pg, pu = ps[b], ps[2 + b]

                    def mm(e, s=s, jj=jj, tt=tt, pg=pg, pu=pu):
                        ins = None
                        for kc in range(KC):
                            ins = e.matmul(pg[:], wg[s][:, kc, jj * 128:(jj + 1) * 128],
                                           hT[:, kc, tt * TT:(tt + 1) * TT], start=(kc == 0), stop=(kc == KC - 1))
                        for kc in range(KC):
                            ins = e.matmul(pu[:], wu[s][:, kc, jj * 128:(jj + 1) * 128],
                                           hT[:, kc, tt * TT:(tt + 1) * TT], start=(kc == 0), stop=(kc == KC - 1))
                        return ins
                    cx.op("pe", mm, reads=[(tag + "wg", s), (tag + "wu", s), (tag + "hT", tt)],
                          writes=[pk[b], pk[2 + b]])
                    cx.op("act", lambda e, b=b, pg=pg: e.activation(sg[b][:], pg[:], AF.Silu),
                          reads=[pk[b]], writes=[(tag + "sg", b)])
                    cx.op("dve", lambda e, b=b, pu=pu, j=j, tt=tt: e.tensor_tensor(
                        out=act[:, j, tt * TT:(tt + 1) * TT], in0=sg[b][:], in1=pu[:], op=ALU.mult),
                        reads=[(tag + "sg", b), pk[2 + b]], writes=[(tag + "act", j, tt)])
        for mb in range(D // WB):
            s = cnt["o"] % 2
            cnt["o"] += 1
            cx.dma("pool", wo[s][:], w_out_r[:, :, mb * WB:(mb + 1) * WB], writes=[(tag + "wo", s)],
                   skey=("wo", s))
            for mm_ in range(WB // 128):
                mc = mb * (WB // 128) + mm_
                for tt in range(NT):
                    gt = (t0 + tt * TT) // TT
                    b = cnt["x"] % 2
                    cnt["x"] += 1
                    py = ps[4 + b]
                    cx.dma("sp", xr[b][:], x_in[mc * 128:(mc + 1) * 128, t0 + tt * TT: t0 + (tt + 1) * TT],
                           reads=[(xkey_in, gt, mc)], writes=[(tag + "xr", b)], skey=("xr", b))

                    def mm2(e, s=s, mm_=mm_, tt=tt, py=py):
                        ins = None
                        for kc in range(JC):
                            ins = e.matmul(py[:], wo[s][:, kc, mm_ * 128:(mm_ + 1) * 128],
                                           act[:, kc, tt * TT:(tt + 1) * TT], start=(kc == 0), stop=(kc == JC - 1))
                        return ins
                    cx.op("pe", mm2, reads=[(tag + "wo", s)] + [(tag + "act", j, tt) for j in range(JC)],
                          writes=[pk[4 + b]])
                    cx.op("dve", lambda e, b=b, py=py: e.scalar_tensor_tensor(
                        out=xn[b][:], in0=py[:], scalar=0.5, in1=xr[b][:], op0=ALU.mult, op1=ALU.add),
                        reads=[pk[4 + b], (tag + "xr", b)], writes=[(tag + "xn", b)])
                    cx.dma("sp", x_out[mc * 128:(mc + 1) * 128, t0 + tt * TT: t0 + (tt + 1) * TT], xn[b][:],
                           reads=[(tag + "xn", b)], writes=[(xkey_out, gt, mc)], skey=("xn", b))
                if h + 1 < NH:
                    hk = mc
                    norm_tile_ops(h + 1, hk // 2, part=("A" if hk % 2 == 0 else "B"))
    cx.barrier()
    cx.pop()


def build_ffn_prog():
    nc = bass.Bass("TRN2", target_bir_lowering=False)
    x = nc.dram_tensor("x", [D, T], F32, kind="ExternalInput").ap()
    nw = nc.dram_tensor("nw", [128, KC], F32, kind="ExternalInput").ap()
    w_in = nc.dram_tensor("w_in", [D, 2 * DFF], F32, kind="ExternalInput").ap()
    w_out = nc.dram_tensor("w_out", [DFF, D], F32, kind="ExternalInput").ap()
    y = nc.dram_tensor("y", [D, T], F32, kind="ExternalOutput").ap()
    cx = Ctx(nc)
    c = consts(cx)
    ffn_phase(cx, c, "f_", x, y, nw, w_in, w_out, "xin", "xout")
    cx.finish()
    return nc


NH_A = 12
DIL = (1, 4, 16)
NEG = -30000.0
SCALE_A = 128 ** -0.5
C_IDENT = 0
C_PSWAP = 128
C_MASK = 256
C_TRI = 512
C_TRIS = 640
C_MINC = 768
C_MSTR = 896
C_LAST = 1024
CST_W = 1152


def host_consts():
    m = np.zeros((128, CST_W), np.float32)
    k = np.arange(128)[:, None]
    q = np.arange(128)[None, :]
    m[:, C_IDENT:C_IDENT + 128] = (k == q)
    m[:, C_PSWAP:C_PSWAP + 128] = (k == (q + 64) % 128)
    m[:, C_MASK:C_MASK + 128] = np.where(k <= q, 0.0, NEG)
    m[:, C_MASK + 128:C_MASK + 256] = np.where(k >= q, 0.0, NEG)
    m[:, C_TRI:C_TRI + 128] = (k <= q)
    m[:, C_TRIS:C_TRIS + 128] = (k > q)
    m[:, C_MINC:C_MINC + 128] = (k <= q)
    m[:, C_MSTR:C_MSTR + 128] = (k < q)
    m[:, C_LAST:C_LAST + 128] = (k == 127)
    return m


def host_rope():
    inv = (1.0 / (10000.0 ** (np.arange(0, 128, 2, dtype=np.float32) / np.float32(128)))).astype(np.float32)
    ang = (np.arange(T, dtype=np.float32)[None, :] * inv[:, None]).astype(np.float32)
    cs, sn = np.cos(ang).astype(np.float32), np.sin(ang).astype(np.float32)
    return np.concatenate([cs, cs], 0), np.concatenate([-sn, sn], 0)


def norm_stage(cx, c, tag, x_in, nw_d, hT, xkey_in, ps, pskey):
    cx.push()
    xt = [cx.sbuf(tag + "nxt%d" % i, [128, KC, TT], F32) for i in range(2)]
    sq = cx.sbuf(tag + "nsq", [128, KC, TT], BF16)
    rstd = cx.sbuf(tag + "nrstd", [128, TT], F32)
    nw = cx.sbuf(tag + "nnw", [128, KC], F32)
    cx.dma("sp", nw[:], nw_d, writes=[tag + "nw"], skey="nw")
    x_in_r = x_in.rearrange("(kc p) t -> p kc t", p=128)
    for tt in range(NTT):
        b = tt % 2
        cx.dma("sp", xt[b][:], x_in_r[:, :, tt * TT:(tt + 1) * TT],
               reads=[(xkey_in, tt, kc) for kc in range(KC)], writes=[(tag + "xt", b)], skey=("xt", b))
        rmsnorm_tile(cx, c, xt[b], nw, lambda kc, tt=tt: hT[:, kc, tt * TT:(tt + 1) * TT], tag,
                     [(tag + "xt", b)], (tag + "hT", tt), pskey, ps, {"sq": sq, "rstd": rstd})
    cx.barrier()
    cx.pop()


def attn_phase(cx, c, tag, x_in, os_d, nw_d, w_in, ctab_d, stab_d, cst_d, xkey_in, okey):
    nc = cx.nc
    cx.push()
    hT = cx.sbuf(tag + "hT", [128, KC, T], BF16)
    ctab = cx.sbuf(tag + "ctab", [128, T], F32)
    stab = cx.sbuf(tag + "stab", [128, T], F32)
    ident = cx.sbuf(tag + "ident", [128, 128], BF16)
    pswap = cx.sbuf(tag + "pswap", [128, 128], BF16)
    maskb = cx.sbuf(tag + "maskb", [128, 256], BF16)
    ps = [cx.psum(tag + "ps%d" % i, [128, TT], F32) for i in range(8)]
    pk = [tag + "ps%d" % i for i in range(8)]
    cx.psum_keys.update(pk)
    cx.dma("sp", ctab[:], ctab_d, writes=[tag + "ctab"], skey="ctab")
    cx.dma("sp", stab[:], stab_d, writes=[tag + "stab"], skey="stab")
    cx.dma("pool", ident[:], cst_d[:, C_IDENT:C_IDENT + 128], writes=[tag + "ident"], skey="ident")
    cx.dma("pool", pswap[:], cst_d[:, C_PSWAP:C_PSWAP + 128], writes=[tag + "pswap"], skey="pswap")
    cx.dma("pool", maskb[:], cst_d[:, C_MASK:C_MASK + 256], writes=[tag + "maskb"], skey="maskb")
    norm_stage(cx, c, tag, x_in, nw_d, hT, xkey_in, ps[6], pk[6])
    hkeys = []

    Qd = cx.sbuf(tag + "Qd", [128, T], BF16)
    Kd = cx.sbuf(tag + "Kd", [128, T], BF16)
    Vd = cx.sbuf(tag + "Vd", [128, 32, 128], BF16)
    numT = cx.sbuf(tag + "numT", [128, 3, T], BF16)
    dent = cx.sbuf(tag + "dent", [128, T], F32)
    wq = [cx.sbuf(tag + "wq%d" % i, [128, KC, 128], BF16) for i in range(2)]
    wk = [cx.sbuf(tag + "wk%d" % i, [128, KC, 128], BF16) for i in range(2)]
    wv = [cx.sbuf(tag + "wv%d" % i, [128, KC, 128], BF16) for i in range(2)]
    qb = [cx.sbuf(tag + "qb%d" % i, [128, TT], BF16) for i in range(3)]
    t1 = [cx.sbuf(tag + "t1%d" % i, [128, TT], F32) for i in range(3)]
    t2 = [cx.sbuf(tag + "t2%d" % i, [128, TT], F32) for i in range(3)]
    pt = [cx.sbuf(tag + "pt%d" % i, [128, 256], BF16) for i in range(4)]
    osc = [cx.sbuf(tag + "osc%d" % i, [128, TT], BF16) for i in range(2)]
    w_in_r = w_in.rearrange("(kc p) n -> p kc n", p=128)
    AW = NH_A * 128
    cnt = {"w": 0, "r": 0, "o": 0}

    def load_w(hd):
        s = cnt["w"] % 2
        cnt["w"] += 1
        for nm, wt, off in (("wq", wq, 0), ("wk", wk, AW), ("wv", wv, 2 * AW)):
            cx.dma("pool", wt[s][:], w_in_r[:, :, off + hd * 128: off + (hd + 1) * 128],
                   writes=[(tag + nm, s)], skey=(nm, s))
        return s

    for hg in range(4):
        for g in range(3):
            d = DIL[g]
            Ls = T // d
            nb = Ls // 128
            hd = g * 4 + hg
            s = load_w(hd)
            def rope_piece(which, wt, dst, dkey, tt, item, s=s, d=d):
                b = item % 3
                bP, bS = (0, 1, 4)[b], (2, 3, 5)[b]
                pp, sw = ps[bP], ps[bS]
                dst_v = dst[:, :].rearrange("p (r j) -> p j r", r=d)

                def mm(e):
                    ins = None
                    for kc in range(KC):
                        ins = e.matmul(pp[:], wt[s][:, kc, :], hT[:, kc, tt * TT:(tt + 1) * TT],
                                       start=(kc == 0), stop=(kc == KC - 1))
                    return ins
                cx.op("pe", mm, reads=[(tag + "w" + which, s)], writes=[pk[bP]])
                yield
                cx.op("act", lambda e: e.activation(qb[b][:], pp[:], AF.Copy),
                      reads=[pk[bP]], writes=[(tag + "qb", b)])
                yield "NEXT"
                cx.op("pe", lambda e: e.matmul(sw[:], pswap[:], qb[b][:], start=True, stop=True),
                      reads=[(tag + "qb", b), tag + "pswap"], writes=[pk[bS]])
                cx.op("dve", lambda e: e.tensor_tensor(
                    out=t1[b][:], in0=qb[b][:], in1=ctab[:, tt * TT:(tt + 1) * TT], op=ALU.mult),
                    reads=[(tag + "qb", b), tag + "ctab"], writes=[(tag + "t1", b)])
                yield
                cx.op("dve", lambda e: e.tensor_tensor(
                    out=t2[b][:], in0=sw[:], in1=stab[:, tt * TT:(tt + 1) * TT], op=ALU.mult),
                    reads=[pk[bS], tag + "stab"], writes=[(tag + "t2", b)])
                yield
                j0 = tt * TT // d
                cx.op("dve", lambda e: e.tensor_tensor(
                    out=dst_v[:, j0:j0 + TT // d, :],
                    in0=t1[b][:, :].rearrange("p (j r) -> p j r", r=d),
                    in1=t2[b][:, :].rearrange("p (j r) -> p j r", r=d), op=ALU.add),
                    reads=[(tag + "t1", b), (tag + "t2", b)], writes=[(dkey, tt)])
                yield

            pieces = [(w_, wt_, dst_, dk_, tt) for (w_, wt_, dst_, dk_) in
                      (("q", wq, Qd, tag + "Qd"), ("k", wk, Kd, tag + "Kd")) for tt in range(NTT)]
            act_g, nx = [], [0]

            def start_r():
                if nx[0] < len(pieces) and len(act_g) < 3:
                    w_, wt_, dst_, dk_, tt = pieces[nx[0]]
                    act_g.append(rope_piece(w_, wt_, dst_, dk_, tt, cnt["r"]))
                    cnt["r"] += 1
                    nx[0] += 1
            start_r()
            while act_g:
                for g_ in list(act_g):
                    try:
                        v = next(g_)
                    except StopIteration:
                        act_g.remove(g_)
                        start_r()
                        continue
                    if v == "NEXT" and g_ is act_g[-1]:
                        start_r()
            for b4 in range(8):
                b = cnt["r"] % 2
                cnt["r"] += 1
                pp = ps[b]

                def mmv(e, s=s, b4=b4, pp=pp, d=d, Ls=Ls):
                    ins = None
                    for i in range(4):
                        B = b4 * 4 + i
                        r, n = divmod(B, Ls // 128)
                        t0 = n * 128 * d + r
                        for kc in range(KC):
                            ins = e.matmul(pp[:, i * 128:(i + 1) * 128],
                                           hT[:, kc, t0: t0 + 127 * d + 1: d], wv[s][:, kc, :],
                                           start=(kc == 0), stop=(kc == KC - 1))
                    return ins
                cx.op("pe", mmv, reads=[(tag + "wv", s)], writes=[pk[b]])
                cx.op("act", lambda e, b4=b4, pp=pp: e.activation(
                    Vd[:, b4 * 4:(b4 + 1) * 4, :], pp[:, :].rearrange("p (i e) -> p i e", i=4), AF.Copy),
                    reads=[pk[b]], writes=[(tag + "Vd", b4)])
            qk_keys = [(tag + "Qd", tt) for tt in range(NTT)] + [(tag + "Kd", tt) for tt in range(NTT)]

            def scores(B, nb=nb):
                sb = (4, 5, 2, 3)[B % 4]
                nq = 256 if (B + 1) % nb != 0 else 128
                st = ps[sb]

                def mm(e, B=B, nq=nq, st=st):
                    e.matmul(st[:, 0:nq], Kd[:, B * 128:(B + 1) * 128], Qd[:, B * 128:B * 128 + nq],
                             start=True, stop=False)
                    return e.matmul(st[:, 0:nq], ident[:], maskb[:, 0:nq], start=False, stop=True)
                cx.op("pe", mm, reads=qk_keys + [tag + "ident", tag + "maskb"], writes=[pk[sb]])
                cx.op("act", lambda e, B=B, nq=nq, st=st: e.activation(
                    pt[B % 4][:, 0:nq], st[:, 0:nq], AF.Exp, scale=SCALE_A),
                    reads=[pk[sb]], writes=[(tag + "pt", B % 4)])

            def pv(B, nb=nb, g=g, d=d, Ls=Ls):
                first = (B % nb == 0)
                col = (B % 4) * 128
                bN, bD = ((6, 7), (0, 1))[(B // 4) % 2]

                def mm(e, B=B, first=first, col=col, bN=bN, bD=bD):
                    ins = None
                    for dst, lhs_fn in ((ps[bN], lambda blk: Vd[:, blk, :]), (ps[bD], lambda blk: c["ones_bf"][:])):
                        if not first:
                            e.matmul(dst[:, col:col + 128], lhs_fn(B - 1), pt[(B - 1) % 4][:, 128:256],
                                     start=True, stop=False)
                        ins = e.matmul(dst[:, col:col + 128], lhs_fn(B), pt[B % 4][:, 0:128],
                                       start=first, stop=True)
                    return ins
                rk = [(tag + "pt", B % 4), (tag + "Vd", B // 4), "ones_bf"]
                if not first:
                    rk += [(tag + "pt", (B - 1) % 4), (tag + "Vd", (B - 1) // 4)]
                cx.op("pe", mm, reads=rk, writes=[pk[bN], pk[bD]])
                if B % 4 == 3:
                    u0 = (B - 3) * 128
                    if d == 1:
                        def nat(ap2d):
                            return ap2d[:, u0:u0 + 512]
                        def src(p):
                            return p[:, :]
                    else:
                        r0, j0 = divmod(u0, Ls)
                        nr = max(1, 512 // Ls)
                        nj = 512 // nr
                        def nat(ap2d, r0=r0, j0=j0, nr=nr, nj=nj, d=d):
                            return ap2d.rearrange("p (j r) -> p r j", r=d)[:, r0:r0 + nr, j0:j0 + nj]
                        def src(p, nr=nr):
                            return p[:, :].rearrange("p (r j) -> p r j", r=nr)
                    cx.op("act", lambda e, nat=nat, src=src, g=g, bN=bN: e.activation(
                        nat(numT[:, g, :]), src(ps[bN]), AF.Copy),
                        reads=[pk[bN]], writes=[(tag + "numT", g)])
                    if g == 0:
                        cx.op("dve", lambda e, nat=nat, src=src, bD=bD: e.tensor_copy(nat(dent[:, :]), src(ps[bD])),
                              reads=[pk[bD]], writes=[tag + "dent"])
                    else:
                        cx.op("dve", lambda e, nat=nat, src=src, bD=bD: e.tensor_tensor(
                            out=nat(dent[:, :]), in0=nat(dent[:, :]), in1=src(ps[bD]), op=ALU.add),
                            reads=[pk[bD], tag + "dent"], writes=[tag + "dent"])

            scores(0)
            scores(1)
            for B in range(32):
                if B + 2 < 32:
                    scores(B + 2)
                pv(B)
        cx.op("dve", lambda e: e.reciprocal(dent[:, :], dent[:, :]), reads=[tag + "dent"], writes=[tag + "dent"])
        for g in range(3):
            hd = g * 4 + hg
            for tt in range(NTT):
                b = cnt["o"] % 2
                cnt["o"] += 1
                cx.op("dve", lambda e, b=b, g=g, tt=tt: e.tensor_tensor(
                    out=osc[b][:], in0=numT[:, g, tt * TT:(tt + 1) * TT], in1=dent[:, tt * TT:(tt + 1) * TT],
                    op=ALU.mult), reads=[tag + "dent", (tag + "numT", g)], writes=[(tag + "osc", b)])
                cx.dma("sp", os_d[hd * 128:(hd + 1) * 128, tt * TT:(tt + 1) * TT], osc[b][:],
                       reads=[(tag + "osc", b)], writes=[(okey, hd, tt)], skey=("osc", b))
    cx.barrier()
    cx.pop()


def outproj_phase(cx, c, tag, x_in, x_out, os_d, w_out, nheads, xkey_in, xkey_out, okey):
    cx.push()
    HT = 2048
    NT = HT // TT
    WB = 256
    osb = cx.sbuf(tag + "osb", [128, nheads, HT], BF16)
    wo = [cx.sbuf(tag + "wo%d" % i, [128, nheads, WB], BF16) for i in range(2)]
    xr = [cx.sbuf(tag + "xr%d" % i, [128, TT], F32) for i in range(2)]
    xn = [cx.sbuf(tag + "xn%d" % i, [128, TT], F32) for i in range(2)]
    ps = [cx.psum(tag + "ps%d" % i, [128, TT], F32) for i in range(2)]
    pk = [tag + "ps%d" % i for i in range(2)]
    cx.psum_keys.update(pk)
    w_out_r = w_out.rearrange("(kc p) n -> p kc n", p=128)
    os_r = os_d.rearrange("(h p) t -> p h t", p=128)
    cnt = {"o": 0, "x": 0}
    for h in range(T // HT):
        t0 = h * HT
        for hd in range(nheads):
            cx.dma("sp", osb[:, hd, :], os_r[:, hd, t0:t0 + HT],
                   reads=[(okey, hd, (t0 // TT) + i) for i in range(NT)], writes=[(tag + "osb", hd)], skey=("osb", hd))
        for mb in range(D // WB):
            s = cnt["o"] % 2
            cnt["o"] += 1
            cx.dma("pool", wo[s][:], w_out_r[:, :, mb * WB:(mb + 1) * WB], writes=[(tag + "wo", s)], skey=("wo", s))
            for mm_ in range(WB // 128):
                mc = mb * (WB // 128) + mm_
                for tt in range(NT):
                    gt = (t0 + tt * TT) // TT
                    b = cnt["x"] % 2
                    cnt["x"] += 1
                    py = ps[b]
                    cx.dma("sp", xr[b][:], x_in[mc * 128:(mc + 1) * 128, t0 + tt * TT: t0 + (tt + 1) * TT],
                           reads=[(xkey_in, gt, mc)], writes=[(tag + "xr", b)], skey=("xr", b))

                    def mm2(e, s=s, mm_=mm_, tt=tt, py=py):
                        ins = None
                        for kc in range(nheads):
                            ins = e.matmul(py[:], wo[s][:, kc, mm_ * 128:(mm_ + 1) * 128],
                                           osb[:, kc, tt * TT:(tt + 1) * TT], start=(kc == 0), stop=(kc == nheads - 1))
                        return ins
                    cx.op("pe", mm2, reads=[(tag + "wo", s)] + [(tag + "osb", hd) for hd in range(nheads)],
                          writes=[pk[b]])
                    cx.op("dve", lambda e, b=b, py=py: e.tensor_tensor(
                        out=xn[b][:], in0=py[:], in1=xr[b][:], op=ALU.add),
                        reads=[pk[b], (tag + "xr", b)], writes=[(tag + "xn", b)])
                    cx.dma("sp", x_out[mc * 128:(mc + 1) * 128, t0 + tt * TT: t0 + (tt + 1) * TT], xn[b][:],
                           reads=[(tag + "xn", b)], writes=[(xkey_out, gt, mc)], skey=("xn", b))
    cx.barrier()
    cx.pop()


def build_attn_prog():
    nc = bass.Bass("TRN2", target_bir_lowering=False)
    x = nc.dram_tensor("x", [D, T], F32, kind="ExternalInput").ap()
    nw = nc.dram_tensor("nw", [128, KC], F32, kind="ExternalInput").ap()
    w_in = nc.dram_tensor("w_in", [D, 3 * NH_A * 128], F32, kind="ExternalInput").ap()
    w_out = nc.dram_tensor("w_out", [NH_A * 128, D], F32, kind="ExternalInput").ap()
    ctab = nc.dram_tensor("ctab", [128, T], F32, kind="ExternalInput").ap()
    stab = nc.dram_tensor("stab", [128, T], F32, kind="ExternalInput").ap()
    cst = nc.dram_tensor("cst", [128, CST_W], F32, kind="ExternalInput").ap()
    os_d = nc.dram_tensor("os_scratch", [NH_A * 128, T], BF16).ap()
    y = nc.dram_tensor("y", [D, T], F32, kind="ExternalOutput").ap()
    cx = Ctx(nc)
    c = consts(cx)
    attn_phase(cx, c, "a_", x, os_d, nw, w_in, ctab, stab, cst, "xin", "os")
    outproj_phase(cx, c, "ao_", x, y, os_d, w_out, NH_A, "xin", "xout", "os")
    cx.finish()
    return nc


NVH = 16
NKH = 8
GP = 6176
NTILE = T // 128
DBG = {}


class _Stop(Exception):
    pass


def _chk(n):
    if DBG.get("stop", 99) <= n:
        raise _Stop()


def gdn_core_phase(*a):
    cx = a[0]
    depth = len(cx.scopes)
    try:
        _gdn_core_phase(*a)
    except _Stop:
        cx.barrier()
        while len(cx.scopes) > depth:
            cx.pop()


def _gdn_core_phase(cx, c, tag, x_in, oraw_d, nw_d, w_in, convw_d, alog_d, dtb_d, cst_d, xkey_in, okey):
    cx.push()
    hT = cx.sbuf(tag + "hT", [128, KC, T], BF16)
    ident = cx.sbuf(tag + "ident", [128, 128], BF16)
    id4 = cx.sbuf(tag + "id4", [128, 4, 128], BF16)
    minc4 = cx.sbuf(tag + "minc4", [128, 4, 128], BF16)
    mlow4 = cx.sbuf(tag + "mlow4", [128, 4, 128], BF16)
    cw = cx.sbuf(tag + "cw", [128, 32, 4], F32)
    beta = cx.sbuf(tag + "beta", [128, NTILE, NVH], F32)
    nbeta = cx.sbuf(tag + "nbeta", [128, NTILE, NVH], F32)
    gc = cx.sbuf(tag + "gc", [128, NTILE, NVH], F32)
    kbs = cx.sbuf(tag + "kbs", [128, NTILE, NVH], F32)
    egr = cx.sbuf(tag + "egr", [128, NTILE, NVH], F32)
    cd = cx.sbuf(tag + "cd", [128, NTILE, NVH], F32)
    gchi = cx.sbuf(tag + "gchi", [128, NTILE, NVH], BF16)
    gclo = cx.sbuf(tag + "gclo", [128, NTILE, NVH], BF16)
    ps = [cx.psum(tag + "ps%d" % i, [128, TT], F32) for i in range(8)]
    pk = [tag + "ps%d" % i for i in range(8)]
    cx.psum_keys.update(pk)
    K = lambda n: tag + n
    cx.push()
    alog = cx.sbuf(tag + "alog", [128, 512], F32)
    dtb = cx.sbuf(tag + "dtb", [128, 512], F32)
    gt = cx.sbuf(tag + "gt", [128, NTILE, NVH], F32)
    wab = cx.sbuf(tag + "wab", [128, KC, 32], BF16)
    ghi = cx.sbuf(tag + "ghi", [128, NTILE, NVH], BF16)
    glo = cx.sbuf(tag + "glo", [128, NTILE, NVH], BF16)
    trib = cx.sbuf(tag + "trib", [128, 128], BF16)
    trisb = cx.sbuf(tag + "trisb", [128, 128], BF16)
    cx.dma("pool", ident[:], cst_d[:, C_IDENT:C_IDENT + 128], writes=[K("ident")], skey="ident")
    for i in range(4):
        cx.dma("pool", id4[:, i, :], cst_d[:, C_IDENT:C_IDENT + 128], writes=[K("id4")], skey=("id4", i))
        cx.dma("pool", minc4[:, i, :], cst_d[:, C_MINC:C_MINC + 128], writes=[K("minc4")], skey=("minc4", i))
        cx.dma("pool", mlow4[:, i, :], cst_d[:, C_TRIS:C_TRIS + 128], writes=[K("mlow4")], skey=("mlow4", i))
    cx.dma("sp", cw[:], convw_d, writes=[K("cw")], skey="cw")
    cx.dma("sp", alog[:], alog_d, writes=[K("alog")], skey="alog")
    cx.dma("sp", dtb[:], dtb_d, writes=[K("dtb")], skey="dtb")
    w_in_r = w_in.rearrange("(kc p) n -> p kc n", p=128)
    cx.dma("pool", wab[:], w_in_r[:, :, 6144:6176], writes=[K("wab")], skey="wab")
    _chk(0)
    norm_stage(cx, c, tag, x_in, nw_d, hT, xkey_in, ps[6], pk[6])
    _chk(1)

    f2 = lambda t: t[:, :, :].rearrange("p c h -> p (c h)")
    for which in range(2):
        def mm(e, which=which):
            ins = None
            for ct in range(NTILE):
                for kc in range(KC):
                    ins = e.matmul(ps[which][:, ct * 16:(ct + 1) * 16], hT[:, kc, ct * 128:(ct + 1) * 128],
                                   wab[:, kc, which * 16:(which + 1) * 16], start=(kc == 0), stop=(kc == KC - 1))
            return ins
        cx.op("pe", mm, reads=[], writes=[pk[which]])
    _chk(1.05)
    cx.op("act", lambda e: e.activation(f2(beta), ps[0][:], AF.Sigmoid), reads=[pk[0]], writes=[K("beta")])
    _chk(1.1)
    cx.op("dve", lambda e: e.tensor_tensor(out=f2(gt), in0=ps[1][:], in1=dtb[:], op=ALU.add),
          reads=[pk[1], K("dtb")], writes=[K("gt")])
    _chk(1.2)
    cx.op("act", lambda e: e.activation(f2(gt), f2(gt), AF.Exp), reads=[K("gt")], writes=[K("gt")])
    _chk(1.4)
    cx.op("act", lambda e: e.activation(f2(gt), f2(gt), AF.Ln, bias=1.0), reads=[K("gt")], writes=[K("gt")])
    _chk(1.6)
    cx.op("act", lambda e: e.activation(alog[:], alog[:], AF.Exp), reads=[K("alog")], writes=[K("alog")])
    _chk(1.8)
    cx.op("dve", lambda e: e.scalar_tensor_tensor(out=f2(gt), in0=f2(gt), scalar=-1.0, in1=alog[:],
                                                   op0=ALU.mult, op1=ALU.mult),
          reads=[K("gt"), K("alog")], writes=[K("gt")])
    cx.op("dve", lambda e: e.tensor_scalar(f2(nbeta), f2(beta), -1.0, None, ALU.mult),
          reads=[K("beta")], writes=[K("nbeta")])
    _chk(2)

    cx.dma("pool", trib[:], cst_d[:, C_TRI:C_TRI + 128], writes=[K("trib")], skey="trib")
    cx.dma("pool", trisb[:], cst_d[:, C_TRIS:C_TRIS + 128], writes=[K("trisb")], skey="trisb")
    cx.op("dve", lambda e: e.tensor_copy(f2(ghi), f2(gt)), reads=[K("gt")], writes=[K("ghi")])
    cx.op("dve", lambda e: e.tensor_tensor(out=f2(glo), in0=f2(gt), in1=f2(ghi), op=ALU.subtract),
          reads=[K("gt"), K("ghi")], writes=[K("glo")])
    _chk(2.2)

    def mmg(e):
        ins = None
        for ct in range(NTILE):
            for dst, m in ((ps[2], trib), (ps[3], trisb), (ps[4], c["ones_bf"])):
                e.matmul(dst[:, ct * 16:(ct + 1) * 16], m[:], ghi[:, ct, :], start=True, stop=False)
                ins = e.matmul(dst[:, ct * 16:(ct + 1) * 16], m[:], glo[:, ct, :], start=False, stop=True)
        return ins
    cx.op("pe", mmg, reads=[K("ghi"), K("glo"), K("trib"), K("trisb"), "ones_bf"], writes=[pk[2], pk[3], pk[4]])
    _chk(2.4)
    cx.op("dve", lambda e: e.tensor_copy(f2(gchi), ps[2][:]), reads=[pk[2]], writes=[K("gchi")])
    cx.op("dve", lambda e: e.tensor_tensor(out=f2(gclo), in0=ps[2][:], in1=f2(gchi), op=ALU.subtract),
          reads=[pk[2], K("gchi")], writes=[K("gclo")])
    _chk(2.6)
    cx.op("dve", lambda e: e.tensor_copy(f2(gc), ps[2][:]), reads=[pk[2]], writes=[K("gc")])
    cx.op("act", lambda e: e.activation(f2(kbs), ps[2][:], AF.Exp), reads=[pk[2]], writes=[K("kbs")])
    _chk(2.7)
    cx.op("dve", lambda e: e.tensor_tensor(out=f2(kbs), in0=f2(kbs), in1=f2(beta), op=ALU.mult),
          reads=[K("kbs"), K("beta")], writes=[K("kbs")])
    _chk(2.8)
    cx.op("act", lambda e: e.activation(f2(egr), ps[3][:], AF.Exp), reads=[pk[3]], writes=[K("egr")])
    cx.op("act", lambda e: e.activation(f2(cd), ps[4][:], AF.Exp), reads=[pk[4]], writes=[K("cd")])
    _chk(2.9)
    cx.barrier()
    cx.pop()
    _chk(3)

    qT = cx.sbuf(tag + "qT", [128, 2, T], BF16)
    kT = cx.sbuf(tag + "kT", [128, 2, T], BF16)
    ktok = cx.sbuf(tag + "ktok", [128, NTILE, 2, 128], BF16)
    vtok = cx.sbuf(tag + "vtok", [128, NTILE, 4, 128], BF16)
    PW = 1024
    for hgp in range(DBG.get('hgps', 4)):
        cx.push()
        NB = 3
        wch = [cx.sbuf(tag + "wch%d" % i, [128, KC, 128], BF16) for i in range(2)]
        dg = [cx.sbuf(tag + "dg%d" % i, [128, 4, 128], BF16) for i in range(2)]
        xcb = [cx.sbuf(tag + "xcb%d" % i, [128, 4 + TT], BF16) for i in range(NB)]
        sil = [cx.sbuf(tag + "sil%d" % i, [128, TT], F32) for i in range(NB)]
        silb = [cx.sbuf(tag + "silb%d" % i, [128, TT], BF16) for i in range(NB)]
        sqb = [cx.sbuf(tag + "sqb%d" % i, [128, TT], BF16) for i in range(NB)]
        rn = [cx.sbuf(tag + "rn%d" % i, [128, TT], F32) for i in range(NB)]
        chunks = [("q", 0), ("q", 1), ("k", 0), ("k", 1), ("v", 0), ("v", 1), ("v", 2), ("v", 3)]

        def g1_piece(kind, li, gch, ws, tt, item):
            b = item % NB
            i2 = item % 2
            bP, bC, bS, bT = i2, 2 + i2, 4 + i2, 6 + i2
            XK = lambda i: (K("xcb"), i)
            if tt == 0:
                cx.op("act", lambda e: e.activation(xcb[b][:, 0:4], xcb[b][:, 0:4], AF.Copy, scale=0.0),
                      reads=[], writes=[(K("xh"), b)])

            def mm(e):
                ins = None
                for kc in range(KC):
                    ins = e.matmul(ps[bP][:], wch[ws][:, kc, :], hT[:, kc, tt * TT:(tt + 1) * TT],
                                   start=(kc == 0), stop=(kc == KC - 1))
                return ins
            cx.op("pe", mm, reads=[(K("wch"), ws)], writes=[pk[bP]])
            yield
            cx.op("act", lambda e: e.activation(xcb[b][:, 4:4 + TT], ps[bP][:], AF.Copy),
                  reads=[pk[bP]], writes=[XK(b)])
            if tt < NTT - 1:
                nb_ = (item + 1) % NB
                cx.op("act", lambda e: e.activation(xcb[nb_][:, 0:4], xcb[b][:, TT:TT + 4], AF.Copy),
                      reads=[XK(b)], writes=[(K("xh"), nb_)])
            yield "NEXT"

            def mmc(e):
                ins = None
                for j in range(4):
                    ins = e.matmul(ps[bC][:], dg[ws][:, j, :], xcb[b][:, 1 + j:1 + j + TT], start=(j == 0), stop=(j == 3))
                return ins
            cx.op("pe", mmc, reads=[XK(b), (K("xh"), b), (K("dg"), ws)], writes=[pk[bC]])
            yield
            tsl = slice(tt * TT, (tt + 1) * TT)
            if kind == "v":
                cx.op("act", lambda e: e.activation(silb[b][:], ps[bC][:], AF.Silu), reads=[pk[bC]], writes=[(K("silb"), b)])
                yield
            else:
                cx.op("act", lambda e: e.activation(sil[b][:], ps[bC][:], AF.Silu), reads=[pk[bC]], writes=[(K("sil"), b)])
                yield
                cx.op("dve", lambda e: e.tensor_tensor(out=sqb[b][:], in0=sil[b][:], in1=sil[b][:], op=ALU.mult),
                      reads=[(K("sil"), b)], writes=[(K("sqb"), b)])
                yield
                cx.op("pe", lambda e: e.matmul(ps[bS][:], c["ones_bf"][:], sqb[b][:], start=True, stop=True),
                      reads=[(K("sqb"), b), "ones_bf"], writes=[pk[bS]])
                yield
                cx.op("act", lambda e: e.activation(rn[b][:], ps[bS][:], AF.Sqrt, bias=1e-6, scale=1.0),
                      reads=[pk[bS]], writes=[(K("rn"), b)])
                yield
                cx.op("dve", lambda e: e.reciprocal(rn[b][:], rn[b][:]), reads=[(K("rn"), b)], writes=[(K("rn"), b)])
                dstT = qT if kind == "q" else kT
                sc = (128 ** -0.5) if kind == "q" else 1.0
                cx.op("dve", lambda e: e.scalar_tensor_tensor(
                    out=dstT[:, li, tsl], in0=sil[b][:], scalar=sc, in1=rn[b][:], op0=ALU.mult, op1=ALU.mult),
                    reads=[(K("sil"), b), (K("rn"), b)], writes=[(K(kind + "T"), li, tt)])
                yield
            if kind in ("k", "v"):
                ptv = ps[bT][:, :].bitcast(BF16)

                def mmt(e):
                    ins = None
                    for i in range(4):
                        if kind == "v":
                            src = silb[b][:, i * 128:(i + 1) * 128]
                        else:
                            src = kT[:, li, tt * TT + i * 128: tt * TT + (i + 1) * 128]
                        ins = e.transpose(ptv[:, i * 128:(i + 1) * 128], src, ident[:])
                    return ins
                rk = [(K("silb"), b)] if kind == "v" else [(K("kT"), li, tt)]
                cx.op("pe", mmt, reads=rk + [K("ident")], writes=[pk[bT]])
                yield
                dst = vtok if kind == "v" else ktok
                cx.op("act", lambda e: e.activation(
                    dst[:, tt * 4:tt * 4 + 4, li, :], ptv[:, 0:512].rearrange("p (i x) -> p i x", i=4), AF.Copy),
                    reads=[pk[bT]], writes=[(K(kind + "tok"), li, tt)])
                yield

        items = []
        wcnt = 0
        for kind, li in chunks:
            gch = {"q": 2 * hgp + li, "k": 8 + 2 * hgp + li, "v": 16 + 4 * hgp + li}[kind]
            ws = wcnt % 2
            wcnt += 1
            for tt in range(NTT):
                items.append((kind, li, gch, ws, tt))
        active, nxt_i = [], [0]

        def start1():
            while nxt_i[0] < len(items) and len(active) < NB:
                kind, li, gch, ws, tt = items[nxt_i[0]]
                if tt == 0:
                    cx.dma("pool", wch[ws][:], w_in_r[:, :, gch * 128:(gch + 1) * 128], writes=[(K("wch"), ws)],
                           skey=("wch", ws))
                    for j in range(4):
                        cx.op("dve", lambda e, ws=ws, gch=gch, j=j: e.tensor_scalar(
                            dg[ws][:, j, :], ident[:], cw[:, gch, j:j + 1], None, ALU.mult),
                            reads=[K("ident"), K("cw")], writes=[(K("dg"), ws)])
                active.append(g1_piece(kind, li, gch, ws, tt, nxt_i[0]))
                nxt_i[0] += 1
                return
        start1()
        while active:
            for g in list(active):
                try:
                    v = next(g)
                except StopIteration:
                    active.remove(g)
                    start1()
                    continue
                if v == "NEXT" and g is active[-1]:
                    start1()
        cx.barrier()
        cx.pop()
        _chk(5)

        cx.push()
        mk = lambda nm, dt_, n=2: [cx.sbuf(tag + nm + "%d" % i, [128, 4, 128], dt_) for i in range(n)]
        Abuf0, Abuf1, Atbuf0, Atbuf1 = mk("Aa", BF16), mk("Ab", BF16), mk("Ata", BF16), mk("Atb", BF16)
        A_O, AtO, Tt = mk("A_O", BF16), mk("AtO", BF16), mk("Tt", BF16)
        qkT, qdT, kd, vb, kb = mk("qkT", BF16), mk("qdT", BF16), mk("kd", BF16), mk("vb", BF16), mk("kb", BF16)
        dmT = cx.sbuf(tag + "dmT", [128, 4, 128], F32)
        dm2 = cx.sbuf(tag + "dm2", [128, 4, 128], F32)
        eB = cx.sbuf(tag + "eB", [128, 4, 128], F32)
        rr = cx.sbuf(tag + "rr", [128, 4, 128], F32)
        u = cx.sbuf(tag + "u", [128, 4, 128], F32)
        wT = cx.sbuf(tag + "wT", [128, 4, 128], BF16)
        vnew = cx.sbuf(tag + "vnew", [128, 4, 128], BF16)
        S32 = cx.sbuf(tag + "S32", [128, 4, 128], F32)
        S16 = cx.sbuf(tag + "S16", [128, 4, 128], BF16)
        ost = [cx.sbuf(tag + "ost%d" % i, [128, 4, 128], F32) for i in range(2)]
        fl = lambda t: t[:, :, :].rearrange("p h x -> p (h x)")
        cx.op("dve", lambda e: e.memset(fl(S32), 0.0), writes=[K("S32")])
        cx.op("dve", lambda e: e.memset(fl(S16), 0.0), writes=[K("S16")])
        hs = [4 * hgp + hl for hl in range(4)]
        B0, B1 = 0, 1

        def g2_tile(ct, hs, hgp=hgp):
            par = ct % 2
            bA, bAt, bTu = 2 + 3 * par, 3 + 3 * par, 4 + 3 * par
            tsl = slice(ct * 128, (ct + 1) * 128)
            P = lambda n: (K(n), par)
            AO, AtOp, Ttp = A_O[par], AtO[par], Tt[par]
            bufs = [(Abuf0[par], Atbuf0[par]), (Abuf1[par], Atbuf1[par]), (AO, AtOp)]

            def mm1(e):
                for hl in range(4):
                    e.matmul(ps[B0][:, hl * 128:(hl + 1) * 128], gchi[:, ct, hs[hl]:hs[hl] + 1].to_broadcast([128, 128]),
                             ident[:], start=True, stop=False)
                    e.matmul(ps[B0][:, hl * 128:(hl + 1) * 128], gclo[:, ct, hs[hl]:hs[hl] + 1].to_broadcast([128, 128]),
                             ident[:], start=False, stop=True)
                ins = None
                for khl in range(2):
                    e.matmul(ps[B1][:, khl * 128:(khl + 1) * 128], kT[:, khl, tsl], kT[:, khl, tsl], start=True, stop=True)
                    ins = e.matmul(ps[B1][:, (2 + khl) * 128:(3 + khl) * 128], kT[:, khl, tsl], qT[:, khl, tsl],
                                   start=True, stop=True)
                return ins
            cx.op("pe", mm1, reads=[K("ident")], writes=[pk[B0], pk[B1]])
            yield
            for hl in range(4):
                h = hs[hl]
                cx.op("dve", lambda e, hl=hl, h=h: e.tensor_scalar(
                    dmT[:, hl, :], ps[B0][:, hl * 128:(hl + 1) * 128], gc[:, ct, h:h + 1], 0.0, ALU.subtract, ALU.min),
                    reads=[pk[B0]], writes=[K("dmT")])
                cx.op("dve", lambda e, hl=hl, h=h: e.tensor_scalar(
                    dm2[:, hl, :], ps[B0][:, hl * 128:(hl + 1) * 128], gc[:, ct, h:h + 1], 0.0, ALU.subtract, ALU.max),
                    reads=[pk[B0]], writes=[K("dm2")])
                yield
            cx.op("act", lambda e: e.activation(fl(eB), ps[B0][:], AF.Exp), reads=[pk[B0]], writes=[K("eB")])
            cx.op("act", lambda e: e.activation(fl(dmT), fl(dmT), AF.Exp), reads=[K("dmT")], writes=[K("dmT")])
            cx.op("act", lambda e: e.activation(fl(dm2), fl(dm2), AF.Exp, scale=-1.0), reads=[K("dm2")], writes=[K("dm2")])
            yield
            cx.op("dve", lambda e: e.tensor_tensor(out=fl(dmT), in0=fl(dmT), in1=fl(minc4), op=ALU.mult),
                  reads=[K("dmT")], writes=[K("dmT")])
            cx.op("dve", lambda e: e.tensor_tensor(out=fl(dm2), in0=fl(dm2), in1=fl(mlow4), op=ALU.mult),
                  reads=[K("dm2")], writes=[K("dm2")])
            yield
            for hl in range(4):
                h = hs[hl]
                khl = hl // 2
                cx.op("dve", lambda e, hl=hl, h=h, khl=khl: e.scalar_tensor_tensor(
                    out=AO[:, hl, :], in0=ps[B1][:, khl * 128:(khl + 1) * 128], scalar=nbeta[:, ct, h:h + 1],
                    in1=dm2[:, hl, :], op0=ALU.mult, op1=ALU.mult), reads=[pk[B1], K("dm2")], writes=[(P("A"), 2)])
                cx.op("dve", lambda e, hl=hl, khl=khl: e.tensor_tensor(
                    out=qkT[par][:, hl, :], in0=ps[B1][:, (2 + khl) * 128:(3 + khl) * 128], in1=dmT[:, hl, :], op=ALU.mult),
                    reads=[pk[B1], K("dmT")], writes=[P("qkT")])
                cx.op("dve", lambda e, hl=hl, khl=khl: e.tensor_tensor(
                    out=qdT[par][:, hl, :], in0=qT[:, khl, tsl], in1=eB[:, hl, :], op=ALU.mult),
                    reads=[K("eB")], writes=[P("qdT")])
                cx.op("pool", lambda e, hl=hl, h=h: e.tensor_scalar(
                    vb[par][:, hl, :], vtok[:, ct, hl, :], beta[:, ct, h:h + 1], None, ALU.mult), writes=[P("vb")])
                cx.op("act", lambda e, hl=hl, h=h, khl=khl: e.activation(
                    kb[par][:, hl, :], ktok[:, ct, khl, :], AF.Copy, scale=kbs[:, ct, h:h + 1]), writes=[P("kb")])
                cx.op("pool", lambda e, hl=hl, h=h, khl=khl: e.tensor_scalar(
                    kd[par][:, hl, :], ktok[:, ct, khl, :], egr[:, ct, h:h + 1], None, ALU.mult), writes=[P("kd")])
                yield
            ptv = ps[bA][:, :].bitcast(BF16)

            def mmT(e):
                ins = None
                for hl in range(4):
                    ins = e.transpose(ptv[:, hl * 128:(hl + 1) * 128], AO[:, hl, :], ident[:])
                return ins
            cx.op("pe", mmT, reads=[(P("A"), 2), K("ident")], writes=[pk[bA]])
            yield
            cx.op("act", lambda e: e.activation(fl(AtOp), ptv[:, 0:512], AF.Copy),
                  reads=[pk[bA]], writes=[(P("At"), 2)])
            yield
            cx.op("dve", lambda e: e.tensor_tensor(out=fl(Ttp), in0=fl(AtOp), in1=fl(id4), op=ALU.add),
                  reads=[(P("At"), 2), K("id4")], writes=[P("Tt")])
            yield
            cur = 2
            NSTEP = 6
            for p in range(1, NSTEP + 1):
                nxt = 0 if cur == 2 else 1 - cur
                A, At = bufs[cur]
                An, Atn = bufs[nxt]
                last = (p == NSTEP)

                def mmsq(e, A=A, At=At, last=last):
                    ins = None
                    for hl in range(4):
                        ins = e.matmul(ps[bA][:, hl * 128:(hl + 1) * 128], At[:, hl, :], A[:, hl, :], start=True, stop=True)
                    if not last:
                        for hl in range(4):
                            ins = e.matmul(ps[bAt][:, hl * 128:(hl + 1) * 128], A[:, hl, :], At[:, hl, :],
                                           start=True, stop=True)
                    return ins
                cx.op("pe", mmsq, reads=[(P("A"), cur), (P("At"), cur)], writes=[pk[bA]] + ([] if last else [pk[bAt]]))
                yield
                cx.op("act", lambda e, An=An: e.activation(fl(An), ps[bA][:], AF.Copy),
                      reads=[pk[bA]], writes=[(P("A"), nxt)])
                if not last:
                    cx.op("dve", lambda e, Atn=Atn: e.tensor_copy(fl(Atn), ps[bAt][:]),
                          reads=[pk[bAt]], writes=[(P("At"), nxt)])
                yield

                def mmtu(e, An=An):
                    ins = None
                    for hl in range(4):
                        ins = e.matmul(ps[bTu][:, hl * 128:(hl + 1) * 128], An[:, hl, :], Ttp[:, hl, :], start=True, stop=True)
                    return ins
                cx.op("pe", mmtu, reads=[(P("A"), nxt), P("Tt")], writes=[pk[bTu]])
                yield
                cx.op("dve", lambda e: e.tensor_tensor(out=fl(Ttp), in0=fl(Ttp), in1=ps[bTu][:], op=ALU.add),
                      reads=[pk[bTu], P("Tt")], writes=[P("Tt")])
                cur = nxt
                if p == 3:
                    yield "MID"
                else:
                    yield
            T0, Rb = Atbuf0[par], Abuf0[par]
            ptv3 = ps[bAt][:, :].bitcast(BF16)

            def mmT0(e):
                ins = None
                for hl in range(4):
                    ins = e.transpose(ptv3[:, hl * 128:(hl + 1) * 128], Ttp[:, hl, :], ident[:])
                return ins
            cx.op("pe", mmT0, reads=[P("Tt"), K("ident")], writes=[pk[bAt]])
            yield
            cx.op("act", lambda e: e.activation(fl(T0), ptv3[:, 0:512], AF.Copy),
                  reads=[pk[bAt]], writes=[(P("At"), 0)])
            yield

            def mmR(e):
                ins = None
                for hl in range(4):
                    e.matmul(ps[bA][:, hl * 128:(hl + 1) * 128], AtOp[:, hl, :], T0[:, hl, :], start=True, stop=False)
                    ins = e.matmul(ps[bA][:, hl * 128:(hl + 1) * 128], ident[:], ident[:], start=False, stop=True)
                return ins
            cx.op("pe", mmR, reads=[(P("At"), 2), (P("At"), 0), K("ident")], writes=[pk[bA]])
            yield
            cx.op("dve", lambda e: e.scalar_tensor_tensor(out=fl(Rb), in0=fl(T0), scalar=-1.0, in1=ps[bA][:],
                                                           op0=ALU.mult, op1=ALU.add),
                  reads=[(P("At"), 0), pk[bA]], writes=[(P("A"), 0)])
            yield

            def mmU(e):
                ins = None
                for hl in range(4):
                    ins = e.matmul(ps[bTu][:, hl * 128:(hl + 1) * 128], Rb[:, hl, :], Ttp[:, hl, :], start=True, stop=True)
                return ins
            cx.op("pe", mmU, reads=[(P("A"), 0), P("Tt")], writes=[pk[bTu]])
            yield
            cx.op("dve", lambda e: e.tensor_tensor(out=fl(Ttp), in0=fl(Ttp), in1=ps[bTu][:], op=ALU.add),
                  reads=[pk[bTu], P("Tt")], writes=[P("Tt")])
            yield

            def mmuw(e):
                ins = None
                for hl in range(4):
                    e.matmul(ps[bA][:, hl * 128:(hl + 1) * 128], Ttp[:, hl, :], vb[par][:, hl, :], start=True, stop=True)
                    ins = e.matmul(ps[bAt][:, hl * 128:(hl + 1) * 128], kb[par][:, hl, :], Ttp[:, hl, :], start=True, stop=True)
                return ins
            cx.op("pe", mmuw, reads=[P("Tt"), P("vb"), P("kb")], writes=[pk[bA], pk[bAt]])
            yield
            cx.op("dve", lambda e: e.tensor_copy(fl(u), ps[bA][:]), reads=[pk[bA]], writes=[K("u")])
            cx.op("act", lambda e: e.activation(fl(wT), ps[bAt][:], AF.Copy), reads=[pk[bAt]], writes=[K("wT")])
            yield

            def mmvn(e):
                ins = None
                for hl in range(4):
                    ins = e.matmul(ps[bTu][:, hl * 128:(hl + 1) * 128], wT[:, hl, :], S16[:, hl, :], start=True, stop=True)
                return ins
            cx.op("pe", mmvn, reads=[K("wT"), K("S16")], writes=[pk[bTu]])
            yield
            cx.op("dve", lambda e: e.tensor_tensor(out=fl(vnew), in0=fl(u), in1=ps[bTu][:], op=ALU.subtract),
                  reads=[K("u"), pk[bTu]], writes=[K("vnew")])
            yield

            def mmo(e):
                ins = None
                for hl in range(4):
                    e.matmul(ps[bA][:, hl * 128:(hl + 1) * 128], S16[:, hl, :], qdT[par][:, hl, :], start=True, stop=False)
                    e.matmul(ps[bA][:, hl * 128:(hl + 1) * 128], vnew[:, hl, :], qkT[par][:, hl, :], start=False, stop=True)
                for hl in range(4):
                    ins = e.matmul(ps[bAt][:, hl * 128:(hl + 1) * 128], kd[par][:, hl, :], vnew[:, hl, :], start=True, stop=True)
                return ins
            cx.op("pe", mmo, reads=[K("S16"), P("qdT"), K("vnew"), P("qkT"), P("kd")], writes=[pk[bA], pk[bAt]])
            yield
            ob = ct % 2
            for hl in range(4):
                h = hs[hl]
                cx.op("dve", lambda e, hl=hl, h=h: e.scalar_tensor_tensor(
                    out=S32[:, hl, :], in0=S32[:, hl, :], scalar=cd[:, ct, h:h + 1], in1=ps[bAt][:, hl * 128:(hl + 1) * 128],
                    op0=ALU.mult, op1=ALU.add), reads=[pk[bAt], K("S32")], writes=[K("S32")])
            cx.op("act", lambda e: e.activation(fl(S16), fl(S32), AF.Copy), reads=[K("S32")], writes=[K("S16")])
            cx.op("act", lambda e: e.activation(fl(ost[ob]), ps[bA][:], AF.Copy),
                  reads=[pk[bA]], writes=[(K("ost"), ob)])
            for hl in range(4):
                h = hs[hl]
                cx.dma("sp", oraw_d[h * 128:(h + 1) * 128, ct * 128:(ct + 1) * 128], ost[ob][:, hl, :],
                       reads=[(K("ost"), ob)], writes=[(okey, h, ct)], skey=("ost", ob, hl))
            yield

        ntile = DBG.get('tiles', NTILE)
        active, nxt_t = [], [0]

        def start():
            if nxt_t[0] < ntile and len(active) < 2:
                active.append(g2_tile(nxt_t[0], list(hs)))
                nxt_t[0] += 1
        start()
        while active:
            for g in list(active):
                try:
                    v = next(g)
                except StopIteration:
                    active.remove(g)
                    start()
                    continue
                if v == "MID":
                    start()
        cx.barrier()
        cx.pop()
    cx.barrier()
    cx.pop()


def gdn_out_phase(cx, c, tag, x_in, x_out, oraw_d, nw_d, gnw_d, w_in, w_out, xkey_in, xkey_out, okey):
    cx.push()
    HT = 2048
    NT = HT // TT
    WB = 256
    nheads = NVH
    K = lambda n: tag + n
    hT = cx.sbuf(tag + "hT", [128, KC, HT], BF16)
    osb = cx.sbuf(tag + "osb", [128, nheads, HT], BF16)
    gnw = cx.sbuf(tag + "gnw", [128, 1], F32)
    nw = cx.sbuf(tag + "nw", [128, KC], F32)
    xt = [cx.sbuf(tag + "xt%d" % i, [128, KC, TT], F32) for i in range(1)]
    sq = cx.sbuf(tag + "sq", [128, KC, TT], BF16)
    rstd = cx.sbuf(tag + "rstd", [128, TT], F32)
    wz = [cx.sbuf(tag + "wz%d" % i, [128, KC, 128], BF16) for i in range(2)]
    orw = [cx.sbuf(tag + "orw%d" % i, [128, TT], F32) for i in range(3)]
    osq = [cx.sbuf(tag + "osq%d" % i, [128, TT], BF16) for i in range(3)]
    orn = [cx.sbuf(tag + "orn%d" % i, [128, TT], F32) for i in range(3)]
    sz = [cx.sbuf(tag + "sz%d" % i, [128, TT], F32) for i in range(3)]
    wo = [cx.sbuf(tag + "wo%d" % i, [128, nheads, WB], BF16) for i in range(2)]
    xr = [cx.sbuf(tag + "xr%d" % i, [128, TT], F32) for i in range(2)]
    xn = [cx.sbuf(tag + "xn%d" % i, [128, TT], F32) for i in range(2)]
    ps = [cx.psum(tag + "ps%d" % i, [128, TT], F32) for i in range(8)]
    pk = [tag + "ps%d" % i for i in range(8)]
    cx.psum_keys.update(pk)
    w_in_r = w_in.rearrange("(kc p) n -> p kc n", p=128)
    w_out_r = w_out.rearrange("(kc p) n -> p kc n", p=128)
    x_in_r = x_in.rearrange("(kc p) t -> p kc t", p=128)
    cx.dma("sp", nw[:], nw_d, writes=[K("nw")], skey="nw")
    cx.dma("sp", gnw[:], gnw_d, writes=[K("gnw")], skey="gnw")
    cnt = {"o": 0, "x": 0, "z": 0, "w": 0}
    def norm_tile_ops(hf_, tt, part=None):
        t0_ = hf_ * HT
        gt_ = (t0_ + tt * TT) // TT
        if part in (None, "A"):
            cx.dma("sp", xt[0][:], x_in_r[:, :, t0_ + tt * TT: t0_ + (tt + 1) * TT],
                   reads=[(xkey_in, gt_, kc) for kc in range(KC)], writes=[K("xt")], skey="xt")
        rmsnorm_tile(cx, c, xt[0], nw, lambda kc, tt=tt: hT[:, kc, tt * TT:(tt + 1) * TT], tag,
                     [K("xt")], (K("hT"), tt), pk[5], ps[5], {"sq": sq, "rstd": rstd}, part=part)

    for hf in range(T // HT):
        t0 = hf * HT
        if hf == 0:
            for tt in range(NT):
                norm_tile_ops(0, tt)
        def gate_piece(h, ws, tt, item):
            b = item % 3
            bZ, bQ = b, 3 + b
            gt_ = (t0 + tt * TT) // TT
            cx.dma("sp", orw[b][:], oraw_d[h * 128:(h + 1) * 128, t0 + tt * TT: t0 + (tt + 1) * TT],
                   reads=[(okey, h, gt_)], writes=[(K("orw"), b)], skey=("orw", b))

            def mmz(e):
                ins = None
                for kc in range(KC):
                    ins = e.matmul(ps[bZ][:], wz[ws][:, kc, :], hT[:, kc, tt * TT:(tt + 1) * TT],
                                   start=(kc == 0), stop=(kc == KC - 1))
                return ins
            cx.op("pe", mmz, reads=[(K("wz"), ws), (K("hT"), tt)], writes=[pk[bZ]])
            yield
            cx.op("act", lambda e: e.activation(osq[b][:], orw[b][:], AF.Square),
                  reads=[(K("orw"), b)], writes=[(K("osq"), b)])
            yield "NEXT"
            cx.op("pe", lambda e: e.matmul(ps[bQ][:], c["ones_bf"][:], osq[b][:], start=True, stop=True),
                  reads=[(K("osq"), b), "ones_bf"], writes=[pk[bQ]])
            yield
            cx.op("act", lambda e: e.activation(orn[b][:], ps[bQ][:], AF.Sqrt, bias=EPS, scale=1.0 / 128),
                  reads=[pk[bQ]], writes=[(K("orn"), b)])
            cx.op("act", lambda e: e.activation(sz[b][:], ps[bZ][:], AF.Silu), reads=[pk[bZ]],
                  writes=[(K("sz"), b)])
            yield
            cx.op("dve", lambda e: e.reciprocal(orn[b][:], orn[b][:]), reads=[(K("orn"), b)],
                  writes=[(K("orn"), b)])
            yield
            cx.op("dve", lambda e: e.scalar_tensor_tensor(
                out=orn[b][:], in0=orw[b][:], scalar=gnw[:, 0:1], in1=orn[b][:], op0=ALU.mult, op1=ALU.mult),
                reads=[(K("orw"), b), (K("orn"), b), K("gnw")], writes=[(K("orn"), b)])
            yield
            cx.op("dve", lambda e: e.tensor_tensor(
                out=osb[:, h, tt * TT:(tt + 1) * TT], in0=orn[b][:], in1=sz[b][:], op=ALU.mult),
                reads=[(K("orn"), b), (K("sz"), b)], writes=[(K("osb"), h)])
            yield

        gitems = [(h, tt) for h in range(nheads) for tt in range(NT)]
        act_g, nx = [], [0]

        def start_g():
            if nx[0] < len(gitems) and len(act_g) < 3:
                h, tt = gitems[nx[0]]
                if tt == 0:
                    ws = cnt["w"] % 2
                    cnt["w"] += 1
                    cx.dma("pool", wz[ws][:], w_in_r[:, :, 4096 + h * 128: 4096 + (h + 1) * 128],
                           writes=[(K("wz"), ws)], skey=("wz", ws))
                ws = (cnt["w"] - 1) % 2
                act_g.append(gate_piece(h, ws, tt, cnt["z"]))
                cnt["z"] += 1
                nx[0] += 1
        start_g()
        while act_g:
            for g_ in list(act_g):
                try:
                    v = next(g_)
                except StopIteration:
                    act_g.remove(g_)
                    start_g()
                    continue
                if v == "NEXT" and g_ is act_g[-1]:
                    start_g()
        for mb in range(D // WB):
            s = cnt["o"] % 2
            cnt["o"] += 1
            cx.dma("pool", wo[s][:], w_out_r[:, :, mb * WB:(mb + 1) * WB], writes=[(K("wo"), s)], skey=("wo", s))
            for mm_ in range(WB // 128):
                mc = mb * (WB // 128) + mm_
                for tt in range(NT):
                    gt_ = (t0 + tt * TT) // TT
                    b = cnt["x"] % 2
                    cnt["x"] += 1
                    py = ps[6 + b]
                    cx.dma("sp", xr[b][:], x_in[mc * 128:(mc + 1) * 128, t0 + tt * TT: t0 + (tt + 1) * TT],
                           reads=[(xkey_in, gt_, mc)], writes=[(K("xr"), b)], skey=("xr", b))

                    def mm2(e, s=s, mm_=mm_, tt=tt, py=py):
                        ins = None
                        for kc in range(nheads):
                            ins = e.matmul(py[:], wo[s][:, kc, mm_ * 128:(mm_ + 1) * 128],
                                           osb[:, kc, tt * TT:(tt + 1) * TT], start=(kc == 0), stop=(kc == nheads - 1))
                        return ins
                    cx.op("pe", mm2, reads=[(K("wo"), s)] + [(K("osb"), hd) for hd in range(nheads)],
                          writes=[pk[6 + b]])
                    cx.op("dve", lambda e, b=b, py=py: e.tensor_tensor(
                        out=xn[b][:], in0=py[:], in1=xr[b][:], op=ALU.add),
                        reads=[pk[6 + b], (K("xr"), b)], writes=[(K("xn"), b)])
                    cx.dma("sp", x_out[mc * 128:(mc + 1) * 128, t0 + tt * TT: t0 + (tt + 1) * TT], xn[b][:],
                           reads=[(K("xn"), b)], writes=[(xkey_out, gt_, mc)], skey=("xn", b))
                if hf + 1 < T // HT:
                    norm_tile_ops(hf + 1, mc // 2, part=("A" if mc % 2 == 0 else "B"))
    cx.barrier()
    cx.pop()


def build_gdn_prog():
    nc = bass.Bass("TRN2", target_bir_lowering=False)
    x = nc.dram_tensor("x", [D, T], F32, kind="ExternalInput").ap()
    nw = nc.dram_tensor("nw", [128, KC], F32, kind="ExternalInput").ap()
    w_in = nc.dram_tensor("w_in", [D, GP], F32, kind="ExternalInput").ap()
    w_out = nc.dram_tensor("w_out", [NVH * 128, D], F32, kind="ExternalInput").ap()
    convw = nc.dram_tensor("convw", [128, 32, 4], F32, kind="ExternalInput").ap()
    alog = nc.dram_tensor("alog", [128, 512], F32, kind="ExternalInput").ap()
    dtb = nc.dram_tensor("dtb", [128, 512], F32, kind="ExternalInput").ap()
    gnw = nc.dram_tensor("gnw", [128, 1], F32, kind="ExternalInput").ap()
    cst = nc.dram_tensor("cst", [128, CST_W], F32, kind="ExternalInput").ap()
    oraw = nc.dram_tensor("oraw_scratch", [NVH * 128, T], F32, kind=("ExternalOutput" if DBG.get("noout") else "Internal")).ap()
    y = nc.dram_tensor("y", [D, T], F32, kind="ExternalOutput").ap()
    cx = Ctx(nc)
    c = consts(cx)
    gdn_core_phase(cx, c, "g_", x, oraw, nw, w_in, convw, alog, dtb, cst, "xin", "or")
    if not DBG.get("noout"):
        gdn_out_phase(cx, c, "go_", x, y, oraw, nw, gnw, w_in, w_out, "xin", "xout", "or")
    cx.finish()
    return nc


def final_norm_phase(cx, c, tag, x_in, y_out, nw_d, xkey_in):
    cx.push()
    xt = [cx.sbuf(tag + "xt%d" % i, [128, KC, TT], F32) for i in range(2)]
    yo = [cx.sbuf(tag + "yo%d" % i, [128, KC, TT], F32) for i in range(2)]
    sq = cx.sbuf(tag + "sq", [128, KC, TT], BF16)
    rstd = cx.sbuf(tag + "rstd", [128, TT], F32)
    nw = cx.sbuf(tag + "nw", [128, KC], F32)
    ps = cx.psum(tag + "ps", [128, TT], F32)
    cx.psum_keys.add(tag + "ps")
    cx.dma("sp", nw[:], nw_d, writes=[tag + "nw"], skey="nw")
    x_in_r = x_in.rearrange("(kc p) t -> p kc t", p=128)
    y_out_r = y_out.rearrange("(kc p) t -> p kc t", p=128)
    for tt in range(NTT):
        b = tt % 2
        cx.dma("sp", xt[b][:], x_in_r[:, :, tt * TT:(tt + 1) * TT],
               reads=[(xkey_in, tt, kc) for kc in range(KC)], writes=[(tag + "xt", b)], skey=("xt", b))
        rmsnorm_tile(cx, c, xt[b], nw, lambda kc, b=b: yo[b][:, kc, :], tag,
                     [(tag + "xt", b)], (tag + "yo", b), tag + "ps", ps, {"sq": sq, "rstd": rstd})
        cx.dma("sp", y_out_r[:, :, tt * TT:(tt + 1) * TT], yo[b][:], reads=[(tag + "yo", b)],
               writes=[("yout", tt)], skey=("yo", b))
    cx.barrier()
    cx.pop()


def build_fin_prog():
    nc = bass.Bass("TRN2", target_bir_lowering=False)
    x = nc.dram_tensor("x", [D, T], F32, kind="ExternalInput").ap()
    nw = nc.dram_tensor("nw", [128, KC], F32, kind="ExternalInput").ap()
    y = nc.dram_tensor("y", [D, T], F32, kind="ExternalOutput").ap()
    cx = Ctx(nc)
    c = consts(cx)
    final_norm_phase(cx, c, "n_", x, y, nw, "xin")
    cx.finish()
    return nc


def build_full_prog():
    nc = bass.Bass("TRN2", target_bir_lowering=False)
    dt = lambda name, shape, kind="ExternalInput", dtype=F32: nc.dram_tensor(name, list(shape), dtype, kind=kind).ap()
    x = dt("x", [D, T])
    nws = dt("nws", [13, 128, KC])
    ffn_w_in = dt("ffn_w_in", [4, 2, D, 2 * DFF])
    ffn_w_out = dt("ffn_w_out", [4, 2, DFF, D])
    attn_w_in = dt("attn_w_in", [2, D, 3 * NH_A * 128])
    attn_w_out = dt("attn_w_out", [2, NH_A * 128, D])
    gdn_w_in = dt("gdn_w_in", [2, D, GP])
    gdn_w_out = dt("gdn_w_out", [2, NVH * 128, D])
    convw = dt("convw", [2, 128, 32, 4])
    alog = dt("alog", [2, 128, 512])
    dtb = dt("dtb", [2, 128, 512])
    gnw = dt("gnw", [2, 128, 1])
    ctab = dt("ctab", [128, T])
    stab = dt("stab", [128, T])
    cst = dt("cst", [128, CST_W])
    y = dt("y", [D, T], kind="ExternalOutput")
    xbuf = [nc.dram_tensor("xres%d" % i, [D, T], F32).ap() for i in range(2)]
    os_d = nc.dram_tensor("os_scratch", [NH_A * 128, T], BF16).ap()
    oraw = nc.dram_tensor("oraw_scratch", [NVH * 128, T], F32).ap()
    cx = Ctx(nc)
    c = consts(cx)
    cur, nb_, ph = x, 0, 0
    ia = ib = 0
    for i in range(4):
        ffn_phase(cx, c, "f%da_" % i, cur, xbuf[nb_], nws[ph], ffn_w_in[i, 0], ffn_w_out[i, 0], "xi", "xo")
        cur, nb_, ph = xbuf[nb_], 1 - nb_, ph + 1
        if i % 2 == 0:
            attn_phase(cx, c, "a%d_" % i, cur, os_d, nws[ph], attn_w_in[ia], ctab, stab, cst, "xi", "os")
            outproj_phase(cx, c, "ao%d_" % i, cur, xbuf[nb_], os_d, attn_w_out[ia], NH_A, "xi", "xo", "os")
            ia += 1
        else:
            gdn_core_phase(cx, c, "g%d_" % i, cur, oraw, nws[ph], gdn_w_in[ib], convw[ib], alog[ib], dtb[ib], cst, "xi", "or")
            gdn_out_phase(cx, c, "go%d_" % i, cur, xbuf[nb_], oraw, nws[ph], gnw[ib], gdn_w_in[ib], gdn_w_out[ib],
                          "xi", "xo", "or")
            ib += 1
        cur, nb_, ph = xbuf[nb_], 1 - nb_, ph + 1
        ffn_phase(cx, c, "f%db_" % i, cur, xbuf[nb_], nws[ph], ffn_w_in[i, 1], ffn_w_out[i, 1], "xi", "xo")
        cur, nb_, ph = xbuf[nb_], 1 - nb_, ph + 1
    final_norm_phase(cx, c, "n_", cur, y, nws[ph], "xi")
    cx.finish()
    return nc


_PROGS = {}


def _prog(name):
    if name not in _PROGS:
        _PROGS[name] = {"ffn": build_ffn_prog, "attn": build_attn_prog, "gdn": build_gdn_prog,
                        "fin": build_fin_prog}[name]()
    return _PROGS[name]


def _lay_nw(v):
    return np.ascontiguousarray(np.asarray(v, np.float32).reshape(KC, 128).T)


def _launch(name, xs, shared):
    nc = _prog(name)
    n = len(xs)
    in_maps = [dict(shared, x=xs[i]) for i in range(n)]
    res = run_bass_kernel_spmd(nc, in_maps, core_ids=list(range(n)))
    return [np.asarray(res.results[i]["y"]) for i in range(n)]


def kernel(x, norm_w, ffn_w_in, ffn_w_out, attn_w_in, attn_w_out, gdn_w_in, gdn_conv_w, gdn_a_log,
           gdn_dt_bias, gdn_norm_w, gdn_w_out, final_norm_w):
    f = lambda a: np.ascontiguousarray(np.asarray(a, np.float32))
    x = f(x)
    B = x.shape[0]
    ctab, stab = host_rope()
    nws = np.stack([_lay_nw(norm_w[i, j]) for i in range(4) for j in range(3)] + [_lay_nw(final_norm_w)], 0)
    shared = {
        "nws": np.ascontiguousarray(nws), "ffn_w_in": f(ffn_w_in), "ffn_w_out": f(ffn_w_out),
        "attn_w_in": f(attn_w_in), "attn_w_out": f(attn_w_out), "gdn_w_in": f(gdn_w_in), "gdn_w_out": f(gdn_w_out),
        "convw": np.ascontiguousarray(f(gdn_conv_w).reshape(2, 4, 32, 128).transpose(0, 3, 2, 1)),
        "alog": np.ascontiguousarray(np.tile(f(gdn_a_log)[:, None, :], (1, 128, 32))),
        "dtb": np.ascontiguousarray(np.tile(f(gdn_dt_bias)[:, None, :], (1, 128, 32))),
        "gnw": np.ascontiguousarray(f(gdn_norm_w).reshape(2, 128, 1)),
        "ctab": ctab, "stab": stab, "cst": host_consts(),
    }
    if "full" not in _PROGS:
        _PROGS["full"] = build_full_prog()
    nc = _PROGS["full"]
    in_maps = [dict(shared, x=np.ascontiguousarray(x[b].T)) for b in range(B)]
    res = run_bass_kernel_spmd(nc, in_maps, core_ids=list(range(B)))
    return np.stack([np.asarray(res.results[b]["y"]).T for b in range(B)], 0).astype(np.float32)
```
